# Optimizing a Trainium2 kernel written in Bass

```python
import math
import jax, jax.numpy as jnp
from jax import lax
import numpy as np

D_MODEL = 1024
BATCH = 8
SEQ = 2048
DEPTH = 4
DEC_BATCH = 128
DEC_SEQ = 8
PAST_LEN = 16384
PAGE_SIZE = 128

MIX_WIDTH = D_MODEL
CONV_CH = MIX_WIDTH // 2
RWKV_DIM = MIX_WIDTH - CONV_CH
RWKV_HEAD = 64
RWKV_HEADS = RWKV_DIM // RWKV_HEAD
CONV_WIDTH = 31
CONV_BUF = CONV_WIDTH - 1
DECAY_RANK = 64
ICLR_RANK = 64
GATE_RANK = 128
RWKV_COLS = 3 * RWKV_DIM + DECAY_RANK + ICLR_RANK + GATE_RANK
IN_COLS = 2 * CONV_CH + RWKV_COLS
RWKV_SPLITS = (RWKV_DIM, 2 * RWKV_DIM, 3 * RWKV_DIM,
               3 * RWKV_DIM + DECAY_RANK, 3 * RWKV_DIM + DECAY_RANK + ICLR_RANK)
N_MEM = 256
MEM_HEADS = 4
MEM_HEAD_DIM = D_MODEL // MEM_HEADS
D_FF = 4 * D_MODEL
RMS_EPS = 1e-6
LN_EPS = 1e-5
GN_EPS = 64e-5

kernel_name = "hymba_conformer_rwkv7_memxattn_step"


def rmsnorm(x, g):
    xf = x.astype(jnp.float32)
    y = xf * lax.rsqrt(jnp.mean(xf * xf, axis=-1, keepdims=True) + RMS_EPS)
    return (y * g.astype(jnp.float32)).astype(x.dtype)


def layernorm(x, g, b):
    xf = x.astype(jnp.float32)
    mu = jnp.mean(xf, axis=-1, keepdims=True)
    var = jnp.mean(jnp.square(xf - mu), axis=-1, keepdims=True)
    y = (xf - mu) * lax.rsqrt(var + LN_EPS)
    return (y * g.astype(jnp.float32) + b.astype(jnp.float32)).astype(x.dtype)


def head_groupnorm(o, g, b):
    mu = jnp.mean(o, axis=-1, keepdims=True)
    var = jnp.mean(jnp.square(o - mu), axis=-1, keepdims=True)
    y = (o - mu) * lax.rsqrt(var + GN_EPS)
    return (y * g.astype(jnp.float32).reshape(RWKV_HEADS, RWKV_HEAD)
            + b.astype(jnp.float32).reshape(RWKV_HEADS, RWKV_HEAD))


def conformer_conv_group(z_val, z_gate, conv_buf, conv_w, conv_b, ln_g, ln_b):
    u = z_val * jax.nn.sigmoid(z_gate)
    u_ext = jnp.concatenate([conv_buf.astype(u.dtype), u], axis=1)
    y = lax.conv_general_dilated(
        u_ext, conv_w[:, None, :].astype(u.dtype), window_strides=(1,), padding='VALID',
        dimension_numbers=('NWC', 'WIO', 'NWC'), feature_group_count=CONV_CH)
    y = y + conv_b.astype(u.dtype)
    y = jax.nn.silu(layernorm(y, ln_g, ln_b))
    return y, u_ext[:, -CONV_BUF:]


def rwkv7_recurrence(S0, r, w, k, v, kk, a):
    def step(S, inp):
        r_t, w_t, k_t, v_t, kk_t, a_t = inp
        sa = jnp.einsum('bhij,bhj->bhi', S, -kk_t)
        S = (S * w_t[:, :, None, :]
             + sa[..., None] * (kk_t * a_t)[:, :, None, :]
             + v_t[..., None] * k_t[:, :, None, :])
        o_t = jnp.einsum('bhij,bhj->bhi', S, r_t)
        return S, o_t
    xs = tuple(jnp.moveaxis(t, 1, 0) for t in (r, w, k, v, kk, a))
    S, o = lax.scan(step, S0, xs)
    return jnp.moveaxis(o, 0, 1), S


def rwkv7_group(zr, zr_prev, S0, mu, w0, w_up, a0, a_up, g_up, k_k, k_a, r_k, gn_g, gn_b):
    f32 = jnp.float32
    zr = zr.astype(f32)
    zs = zr + (zr_prev.astype(f32) - zr) * mu.astype(f32)
    r, k, v, dw, da, dg = jnp.split(zs, RWKV_SPLITS, axis=-1)
    w_raw = -jax.nn.softplus(-(w0.astype(f32) + jnp.tanh(dw) @ w_up.astype(f32))) - 0.5
    decay = jnp.exp(-jnp.exp(w_raw))
    a = jax.nn.sigmoid(a0.astype(f32) + da @ a_up.astype(f32))
    gate = jax.nn.sigmoid(dg) @ g_up.astype(f32)
    B, T = zr.shape[0], zr.shape[1]
    hs = lambda t: t.reshape(B, T, RWKV_HEADS, RWKV_HEAD)
    r, k, v, decay, a = hs(r), hs(k), hs(v), hs(decay), hs(a)
    kk = k * k_k.astype(f32).reshape(RWKV_HEADS, RWKV_HEAD)
    kk = kk / jnp.maximum(jnp.sqrt(jnp.sum(kk * kk, axis=-1, keepdims=True)), 1e-12)
    k = k * (1.0 + (a - 1.0) * k_a.astype(f32).reshape(RWKV_HEADS, RWKV_HEAD))
    o, S = rwkv7_recurrence(S0.astype(f32), r, decay, k, v, kk, a)
    bonus = jnp.sum(r * k * r_k.astype(f32), axis=-1, keepdims=True) * v
    o = head_groupnorm(o, gn_g, gn_b) + bonus
    return o.reshape(B, T, RWKV_DIM) * gate, S


def hybrid_mixer(l, h, h_prev, conv_buf, S0, P):
    w_in = P['w_in'][l]
    z = h @ w_in
    z_val, z_gate, zr = z[..., :CONV_CH], z[..., CONV_CH:2 * CONV_CH], z[..., 2 * CONV_CH:]
    zr_prev_row = h_prev.astype(h.dtype) @ w_in[:, 2 * CONV_CH:]
    zr_prev = jnp.concatenate([zr_prev_row[:, None, :], zr[:, :-1]], axis=1)
    conv_out, new_conv = conformer_conv_group(
        z_val, z_gate, conv_buf, P['conv_w'][l], P['conv_b'][l],
        P['conv_ln_g'][l], P['conv_ln_b'][l])
    rwkv_out, new_S = rwkv7_group(
        zr, zr_prev, S0, P['mu_shift'][l], P['rwkv_w0'][l], P['rwkv_w_up'][l],
        P['rwkv_a0'][l], P['rwkv_a_up'][l], P['rwkv_g_up'][l], P['rwkv_k_k'][l],
        P['rwkv_k_a'][l], P['rwkv_r_k'][l], P['rwkv_gn_g'][l], P['rwkv_gn_b'][l])
    mixed = jnp.concatenate([conv_out, rwkv_out.astype(h.dtype)], axis=-1) @ P['w_out'][l]
    return mixed, new_conv, new_S, h[:, -1]


def mem_cross_attn(h, mk, mv, wq, wo):
    B, T = h.shape[0], h.shape[1]
    q = (h @ wq).reshape(B, T, MEM_HEADS, MEM_HEAD_DIM)
    s = jnp.einsum('bqhd,bkhd->bhqk', q, mk.astype(h.dtype)).astype(jnp.float32)
    p = jax.nn.softmax(s * (MEM_HEAD_DIM ** -0.5), axis=-1).astype(h.dtype)
    o = jnp.einsum('bhqk,bkhd->bqhd', p, mv.astype(h.dtype)).reshape(B, T, D_MODEL)
    return o @ wo


def trunk_layer(l, x, shift_prev, conv_buf, S0, mk, mv, P):
    g = P['norm_g'][l]
    h = rmsnorm(x, g[0])
    m, new_conv, new_S, new_shift = hybrid_mixer(l, h, shift_prev, conv_buf, S0, P)
    x = x + rmsnorm(m, g[1])
    h = rmsnorm(x, g[2])
    x = x + rmsnorm(mem_cross_attn(h, mk, mv, P['w_q'][l], P['w_o'][l]), g[3])
    h = rmsnorm(x, g[4])
    f = jnp.square(jax.nn.relu(h @ P['w_ffn1'][l])) @ P['w_ffn2'][l]
    x = x + rmsnorm(f, g[5])
    return x, new_shift, new_conv, new_S


def setup_inputs(seed: int = 0) -> dict:
    key = jax.random.key(seed)
    ks = jax.random.split(key, 40)
    f32 = jnp.float32
    nrm = lambda k, shape, scale: jax.random.normal(k, shape, f32) * scale
    return {
        "x_prompt": nrm(ks[0], (BATCH, SEQ, D_MODEL), 1.0),
        "x_sample": nrm(ks[1], (DEC_BATCH, DEC_SEQ, D_MODEL), 1.0),
        "cache_mem_k": nrm(ks[2], (DEPTH, DEC_BATCH, N_MEM, MEM_HEADS, MEM_HEAD_DIM), 1.0),
        "cache_mem_v": nrm(ks[3], (DEPTH, DEC_BATCH, N_MEM, MEM_HEADS, MEM_HEAD_DIM), 1.0),
        "state_wkv": nrm(ks[4], (DEPTH, DEC_BATCH, RWKV_HEADS, RWKV_HEAD, RWKV_HEAD), 0.5),
        "state_conv": nrm(ks[5], (DEPTH, DEC_BATCH, CONV_BUF, CONV_CH), 0.5),
        "state_shift": nrm(ks[6], (DEPTH, DEC_BATCH, D_MODEL), 1.0),
        "mem_prompt": nrm(ks[7], (BATCH, N_MEM, D_MODEL), 1.0),
        "w_in": nrm(ks[8], (DEPTH, D_MODEL, IN_COLS), D_MODEL ** -0.5),
        "mu_shift": jax.random.uniform(ks[9], (DEPTH, RWKV_COLS), f32, 0.1, 0.9),
        "conv_w": nrm(ks[10], (DEPTH, CONV_WIDTH, CONV_CH), CONV_WIDTH ** -0.5),
        "conv_b": nrm(ks[11], (DEPTH, CONV_CH), 0.01),
        "conv_ln_g": 1.0 + nrm(ks[12], (DEPTH, CONV_CH), 0.02),
        "conv_ln_b": nrm(ks[13], (DEPTH, CONV_CH), 0.01),
        "rwkv_w0": -1.0 + nrm(ks[14], (DEPTH, RWKV_DIM), 0.5),
        "rwkv_w_up": nrm(ks[15], (DEPTH, DECAY_RANK, RWKV_DIM), 0.1),
        "rwkv_a0": nrm(ks[16], (DEPTH, RWKV_DIM), 0.1),
        "rwkv_a_up": nrm(ks[17], (DEPTH, ICLR_RANK, RWKV_DIM), ICLR_RANK ** -0.5),
        "rwkv_g_up": nrm(ks[18], (DEPTH, GATE_RANK, RWKV_DIM), GATE_RANK ** -0.5),
        "rwkv_k_k": 0.85 + nrm(ks[19], (DEPTH, RWKV_DIM), 0.02),
        "rwkv_k_a": 1.0 + nrm(ks[20], (DEPTH, RWKV_DIM), 0.02),
        "rwkv_r_k": nrm(ks[21], (DEPTH, RWKV_HEADS, RWKV_HEAD), 0.1),
        "rwkv_gn_g": 1.0 + nrm(ks[22], (DEPTH, RWKV_DIM), 0.02),
        "rwkv_gn_b": nrm(ks[23], (DEPTH, RWKV_DIM), 0.01),
        "w_out": nrm(ks[24], (DEPTH, MIX_WIDTH, D_MODEL), MIX_WIDTH ** -0.5),
        "mem_norm_g": 1.0 + nrm(ks[25], (D_MODEL,), 0.02),
        "w_q": nrm(ks[26], (DEPTH, D_MODEL, D_MODEL), D_MODEL ** -0.5),
        "w_k": nrm(ks[27], (DEPTH, D_MODEL, D_MODEL), D_MODEL ** -0.5),
        "w_v": nrm(ks[28], (DEPTH, D_MODEL, D_MODEL), D_MODEL ** -0.5),
        "w_o": nrm(ks[29], (DEPTH, D_MODEL, D_MODEL), D_MODEL ** -0.5),
        "w_ffn1": nrm(ks[30], (DEPTH, D_MODEL, D_FF), D_MODEL ** -0.5),
        "w_ffn2": nrm(ks[31], (DEPTH, D_FF, D_MODEL), D_FF ** -0.5),
        "norm_g": 1.0 + nrm(ks[32], (DEPTH, 6, D_MODEL), 0.02),
    }


def reference(x_prompt, x_sample, cache_mem_k, cache_mem_v, state_wkv, state_conv, state_shift,
              mem_prompt, w_in, mu_shift, conv_w, conv_b, conv_ln_g, conv_ln_b,
              rwkv_w0, rwkv_w_up, rwkv_a0, rwkv_a_up, rwkv_g_up, rwkv_k_k, rwkv_k_a, rwkv_r_k,
              rwkv_gn_g, rwkv_gn_b, w_out, mem_norm_g, w_q, w_k, w_v, w_o, w_ffn1, w_ffn2, norm_g):
    P = dict(w_in=w_in, mu_shift=mu_shift, conv_w=conv_w, conv_b=conv_b, conv_ln_g=conv_ln_g,
             conv_ln_b=conv_ln_b, rwkv_w0=rwkv_w0, rwkv_w_up=rwkv_w_up, rwkv_a0=rwkv_a0,
             rwkv_a_up=rwkv_a_up, rwkv_g_up=rwkv_g_up, rwkv_k_k=rwkv_k_k, rwkv_k_a=rwkv_k_a,
             rwkv_r_k=rwkv_r_k, rwkv_gn_g=rwkv_gn_g, rwkv_gn_b=rwkv_gn_b, w_out=w_out,
             w_q=w_q, w_o=w_o, w_ffn1=w_ffn1, w_ffn2=w_ffn2, norm_g=norm_g)

    Bp = x_prompt.shape[0]
    mem_n = rmsnorm(mem_prompt, mem_norm_g)
    shift0 = jnp.zeros((Bp, D_MODEL), x_prompt.dtype)
    conv0 = jnp.zeros((Bp, CONV_BUF, CONV_CH), x_prompt.dtype)
    wkv0 = jnp.zeros((Bp, RWKV_HEADS, RWKV_HEAD, RWKV_HEAD), jnp.float32)
    x = x_prompt
    mk_p, mv_p, wkv_p, conv_p, shift_p = [], [], [], [], []
    for l in range(DEPTH):
        mk = (mem_n @ w_k[l]).reshape(Bp, N_MEM, MEM_HEADS, MEM_HEAD_DIM)
        mv = (mem_n @ w_v[l]).reshape(Bp, N_MEM, MEM_HEADS, MEM_HEAD_DIM)
        x, s_shift, s_conv, s_wkv = trunk_layer(l, x, shift0, conv0, wkv0, mk, mv, P)
        mk_p.append(mk); mv_p.append(mv); wkv_p.append(s_wkv)
        conv_p.append(s_conv); shift_p.append(s_shift)
    y_prompt = x

    x = x_sample
    wkv_s, conv_s, shift_s = [], [], []
    for l in range(DEPTH):
        x, s_shift, s_conv, s_wkv = trunk_layer(
            l, x, state_shift[l], state_conv[l], state_wkv[l], cache_mem_k[l], cache_mem_v[l], P)
        wkv_s.append(s_wkv); conv_s.append(s_conv); shift_s.append(s_shift)
    y_sample = x

    new_mem_k_prompt = jnp.stack(mk_p)
    new_mem_v_prompt = jnp.stack(mv_p)
    new_wkv_prompt = jnp.stack(wkv_p)
    new_conv_prompt = jnp.stack(conv_p)
    new_shift_prompt = jnp.stack(shift_p)
    new_wkv_sample = jnp.stack(wkv_s)
    new_conv_sample = jnp.stack(conv_s)
    new_shift_sample = jnp.stack(shift_s)
    return (y_prompt, y_sample, new_mem_k_prompt, new_mem_v_prompt, new_wkv_prompt,
            new_conv_prompt, new_shift_prompt, new_wkv_sample, new_conv_sample, new_shift_sample)
```

```python
import numpy as np
import concourse.bass as bass
import concourse.mybir as mybir
from concourse.bass_utils import run_bass_kernel_spmd

F32 = mybir.dt.float32
BF16 = mybir.dt.bfloat16
AF = mybir.ActivationFunctionType
ALU = mybir.AluOpType
AX = mybir.AxisListType

DEPTH = 4
NCORE = 8
CDEC = 0.6065306597126334

PP_NG, PP_MG, PP_CW, PP_CB, PP_LG, PP_LB, PP_MUD, NPP = 0, 192, 200, 696, 712, 728, 744, 752
C_ID, C_MU, C_MI, C_ML, C_MUS, C_MIS, C_MLS, C_SEQ, C_EPS, NCON = 0, 128, 256, 384, 512, 640, 768, 896, 912, 920
NBC = 2560 + 1792

ENGS = ['pe', 'act', 'dve', 'pool', 'sp']
HOLE_LO, HOLE_HI = 207 * 256, 207 * 256
NDSEM = 90
NSW = 45
DBGV = {}
DBG = dict(layers=4, halves=2, att=True, ffn=True, samp=True, stop=99, mix=True, rwkv=True)


class Hd:
    __slots__ = ('name', 'w', 'r', 'excl')

    def __init__(self, name, excl=False):
        self.name = name
        self.w = None
        self.r = []
        self.excl = excl


class Emitter:
    def __init__(self):
        self.ops = []
        self.handles = []
        self.last_compute = {e: None for e in ENGS}
        self.out_dma = []
        self.group = None
        self.ngroups_phase = 0
        self.groups = []

    def H(self, name, excl=False):
        h = Hd(name, excl)
        self.handles.append(h)
        return h

    def op(self, eng, fn, r=(), w=(), dma=False):
        i = len(self.ops)
        deps = set()
        w = list(w) + [h for h in r if h.excl]
        r = [h for h in r if not h.excl]
        for h in r:
            if h.w is not None:
                deps.add(h.w)
        for h in w:
            if h.w is not None:
                deps.add(h.w)
            deps.update(h.r)
        for h in r:
            h.r.append(i)
        for h in w:
            h.w = i
            h.r = []
        deps.discard(i)
        gid = None
        if dma:
            if self.group is not None:
                gid = self.group
            else:
                gid = len(self.groups)
                self.groups.append([])
                self.ngroups_phase += 1
            if self.group is not None:
                deps = set(d for d in deps if not (self.ops[d]['dma'] and self.ops[d]['gid'] == gid))
            self.groups[gid].append(i)
            self.out_dma.append(i)
        else:
            self.last_compute[eng] = i
        self.ops.append(dict(eng=eng, fn=fn, deps=deps, dma=dma, gid=gid, sig=False))
        return i

    def begin_group(self):
        self.group = len(self.groups)
        self.groups.append([])
        self.ngroups_phase += 1

    def end_group(self):
        self.group = None

    def barrier(self):
        deps = set(v for v in self.last_compute.values() if v is not None) | set(self.out_dma)
        for e in ENGS:
            self.ops.append(dict(eng=e, fn=None, deps=set(deps), dma=False, gid=None, sig=False, bar=True))
        for h in self.handles:
            h.w = None
            h.r = []
        self.out_dma = []
        self.ngroups_phase = 0
        self.ops[-1]['phase_end'] = True

    def finalize(self):
        ops = self.ops
        for j, o in enumerate(ops):
            for d in o['deps']:
                p = ops[d]
                if p['dma']:
                    continue
                if p['eng'] == 'pe' and o['eng'] == 'pe' and not o.get('bar'):
                    continue
                p['sig'] = True
        cnt = {e: 0 for e in ENGS}
        for o in ops:
            if o['sig']:
                cnt[o['eng']] += 1
                o['sigval'] = cnt[o['eng']]
        gsem = {}
        semcnt = [0] * NDSEM
        nxt = {'pool': 0, 'sp': NSW}
        lim = {'pool': NSW, 'sp': NDSEM}
        gfinal = {}
        for j, o in enumerate(ops):
            if o['dma']:
                g = o['gid']
                q = o['eng']
                if g not in gsem:
                    gsem[g] = nxt[q]
                    nxt[q] += 1
                    assert nxt[q] <= lim[q], (q, nxt[q])
                k = gsem[g]
                semcnt[k] += 16
                gfinal[g] = semcnt[k]
            if o.get('phase_end'):
                nxt = {'pool': 0, 'sp': NSW}
        self.gsem, self.gfinal = gsem, gfinal

    def emit(self, eng_name, engobj, csem, dsem):
        ops = self.ops
        waited = {}
        for o in ops:
            if o['eng'] != eng_name:
                continue
            for d in sorted(o['deps']):
                p = ops[d]
                if p['dma']:
                    key = ('d', self.gsem[p['gid']])
                    val = self.gfinal[p['gid']]
                    sem = dsem[self.gsem[p['gid']]]
                else:
                    if not p['sig']:
                        continue
                    key = ('c', p['eng'])
                    val = p['sigval']
                    sem = csem[p['eng']]
                if waited.get(key, 0) < val:
                    engobj.wait_ge(sem, val)
                    waited[key] = val
            if o['fn'] is None:
                continue
            ins = o['fn'](engobj)
            if o['dma']:
                ins.then_inc(dsem[self.gsem[o['gid']]], 16)
            elif o['sig']:
                ins.then_inc(csem[eng_name], 1)


class Arena:
    def __init__(self, ap, words):
        self.ap = ap
        self.words = words
        self.tops = [0, HOLE_HI]
        self.lims = [HOLE_LO, words]
        self.stack = []

    def alloc(self, free_shape, dtype):
        n = 1
        for s in free_shape:
            n *= s
        nbytes = n * (2 if dtype == BF16 else 4)
        w = (nbytes + 31) // 32 * 8
        for r in (0, 1):
            if self.tops[r] + w <= self.lims[r]:
                off = self.tops[r]
                self.tops[r] += w
                self.last = (off, n, dtype == BF16)
                break
        else:
            raise AssertionError(("arena overflow", self.tops, w, self.lims))
        v = self.ap[:, off:off + w]
        if dtype == BF16:
            v = v.bitcast(BF16)[:, 0:n]
        else:
            v = v[:, 0:n]
        if len(free_shape) == 2:
            v = v.rearrange("p (a b) -> p a b", a=free_shape[0])
        elif len(free_shape) == 3:
            v = v.rearrange("p (a b c) -> p a b c", a=free_shape[0], b=free_shape[1])
        elif len(free_shape) == 4:
            v = v.rearrange("p (a b c d) -> p a b c d", a=free_shape[0], b=free_shape[1], c=free_shape[2])
        return v

    def push(self):
        self.stack.append(list(self.tops))

    def pop(self):
        self.tops = self.stack.pop()


def build_program():
    nc = bass.Bass("TRN2", target_bir_lowering=False)
    E = Emitter()

    def din(name, shape):
        return nc.dram_tensor(name, list(shape), F32, kind="ExternalInput").ap()

    def dout(name, shape):
        return nc.dram_tensor(name, list(shape), F32, kind="ExternalOutput").ap()

    d_xT = din("xT", [1024, 2176])
    d_memT = din("memT", [1024, 256])
    d_kT = din("kTc", [4, 16, 4, 256, 256])
    d_v = din("vc", [4, 16, 256, 1024])
    d_wkv = din("wkvT", [4, 16, 8, 64, 64])
    d_conv = din("convT", [4, 16, 512, 30])
    d_shift = din("shiftT", [4, 1024, 16])
    d_win = din("w_in", [4, 1024, 2816])
    d_wout = din("w_out", [4, 1024, 1024])
    d_wq = din("w_q", [4, 1024, 1024])
    d_wk = din("w_k", [4, 1024, 1024])
    d_wv = din("w_v", [4, 1024, 1024])
    d_wo = din("w_o", [4, 1024, 1024])
    d_w1 = din("w_ffn1", [4, 1024, 4096])
    d_w2 = din("w_ffn2", [4, 4096, 1024])
    d_pp = din("pp", [128, NPP])
    d_bc = din("bc", [4, 128, NBC])
    d_rows = din("rows", [4, 1024])
    d_wup = din("wup", [4, 128, 512])
    d_gup = din("gup", [4, 128, 512])
    d_con = din("consts", [128, NCON])
    d_mrg = din("mrg", [128, 896])

    o_yT = dout("yT", [1024, 2176])
    o_nk = dout("nk", [4, 256, 1024])
    o_nv = dout("nv", [4, 256, 1024])
    o_wkvp = dout("wkvp", [4, 8, 64, 64])
    o_convp = dout("convp", [4, 512, 30])
    o_shiftp = dout("shiftp", [4, 128, 8])
    o_wkvs = dout("wkvs", [4, 16, 8, 64, 64])
    o_convs = dout("convs", [4, 16, 512, 30])
    o_shifts = dout("shifts", [4, 128, 8, 16])

    ARENA_WORDS = 207 * 256
    with (
        nc.sbuf_tensor("arena", [128, ARENA_WORDS], F32) as arena_t,
        nc.psum_tensor("ps0", [128, 512], F32) as ps0, nc.psum_tensor("ps1", [128, 512], F32) as ps1,
        nc.psum_tensor("ps2", [128, 512], F32) as ps2, nc.psum_tensor("ps3", [128, 512], F32) as ps3,
        nc.psum_tensor("ps4", [128, 512], F32) as ps4, nc.psum_tensor("ps5", [128, 512], F32) as ps5,
        nc.psum_tensor("ps6", [128, 512], F32) as ps6, nc.psum_tensor("ps7", [128, 512], F32) as ps7,
    ):
        AR = Arena(arena_t[:, :], ARENA_WORDS)
        banks = [(p[:, :], E.H("bank%d" % i, excl=True)) for i, p in enumerate([ps0, ps1, ps2, ps3, ps4, ps5, ps6, ps7])]
        bank_ctr = [0]

        def bank():
            b = banks[bank_ctr[0] % 8]
            bank_ctr[0] += 1
            return b

        def mm(out, lhsT, rhs, start, stop, r, w):
            E.op('pe', lambda e: e.matmul(out, lhsT, rhs, start=start, stop=stop), r, w)

        def tr(out, in_, ident, r, w):
            E.op('pe', lambda e: e.transpose(out, in_, ident), r, w)

        def act(out, in_, func, r, w, scale=None, bias=None):
            kw = {}
            if scale is not None:
                kw['scale'] = scale
            if bias is not None:
                kw['bias'] = bias
            E.op('act', lambda e: e.activation(out, in_, func, **kw), r, w)

        def tt(eng, out, in0, in1, op, r, w):
            E.op(eng, lambda e: e.tensor_tensor(out, in0, in1, op), r, w)

        def ts(eng, out, in0, s1, op0, r, w, s2=None, op1=None):
            if op1 is None:
                E.op(eng, lambda e: e.tensor_scalar(out, in0, s1, None, op0), r, w)
            else:
                E.op(eng, lambda e: e.tensor_scalar(out, in0, s1, s2, op0, op1), r, w)

        def stt(out, in0, scalar, in1, op0, op1, r, w):
            E.op('dve', lambda e: e.scalar_tensor_tensor(out, in0, scalar, in1, op0, op1), r, w)

        def cp(eng, out, in_, r, w):
            if eng == 'act':
                E.op('act', lambda e: e.copy(out, in_), r, w)
            else:
                E.op(eng, lambda e: e.tensor_copy(out, in_), r, w)

        def red(out, in_, r, w):
            E.op('dve', lambda e: e.tensor_reduce(out, in_, AX.X, ALU.add), r, w)

        def recip(out, in_, r, w):
            E.op('dve', lambda e: e.reciprocal(out, in_), r, w)

        def mset(eng, ap, val, w):
            E.op(eng, lambda e: e.memset(ap, val), (), w)

        def dma(q, out, in_, r, w):
            E.op(q, lambda e: e.dma_start(out=out, in_=in_), r, w, dma=True)

        con_f = AR.alloc([NCON], F32); h_con = E.H("con")
        con_b = AR.alloc([NCON], BF16)
        pp = AR.alloc([NPP], F32); h_pp = E.H("pp")
        xT = AR.alloc([8, 1152], F32)
        hT = AR.alloc([8, 1153], BF16)
        memn = AR.alloc([8, 256], BF16); h_memn = E.H("memn")
        st_S = AR.alloc([DEPTH, 4, 64], F32); h_stS = [E.H("stS%d" % l) for l in range(DEPTH)]
        st_conv = AR.alloc([DEPTH, 4, 30], BF16); h_stconv = [E.H("stc%d" % l) for l in range(DEPTH)]
        st_shift = AR.alloc([DEPTH, 8], BF16); h_stshift = [E.H("sts%d" % l) for l in range(DEPTH)]

        ident_b = con_b[:, C_ID:C_ID + 128]
        ident_f = con_f[:, C_ID:C_ID + 128]
        ones_b = None

        def epsc(i):
            return con_f[:, C_EPS + i:C_EPS + i + 1]

        dma('sp', con_f, d_con, (), [h_con])
        dma('pool', con_b, d_con, (), [h_con])
        dma('sp', pp, d_pp, (), [h_pp])
        onesb = AR.alloc([128], BF16)
        mset('dve', onesb, 1.0, [h_con])
        mset('dve', st_S, 0.0, h_stS)
        mset('dve', st_conv, 0.0, h_stconv)
        mset('dve', st_shift, 0.0, h_stshift)
        h_hinit = E.H('hinit')
        mset('dve', hT[:, :, 0:1], 0.0, [h_hinit])

        def rms_rstd(src_list, n, rsrc, sqbuf, rstd_out, h_tmp, eps_i=0, dim=1024.0):
            pb, hb = bank()
            nk = len(src_list)
            for k, s in enumerate(src_list):
                act(sqbuf[:, k, 0:n], s, AF.Square, rsrc, [h_tmp])
            for k in range(nk):
                mm(pb[:, 0:n], onesb, sqbuf[:, k, 0:n], k == 0, k == nk - 1, [h_tmp, h_con], [hb])
            act(rstd_out, pb[:, 0:n], AF.Sqrt, [hb, h_con], [h_tmp], scale=1.0 / dim, bias=epsc(eps_i))
            recip(rstd_out, rstd_out, [h_tmp], [h_tmp])


        def layer(half, l, groups, h_x, h_h):
            def ngc(i, k):
                o = PP_NG + l * 48 + i * 8 + k
                return pp[:, o:o + 1]

            def norm_to_h(i, sq, rstd, h_tmp):
                for g, (c, n, kind) in enumerate(groups):
                    rms_rstd([xT[:, k, c:c + n] for k in range(8)], n, [h_x[g]], sq, rstd[:, 0:n], h_tmp)
                    for k in range(8):
                        stt(hT[:, k, 1 + c:1 + c + n], xT[:, k, c:c + n], ngc(i, k), rstd[:, 0:n],
                            ALU.mult, ALU.mult, [h_x[g], h_tmp, h_pp], [h_h[g]])

            def post_res(i, g, src_list, rsrc, sq, rstd, tmp, h_tmp, h_tmp2):
                c, n, kind = groups[g]
                rms_rstd(src_list, n, rsrc, sq, rstd[:, 0:n], h_tmp)
                for k in range(8):
                    stt(tmp[:, 0:n], src_list[k], ngc(i, k), rstd[:, 0:n], ALU.mult, ALU.mult,
                        rsrc + [h_tmp, h_pp], [h_tmp2])
                    tt('pool', xT[:, k, c:c + n], xT[:, k, c:c + n], tmp[:, 0:n], ALU.add,
                       [h_x[g], h_tmp2], [h_x[g]])

            def load_w(q, dst, dsrc, h, ncol_split=None):
                v = dsrc.rearrange("(k p) n -> p k n", p=128)
                E.begin_group()
                for k in range(8):
                    dma(q, dst[:, k, :], v[:, k, :], (), [h])
                E.end_group()

            def attention_sub():
                AR.push()
                wq = AR.alloc([8, 1024], BF16); h_wq = E.H("wq")
                wk = AR.alloc([8, 1024], BF16); h_wk = E.H("wk")
                wv = AR.alloc([8, 1024], BF16); h_wv = E.H("wv")
                wo = AR.alloc([8, 1024], BF16); h_wo = E.H("wo")
                load_w('pool', wq, d_wq[l], h_wq)
                if DBG.get('nw', 4) > 1:
                    load_w('pool', wk, d_wk[l], h_wk)
                    load_w('pool', wv, d_wv[l], h_wv)
                    load_w('pool', wo, d_wo[l], h_wo)
                if DBG['stop'] <= 0:
                    E.barrier(); AR.pop(); return
                if DBG.get('wbar'):
                    E.barrier()
                stg = AR.alloc([2, 512], F32); h_stg = [E.H("stg0"), E.H("stg1")]
                sq = AR.alloc([8, 512], BF16); h_t1 = E.H("t1")
                rstd = AR.alloc([512], F32)
                tmp = AR.alloc([512], F32); h_t2 = E.H("t2")
                msb = AR.alloc([8, 512], F32); h_msb = E.H("msb")
                KT = AR.alloc([8, 256], BF16); h_KT = E.H("KT")
                Vb = AR.alloc([2, 1024], BF16); h_Vb = E.H("Vb")
                qT = sq; h_qT = h_t1
                oT = AR.alloc([8, 512], BF16); h_oT = E.H("oT")
                if DBG.get('stghi'):
                    stg = AR.alloc([2, 512], F32)
                PT = AR.alloc([2, 512], BF16); h_PT = E.H("PT")
                rs = AR.alloc([512], F32); h_rs = E.H("rs")
                if not DBG.get('skip1'):
                    norm_to_h(2, sq, rstd, h_t1)
                if DBG['stop'] <= 1:
                    E.barrier(); AR.pop(); return
                for oc in range(0 if DBG.get('skip2') else 8):
                    pb, hb = bank()
                    for k in range(8):
                        mm(pb[:, 0:256], wk[:, k, oc * 128:(oc + 1) * 128], memn[:, k, :], k == 0, k == 7,
                           [h_wk, h_memn], [hb])
                    cp('act', KT[:, oc, :], pb[:, 0:256], [hb], [h_KT])
                if DBG['stop'] <= 2:
                    E.barrier(); AR.pop(); return
                si = 0
                for which, wsrc, hw, dout_ in ((0, wv, h_wv, o_nv), (1, wk, h_wk, o_nk)):
                    if which == 1 and half == 1:
                        continue
                    if which >= DBG.get('vw', 2):
                        continue
                    for mc in range(DBG.get('vmc', 2)):
                        for hf in range(DBG.get('vhf', 2)):
                            pb, hb = bank()
                            for k in range(8):
                                mm(pb, memn[:, k, mc * 128:(mc + 1) * 128], wsrc[:, k, hf * 512:(hf + 1) * 512],
                                   k == 0, k == 7, [hw, h_memn], [hb])
                            if which == 0 and not DBG.get('novb'):
                                cp('act', Vb[:, mc, hf * 512:(hf + 1) * 512], pb, [hb], [h_Vb])
                            if half == 0:
                                s = (si % 2) if not DBG.get('stg0') else 0
                                si += 1
                                if not DBG.get('nostg'):
                                    cp(DBG.get('stgeng', 'dve'), (msb if DBG.get('stgmsb') else stg)[:, s, :], pb, ([h_Vb] if DBG.get('chain') else [hb]), [h_stg[s]])
                                if not DBG.get('nodma'):
                                    dma('sp', dout_[l, mc * 128:(mc + 1) * 128, hf * 512:(hf + 1) * 512], stg[:, s, :],
                                        [h_stg[s]], ())
                if DBG['stop'] <= 3:
                    E.barrier(); AR.pop(); return
                for g, (c, n, kind) in enumerate(groups):
                    for oc in range(8):
                        pb, hb = bank()
                        for k in range(8):
                            mm(pb[:, 0:n], wq[:, k, oc * 128:(oc + 1) * 128], hT[:, k, 1 + c:1 + c + n], k == 0, k == 7,
                               [h_wq, h_h[g]], [hb])
                        cp('act', qT[:, oc, 0:n], pb[:, 0:n], [hb], [h_qT])
                    if kind == 'p':
                        for hd in range(4):
                            for mc in range(2):
                                pb, hb = bank()
                                for dc in range(2):
                                    mm(pb[:, 0:n], KT[:, 2 * hd + dc, mc * 128:(mc + 1) * 128], qT[:, 2 * hd + dc, 0:n],
                                       dc == 0, dc == 1, [h_KT, h_qT], [hb])
                                act(PT[:, mc, 0:n], pb[:, 0:n], AF.Exp, [hb], [h_PT], scale=1.0 / 16.0)
                            pb, hb = bank()
                            for mc in range(2):
                                mm(pb[:, 0:n], onesb, PT[:, mc, 0:n], mc == 0, mc == 1, [h_PT, h_con], [hb])
                            recip(rs[:, 0:n], pb[:, 0:n], [hb], [h_rs])
                            for dc in range(2):
                                pb, hb = bank()
                                for mc in range(2):
                                    mm(pb[:, 0:n], Vb[:, mc, (2 * hd + dc) * 128:(2 * hd + dc + 1) * 128], PT[:, mc, 0:n],
                                       mc == 0, mc == 1, [h_Vb, h_PT], [hb])
                                tt('dve', oT[:, 2 * hd + dc, 0:n], pb[:, 0:n], rs[:, 0:n], ALU.mult, [hb, h_rs], [h_oT])
                    else:
                        AR.push()
                        KTs = AR.alloc([1, 4, 2, 256], BF16); h_KTs = [E.H("KTs0")]
                        Vs = AR.alloc([1, 2, 1024], BF16); h_Vs = [E.H("Vs0")]
                        PTs = PT.rearrange("p a (b h m t) -> p (a b) h m t", b=8, h=4, m=2); h_PTs = h_PT
                        rss = rs.rearrange("p (b h t) -> p b h t", b=16, h=4); h_rss = h_rs
                        for bh in range(2):
                            pbs, hbs = bank()
                            pbo0, hbo0 = bank()
                            sview = pbs.rearrange("p (b h m t) -> p b h m t", b=8, h=4, m=2)
                            oview = pbo0.rearrange("p (b h d t) -> p b h d t", b=8, h=4, d=2)
                            for bb in range(8):
                                b = bh * 8 + bb
                                s = 0
                                dma('pool', KTs[:, s], d_kT[l, b].rearrange("h (dc p) m -> p h dc m", p=128), (), [h_KTs[s]])
                                dma('pool', Vs[:, s], d_v[l, b].rearrange("(mc p) n -> p mc n", p=128), (), [h_Vs[s]])
                                for hd in range(4):
                                    for mc in range(2):
                                        for dc in range(2):
                                            mm(sview[:, bb, hd, mc, :], KTs[:, s, hd, dc, mc * 128:(mc + 1) * 128],
                                               qT[:, 2 * hd + dc, b * 8:(b + 1) * 8], dc == 0, dc == 1,
                                               [h_KTs[s], h_qT], [hbs])
                                act(PTs[:, b], sview[:, bb], AF.Exp, [hbs], [h_PTs], scale=1.0 / 16.0)
                                for hd in range(4):
                                    for dc in range(2):
                                        for mc in range(2):
                                            mm(oview[:, bb, hd, dc, :], Vs[:, s, mc, (2 * hd + dc) * 128:(2 * hd + dc + 1) * 128],
                                               PTs[:, b, hd, mc, :], mc == 0, mc == 1, [h_Vs[s], h_PTs], [hbo0])
                            pbr, hbr = bank()
                            rview = pbr[:, 0:256].rearrange("p (b h t) -> p b h t", b=8, h=4)
                            for mc in range(2):
                                mm(rview, onesb, PTs[:, bh * 8:(bh + 1) * 8, :, mc, :], mc == 0, mc == 1, [h_PTs, h_con], [hbr])
                            recip(rss[:, bh * 8:(bh + 1) * 8], rview, [hbr], [h_rss])
                            for dc in range(2):
                                ov = oT[:, :, 0:128].rearrange("p (h d) (b t) -> p d b h t", d=2, t=8)[:, dc, bh * 8:(bh + 1) * 8]
                                tt('dve', ov, oview[:, :, :, dc, :], rss[:, bh * 8:(bh + 1) * 8], ALU.mult,
                                   [hbo0, h_rss], [h_oT])
                        AR.pop()
                    for oc in range(8):
                        pb, hb = bank()
                        for k in range(8):
                            mm(pb[:, 0:n], wo[:, k, oc * 128:(oc + 1) * 128], oT[:, k, 0:n], k == 0, k == 7,
                               [h_wo, h_oT], [hb])
                        cp('act', msb[:, oc, 0:n], pb[:, 0:n], [hb], [h_msb])
                    post_res(3, g, [msb[:, k, 0:n] for k in range(8)], [h_msb], sq, rstd, tmp, h_t1, h_t2)
                E.barrier()
                AR.pop()


            def ffn_sub():
                AR.push()
                NT_ = 1024 + (128 if half == 1 else 0)
                facc = AR.alloc([8, NT_], F32); h_f = [E.H("f%d" % g) for g in range(len(groups))]
                w1 = AR.alloc([2, 8, 512], BF16); h_w1 = [E.H("w1a"), E.H("w1b")]
                w2 = AR.alloc([2, 4, 1024], BF16); h_w2 = [E.H("w2a"), E.H("w2b")]
                sq = AR.alloc([8, 512], BF16); h_t1 = E.H("t1f")
                rstd = AR.alloc([512], F32)
                tmp = AR.alloc([512], F32); h_t2 = E.H("t2f")
                rl = AR.alloc([2, 512], BF16); h_rl = [E.H("rl0"), E.H("rl1")]
                hid = AR.alloc([2, 4, 512], BF16); h_hid = [E.H("hid0"), E.H("hid1")]
                norm_to_h(4, sq, rstd, h_t1)
                w1v = d_w1[l].rearrange("(k p) n -> p k n", p=128)
                w2v = d_w2[l].rearrange("(c p) n -> p c n", p=128)
                hi = 0
                for hg in range(8):
                    s = hg % 2
                    E.begin_group()
                    for k in range(8):
                        dma('pool', w1[:, s, k, :], w1v[:, k, hg * 512:(hg + 1) * 512], (), [h_w1[s]])
                    E.end_group()
                    E.begin_group()
                    for cc in range(4):
                        dma('pool', w2[:, s, cc, :], w2v[:, hg * 4 + cc, :], (), [h_w2[s]])
                    E.end_group()
                    for g, (c, n, kind) in enumerate(groups):
                        hs = hi % 2
                        hi += 1
                        for hc in range(4):
                            pb, hb = bank()
                            for k in range(8):
                                mm(pb[:, 0:n], w1[:, s, k, hc * 128:(hc + 1) * 128], hT[:, k, 1 + c:1 + c + n], k == 0, k == 7,
                                   [h_w1[s], h_h[g]], [hb])
                            rs_ = hc % 2
                            act(rl[:, rs_, 0:n], pb[:, 0:n], AF.Relu, [hb], [h_rl[rs_]])
                            tt('pool', hid[:, hs, hc, 0:n], rl[:, rs_, 0:n], rl[:, rs_, 0:n], ALU.mult, [h_rl[rs_]], [h_hid[hs]])
                        for oc in range(8):
                            pb, hb = bank()
                            for hc in range(4):
                                mm(pb[:, 0:n], w2[:, s, hc, oc * 128:(oc + 1) * 128], hid[:, hs, hc, 0:n], hc == 0, hc == 3,
                                   [h_w2[s], h_hid[hs]], [hb])
                            if hg == 0:
                                cp('act', facc[:, oc, c:c + n], pb[:, 0:n], [hb], [h_f[g]])
                            else:
                                tt('dve', facc[:, oc, c:c + n], pb[:, 0:n], facc[:, oc, c:c + n], ALU.add, [hb, h_f[g]], [h_f[g]])
                for g, (c, n, kind) in enumerate(groups):
                    post_res(5, g, [facc[:, k, c:c + n] for k in range(8)], [h_f[g]], sq, rstd, tmp, h_t1, h_t2)
                E.barrier()
                AR.pop()


            def rwkv_sub(mixT, h_mix):
                AR.push()
                winv = d_win[l].rearrange("(k p) n -> p k n", p=128)
                Wr = AR.alloc([8, 1792], BF16); h_Wr = E.H("Wr")
                E.begin_group()
                for k in range(8):
                    dma('pool', Wr[:, k, :], winv[:, k, 1024:2816], (), [h_Wr])
                E.end_group()
                bct = AR.alloc([4096], F32); h_bc = E.H("bct")
                dma('sp', bct, d_bc[l][:, 0:4096], (), [h_bc])
                kk_t, ka_t, rk_t, gg_t, gb_t = [bct[:, i * 512:(i + 1) * 512] for i in range(5)]
                mu_t = [bct[:, 2560 + i * 512:2560 + (i + 1) * 512] for i in range(3)]
                rows_b = AR.alloc([1024], BF16); h_rows = E.H("rowsb")
                E.begin_group()
                dma('pool', rows_b[0:1, 0:512], d_rows[l:l + 1, 0:512], (), [h_rows])
                dma('pool', rows_b[64:65, 0:512], d_rows[l:l + 1, 512:1024], (), [h_rows])
                E.end_group()
                wup_b = AR.alloc([512], BF16); h_wup = E.H("wupb")
                dma('pool', wup_b, d_wup[l], (), [h_wup])
                gup_b = AR.alloc([512], BF16); h_gup = E.H("gupb")
                dma('pool', gup_b, d_gup[l], (), [h_gup])
                S_b = AR.alloc([4, 64], BF16); h_Sb = E.H("Sb")
                cp('act', S_b, st_S[:, l, :, :], [h_stS[l]], [h_Sb])
                FT = {}
                for nm in ("Asb", "dlt", "Zr", "Zk", "sw", "av", "gate", "G", "Ginv", "Gp", "kk", "sq_"):
                    FT[nm] = (AR.alloc([512], F32), E.H("f_" + nm))
                    DBGV[nm] = AR.last
                FT["kkn"] = FT["kk"]
                FT["t1"] = FT["Asb"]
                FT["kmod"] = FT["sw"]
                DBGV["kkn"] = DBGV["kk"]; DBGV["kmod"] = DBGV["sw"]
                BT = {}
                for nm in ("at", "bt", "kt", "rt", "vb", "Xb", "ro"):
                    BT[nm] = (AR.alloc([512], BF16), E.H("b_" + nm))
                    DBGV[nm] = AR.last
                arT = AR.alloc([4, 2, 128], BF16); h_arT = E.H("arT")
                kbT = AR.alloc([4, 2, 128], BF16); h_kbT = E.H("kbT")
                AkaAkr = AR.alloc([8, 256], BF16); h_Aka = E.H("Aka")
                NAbr = AR.alloc([8, 256], BF16); h_NAbr = E.H("NAbr")
                mrg_b = AR.alloc([896], BF16); h_mrg = E.H("mrg")
                dma('pool', mrg_b, d_mrg, (), [h_mrg])
                Ad = AR.alloc([2, 128], F32); h_Ad = E.H("Ad")
                Dd = AR.alloc([2, 128], F32); h_Dd = E.H("Dd")
                tda = AR.alloc([128], BF16); h_tda = E.H("tda")
                sdg = AR.alloc([128], BF16); h_sdg = E.H("sdg")
                sm8 = AR.alloc([6, 8], F32); h_sm8 = E.H("sm8")
                gcT = AR.alloc([4, 16], F32); h_gc = E.H("gcT")
                tmpS = AR.alloc([4, 64], F32); h_tmpS = E.H("tmpS")
                has_s = (half == 1 and DBG['samp'])
                if has_s:
                    sst = AR.alloc([8, 16], F32); h_sst = E.H("sst")
                    dma('sp', sst, d_shift[l].rearrange("(k p) b -> p k b", p=128), (), [h_sst])
                    hprev_s = AR.alloc([8, 16, 8], BF16); h_hps = E.H("hps")
                    hsv = hT[:, :, 1025:1153].rearrange("p k (b t) -> p k b t", t=8)
                    cp('dve', hprev_s[:, :, :, 0], sst, [h_sst], [h_hps])
                    cp('dve', hprev_s[:, :, :, 1:8], hsv[:, :, :, 0:7], [h_h[2]], [h_hps])
                    S_sb = AR.alloc([16, 4, 64], BF16); h_Ssb = E.H("Ssb")
                    P1sb = AR.alloc([4, 128], BF16); h_P1sb = E.H("P1sb")
                    S_sf = FT["Gp"][0].rearrange("p (a b c) -> p a b c", a=2, b=4); h_Ssf = [FT["Gp"][1], FT["Gp"][1]]
                    vm = FT["G"][0].bitcast(BF16).rearrange("p (a b) -> p a b", a=2); h_vm = [FT["G"][1], FT["G"][1]]
                    um = FT["Ginv"][0].bitcast(BF16).rearrange("p (a b) -> p a b", a=2); h_um = [FT["Ginv"][1], FT["Ginv"][1]]
                    wkv_in = d_wkv[l].rearrange("b (p4 h2) j i -> h2 j b p4 i", h2=2)
                    wkv_out = o_wkvs[l].rearrange("b (p4 h2) j i -> h2 j b p4 i", h2=2)
                    E.begin_group()
                    for h2 in range(2):
                        for p4 in range(4):
                            dma('pool', S_sb[h2 * 64:(h2 + 1) * 64, :, p4, :], wkv_in[h2][:, :, p4, :], (), [h_Ssb])
                    E.end_group()

                def F(nm):
                    return FT[nm]

                def rw_tile(tc, kind, g):
                    smp = (kind == 's')
                    nseq = 16 if smp else 1
                    nap = 3 if smp else 7
                    MuMi = con_b[:, (C_MUS if smp else C_MU):(C_MUS if smp else C_MU) + 256]
                    MLb = con_b[:, (C_MLS if smp else C_ML):(C_MLS if smp else C_ML) + 128]
                    Tri = con_f[:, (C_MIS if smp else C_MI):(C_MIS if smp else C_MI) + 128]
                    seqf = con_f[:, C_SEQ:C_SEQ + 16] if smp else con_f[:, C_EPS + 4:C_EPS + 5]
                    hg = [h_h[g]] + ([h_h[g - 1]] if g > 0 else []) + ([h_hps] if smp else [])

                    def hc(k):
                        return hT[:, k, 1 + tc:1 + tc + 128]

                    def hp(k):
                        return hprev_s[:, k, :, :] if smp else hT[:, k, tc:tc + 128]
                    (Asb, hA), (dlt, hD) = F("Asb"), F("dlt")
                    Z = {}
                    for qi, qn in enumerate(("Zr", "Zk", "vb")):
                        bA, hbA = bank()
                        for k in range(8):
                            mm(bA, hc(k), Wr[:, k, qi * 512:(qi + 1) * 512], k == 0, k == 7, hg + [h_Wr], [hbA])
                        bB, hbB = bank()
                        for k in range(8):
                            mm(bB, hp(k), Wr[:, k, qi * 512:(qi + 1) * 512], k == 0, k == 7, hg + [h_Wr], [hbB])
                        cp('act', Asb, bA, [hbA], [hA])
                        tt('dve', dlt, bB, Asb, ALU.subtract, [hbB, hA], [hD])
                        tt('dve', dlt, dlt, mu_t[qi], ALU.mult, [hD, h_bc], [hD])
                        dst, hdst = (F(qn) if qi < 2 else BT["vb"])
                        tt('pool', dst, dlt, Asb, ALU.add, [hD, hA], [hdst])
                    (Zr, hZr), (Zk, hZk) = F("Zr"), F("Zk")
                    vb, hvb = BT["vb"]
                    bD, hbD = bank()
                    for ci, c0 in enumerate((1536, 1664)):
                        for k in range(8):
                            mm(bD[:, ci * 256:ci * 256 + 128], Wr[:, k, c0:c0 + 128], hc(k), k == 0, k == 7, hg + [h_Wr], [hbD])
                        for k in range(8):
                            mm(bD[:, ci * 256 + 128:ci * 256 + 256], Wr[:, k, c0:c0 + 128], hp(k), k == 0, k == 7, hg + [h_Wr], [hbD])
                    for ci in range(2):
                        mucol = pp[:, PP_MUD + l * 2 + ci:PP_MUD + l * 2 + ci + 1]
                        cp('act', Ad[:, ci, :], bD[:, ci * 256:ci * 256 + 128], [hbD], [h_Ad])
                        tt('dve', Dd[:, ci, :], bD[:, ci * 256 + 128:ci * 256 + 256], Ad[:, ci, :], ALU.subtract, [hbD, h_Ad], [h_Dd])
                        stt(Dd[:, ci, :], Dd[:, ci, :], mucol, Ad[:, ci, :], ALU.mult, ALU.add, [h_Dd, h_Ad, h_pp], [h_Dd])
                    act(tda[0:64, :], Dd[0:64, 0, :], AF.Tanh, [h_Dd], [h_tda])
                    cp('act', tda[64:128, :], Dd[64:128, 0, :], [h_Dd], [h_tda])
                    act(sdg, Dd[:, 1, :], AF.Sigmoid, [h_Dd], [h_sdg])
                    (sw, hsw), (av, hav), (gate, hgate) = F("sw"), F("av"), F("gate")
                    bW, hbW = bank()
                    mm(bW, onesb[0:1, 0:128], rows_b[0:1, 0:512], True, False, [h_con, h_rows], [hbW])
                    mm(bW, tda[0:64, :], wup_b[0:64, :], False, True, [h_tda, h_wup], [hbW])
                    act(sw, bW, AF.Sigmoid, [hbW], [hsw])
                    bAa, hbAa = bank()
                    mm(bAa, onesb[64:65, 0:128], rows_b[64:65, 0:512], True, False, [h_con, h_rows], [hbAa])
                    mm(bAa, tda[64:128, :], wup_b[64:128, :], False, True, [h_tda, h_wup], [hbAa])
                    act(av, bAa, AF.Sigmoid, [hbAa], [hav])
                    bG, hbG = bank()
                    mm(bG, sdg, gup_b, True, True, [h_sdg, h_gup], [hbG])
                    cp('act', gate, bG, [hbG], [hgate])
                    (G, hG), (Ginv, hGi), (Gp, hGp) = F("G"), F("Ginv"), F("Gp")
                    bC, hbC = bank()
                    mm(bC, Tri, sw, True, True, [h_con, hsw], [hbC])
                    act(G, bC, AF.Exp, [hbC], [hG], scale=-CDEC)
                    act(Ginv, bC, AF.Exp, [hbC], [hGi], scale=CDEC)
                    tt('dve', Gp, bC, sw, ALU.subtract, [hbC, hsw], [hGp])
                    act(Gp, Gp, AF.Exp, [hGp], [hGp], scale=-CDEC)
                    bGC, hbGC = bank()
                    for p4 in range(4):
                        mm(bGC[:, p4 * 16:p4 * 16 + nseq], sw[:, p4 * 128:(p4 + 1) * 128], seqf, True, True, [hsw, h_con], [hbGC])
                    act(gcT[:, :, 0:nseq], bGC[:, 0:64].rearrange("p (a b) -> p a b", a=4)[:, :, 0:nseq], AF.Exp, [hbGC], [h_gc], scale=-CDEC)
                    (kk, hkk), (sq_, hsq), (kkn, hkkn), (t1, ht1), (kmod, hkm) = F("kk"), F("sq_"), F("kkn"), F("t1"), F("kmod")
                    tt('dve', kk, Zk, kk_t, ALU.mult, [hZk, h_bc], [hkk])
                    tt('pool', sq_, kk, kk, ALU.mult, [hkk], [hsq])
                    red(sm8[:, 0, :], sq_.rearrange("p (h d) -> p h d", h=8), [hsq], [h_sm8])
                    act(sm8[:, 0, :], sm8[:, 0, :], AF.Sqrt, [h_sm8], [h_sm8])
                    ts('dve', sm8[:, 0, :], sm8[:, 0, :], 1e-12, ALU.max, [h_sm8], [h_sm8])
                    recip(sm8[:, 0, :], sm8[:, 0, :], [h_sm8], [h_sm8])
                    tt('dve', kkn.rearrange("p (h d) -> p h d", h=8), kk.rearrange("p (h d) -> p h d", h=8),
                       sm8[:, 0, :].unsqueeze(2).to_broadcast([128, 8, 64]), ALU.mult, [hkk, h_sm8], [hkkn])
                    stt(t1, av, -1.0, ka_t, ALU.add, ALU.mult, [hav, h_bc], [ht1])
                    stt(kmod, t1, 1.0, Zk, ALU.add, ALU.mult, [ht1, hZk], [hkm])
                    (at, hat), (bt, hbt), (kt, hkt), (rt, hrt) = BT["at"], BT["bt"], BT["kt"], BT["rt"]
                    stt(at, kkn, -1.0, Gp, ALU.mult, ALU.mult, [hkkn, hGp], [hat])
                    tt('pool', t1, kkn, av, ALU.mult, [hkkn, hav], [ht1])
                    tt('pool', bt, t1, Ginv, ALU.mult, [ht1, hGi], [hbt])
                    tt('pool', kt, kmod, Ginv, ALU.mult, [hkm, hGi], [hkt])
                    tt('dve', rt, Zr, G, ALU.mult, [hZr, hG], [hrt])
                    tt('dve', sq_, Zr, kmod, ALU.mult, [hZr, hkm], [hsq])
                    tt('pool', sq_, sq_, rk_t, ALU.mult, [hsq, h_bc], [hsq])
                    red(sm8[:, 1, :], sq_.rearrange("p (h d) -> p h d", h=8), [hsq], [h_sm8])
                    for (srcs, dstT, hdT) in (((at, hat, rt, hrt), arT, h_arT), ((kt, hkt, bt, hbt), kbT, h_kbT)):
                        bT_, hbT_ = bank()
                        tv = bT_.bitcast(BF16)[:, 0:1024].rearrange("p (a b c) -> p a b c", a=4, b=2)
                        for p4 in range(4):
                            tr(tv[:, p4, 0, :], srcs[0][:, p4 * 128:(p4 + 1) * 128], ident_b, [srcs[1], h_con], [hbT_])
                            tr(tv[:, p4, 1, :], srcs[2][:, p4 * 128:(p4 + 1) * 128], ident_b, [srcs[3], h_con], [hbT_])
                        cp('act', dstT, tv, [hbT_], [hdT])
                    mk = MuMi.unsqueeze(1).to_broadcast([128, 2, 256])
                    for hb4 in range(2):
                        for par in range(2):
                            bA_, hbA_ = bank()
                            bB_, hbB_ = bank()
                            base = 64 * par
                            for hh in range(2):
                                h = hb4 * 4 + 2 * hh + par
                                p4 = h // 2
                                ar2 = arT[base:base + 64, p4, :, :]
                                mm(bA_[:, hh * 256:(hh + 1) * 256], kbT[base:base + 64, p4, 0, :], ar2, True, True, [h_kbT, h_arT], [hbA_])
                                mm(bB_[:, hh * 256:(hh + 1) * 256], kbT[base:base + 64, p4, 1, :], ar2, True, True, [h_kbT, h_arT], [hbB_])
                            h0 = hb4 * 4 + par
                            tt('dve', AkaAkr[:, h0:h0 + 3:2, :], bA_.rearrange("p (a b) -> p a b", a=2), mk, ALU.mult, [hbA_, h_con], [h_Aka])
                            tt('dve', NAbr[:, h0:h0 + 3:2, :], bB_.rearrange("p (a b) -> p a b", a=2), mk, ALU.mult, [hbB_, h_con], [h_NAbr])
                    (Xb, hXb) = BT["Xb"]
                    X, hX = F("Asb")

                    def s0_terms(bX_, hbX_, which, post):
                        if not smp:
                            for h in range(8):
                                p4, base = h // 2, 64 * (h % 2)
                                mm(bX_[:, h * 64:(h + 1) * 64], arT[base:base + 64, p4, which, :], S_b[base:base + 64, p4, :],
                                   True, False, [h_arT, h_Sb], [hbX_])
                                post(h, True)
                        else:
                            for h2 in range(2):
                                base = 64 * h2
                                bP, hbP = bank()
                                for p4 in range(4):
                                    for s in range(16):
                                        mm(bP[base:base + 64, p4 * 128 + s * 8:p4 * 128 + s * 8 + 8], S_sb[base:base + 64, s, p4, :],
                                           arT[base:base + 64, p4, which, s * 8:(s + 1) * 8], True, True, [h_Ssb, h_arT], [hbP])
                                cp('act', P1sb[base:base + 64, :, :], bP[base:base + 64, :].rearrange("p (a b) -> p a b", a=4), [hbP], [h_P1sb])
                            for p4 in range(4):
                                mm(bX_[:, p4 * 128:(p4 + 1) * 128], P1sb[:, p4, :], ident_b, True, False, [h_P1sb, h_con], [hbX_])
                                post(2 * p4, False)
                                post(2 * p4 + 1, True)
                    bX, hbX = bank()

                    def post_x(h, last):
                        mm(bX[:, h * 64:(h + 1) * 64], AkaAkr[:, h, 0:128], vb[:, h * 64:(h + 1) * 64], False, last, [h_Aka, hvb], [hbX])
                    s0_terms(bX, hbX, 0, post_x)
                    cp('act', X, bX, [hbX], [hX])
                    cp('dve', Xb, X, [hX], [hXb])
                    def b8(nm):
                        return FT[nm][0].bitcast(BF16).rearrange("p (a b) -> p a b", a=8), FT[nm][1]
                    (Tm, hTm), (TmT, hTmT), (Noff, hNo), (YT, hYT) = b8("kk"), b8("sq_"), b8("Zr"), b8("av")
                    idb = ident_b.unsqueeze(1).to_broadcast([128, 8, 128])
                    cp('pool', Tm, idb, [h_con], [hTm])
                    cp('pool', TmT, idb, [h_con], [hTmT])
                    for lv in (range(3) if smp else range(7)):
                        mo = mrg_b[:, lv * 128:(lv + 1) * 128].unsqueeze(1).to_broadcast([128, 8, 128])
                        tt('pool', Noff, NAbr[:, :, 0:128], mo, ALU.mult, [h_NAbr, h_mrg], [hNo])
                        for q in range(2):
                            bY_, hbY_ = bank()
                            for hh in range(4):
                                h = q * 4 + hh
                                mm(bY_[:, hh * 128:(hh + 1) * 128], Noff[:, h, :], TmT[:, h, :], True, True, [hNo, hTmT], [hbY_])
                            cp('act' if q == 0 else 'dve', YT[:, q * 4:q * 4 + 4, :], bY_.rearrange("p (a b) -> p a b", a=4), [hbY_], [hYT])
                        zb = []
                        for q in range(2):
                            bZ, hbZ = bank()
                            bZT, hbZT = bank()
                            for hh in range(4):
                                h = q * 4 + hh
                                mm(bZ[:, hh * 128:(hh + 1) * 128], YT[:, h, :], Tm[:, h, :], True, True, [hYT, hTm], [hbZ])
                                mm(bZT[:, hh * 128:(hh + 1) * 128], Tm[:, h, :], YT[:, h, :], True, True, [hYT, hTm], [hbZT])
                            zb.append((bZ, hbZ, bZT, hbZT))
                        for q in range(2):
                            bZ, hbZ, bZT, hbZT = zb[q]
                            tt('dve', Tm[:, q * 4:q * 4 + 4, :], bZ.rearrange("p (a b) -> p a b", a=4), Tm[:, q * 4:q * 4 + 4, :], ALU.add, [hbZ, hTm], [hTm])
                            tt('dve', TmT[:, q * 4:q * 4 + 4, :], bZT.rearrange("p (a b) -> p a b", a=4), TmT[:, q * 4:q * 4 + 4, :], ALU.add, [hbZT, hTmT], [hTmT])
                    bY, hbY = bank()
                    for h in range(8):
                        mm(bY[:, h * 64:(h + 1) * 64], Tm[:, h, :], Xb[:, h * 64:(h + 1) * 64], True, True, [hTm, hXb], [hbY])
                    cp('act', X, bY, [hbY], [hX])
                    cp('dve', Xb, X, [hX], [hXb])
                    bO, hbO = bank()

                    def post_o(h, last):
                        mm(bO[:, h * 64:(h + 1) * 64], AkaAkr[:, h, 128:256], vb[:, h * 64:(h + 1) * 64], False, False, [h_Aka, hvb], [hbO])
                        mm(bO[:, h * 64:(h + 1) * 64], NAbr[:, h, 128:256], Xb[:, h * 64:(h + 1) * 64], False, last, [h_NAbr, hXb], [hbO])
                    s0_terms(bO, hbO, 1, post_o)
                    (osb, hos) = F("dlt")
                    cp('act', osb, bO, [hbO], [hos])
                    if not smp:
                        bS, hbS = bank()
                        for h in range(8):
                            p4, base = h // 2, 64 * (h % 2)
                            mm(bS[base:base + 64, p4 * 64:(p4 + 1) * 64], kt[:, h * 64:(h + 1) * 64], vb[:, h * 64:(h + 1) * 64], True, False, [hkt, hvb], [hbS])
                            mm(bS[base:base + 64, p4 * 64:(p4 + 1) * 64], bt[:, h * 64:(h + 1) * 64], Xb[:, h * 64:(h + 1) * 64], False, True, [hbt, hXb], [hbS])
                        tt('dve', tmpS, bS[:, 0:256].rearrange("p (a b) -> p a b", a=4), st_S[:, l, :, :], ALU.add, [hbS, h_stS[l]], [h_tmpS])
                        tt('dve', st_S[:, l, :, :], tmpS, gcT[:, :, 0:1].to_broadcast([128, 4, 64]), ALU.mult, [h_tmpS, h_gc], [h_stS[l]])
                        cp('act', S_b, st_S[:, l, :, :], [h_stS[l]], [h_Sb])
                    else:
                        for s in range(16):
                            ss_ = s % 2
                            E.begin_group()
                            for h2 in range(2):
                                dma('sp', S_sf[h2 * 64:(h2 + 1) * 64, ss_, :, :], wkv_in[h2][:, s, :, :], (), [h_Ssf[ss_]])
                            E.end_group()
                            ts('pool', vm[:, ss_, :], vb, seqf[:, s:s + 1], ALU.mult, [hvb, h_con], [h_vm[ss_]])
                            ts('pool', um[:, ss_, :], Xb, seqf[:, s:s + 1], ALU.mult, [hXb, h_con], [h_um[ss_]])
                            bS, hbS = bank()
                            for h in range(8):
                                p4, base = h // 2, 64 * (h % 2)
                                mm(bS[base:base + 64, p4 * 64:(p4 + 1) * 64], kt[:, h * 64:(h + 1) * 64], vm[:, ss_, h * 64:(h + 1) * 64], True, False, [hkt, h_vm[ss_]], [hbS])
                                mm(bS[base:base + 64, p4 * 64:(p4 + 1) * 64], bt[:, h * 64:(h + 1) * 64], um[:, ss_, h * 64:(h + 1) * 64], False, True, [hbt, h_um[ss_]], [hbS])
                            tt('dve', S_sf[:, ss_, :, :], bS[:, 0:256].rearrange("p (a b) -> p a b", a=4), S_sf[:, ss_, :, :], ALU.add, [hbS, h_Ssf[ss_]], [h_Ssf[ss_]])
                            tt('dve', S_sf[:, ss_, :, :], S_sf[:, ss_, :, :], gcT[:, :, s:s + 1].to_broadcast([128, 4, 64]), ALU.mult, [h_Ssf[ss_], h_gc], [h_Ssf[ss_]])
                            E.begin_group()
                            for h2 in range(2):
                                dma('sp', wkv_out[h2][:, s, :, :], S_sf[h2 * 64:(h2 + 1) * 64, ss_, :, :], [h_Ssf[ss_]], ())
                            E.end_group()
                    (osq, hoq), (on, hon), (bon, hbon) = F("G"), F("Ginv"), F("Gp")
                    o3 = osb.rearrange("p (h d) -> p h d", h=8)
                    red(sm8[:, 2, :], o3, [hos], [h_sm8])
                    tt('pool', osq, osb, osb, ALU.mult, [hos], [hoq])
                    red(sm8[:, 3, :], osq.rearrange("p (h d) -> p h d", h=8), [hoq], [h_sm8])
                    ts('dve', sm8[:, 2, :], sm8[:, 2, :], 1.0 / 64.0, ALU.mult, [h_sm8], [h_sm8])
                    tt('dve', sm8[:, 4, :], sm8[:, 2, :], sm8[:, 2, :], ALU.mult, [h_sm8], [h_sm8])
                    stt(sm8[:, 3, :], sm8[:, 3, :], 1.0 / 64.0, sm8[:, 4, :], ALU.mult, ALU.subtract, [h_sm8], [h_sm8])
                    act(sm8[:, 3, :], sm8[:, 3, :], AF.Sqrt, [h_sm8, h_con], [h_sm8], scale=1.0, bias=epsc(2))
                    recip(sm8[:, 3, :], sm8[:, 3, :], [h_sm8], [h_sm8])
                    on3 = on.rearrange("p (h d) -> p h d", h=8)
                    tt('dve', on3, o3, sm8[:, 2, :].unsqueeze(2).to_broadcast([128, 8, 64]), ALU.subtract, [hos, h_sm8], [hon])
                    tt('dve', on3, on3, sm8[:, 3, :].unsqueeze(2).to_broadcast([128, 8, 64]), ALU.mult, [hon, h_sm8], [hon])
                    tt('pool', on, on, gg_t, ALU.mult, [hon, h_bc], [hon])
                    tt('pool', on, on, gb_t, ALU.add, [hon, h_bc], [hon])
                    tt('dve', bon.rearrange("p (h d) -> p h d", h=8), vb.rearrange("p (h d) -> p h d", h=8),
                       sm8[:, 1, :].unsqueeze(2).to_broadcast([128, 8, 64]), ALU.mult, [hvb, h_sm8], [hbon])
                    tt('pool', on, on, bon, ALU.add, [hon, hbon], [hon])
                    ro, hro = BT["ro"]
                    tt('dve', ro, on, gate, ALU.mult, [hon, hgate], [hro])
                    bT_, hbT_ = bank()
                    tv = bT_.bitcast(BF16)[:, 0:512].rearrange("p (a b) -> p a b", a=4)
                    for p4 in range(4):
                        tr(tv[:, p4, :], ro[:, p4 * 128:(p4 + 1) * 128], ident_b, [hro, h_con], [hbT_])
                    cp('act', mixT[:, 4:8, tc:tc + 128], tv, [hbT_], [h_mix[g]])

                ntl = 0
                if DBG.get('rwtiles', 99) < 99 or DBG.get('rwonly') is not None:
                    for g, (c, n, kind) in enumerate(groups):
                        mset('pool', mixT[:, 4:8, c:c + n], 0.0, [h_mix[g]])
                for g, (c, n, kind) in enumerate(groups):
                    for ti in range(n // 128):
                        if ntl < DBG.get('rwtiles', 99) and (DBG.get('rwonly') is None or half == 0 or ntl == DBG['rwonly']):
                            rw_tile(c + ti * 128, kind, g)
                        ntl += 1
                if half == 1:
                    for h2 in range(2):
                        dma('sp', o_wkvp[l].rearrange("(p4 h2) j i -> h2 j p4 i", h2=2)[h2], st_S[h2 * 64:(h2 + 1) * 64, l, :, :],
                            [h_stS[l]], ())
                E.barrier()
                AR.pop()

            def mixer_sub():
                AR.push()
                NT_ = 1024 + (128 if (half == 1 and DBG['samp']) else 0)
                winv = d_win[l].rearrange("(k p) n -> p k n", p=128)
                mixT = AR.alloc([8, NT_], BF16); h_mix = [E.H("mix%d" % g) for g in range(len(groups))]
                AR.push()
                sq = AR.alloc([8, 512], BF16); h_t1 = E.H("t1m")
                rstd = AR.alloc([512], F32)
                hl = AR.alloc([8], F32); h_hl = E.H("hl")
                hs_ = AR.alloc([8, 16], F32); h_hs = E.H("hs")
                if half == 1:
                    cp('dve', hT[:, :, 0], st_shift[:, l, :], [h_stshift[l]], [h_h[0]])
                for g, (c, n, kind) in enumerate(groups):
                    rms_rstd([xT[:, k, c:c + n] for k in range(8)], n, [h_x[g]], sq, rstd[:, 0:n], h_t1)
                    for k in range(8):
                        stt(hT[:, k, 1 + c:1 + c + n], xT[:, k, c:c + n], ngc(0, k), rstd[:, 0:n],
                            ALU.mult, ALU.mult, [h_x[g], h_t1, h_pp], [h_h[g]])
                    if kind == 'p' and g == 1:
                        if half == 0:
                            cp('pool', st_shift[:, l, :], hT[:, :, 1024], [h_h[g]], [h_stshift[l]])
                        else:
                            for k in range(8):
                                stt(hl[:, k:k + 1], xT[:, k, 1023:1024], ngc(0, k), rstd[:, 511:512], ALU.mult, ALU.mult,
                                    [h_x[g], h_t1, h_pp], [h_hl])
                            dma('sp', o_shiftp[l], hl, [h_hl], ())
                    if kind == 's':
                        xs7 = xT[:, :, 1024:1152].rearrange("p k (b t) -> p k b t", t=8)
                        r7 = rstd[:, 0:128].rearrange("p (b t) -> p b t", t=8)[:, :, 7]
                        for k in range(8):
                            stt(hs_[:, k, :], xs7[:, k, :, 7], ngc(0, k), r7, ALU.mult, ALU.mult,
                                [h_x[g], h_t1, h_pp], [h_hs])
                        dma('sp', o_shifts[l], hs_, [h_hs], ())
                E.barrier()
                AR.pop()
                AR.push()
                Wc = AR.alloc([8, 1024], BF16); h_Wc = E.H("Wc")
                E.begin_group()
                for k in range(8):
                    dma('pool', Wc[:, k, :], winv[:, k, 0:1024], (), [h_Wc])
                E.end_group()
                uf = AR.alloc([4, 512], F32); h_uf = E.H("uf")
                sg = AR.alloc([512], F32); h_sg = E.H("sg")
                uT = AR.alloc([4, 542], BF16); h_uT = E.H("uT")
                uTs = AR.alloc([4, 16, 38], BF16); h_uTs = E.H("uTs")
                Dg = AR.alloc([31, 128], BF16); h_Dg = E.H("Dg")
                ysb = AR.alloc([4, 512], F32); h_ysb = E.H("ysb")
                ybf = AR.alloc([4, 512], BF16); h_ybf = E.H("ybf")
                ysq = AR.alloc([4, 512], BF16); h_ysq = E.H("ysq")
                mean = AR.alloc([512], F32); h_mean = E.H("mean")
                m2 = AR.alloc([512], F32); h_m2 = E.H("m2")
                rs2 = AR.alloc([512], F32); h_rs2 = E.H("rs2")
                tl = AR.alloc([512], F32); h_tl = E.H("tl")
                for g, (c, n, kind) in enumerate(groups):
                    if kind == 'p':
                        if g == 0:
                            cp('pool', uT[:, :, 0:30], st_conv[:, l, :, :], [h_stconv[l]], [h_uT])
                        else:
                            cp('pool', uT[:, :, 0:30], uT[:, :, 512:542], [h_uT], [h_uT])
                    else:
                        E.begin_group()
                        for c4 in range(4):
                            dma('pool', uTs[:, c4, :, 0:30],
                                d_conv[l][:, c4 * 128:(c4 + 1) * 128, :].rearrange("b p t -> p b t"), (), [h_uTs])
                        E.end_group()
                    for c4 in range(4):
                        pbv, hbv = bank()
                        for k in range(8):
                            mm(pbv[:, 0:n], Wc[:, k, c4 * 128:(c4 + 1) * 128], hT[:, k, 1 + c:1 + c + n], k == 0, k == 7,
                               [h_Wc, h_h[g]], [hbv])
                        pbg, hbg = bank()
                        for k in range(8):
                            mm(pbg[:, 0:n], Wc[:, k, 512 + c4 * 128:512 + (c4 + 1) * 128], hT[:, k, 1 + c:1 + c + n],
                               k == 0, k == 7, [h_Wc, h_h[g]], [hbg])
                        act(sg[:, 0:n], pbg[:, 0:n], AF.Sigmoid, [hbg], [h_sg])
                        tt('dve', uf[:, c4, 0:n], pbv[:, 0:n], sg[:, 0:n], ALU.mult, [hbv, h_sg], [h_uf])
                        if kind == 'p':
                            cp('act', uT[:, c4, 30:30 + n], uf[:, c4, 0:n], [h_uf], [h_uT])
                        else:
                            cp('act', uTs[:, c4, :, 30:38], uf[:, c4, 0:128].rearrange("p (b t) -> p b t", t=8),
                               [h_uf], [h_uTs])
                    if kind == 'p' and g == 1:
                        if half == 0:
                            cp('pool', st_conv[:, l, :, :], uT[:, :, 512:542], [h_uT], [h_stconv[l]])
                        else:
                            dma('sp', o_convp[l].rearrange("(c p) t -> p c t", p=128), uf[:, :, 482:512], [h_uf], ())
                    if kind == 's':
                        dma('sp', o_convs[l][:, :, 0:22], d_conv[l][:, :, 8:30], (), ())
                        for c4 in range(4):
                            dma('sp', o_convs[l][:, c4 * 128:(c4 + 1) * 128, 22:30].rearrange("b p t -> p b t"),
                                uf[:, c4, 0:128].rearrange("p (b t) -> p b t", t=8), [h_uf], ())
                    for c4 in range(4):
                        o_cw = PP_CW + (l * 4 + c4) * 31
                        tt('pool', Dg, ident_b.unsqueeze(1).to_broadcast([128, 31, 128]),
                           pp[:, o_cw:o_cw + 31].unsqueeze(2).to_broadcast([128, 31, 128]), ALU.mult,
                           [h_con, h_pp], [h_Dg])
                        pby, hby = bank()
                        for j in range(31):
                            rhs = uT[:, c4, j:j + n] if kind == 'p' else uTs[:, c4, :, j:j + 8]
                            mm(pby[:, 0:n], Dg[:, j, :], rhs, j == 0, j == 30, [h_Dg, h_uT, h_uTs], [hby])
                        cbc = pp[:, PP_CB + l * 4 + c4:PP_CB + l * 4 + c4 + 1]
                        act(ysb[:, c4, 0:n], pby[:, 0:n], AF.Identity, [hby, h_pp], [h_ysb], bias=cbc, scale=1.0)
                        act(ybf[:, c4, 0:n], pby[:, 0:n], AF.Identity, [hby, h_pp], [h_ybf], bias=cbc, scale=1.0)
                        act(ysq[:, c4, 0:n], pby[:, 0:n], AF.Square, [hby, h_pp], [h_ysq], bias=cbc, scale=1.0)
                    pb1, hb1 = bank()
                    for c4 in range(4):
                        mm(pb1[:, 0:n], onesb, ybf[:, c4, 0:n], c4 == 0, c4 == 3, [h_ybf, h_con], [hb1])
                    pb2, hb2 = bank()
                    for c4 in range(4):
                        mm(pb2[:, 0:n], onesb, ysq[:, c4, 0:n], c4 == 0, c4 == 3, [h_ysq, h_con], [hb2])
                    act(mean[:, 0:n], pb1[:, 0:n], AF.Identity, [hb1], [h_mean], scale=1.0 / 512.0, bias=epsc(3))
                    tt('dve', m2[:, 0:n], mean[:, 0:n], mean[:, 0:n], ALU.mult, [h_mean], [h_m2])
                    stt(m2[:, 0:n], pb2[:, 0:n], 1.0 / 512.0, m2[:, 0:n], ALU.mult, ALU.subtract, [hb2, h_m2], [h_m2])
                    act(rs2[:, 0:n], m2[:, 0:n], AF.Sqrt, [h_m2, h_con], [h_rs2], scale=1.0, bias=epsc(1))
                    recip(rs2[:, 0:n], rs2[:, 0:n], [h_rs2], [h_rs2])
                    for c4 in range(4):
                        tt('dve', tl[:, 0:n], ysb[:, c4, 0:n], mean[:, 0:n], ALU.subtract, [h_ysb, h_mean], [h_tl])
                        tt('dve', tl[:, 0:n], tl[:, 0:n], rs2[:, 0:n], ALU.mult, [h_tl, h_rs2], [h_tl])
                        ol = l * 4 + c4
                        ts('dve', tl[:, 0:n], tl[:, 0:n], pp[:, PP_LG + ol:PP_LG + ol + 1], ALU.mult, [h_tl, h_pp], [h_tl],
                           s2=pp[:, PP_LB + ol:PP_LB + ol + 1], op1=ALU.add)
                        act(mixT[:, c4, c:c + n], tl[:, 0:n], AF.Silu, [h_tl], [h_mix[g]])
                E.barrier()
                AR.pop()
                if DBG.get('rwkv'):
                    rwkv_sub(mixT, h_mix)
                else:
                    for g, (c, n, kind) in enumerate(groups):
                        mset('pool', mixT[:, 4:8, c:c + n], 0.0, [h_mix[g]])
                wout = AR.alloc([8, 1024], BF16); h_wout = E.H("wout")
                load_w('pool', wout, d_wout[l], h_wout)
                sq = AR.alloc([8, 512], BF16); h_t1 = E.H("t1m3")
                rstd = AR.alloc([512], F32)
                tmp = AR.alloc([512], F32); h_t2 = E.H("t2m")
                msb = AR.alloc([8, 512], F32); h_msb = E.H("msbm")
                for g, (c, n, kind) in enumerate(groups):
                    for oc in range(8):
                        pb, hb = bank()
                        for k in range(8):
                            mm(pb[:, 0:n], wout[:, k, oc * 128:(oc + 1) * 128], mixT[:, k, c:c + n], k == 0, k == 7,
                               [h_wout, h_mix[g]], [hb])
                        cp('act', msb[:, oc, 0:n], pb[:, 0:n], [hb], [h_msb])
                    post_res(1, g, [msb[:, k, 0:n] for k in range(8)], [h_msb], sq, rstd, tmp, h_t1, h_t2)
                E.barrier()
                AR.pop()

            if DBG.get('mix'):
                mixer_sub()
            if DBG['att']:
                attention_sub()
            if DBG['ffn']:
                ffn_sub()

        for half in range(DBG['halves']):
            groups = [(0, 512, 'p'), (512, 512, 'p')] + ([(1024, 128, 's')] if (half == 1 and DBG['samp']) else [])
            NTOK = 1024 + (128 if half == 1 else 0)
            h_x = [E.H("x%d" % g) for g in range(len(groups))]
            h_h = [E.H("h%d" % g) for g in range(len(groups))]
            xv = d_xT.rearrange("(k p) n -> p k n", p=128)
            yv = o_yT.rearrange("(k p) n -> p k n", p=128)
            for g, (c, n, kind) in enumerate(groups):
                src = xv[:, :, half * 1024 + c: half * 1024 + c + n] if kind == 'p' else xv[:, :, 2048:2176]
                dma('sp', xT[:, :, c:c + n], src, (), [h_x[g]])

            if half == 0:
                AR.push()
                mraw = AR.alloc([8, 256], F32); h_mraw = E.H("mraw")
                msq = AR.alloc([8, 256], BF16)
                mr = AR.alloc([256], F32); h_mt = E.H("mtmp")
                dma('sp', mraw, d_memT.rearrange("(k p) n -> p k n", p=128), (), [h_mraw])
                rms_rstd([mraw[:, k, :] for k in range(8)], 256, [h_mraw], msq, mr, h_mt)
                for k in range(8):
                    stt(memn[:, k, :], mraw[:, k, :], pp[:, PP_MG + k:PP_MG + k + 1], mr, ALU.mult, ALU.mult,
                        [h_mraw, h_mt, h_pp], [h_memn])
                E.barrier()
                AR.pop()

            for l in range(DBG['layers']):
                layer(half, l, groups, h_x, h_h)

            for g, (c, n, kind) in enumerate(groups):
                dst = yv[:, :, half * 1024 + c: half * 1024 + c + n] if kind == 'p' else yv[:, :, 2048:2176]
                dma('sp', dst, xT[:, :, c:c + n], [h_x[g]], ())
            E.barrier()

        E.finalize()
        import contextlib
        with contextlib.ExitStack() as es:
            csem = {e: es.enter_context(nc.semaphore("c_" + e)) for e in ENGS}
            dsem = [es.enter_context(nc.semaphore("d%d" % i)) for i in range(NDSEM)]
            block = es.enter_context(nc.Block())

            @block.tensor
            def _(e):
                E.emit('pe', e, csem, dsem)

            @block.scalar
            def _(e):
                E.emit('act', e, csem, dsem)

            @block.vector
            def _(e):
                E.emit('dve', e, csem, dsem)

            @block.gpsimd
            def _(e):
                E.emit('pool', e, csem, dsem)

            @block.sync
            def _(e):
                E.emit('sp', e, csem, dsem)
    return nc


_PROG = {}


def _merge_masks():
    m = np.zeros((128, 896), np.float32)
    s, t = np.arange(128)[:, None], np.arange(128)[None, :]
    for lv in range(7):
        b = 1 << lv
        m[:, lv * 128:(lv + 1) * 128] = ((s // (2 * b)) == (t // (2 * b))) & ((s % (2 * b)) < b) & ((t % (2 * b)) >= b)
    return m


def _consts():
    c = np.zeros((128, NCON), np.float32)
    idx = np.arange(128)
    c[:, C_ID:C_ID + 128] = np.eye(128, dtype=np.float32)
    s, t = idx[:, None], idx[None, :]
    c[:, C_MU:C_MU + 128] = (s < t)
    c[:, C_MI:C_MI + 128] = (s <= t)
    c[:, C_ML:C_ML + 128] = (t < s)
    same = (s // 8) == (t // 8)
    c[:, C_MUS:C_MUS + 128] = (s < t) & same
    c[:, C_MIS:C_MIS + 128] = (s <= t) & same
    c[:, C_MLS:C_MLS + 128] = (t < s) & same
    c[:, C_SEQ:C_SEQ + 16] = (idx[:, None] // 8) == np.arange(16)[None, :]
    c[:, C_EPS + 0] = 1e-6
    c[:, C_EPS + 1] = 1e-5
    c[:, C_EPS + 2] = 64e-5
    c[:, C_EPS + 3] = 0.0
    c[:, C_EPS + 4] = 1.0
    return c


def _prepare(inp):
    f = lambda a: np.ascontiguousarray(np.asarray(a), dtype=np.float32)
    I = {k: f(v) for k, v in inp.items()}
    pp = np.zeros((128, NPP), np.float32)
    ng = I['norm_g'].reshape(4, 6, 8, 128)
    pp[:, PP_NG:PP_NG + 192] = ng.transpose(3, 0, 1, 2).reshape(128, 192)
    pp[:, PP_MG:PP_MG + 8] = I['mem_norm_g'].reshape(8, 128).T
    cw = I['conv_w'].reshape(4, 31, 4, 128)
    pp[:, PP_CW:PP_CW + 496] = cw.transpose(3, 0, 2, 1).reshape(128, 496)
    pp[:, PP_CB:PP_CB + 16] = I['conv_b'].reshape(4, 4, 128).transpose(2, 0, 1).reshape(128, 16)
    pp[:, PP_LG:PP_LG + 16] = I['conv_ln_g'].reshape(4, 4, 128).transpose(2, 0, 1).reshape(128, 16)
    pp[:, PP_LB:PP_LB + 16] = I['conv_ln_b'].reshape(4, 4, 128).transpose(2, 0, 1).reshape(128, 16)
    pp[:, PP_MUD:PP_MUD + 8] = I['mu_shift'][:, 1536:1792].reshape(4, 2, 128).transpose(2, 0, 1).reshape(128, 8)
    bc = np.zeros((4, 128, NBC), np.float32)
    for j, nm in enumerate(['rwkv_k_k', 'rwkv_k_a', 'rwkv_r_k', 'rwkv_gn_g', 'rwkv_gn_b']):
        bc[:, :, j * 512:(j + 1) * 512] = I[nm].reshape(4, 1, 512)
    bc[:, :, 2560:] = I['mu_shift'].reshape(4, 1, 1792)
    rows = np.concatenate([I['rwkv_w0'], I['rwkv_a0']], axis=1)
    wup = np.concatenate([I['rwkv_w_up'], I['rwkv_a_up']], axis=1)
    gup = I['rwkv_g_up']
    consts = _consts()
    mrg = _merge_masks()
    in_maps = []
    for c in range(NCORE):
        sb = slice(16 * c, 16 * c + 16)
        xs = I['x_sample'][sb].reshape(128, 1024)
        xT = np.ascontiguousarray(np.concatenate([I['x_prompt'][c], xs], axis=0).T)
        m = {
            "xT": xT,
            "memT": np.ascontiguousarray(I['mem_prompt'][c].T),
            "kTc": np.ascontiguousarray(I['cache_mem_k'][:, sb].transpose(0, 1, 3, 4, 2)),
            "vc": np.ascontiguousarray(I['cache_mem_v'][:, sb].reshape(4, 16, 256, 1024)),
            "wkvT": np.ascontiguousarray(I['state_wkv'][:, sb].transpose(0, 1, 2, 4, 3)),
            "convT": np.ascontiguousarray(I['state_conv'][:, sb].transpose(0, 1, 3, 2)),
            "shiftT": np.ascontiguousarray(I['state_shift'][:, sb].transpose(0, 2, 1)),
            "w_in": I['w_in'], "w_out": I['w_out'], "w_q": I['w_q'], "w_k": I['w_k'], "w_v": I['w_v'],
            "w_o": I['w_o'], "w_ffn1": I['w_ffn1'], "w_ffn2": I['w_ffn2'],
            "pp": pp, "bc": bc, "rows": rows, "wup": wup, "gup": gup, "consts": consts, "mrg": mrg,
        }
        in_maps.append(m)
    return in_maps


def kernel(**inp):
    in_maps = _prepare(inp)
    if 'nc' not in _PROG:
        _PROG['nc'] = build_program()
    res = run_bass_kernel_spmd(_PROG['nc'], in_maps, core_ids=list(range(NCORE)))
    R = res.results
    y_prompt = np.zeros((8, 2048, 1024), np.float32)
    y_sample = np.zeros((128, 8, 1024), np.float32)
    nk = np.zeros((4, 8, 256, 4, 256), np.float32)
    nv = np.zeros((4, 8, 256, 4, 256), np.float32)
    wkvp = np.zeros((4, 8, 8, 64, 64), np.float32)
    convp = np.zeros((4, 8, 30, 512), np.float32)
    shiftp = np.zeros((4, 8, 1024), np.float32)
    wkvs = np.zeros((4, 128, 8, 64, 64), np.float32)
    convs = np.zeros((4, 128, 30, 512), np.float32)
    shifts = np.zeros((4, 128, 1024), np.float32)
    for c in range(NCORE):
        r = R[c]
        sb = slice(16 * c, 16 * c + 16)
        yT = np.asarray(r["yT"])
        y_prompt[c] = yT[:, :2048].T
        y_sample[sb] = yT[:, 2048:].T.reshape(16, 8, 1024)
        nk[:, c] = np.asarray(r["nk"]).reshape(4, 256, 4, 256)
        nv[:, c] = np.asarray(r["nv"]).reshape(4, 256, 4, 256)
        wkvp[:, c] = np.asarray(r["wkvp"]).transpose(0, 1, 3, 2)
        convp[:, c] = np.asarray(r["convp"]).transpose(0, 2, 1)
        shiftp[:, c] = np.asarray(r["shiftp"]).transpose(0, 2, 1).reshape(4, 1024)
        wkvs[:, sb] = np.asarray(r["wkvs"]).transpose(0, 1, 2, 4, 3)
        convs[:, sb] = np.asarray(r["convs"]).transpose(0, 1, 3, 2)
        shifts[:, sb] = np.asarray(r["shifts"]).transpose(0, 3, 2, 1).reshape(4, 16, 1024)
    return (y_prompt, y_sample, nk, nv, wkvp, convp, shiftp, wkvs, convs, shifts)
```

```python
import numpy as np
import concourse.bass as bass
import concourse.mybir as mybir
from concourse.bass_utils import run_bass_kernel_spmd

F32 = mybir.dt.float32
BF16 = mybir.dt.bfloat16
AF = mybir.ActivationFunctionType
ALU = mybir.AluOpType
AX = mybir.AxisListType

DEPTH = 4
NCORE = 8
CDEC = 0.6065306597126334

PP_NG, PP_MG, PP_CW, PP_CB, PP_LG, PP_LB, PP_MUD, NPP = 0, 192, 200, 696, 712, 728, 744, 752
C_ID, C_MU, C_MI, C_ML, C_MUS, C_MIS, C_MLS, C_SEQ, C_EPS, NCON = 0, 128, 256, 384, 512, 640, 768, 896, 912, 920
NBC = 2560 + 1792

ENGS = ['pe', 'act', 'dve', 'pool', 'sp']
HOLE_LO, HOLE_HI = 207 * 256, 207 * 256
NDSEM = 90
NSW = 45
DBGV = {}
DBG = dict(layers=4, halves=2, att=True, ffn=True, samp=True, stop=99, mix=True, rwkv=True)


class Hd:
    __slots__ = ('name', 'w', 'r', 'excl')

    def __init__(self, name, excl=False):
        self.name = name
        self.w = None
        self.r = []
        self.excl = excl


class Emitter:
    def __init__(self):
        self.ops = []
        self.handles = []
        self.last_compute = {e: None for e in ENGS}
        self.out_dma = []
        self.group = None
        self.ngroups_phase = 0
        self.groups = []

    def H(self, name, excl=False):
        h = Hd(name, excl)
        self.handles.append(h)
        return h

    def op(self, eng, fn, r=(), w=(), dma=False):
        i = len(self.ops)
        deps = set()
        w = list(w) + [h for h in r if h.excl]
        r = [h for h in r if not h.excl]
        for h in r:
            if h.w is not None:
                deps.add(h.w)
        for h in w:
            if h.w is not None:
                deps.add(h.w)
            deps.update(h.r)
        for h in r:
            h.r.append(i)
        for h in w:
            h.w = i
            h.r = []
        deps.discard(i)
        gid = None
        if dma:
            if self.group is not None:
                gid = self.group
            else:
                gid = len(self.groups)
                self.groups.append([])
                self.ngroups_phase += 1
            if self.group is not None:
                deps = set(d for d in deps if not (self.ops[d]['dma'] and self.ops[d]['gid'] == gid))
            self.groups[gid].append(i)
            self.out_dma.append(i)
        else:
            self.last_compute[eng] = i
        self.ops.append(dict(eng=eng, fn=fn, deps=deps, dma=dma, gid=gid, sig=False))
        return i

    def begin_group(self):
        self.group = len(self.groups)
        self.groups.append([])
        self.ngroups_phase += 1

    def end_group(self):
        self.group = None

    def barrier(self):
        deps = set(v for v in self.last_compute.values() if v is not None) | set(self.out_dma)
        for e in ENGS:
            self.ops.append(dict(eng=e, fn=None, deps=set(deps), dma=False, gid=None, sig=False, bar=True))
        for h in self.handles:
            h.w = None
            h.r = []
        self.out_dma = []
        self.ngroups_phase = 0
        self.ops[-1]['phase_end'] = True

    def finalize(self):
        ops = self.ops
        for j, o in enumerate(ops):
            for d in o['deps']:
                p = ops[d]
                if p['dma']:
                    continue
                if p['eng'] == 'pe' and o['eng'] == 'pe' and not o.get('bar'):
                    continue
                p['sig'] = True
        cnt = {e: 0 for e in ENGS}
        for o in ops:
            if o['sig']:
                cnt[o['eng']] += 1
                o['sigval'] = cnt[o['eng']]
        gsem = {}
        semcnt = [0] * NDSEM
        nxt = {'pool': 0, 'sp': NSW}
        lim = {'pool': NSW, 'sp': NDSEM}
        gfinal = {}
        for j, o in enumerate(ops):
            if o['dma']:
                g = o['gid']
                q = o['eng']
                if g not in gsem:
                    gsem[g] = nxt[q]
                    nxt[q] += 1
                    assert nxt[q] <= lim[q], (q, nxt[q])
                k = gsem[g]
                semcnt[k] += 16
                gfinal[g] = semcnt[k]
            if o.get('phase_end'):
                nxt = {'pool': 0, 'sp': NSW}
        self.gsem, self.gfinal = gsem, gfinal

    def emit(self, eng_name, engobj, csem, dsem):
        ops = self.ops
        waited = {}
        for o in ops:
            if o['eng'] != eng_name:
                continue
            for d in sorted(o['deps']):
                p = ops[d]
                if p['dma']:
                    key = ('d', self.gsem[p['gid']])
                    val = self.gfinal[p['gid']]
                    sem = dsem[self.gsem[p['gid']]]
                else:
                    if not p['sig']:
                        continue
                    key = ('c', p['eng'])
                    val = p['sigval']
                    sem = csem[p['eng']]
                if waited.get(key, 0) < val:
                    engobj.wait_ge(sem, val)
                    waited[key] = val
            if o['fn'] is None:
                continue
            ins = o['fn'](engobj)
            if o['dma']:
                ins.then_inc(dsem[self.gsem[o['gid']]], 16)
            elif o['sig']:
                ins.then_inc(csem[eng_name], 1)


class Arena:
    def __init__(self, ap, words):
        self.ap = ap
        self.words = words
        self.tops = [0, HOLE_HI]
        self.lims = [HOLE_LO, words]
        self.stack = []

    def alloc(self, free_shape, dtype):
        n = 1
        for s in free_shape:
            n *= s
        nbytes = n * (2 if dtype == BF16 else 4)
        w = (nbytes + 31) // 32 * 8
        for r in (0, 1):
            if self.tops[r] + w <= self.lims[r]:
                off = self.tops[r]
                self.tops[r] += w
                self.last = (off, n, dtype == BF16)
                break
        else:
            raise AssertionError(("arena overflow", self.tops, w, self.lims))
        v = self.ap[:, off:off + w]
        if dtype == BF16:
            v = v.bitcast(BF16)[:, 0:n]
        else:
            v = v[:, 0:n]
        if len(free_shape) == 2:
            v = v.rearrange("p (a b) -> p a b", a=free_shape[0])
        elif len(free_shape) == 3:
            v = v.rearrange("p (a b c) -> p a b c", a=free_shape[0], b=free_shape[1])
        elif len(free_shape) == 4:
            v = v.rearrange("p (a b c d) -> p a b c d", a=free_shape[0], b=free_shape[1], c=free_shape[2])
        return v

    def push(self):
        self.stack.append(list(self.tops))

    def pop(self):
        self.tops = self.stack.pop()


def build_program():
    nc = bass.Bass("TRN2", target_bir_lowering=False)
    E = Emitter()

    def din(name, shape):
        return nc.dram_tensor(name, list(shape), F32, kind="ExternalInput").ap()

    def dout(name, shape):
        return nc.dram_tensor(name, list(shape), F32, kind="ExternalOutput").ap()

    d_xT = din("xT", [1024, 2176])
    d_memT = din("memT", [1024, 256])
    d_kT = din("kTc", [4, 16, 4, 256, 256])
    d_v = din("vc", [4, 16, 256, 1024])
    d_wkv = din("wkvT", [4, 16, 8, 64, 64])
    d_conv = din("convT", [4, 16, 512, 30])
    d_shift = din("shiftT", [4, 1024, 16])
    d_win = din("w_in", [4, 1024, 2816])
    d_wout = din("w_out", [4, 1024, 1024])
    d_wq = din("w_q", [4, 1024, 1024])
    d_wk = din("w_k", [4, 1024, 1024])
    d_wv = din("w_v", [4, 1024, 1024])
    d_wo = din("w_o", [4, 1024, 1024])
    d_w1 = din("w_ffn1", [4, 1024, 4096])
    d_w2 = din("w_ffn2", [4, 4096, 1024])
    d_pp = din("pp", [128, NPP])
    d_bc = din("bc", [4, 128, NBC])
    d_rows = din("rows", [4, 1024])
    d_wup = din("wup", [4, 128, 512])
    d_gup = din("gup", [4, 128, 512])
    d_con = din("consts", [128, NCON])
    d_mrg = din("mrg", [128, 896])

    o_yT = dout("yT", [1024, 2176])
    o_nk = dout("nk", [4, 256, 1024])
    o_nv = dout("nv", [4, 256, 1024])
    o_wkvp = dout("wkvp", [4, 8, 64, 64])
    o_convp = dout("convp", [4, 512, 30])
    o_shiftp = dout("shiftp", [4, 128, 8])
    o_wkvs = dout("wkvs", [4, 16, 8, 64, 64])
    o_convs = dout("convs", [4, 16, 512, 30])
    o_shifts = dout("shifts", [4, 128, 8, 16])

    ARENA_WORDS = 207 * 256
    with (
        nc.sbuf_tensor("arena", [128, ARENA_WORDS], F32) as arena_t,
        nc.psum_tensor("ps0", [128, 512], F32) as ps0, nc.psum_tensor("ps1", [128, 512], F32) as ps1,
        nc.psum_tensor("ps2", [128, 512], F32) as ps2, nc.psum_tensor("ps3", [128, 512], F32) as ps3,
        nc.psum_tensor("ps4", [128, 512], F32) as ps4, nc.psum_tensor("ps5", [128, 512], F32) as ps5,
        nc.psum_tensor("ps6", [128, 512], F32) as ps6, nc.psum_tensor("ps7", [128, 512], F32) as ps7,
    ):
        AR = Arena(arena_t[:, :], ARENA_WORDS)
        banks = [(p[:, :], E.H("bank%d" % i, excl=True)) for i, p in enumerate([ps0, ps1, ps2, ps3, ps4, ps5, ps6, ps7])]
        bank_ctr = [0]

        def bank():
            b = banks[bank_ctr[0] % 8]
            bank_ctr[0] += 1
            return b

        def mm(out, lhsT, rhs, start, stop, r, w):
            E.op('pe', lambda e: e.matmul(out, lhsT, rhs, start=start, stop=stop), r, w)

        def tr(out, in_, ident, r, w):
            E.op('pe', lambda e: e.transpose(out, in_, ident), r, w)

        def act(out, in_, func, r, w, scale=None, bias=None):
            kw = {}
            if scale is not None:
                kw['scale'] = scale
            if bias is not None:
                kw['bias'] = bias
            E.op('act', lambda e: e.activation(out, in_, func, **kw), r, w)

        def tt(eng, out, in0, in1, op, r, w):
            E.op(eng, lambda e: e.tensor_tensor(out, in0, in1, op), r, w)

        def ts(eng, out, in0, s1, op0, r, w, s2=None, op1=None):
            if op1 is None:
                E.op(eng, lambda e: e.tensor_scalar(out, in0, s1, None, op0), r, w)
            else:
                E.op(eng, lambda e: e.tensor_scalar(out, in0, s1, s2, op0, op1), r, w)

        def stt(out, in0, scalar, in1, op0, op1, r, w):
            E.op('dve', lambda e: e.scalar_tensor_tensor(out, in0, scalar, in1, op0, op1), r, w)

        def cp(eng, out, in_, r, w):
            if eng == 'act':
                E.op('act', lambda e: e.copy(out, in_), r, w)
            else:
                E.op(eng, lambda e: e.tensor_copy(out, in_), r, w)

        def red(out, in_, r, w):
            E.op('dve', lambda e: e.tensor_reduce(out, in_, AX.X, ALU.add), r, w)

        def recip(out, in_, r, w):
            E.op('dve', lambda e: e.reciprocal(out, in_), r, w)

        def mset(eng, ap, val, w):
            E.op(eng, lambda e: e.memset(ap, val), (), w)

        def dma(q, out, in_, r, w):
            E.op(q, lambda e: e.dma_start(out=out, in_=in_), r, w, dma=True)

        con_f = AR.alloc([NCON], F32); h_con = E.H("con")
        con_b = AR.alloc([NCON], BF16)
        pp = AR.alloc([NPP], F32); h_pp = E.H("pp")
        xT = AR.alloc([8, 1152], F32)
        hT = AR.alloc([8, 1153], BF16)
        memn = AR.alloc([8, 256], BF16); h_memn = E.H("memn")
        st_S = AR.alloc([DEPTH, 4, 64], F32); h_stS = [E.H("stS%d" % l) for l in range(DEPTH)]
        st_conv = AR.alloc([DEPTH, 4, 30], BF16); h_stconv = [E.H("stc%d" % l) for l in range(DEPTH)]
        st_shift = AR.alloc([DEPTH, 8], BF16); h_stshift = [E.H("sts%d" % l) for l in range(DEPTH)]

        ident_b = con_b[:, C_ID:C_ID + 128]
        ident_f = con_f[:, C_ID:C_ID + 128]
        ones_b = None

        def epsc(i):
            return con_f[:, C_EPS + i:C_EPS + i + 1]

        dma('sp', con_f, d_con, (), [h_con])
        dma('pool', con_b, d_con, (), [h_con])
        dma('sp', pp, d_pp, (), [h_pp])
        onesb = AR.alloc([128], BF16)
        mset('dve', onesb, 1.0, [h_con])
        mset('dve', st_S, 0.0, h_stS)
        mset('dve', st_conv, 0.0, h_stconv)
        mset('dve', st_shift, 0.0, h_stshift)
        h_hinit = E.H('hinit')
        mset('dve', hT[:, :, 0:1], 0.0, [h_hinit])

        def rms_rstd(src_list, n, rsrc, sqbuf, rstd_out, h_tmp, eps_i=0, dim=1024.0):
            pb, hb = bank()
            nk = len(src_list)
            for k, s in enumerate(src_list):
                act(sqbuf[:, k, 0:n], s, AF.Square, rsrc, [h_tmp])
            for k in range(nk):
                mm(pb[:, 0:n], onesb, sqbuf[:, k, 0:n], k == 0, k == nk - 1, [h_tmp, h_con], [hb])
            act(rstd_out, pb[:, 0:n], AF.Sqrt, [hb, h_con], [h_tmp], scale=1.0 / dim, bias=epsc(eps_i))
            recip(rstd_out, rstd_out, [h_tmp], [h_tmp])


        def layer(half, l, groups, h_x, h_h):
            def ngc(i, k):
                o = PP_NG + l * 48 + i * 8 + k
                return pp[:, o:o + 1]

            def norm_to_h(i, sq, rstd, h_tmp):
                for g, (c, n, kind) in enumerate(groups):
                    rms_rstd([xT[:, k, c:c + n] for k in range(8)], n, [h_x[g]], sq, rstd[:, 0:n], h_tmp)
                    for k in range(8):
                        stt(hT[:, k, 1 + c:1 + c + n], xT[:, k, c:c + n], ngc(i, k), rstd[:, 0:n],
                            ALU.mult, ALU.mult, [h_x[g], h_tmp, h_pp], [h_h[g]])

            def post_res(i, g, src_list, rsrc, sq, rstd, tmp, h_tmp, h_tmp2):
                c, n, kind = groups[g]
                rms_rstd(src_list, n, rsrc, sq, rstd[:, 0:n], h_tmp)
                for k in range(8):
                    tt('pool', src_list[k], src_list[k], rstd[:, 0:n], ALU.mult, rsrc + [h_tmp], rsrc)
                for k in range(8):
                    stt(xT[:, k, c:c + n], src_list[k], ngc(i, k), xT[:, k, c:c + n], ALU.mult, ALU.add,
                        rsrc + [h_pp, h_x[g]], [h_x[g]])

            def load_w(q, dst, dsrc, h, ncol_split=None):
                v = dsrc.rearrange("(k p) n -> p k n", p=128)
                E.begin_group()
                for k in range(8):
                    dma(q, dst[:, k, :], v[:, k, :], (), [h])
                E.end_group()

            def attention_sub():
                AR.push()
                wq = AR.alloc([8, 1024], BF16); h_wq = E.H("wq")
                wk = AR.alloc([8, 1024], BF16); h_wk = E.H("wk")
                wv = AR.alloc([8, 1024], BF16); h_wv = E.H("wv")
                wo = AR.alloc([8, 1024], BF16); h_wo = E.H("wo")
                load_w('pool', wq, d_wq[l], h_wq)
                if DBG.get('nw', 4) > 1:
                    load_w('pool', wk, d_wk[l], h_wk)
                    load_w('pool', wv, d_wv[l], h_wv)
                    load_w('pool', wo, d_wo[l], h_wo)
                if DBG['stop'] <= 0:
                    E.barrier(); AR.pop(); return
                if DBG.get('wbar'):
                    E.barrier()
                stg = AR.alloc([2, 512], F32); h_stg = [E.H("stg0"), E.H("stg1")]
                sq = AR.alloc([8, 512], BF16); h_t1 = E.H("t1")
                rstd = AR.alloc([512], F32)
                tmp = AR.alloc([512], F32); h_t2 = E.H("t2")
                msb = AR.alloc([8, 512], F32); h_msb = E.H("msb")
                KT = AR.alloc([8, 256], BF16); h_KT = E.H("KT")
                Vb = AR.alloc([2, 1024], BF16); h_Vb = E.H("Vb")
                qT = sq; h_qT = h_t1
                oT = AR.alloc([8, 512], BF16); h_oT = E.H("oT")
                if DBG.get('stghi'):
                    stg = AR.alloc([2, 512], F32)
                PT = AR.alloc([2, 512], BF16); h_PT = E.H("PT")
                rs = AR.alloc([512], F32); h_rs = E.H("rs")
                if not DBG.get('skip1'):
                    norm_to_h(2, sq, rstd, h_t1)
                if DBG['stop'] <= 1:
                    E.barrier(); AR.pop(); return
                for oc in range(0 if DBG.get('skip2') else 8):
                    pb, hb = bank()
                    for k in range(8):
                        mm(pb[:, 0:256], wk[:, k, oc * 128:(oc + 1) * 128], memn[:, k, :], k == 0, k == 7,
                           [h_wk, h_memn], [hb])
                    cp('act', KT[:, oc, :], pb[:, 0:256], [hb], [h_KT])
                if DBG['stop'] <= 2:
                    E.barrier(); AR.pop(); return
                si = 0
                for which, wsrc, hw, dout_ in ((0, wv, h_wv, o_nv), (1, wk, h_wk, o_nk)):
                    if which == 1 and half == 1:
                        continue
                    if which >= DBG.get('vw', 2):
                        continue
                    for mc in range(DBG.get('vmc', 2)):
                        for hf in range(DBG.get('vhf', 2)):
                            pb, hb = bank()
                            for k in range(8):
                                mm(pb, memn[:, k, mc * 128:(mc + 1) * 128], wsrc[:, k, hf * 512:(hf + 1) * 512],
                                   k == 0, k == 7, [hw, h_memn], [hb])
                            if which == 0 and not DBG.get('novb'):
                                cp('act', Vb[:, mc, hf * 512:(hf + 1) * 512], pb, [hb], [h_Vb])
                            if half == 0:
                                s = (si % 2) if not DBG.get('stg0') else 0
                                si += 1
                                if not DBG.get('nostg'):
                                    cp(DBG.get('stgeng', 'dve'), (msb if DBG.get('stgmsb') else stg)[:, s, :], pb, ([h_Vb] if DBG.get('chain') else [hb]), [h_stg[s]])
                                if not DBG.get('nodma'):
                                    dma('sp', dout_[l, mc * 128:(mc + 1) * 128, hf * 512:(hf + 1) * 512], stg[:, s, :],
                                        [h_stg[s]], ())
                if DBG['stop'] <= 3:
                    E.barrier(); AR.pop(); return
                for g, (c, n, kind) in enumerate(groups):
                    for oc in range(8):
                        pb, hb = bank()
                        for k in range(8):
                            mm(pb[:, 0:n], wq[:, k, oc * 128:(oc + 1) * 128], hT[:, k, 1 + c:1 + c + n], k == 0, k == 7,
                               [h_wq, h_h[g]], [hb])
                        cp('act', qT[:, oc, 0:n], pb[:, 0:n], [hb], [h_qT])
                    if kind == 'p':
                        for hd in range(4):
                            for mc in range(2):
                                pb, hb = bank()
                                for dc in range(2):
                                    mm(pb[:, 0:n], KT[:, 2 * hd + dc, mc * 128:(mc + 1) * 128], qT[:, 2 * hd + dc, 0:n],
                                       dc == 0, dc == 1, [h_KT, h_qT], [hb])
                                act(PT[:, mc, 0:n], pb[:, 0:n], AF.Exp, [hb], [h_PT], scale=1.0 / 16.0)
                            pb, hb = bank()
                            for mc in range(2):
                                mm(pb[:, 0:n], onesb, PT[:, mc, 0:n], mc == 0, mc == 1, [h_PT, h_con], [hb])
                            recip(rs[:, 0:n], pb[:, 0:n], [hb], [h_rs])
                            for dc in range(2):
                                pb, hb = bank()
                                for mc in range(2):
                                    mm(pb[:, 0:n], Vb[:, mc, (2 * hd + dc) * 128:(2 * hd + dc + 1) * 128], PT[:, mc, 0:n],
                                       mc == 0, mc == 1, [h_Vb, h_PT], [hb])
                                tt('dve', oT[:, 2 * hd + dc, 0:n], pb[:, 0:n], rs[:, 0:n], ALU.mult, [hb, h_rs], [h_oT])
                    else:
                        AR.push()
                        KTs = AR.alloc([1, 4, 2, 256], BF16); h_KTs = [E.H("KTs0")]
                        Vs = AR.alloc([1, 2, 1024], BF16); h_Vs = [E.H("Vs0")]
                        PTs = PT.rearrange("p a (b h m t) -> p (a b) h m t", b=8, h=4, m=2); h_PTs = h_PT
                        rss = rs.rearrange("p (b h t) -> p b h t", b=16, h=4); h_rss = h_rs
                        for bh in range(2):
                            pbs, hbs = bank()
                            pbo0, hbo0 = bank()
                            sview = pbs.rearrange("p (b h m t) -> p b h m t", b=8, h=4, m=2)
                            oview = pbo0.rearrange("p (b h d t) -> p b h d t", b=8, h=4, d=2)
                            for bb in range(8):
                                b = bh * 8 + bb
                                s = 0
                                dma('pool', KTs[:, s], d_kT[l, b].rearrange("h (dc p) m -> p h dc m", p=128), (), [h_KTs[s]])
                                dma('pool', Vs[:, s], d_v[l, b].rearrange("(mc p) n -> p mc n", p=128), (), [h_Vs[s]])
                                for hd in range(4):
                                    for mc in range(2):
                                        for dc in range(2):
                                            mm(sview[:, bb, hd, mc, :], KTs[:, s, hd, dc, mc * 128:(mc + 1) * 128],
                                               qT[:, 2 * hd + dc, b * 8:(b + 1) * 8], dc == 0, dc == 1,
                                               [h_KTs[s], h_qT], [hbs])
                                act(PTs[:, b], sview[:, bb], AF.Exp, [hbs], [h_PTs], scale=1.0 / 16.0)
                                for hd in range(4):
                                    for dc in range(2):
                                        for mc in range(2):
                                            mm(oview[:, bb, hd, dc, :], Vs[:, s, mc, (2 * hd + dc) * 128:(2 * hd + dc + 1) * 128],
                                               PTs[:, b, hd, mc, :], mc == 0, mc == 1, [h_Vs[s], h_PTs], [hbo0])
                            pbr, hbr = bank()
                            rview = pbr[:, 0:256].rearrange("p (b h t) -> p b h t", b=8, h=4)
                            for mc in range(2):
                                mm(rview, onesb, PTs[:, bh * 8:(bh + 1) * 8, :, mc, :], mc == 0, mc == 1, [h_PTs, h_con], [hbr])
                            recip(rss[:, bh * 8:(bh + 1) * 8], rview, [hbr], [h_rss])
                            for dc in range(2):
                                ov = oT[:, :, 0:128].rearrange("p (h d) (b t) -> p d b h t", d=2, t=8)[:, dc, bh * 8:(bh + 1) * 8]
                                tt('dve', ov, oview[:, :, :, dc, :], rss[:, bh * 8:(bh + 1) * 8], ALU.mult,
                                   [hbo0, h_rss], [h_oT])
                        AR.pop()
                    for oc in range(8):
                        pb, hb = bank()
                        for k in range(8):
                            mm(pb[:, 0:n], wo[:, k, oc * 128:(oc + 1) * 128], oT[:, k, 0:n], k == 0, k == 7,
                               [h_wo, h_oT], [hb])
                        cp('act', msb[:, oc, 0:n], pb[:, 0:n], [hb], [h_msb])
                    post_res(3, g, [msb[:, k, 0:n] for k in range(8)], [h_msb], sq, rstd, tmp, h_t1, h_t2)
                E.barrier()
                AR.pop()


            def ffn_sub():
                AR.push()
                NT_ = 1024 + (128 if half == 1 else 0)
                facc = AR.alloc([8, NT_], F32); h_f = [E.H("f%d" % g) for g in range(len(groups))]
                w1 = AR.alloc([2, 8, 512], BF16); h_w1 = [E.H("w1a"), E.H("w1b")]
                w2 = AR.alloc([2, 4, 1024], BF16); h_w2 = [E.H("w2a"), E.H("w2b")]
                sq = AR.alloc([8, 512], BF16); h_t1 = E.H("t1f")
                rstd = AR.alloc([512], F32)
                tmp = AR.alloc([512], F32); h_t2 = E.H("t2f")
                rl = AR.alloc([2, 512], BF16); h_rl = [E.H("rl0"), E.H("rl1")]
                hid = AR.alloc([2, 4, 512], BF16); h_hid = [E.H("hid0"), E.H("hid1")]
                norm_to_h(4, sq, rstd, h_t1)
                w1v = d_w1[l].rearrange("(k p) n -> p k n", p=128)
                w2v = d_w2[l].rearrange("(c p) n -> p c n", p=128)
                hi = 0
                for hg in range(8):
                    s = hg % 2
                    E.begin_group()
                    for k in range(8):
                        dma('pool', w1[:, s, k, :], w1v[:, k, hg * 512:(hg + 1) * 512], (), [h_w1[s]])
                    E.end_group()
                    E.begin_group()
                    for cc in range(4):
                        dma('pool', w2[:, s, cc, :], w2v[:, hg * 4 + cc, :], (), [h_w2[s]])
                    E.end_group()
                    for g, (c, n, kind) in enumerate(groups):
                        hs = hi % 2
                        hi += 1
                        for hc in range(4):
                            pb, hb = bank()
                            for k in range(8):
                                mm(pb[:, 0:n], w1[:, s, k, hc * 128:(hc + 1) * 128], hT[:, k, 1 + c:1 + c + n], k == 0, k == 7,
                                   [h_w1[s], h_h[g]], [hb])
                            rs_ = hc % 2
                            act(rl[:, rs_, 0:n], pb[:, 0:n], AF.Relu, [hb], [h_rl[rs_]])
                            tt('pool', hid[:, hs, hc, 0:n], rl[:, rs_, 0:n], rl[:, rs_, 0:n], ALU.mult, [h_rl[rs_]], [h_hid[hs]])
                        for oc in range(8):
                            pb, hb = bank()
                            for hc in range(4):
                                mm(pb[:, 0:n], w2[:, s, hc, oc * 128:(oc + 1) * 128], hid[:, hs, hc, 0:n], hc == 0, hc == 3,
                                   [h_w2[s], h_hid[hs]], [hb])
                            if hg == 0:
                                cp('act', facc[:, oc, c:c + n], pb[:, 0:n], [hb], [h_f[g]])
                            else:
                                tt('dve', facc[:, oc, c:c + n], pb[:, 0:n], facc[:, oc, c:c + n], ALU.add, [hb, h_f[g]], [h_f[g]])
                for g, (c, n, kind) in enumerate(groups):
                    post_res(5, g, [facc[:, k, c:c + n] for k in range(8)], [h_f[g]], sq, rstd, tmp, h_t1, h_t2)
                E.barrier()
                AR.pop()


            def rwkv_sub(mixT, h_mix):
                AR.push()
                winv = d_win[l].rearrange("(k p) n -> p k n", p=128)
                Wr = AR.alloc([8, 1792], BF16); h_Wr = E.H("Wr")
                E.begin_group()
                for k in range(8):
                    dma('pool', Wr[:, k, :], winv[:, k, 1024:2816], (), [h_Wr])
                E.end_group()
                bct = AR.alloc([4096], F32); h_bc = E.H("bct")
                dma('sp', bct, d_bc[l][:, 0:4096], (), [h_bc])
                kk_t, ka_t, rk_t, gg_t, gb_t = [bct[:, i * 512:(i + 1) * 512] for i in range(5)]
                mu_t = [bct[:, 2560 + i * 512:2560 + (i + 1) * 512] for i in range(3)]
                rows_b = AR.alloc([1024], BF16); h_rows = E.H("rowsb")
                E.begin_group()
                dma('pool', rows_b[0:1, 0:512], d_rows[l:l + 1, 0:512], (), [h_rows])
                dma('pool', rows_b[64:65, 0:512], d_rows[l:l + 1, 512:1024], (), [h_rows])
                E.end_group()
                wup_b = AR.alloc([512], BF16); h_wup = E.H("wupb")
                dma('pool', wup_b, d_wup[l], (), [h_wup])
                gup_b = AR.alloc([512], BF16); h_gup = E.H("gupb")
                dma('pool', gup_b, d_gup[l], (), [h_gup])
                S_b = AR.alloc([4, 64], BF16); h_Sb = E.H("Sb")
                cp('act', S_b, st_S[:, l, :, :], [h_stS[l]], [h_Sb])
                FT = {}
                for nm in ("Asb", "dlt", "Zr", "Zk", "sw", "av", "gate", "G", "Ginv", "Gp", "kk", "sq_"):
                    FT[nm] = (AR.alloc([512], F32), E.H("f_" + nm))
                    DBGV[nm] = AR.last
                FT["kkn"] = FT["kk"]
                FT["t1"] = FT["Asb"]
                FT["kmod"] = FT["sw"]
                DBGV["kkn"] = DBGV["kk"]; DBGV["kmod"] = DBGV["sw"]
                BT = {}
                for nm in ("at", "bt", "kt", "rt", "vb", "Xb", "ro"):
                    BT[nm] = (AR.alloc([512], BF16), E.H("b_" + nm))
                    DBGV[nm] = AR.last
                arT = AR.alloc([4, 2, 128], BF16); h_arT = E.H("arT")
                kbT = AR.alloc([4, 2, 128], BF16); h_kbT = E.H("kbT")
                AkaAkr = AR.alloc([8, 256], BF16); h_Aka = E.H("Aka")
                NAbr = AR.alloc([8, 256], BF16); h_NAbr = E.H("NAbr")
                mrg_b = AR.alloc([896], BF16); h_mrg = E.H("mrg")
                dma('pool', mrg_b, d_mrg, (), [h_mrg])
                Ad = AR.alloc([2, 128], F32); h_Ad = E.H("Ad")
                Dd = AR.alloc([2, 128], F32); h_Dd = E.H("Dd")
                tda = AR.alloc([128], BF16); h_tda = E.H("tda")
                sdg = AR.alloc([128], BF16); h_sdg = E.H("sdg")
                sm8 = AR.alloc([6, 8], F32); h_sm8 = E.H("sm8")
                gcT = AR.alloc([4, 16], F32); h_gc = E.H("gcT")
                tmpS = AR.alloc([4, 64], F32); h_tmpS = E.H("tmpS")
                has_s = (half == 1 and DBG['samp'])
                if has_s:
                    sst = AR.alloc([8, 16], F32); h_sst = E.H("sst")
                    dma('sp', sst, d_shift[l].rearrange("(k p) b -> p k b", p=128), (), [h_sst])
                    hprev_s = AR.alloc([8, 16, 8], BF16); h_hps = E.H("hps")
                    hsv = hT[:, :, 1025:1153].rearrange("p k (b t) -> p k b t", t=8)
                    cp('dve', hprev_s[:, :, :, 0], sst, [h_sst], [h_hps])
                    cp('dve', hprev_s[:, :, :, 1:8], hsv[:, :, :, 0:7], [h_h[2]], [h_hps])
                    S_sb = AR.alloc([16, 4, 64], BF16); h_Ssb = E.H("Ssb")
                    P1sb = AR.alloc([4, 128], BF16); h_P1sb = E.H("P1sb")
                    S_sf = FT["Gp"][0].rearrange("p (a b c) -> p a b c", a=2, b=4); h_Ssf = [FT["Gp"][1], FT["Gp"][1]]
                    vm = FT["G"][0].bitcast(BF16).rearrange("p (a b) -> p a b", a=2); h_vm = [FT["G"][1], FT["G"][1]]
                    um = FT["Ginv"][0].bitcast(BF16).rearrange("p (a b) -> p a b", a=2); h_um = [FT["Ginv"][1], FT["Ginv"][1]]
                    wkv_in = d_wkv[l].rearrange("b (p4 h2) j i -> h2 j b p4 i", h2=2)
                    wkv_out = o_wkvs[l].rearrange("b (p4 h2) j i -> h2 j b p4 i", h2=2)
                    E.begin_group()
                    for h2 in range(2):
                        for p4 in range(4):
                            dma('pool', S_sb[h2 * 64:(h2 + 1) * 64, :, p4, :], wkv_in[h2][:, :, p4, :], (), [h_Ssb])
                    E.end_group()

                def F(nm):
                    return FT[nm]

                def rw_tile(tc, kind, g):
                    smp = (kind == 's')
                    nseq = 16 if smp else 1
                    nap = 3 if smp else 7
                    MuMi = con_b[:, (C_MUS if smp else C_MU):(C_MUS if smp else C_MU) + 256]
                    MLb = con_b[:, (C_MLS if smp else C_ML):(C_MLS if smp else C_ML) + 128]
                    Tri = con_f[:, (C_MIS if smp else C_MI):(C_MIS if smp else C_MI) + 128]
                    seqf = con_f[:, C_SEQ:C_SEQ + 16] if smp else con_f[:, C_EPS + 4:C_EPS + 5]
                    hg = [h_h[g]] + ([h_h[g - 1]] if g > 0 else []) + ([h_hps] if smp else [])

                    def hc(k):
                        return hT[:, k, 1 + tc:1 + tc + 128]

                    def hp(k):
                        return hprev_s[:, k, :, :] if smp else hT[:, k, tc:tc + 128]
                    (Asb, hA), (dlt, hD) = F("Asb"), F("dlt")
                    Z = {}
                    for qi, qn in enumerate(("Zr", "Zk", "vb")):
                        bA, hbA = bank()
                        for k in range(8):
                            mm(bA, hc(k), Wr[:, k, qi * 512:(qi + 1) * 512], k == 0, k == 7, hg + [h_Wr], [hbA])
                        bB, hbB = bank()
                        for k in range(8):
                            mm(bB, hp(k), Wr[:, k, qi * 512:(qi + 1) * 512], k == 0, k == 7, hg + [h_Wr], [hbB])
                        cp('act', Asb, bA, [hbA], [hA])
                        tt('dve', dlt, bB, Asb, ALU.subtract, [hbB, hA], [hD])
                        tt('dve', dlt, dlt, mu_t[qi], ALU.mult, [hD, h_bc], [hD])
                        dst, hdst = (F(qn) if qi < 2 else BT["vb"])
                        tt('pool', dst, dlt, Asb, ALU.add, [hD, hA], [hdst])
                    (Zr, hZr), (Zk, hZk) = F("Zr"), F("Zk")
                    vb, hvb = BT["vb"]
                    bD, hbD = bank()
                    for ci, c0 in enumerate((1536, 1664)):
                        for k in range(8):
                            mm(bD[:, ci * 256:ci * 256 + 128], Wr[:, k, c0:c0 + 128], hc(k), k == 0, k == 7, hg + [h_Wr], [hbD])
                        for k in range(8):
                            mm(bD[:, ci * 256 + 128:ci * 256 + 256], Wr[:, k, c0:c0 + 128], hp(k), k == 0, k == 7, hg + [h_Wr], [hbD])
                    for ci in range(2):
                        mucol = pp[:, PP_MUD + l * 2 + ci:PP_MUD + l * 2 + ci + 1]
                        cp('act', Ad[:, ci, :], bD[:, ci * 256:ci * 256 + 128], [hbD], [h_Ad])
                        tt('dve', Dd[:, ci, :], bD[:, ci * 256 + 128:ci * 256 + 256], Ad[:, ci, :], ALU.subtract, [hbD, h_Ad], [h_Dd])
                        stt(Dd[:, ci, :], Dd[:, ci, :], mucol, Ad[:, ci, :], ALU.mult, ALU.add, [h_Dd, h_Ad, h_pp], [h_Dd])
                    act(tda[0:64, :], Dd[0:64, 0, :], AF.Tanh, [h_Dd], [h_tda])
                    cp('act', tda[64:128, :], Dd[64:128, 0, :], [h_Dd], [h_tda])
                    act(sdg, Dd[:, 1, :], AF.Sigmoid, [h_Dd], [h_sdg])
                    (sw, hsw), (av, hav), (gate, hgate) = F("sw"), F("av"), F("gate")
                    bW, hbW = bank()
                    mm(bW, onesb[0:1, 0:128], rows_b[0:1, 0:512], True, False, [h_con, h_rows], [hbW])
                    mm(bW, tda[0:64, :], wup_b[0:64, :], False, True, [h_tda, h_wup], [hbW])
                    act(sw, bW, AF.Sigmoid, [hbW], [hsw])
                    bAa, hbAa = bank()
                    mm(bAa, onesb[64:65, 0:128], rows_b[64:65, 0:512], True, False, [h_con, h_rows], [hbAa])
                    mm(bAa, tda[64:128, :], wup_b[64:128, :], False, True, [h_tda, h_wup], [hbAa])
                    act(av, bAa, AF.Sigmoid, [hbAa], [hav])
                    bG, hbG = bank()
                    mm(bG, sdg, gup_b, True, True, [h_sdg, h_gup], [hbG])
                    cp('act', gate, bG, [hbG], [hgate])
                    (G, hG), (Ginv, hGi), (Gp, hGp) = F("G"), F("Ginv"), F("Gp")
                    bC, hbC = bank()
                    mm(bC, Tri, sw, True, True, [h_con, hsw], [hbC])
                    act(G, bC, AF.Exp, [hbC], [hG], scale=-CDEC)
                    act(Ginv, bC, AF.Exp, [hbC], [hGi], scale=CDEC)
                    tt('dve', Gp, bC, sw, ALU.subtract, [hbC, hsw], [hGp])
                    act(Gp, Gp, AF.Exp, [hGp], [hGp], scale=-CDEC)
                    bGC, hbGC = bank()
                    for p4 in range(4):
                        mm(bGC[:, p4 * 16:p4 * 16 + nseq], sw[:, p4 * 128:(p4 + 1) * 128], seqf, True, True, [hsw, h_con], [hbGC])
                    act(gcT[:, :, 0:nseq], bGC[:, 0:64].rearrange("p (a b) -> p a b", a=4)[:, :, 0:nseq], AF.Exp, [hbGC], [h_gc], scale=-CDEC)
                    (kk, hkk), (sq_, hsq), (kkn, hkkn), (t1, ht1), (kmod, hkm) = F("kk"), F("sq_"), F("kkn"), F("t1"), F("kmod")
                    tt('dve', kk, Zk, kk_t, ALU.mult, [hZk, h_bc], [hkk])
                    tt('pool', sq_, kk, kk, ALU.mult, [hkk], [hsq])
                    red(sm8[:, 0, :], sq_.rearrange("p (h d) -> p h d", h=8), [hsq], [h_sm8])
                    act(sm8[:, 0, :], sm8[:, 0, :], AF.Sqrt, [h_sm8], [h_sm8])
                    ts('dve', sm8[:, 0, :], sm8[:, 0, :], 1e-12, ALU.max, [h_sm8], [h_sm8])
                    recip(sm8[:, 0, :], sm8[:, 0, :], [h_sm8], [h_sm8])
                    tt('dve', kkn.rearrange("p (h d) -> p h d", h=8), kk.rearrange("p (h d) -> p h d", h=8),
                       sm8[:, 0, :].unsqueeze(2).to_broadcast([128, 8, 64]), ALU.mult, [hkk, h_sm8], [hkkn])
                    stt(t1, av, -1.0, ka_t, ALU.add, ALU.mult, [hav, h_bc], [ht1])
                    stt(kmod, t1, 1.0, Zk, ALU.add, ALU.mult, [ht1, hZk], [hkm])
                    (at, hat), (bt, hbt), (kt, hkt), (rt, hrt) = BT["at"], BT["bt"], BT["kt"], BT["rt"]
                    stt(at, kkn, -1.0, Gp, ALU.mult, ALU.mult, [hkkn, hGp], [hat])
                    tt('pool', t1, kkn, av, ALU.mult, [hkkn, hav], [ht1])
                    tt('pool', bt, t1, Ginv, ALU.mult, [ht1, hGi], [hbt])
                    tt('pool', kt, kmod, Ginv, ALU.mult, [hkm, hGi], [hkt])
                    tt('dve', rt, Zr, G, ALU.mult, [hZr, hG], [hrt])
                    tt('dve', sq_, Zr, kmod, ALU.mult, [hZr, hkm], [hsq])
                    tt('pool', sq_, sq_, rk_t, ALU.mult, [hsq, h_bc], [hsq])
                    red(sm8[:, 1, :], sq_.rearrange("p (h d) -> p h d", h=8), [hsq], [h_sm8])
                    for (srcs, dstT, hdT) in (((at, hat, rt, hrt), arT, h_arT), ((kt, hkt, bt, hbt), kbT, h_kbT)):
                        bT_, hbT_ = bank()
                        tv = bT_.bitcast(BF16)[:, 0:1024].rearrange("p (a b c) -> p a b c", a=4, b=2)
                        for p4 in range(4):
                            tr(tv[:, p4, 0, :], srcs[0][:, p4 * 128:(p4 + 1) * 128], ident_b, [srcs[1], h_con], [hbT_])
                            tr(tv[:, p4, 1, :], srcs[2][:, p4 * 128:(p4 + 1) * 128], ident_b, [srcs[3], h_con], [hbT_])
                        cp('act', dstT, tv, [hbT_], [hdT])
                    mk = MuMi.unsqueeze(1).to_broadcast([128, 2, 256])
                    for hb4 in range(2):
                        for par in range(2):
                            bA_, hbA_ = bank()
                            bB_, hbB_ = bank()
                            base = 64 * par
                            for hh in range(2):
                                h = hb4 * 4 + 2 * hh + par
                                p4 = h // 2
                                ar2 = arT[base:base + 64, p4, :, :]
                                mm(bA_[:, hh * 256:(hh + 1) * 256], kbT[base:base + 64, p4, 0, :], ar2, True, True, [h_kbT, h_arT], [hbA_])
                                mm(bB_[:, hh * 256:(hh + 1) * 256], kbT[base:base + 64, p4, 1, :], ar2, True, True, [h_kbT, h_arT], [hbB_])
                            h0 = hb4 * 4 + par
                            tt('dve', AkaAkr[:, h0:h0 + 3:2, :], bA_.rearrange("p (a b) -> p a b", a=2), mk, ALU.mult, [hbA_, h_con], [h_Aka])
                            tt('dve', NAbr[:, h0:h0 + 3:2, :], bB_.rearrange("p (a b) -> p a b", a=2), mk, ALU.mult, [hbB_, h_con], [h_NAbr])
                    (Xb, hXb) = BT["Xb"]
                    X, hX = F("Asb")

                    def s0_terms(bX_, hbX_, which, post):
                        if not smp:
                            for h in range(8):
                                p4, base = h // 2, 64 * (h % 2)
                                mm(bX_[:, h * 64:(h + 1) * 64], arT[base:base + 64, p4, which, :], S_b[base:base + 64, p4, :],
                                   True, False, [h_arT, h_Sb], [hbX_])
                                post(h, True)
                        else:
                            for h2 in range(2):
                                base = 64 * h2
                                bP, hbP = bank()
                                for p4 in range(4):
                                    for s in range(16):
                                        mm(bP[base:base + 64, p4 * 128 + s * 8:p4 * 128 + s * 8 + 8], S_sb[base:base + 64, s, p4, :],
                                           arT[base:base + 64, p4, which, s * 8:(s + 1) * 8], True, True, [h_Ssb, h_arT], [hbP])
                                cp('act', P1sb[base:base + 64, :, :], bP[base:base + 64, :].rearrange("p (a b) -> p a b", a=4), [hbP], [h_P1sb])
                            for p4 in range(4):
                                mm(bX_[:, p4 * 128:(p4 + 1) * 128], P1sb[:, p4, :], ident_b, True, False, [h_P1sb, h_con], [hbX_])
                                post(2 * p4, False)
                                post(2 * p4 + 1, True)
                    bX, hbX = bank()

                    def post_x(h, last):
                        mm(bX[:, h * 64:(h + 1) * 64], AkaAkr[:, h, 0:128], vb[:, h * 64:(h + 1) * 64], False, last, [h_Aka, hvb], [hbX])
                    s0_terms(bX, hbX, 0, post_x)
                    cp('act', X, bX, [hbX], [hX])
                    cp('dve', Xb, X, [hX], [hXb])
                    def b8(nm):
                        return FT[nm][0].bitcast(BF16).rearrange("p (a b) -> p a b", a=8), FT[nm][1]
                    (Tm, hTm), (TmT, hTmT), (Noff, hNo), (YT, hYT) = b8("kk"), b8("sq_"), b8("Zr"), b8("av")
                    idb = ident_b.unsqueeze(1).to_broadcast([128, 8, 128])
                    cp('pool', Tm, idb, [h_con], [hTm])
                    cp('pool', TmT, idb, [h_con], [hTmT])
                    for lv in (range(3) if smp else range(7)):
                        mo = mrg_b[:, lv * 128:(lv + 1) * 128].unsqueeze(1).to_broadcast([128, 8, 128])
                        tt('pool', Noff, NAbr[:, :, 0:128], mo, ALU.mult, [h_NAbr, h_mrg], [hNo])
                        for q in range(2):
                            bY_, hbY_ = bank()
                            for hh in range(4):
                                h = q * 4 + hh
                                mm(bY_[:, hh * 128:(hh + 1) * 128], Noff[:, h, :], TmT[:, h, :], True, True, [hNo, hTmT], [hbY_])
                            cp('act' if q == 0 else 'dve', YT[:, q * 4:q * 4 + 4, :], bY_.rearrange("p (a b) -> p a b", a=4), [hbY_], [hYT])
                        zb = []
                        for q in range(2):
                            bZ, hbZ = bank()
                            bZT, hbZT = bank()
                            for hh in range(4):
                                h = q * 4 + hh
                                mm(bZ[:, hh * 128:(hh + 1) * 128], YT[:, h, :], Tm[:, h, :], True, True, [hYT, hTm], [hbZ])
                                mm(bZT[:, hh * 128:(hh + 1) * 128], Tm[:, h, :], YT[:, h, :], True, True, [hYT, hTm], [hbZT])
                            zb.append((bZ, hbZ, bZT, hbZT))
                        for q in range(2):
                            bZ, hbZ, bZT, hbZT = zb[q]
                            tt('dve', Tm[:, q * 4:q * 4 + 4, :], bZ.rearrange("p (a b) -> p a b", a=4), Tm[:, q * 4:q * 4 + 4, :], ALU.add, [hbZ, hTm], [hTm])
                            tt('dve', TmT[:, q * 4:q * 4 + 4, :], bZT.rearrange("p (a b) -> p a b", a=4), TmT[:, q * 4:q * 4 + 4, :], ALU.add, [hbZT, hTmT], [hTmT])
                    bY, hbY = bank()
                    for h in range(8):
                        mm(bY[:, h * 64:(h + 1) * 64], Tm[:, h, :], Xb[:, h * 64:(h + 1) * 64], True, True, [hTm, hXb], [hbY])
                    cp('act', X, bY, [hbY], [hX])
                    cp('dve', Xb, X, [hX], [hXb])
                    bO, hbO = bank()

                    def post_o(h, last):
                        mm(bO[:, h * 64:(h + 1) * 64], AkaAkr[:, h, 128:256], vb[:, h * 64:(h + 1) * 64], False, False, [h_Aka, hvb], [hbO])
                        mm(bO[:, h * 64:(h + 1) * 64], NAbr[:, h, 128:256], Xb[:, h * 64:(h + 1) * 64], False, last, [h_NAbr, hXb], [hbO])
                    s0_terms(bO, hbO, 1, post_o)
                    (osb, hos) = F("dlt")
                    cp('act', osb, bO, [hbO], [hos])
                    if not smp:
                        bS, hbS = bank()
                        for h in range(8):
                            p4, base = h // 2, 64 * (h % 2)
                            mm(bS[base:base + 64, p4 * 64:(p4 + 1) * 64], kt[:, h * 64:(h + 1) * 64], vb[:, h * 64:(h + 1) * 64], True, False, [hkt, hvb], [hbS])
                            mm(bS[base:base + 64, p4 * 64:(p4 + 1) * 64], bt[:, h * 64:(h + 1) * 64], Xb[:, h * 64:(h + 1) * 64], False, True, [hbt, hXb], [hbS])
                        tt('dve', tmpS, bS[:, 0:256].rearrange("p (a b) -> p a b", a=4), st_S[:, l, :, :], ALU.add, [hbS, h_stS[l]], [h_tmpS])
                        tt('dve', st_S[:, l, :, :], tmpS, gcT[:, :, 0:1].to_broadcast([128, 4, 64]), ALU.mult, [h_tmpS, h_gc], [h_stS[l]])
                        cp('act', S_b, st_S[:, l, :, :], [h_stS[l]], [h_Sb])
                    else:
                        for s in range(16):
                            ss_ = s % 2
                            E.begin_group()
                            for h2 in range(2):
                                dma('sp', S_sf[h2 * 64:(h2 + 1) * 64, ss_, :, :], wkv_in[h2][:, s, :, :], (), [h_Ssf[ss_]])
                            E.end_group()
                            ts('pool', vm[:, ss_, :], vb, seqf[:, s:s + 1], ALU.mult, [hvb, h_con], [h_vm[ss_]])
                            ts('pool', um[:, ss_, :], Xb, seqf[:, s:s + 1], ALU.mult, [hXb, h_con], [h_um[ss_]])
                            bS, hbS = bank()
                            for h in range(8):
                                p4, base = h // 2, 64 * (h % 2)
                                mm(bS[base:base + 64, p4 * 64:(p4 + 1) * 64], kt[:, h * 64:(h + 1) * 64], vm[:, ss_, h * 64:(h + 1) * 64], True, False, [hkt, h_vm[ss_]], [hbS])
                                mm(bS[base:base + 64, p4 * 64:(p4 + 1) * 64], bt[:, h * 64:(h + 1) * 64], um[:, ss_, h * 64:(h + 1) * 64], False, True, [hbt, h_um[ss_]], [hbS])
                            tt('dve', S_sf[:, ss_, :, :], bS[:, 0:256].rearrange("p (a b) -> p a b", a=4), S_sf[:, ss_, :, :], ALU.add, [hbS, h_Ssf[ss_]], [h_Ssf[ss_]])
                            tt('dve', S_sf[:, ss_, :, :], S_sf[:, ss_, :, :], gcT[:, :, s:s + 1].to_broadcast([128, 4, 64]), ALU.mult, [h_Ssf[ss_], h_gc], [h_Ssf[ss_]])
                            E.begin_group()
                            for h2 in range(2):
                                dma('sp', wkv_out[h2][:, s, :, :], S_sf[h2 * 64:(h2 + 1) * 64, ss_, :, :], [h_Ssf[ss_]], ())
                            E.end_group()
                    (osq, hoq), (on, hon), (bon, hbon) = F("G"), F("Ginv"), F("Gp")
                    o3 = osb.rearrange("p (h d) -> p h d", h=8)
                    red(sm8[:, 2, :], o3, [hos], [h_sm8])
                    tt('pool', osq, osb, osb, ALU.mult, [hos], [hoq])
                    red(sm8[:, 3, :], osq.rearrange("p (h d) -> p h d", h=8), [hoq], [h_sm8])
                    ts('dve', sm8[:, 2, :], sm8[:, 2, :], 1.0 / 64.0, ALU.mult, [h_sm8], [h_sm8])
                    tt('dve', sm8[:, 4, :], sm8[:, 2, :], sm8[:, 2, :], ALU.mult, [h_sm8], [h_sm8])
                    stt(sm8[:, 3, :], sm8[:, 3, :], 1.0 / 64.0, sm8[:, 4, :], ALU.mult, ALU.subtract, [h_sm8], [h_sm8])
                    act(sm8[:, 3, :], sm8[:, 3, :], AF.Sqrt, [h_sm8, h_con], [h_sm8], scale=1.0, bias=epsc(2))
                    recip(sm8[:, 3, :], sm8[:, 3, :], [h_sm8], [h_sm8])
                    on3 = on.rearrange("p (h d) -> p h d", h=8)
                    tt('dve', on3, o3, sm8[:, 2, :].unsqueeze(2).to_broadcast([128, 8, 64]), ALU.subtract, [hos, h_sm8], [hon])
                    tt('dve', on3, on3, sm8[:, 3, :].unsqueeze(2).to_broadcast([128, 8, 64]), ALU.mult, [hon, h_sm8], [hon])
                    tt('pool', on, on, gg_t, ALU.mult, [hon, h_bc], [hon])
                    tt('pool', on, on, gb_t, ALU.add, [hon, h_bc], [hon])
                    tt('dve', bon.rearrange("p (h d) -> p h d", h=8), vb.rearrange("p (h d) -> p h d", h=8),
                       sm8[:, 1, :].unsqueeze(2).to_broadcast([128, 8, 64]), ALU.mult, [hvb, h_sm8], [hbon])
                    tt('pool', on, on, bon, ALU.add, [hon, hbon], [hon])
                    ro, hro = BT["ro"]
                    tt('dve', ro, on, gate, ALU.mult, [hon, hgate], [hro])
                    bT_, hbT_ = bank()
                    tv = bT_.bitcast(BF16)[:, 0:512].rearrange("p (a b) -> p a b", a=4)
                    for p4 in range(4):
                        tr(tv[:, p4, :], ro[:, p4 * 128:(p4 + 1) * 128], ident_b, [hro, h_con], [hbT_])
                    cp('act', mixT[:, 4:8, tc:tc + 128], tv, [hbT_], [h_mix[g]])

                ntl = 0
                if DBG.get('rwtiles', 99) < 99 or DBG.get('rwonly') is not None:
                    for g, (c, n, kind) in enumerate(groups):
                        mset('pool', mixT[:, 4:8, c:c + n], 0.0, [h_mix[g]])
                for g, (c, n, kind) in enumerate(groups):
                    for ti in range(n // 128):
                        if ntl < DBG.get('rwtiles', 99) and (DBG.get('rwonly') is None or half == 0 or ntl == DBG['rwonly']):
                            rw_tile(c + ti * 128, kind, g)
                        ntl += 1
                if half == 1:
                    for h2 in range(2):
                        dma('sp', o_wkvp[l].rearrange("(p4 h2) j i -> h2 j p4 i", h2=2)[h2], st_S[h2 * 64:(h2 + 1) * 64, l, :, :],
                            [h_stS[l]], ())
                E.barrier()
                AR.pop()

            def mixer_sub():
                AR.push()
                NT_ = 1024 + (128 if (half == 1 and DBG['samp']) else 0)
                winv = d_win[l].rearrange("(k p) n -> p k n", p=128)
                mixT = AR.alloc([8, NT_], BF16); h_mix = [E.H("mix%d" % g) for g in range(len(groups))]
                AR.push()
                sq = AR.alloc([8, 512], BF16); h_t1 = E.H("t1m")
                rstd = AR.alloc([512], F32)
                hl = AR.alloc([8], F32); h_hl = E.H("hl")
                hs_ = AR.alloc([8, 16], F32); h_hs = E.H("hs")
                if half == 1:
                    cp('dve', hT[:, :, 0], st_shift[:, l, :], [h_stshift[l]], [h_h[0]])
                for g, (c, n, kind) in enumerate(groups):
                    rms_rstd([xT[:, k, c:c + n] for k in range(8)], n, [h_x[g]], sq, rstd[:, 0:n], h_t1)
                    for k in range(8):
                        stt(hT[:, k, 1 + c:1 + c + n], xT[:, k, c:c + n], ngc(0, k), rstd[:, 0:n],
                            ALU.mult, ALU.mult, [h_x[g], h_t1, h_pp], [h_h[g]])
                    if kind == 'p' and g == 1:
                        if half == 0:
                            cp('pool', st_shift[:, l, :], hT[:, :, 1024], [h_h[g]], [h_stshift[l]])
                        else:
                            for k in range(8):
                                stt(hl[:, k:k + 1], xT[:, k, 1023:1024], ngc(0, k), rstd[:, 511:512], ALU.mult, ALU.mult,
                                    [h_x[g], h_t1, h_pp], [h_hl])
                            dma('sp', o_shiftp[l], hl, [h_hl], ())
                    if kind == 's':
                        xs7 = xT[:, :, 1024:1152].rearrange("p k (b t) -> p k b t", t=8)
                        r7 = rstd[:, 0:128].rearrange("p (b t) -> p b t", t=8)[:, :, 7]
                        for k in range(8):
                            stt(hs_[:, k, :], xs7[:, k, :, 7], ngc(0, k), r7, ALU.mult, ALU.mult,
                                [h_x[g], h_t1, h_pp], [h_hs])
                        dma('sp', o_shifts[l], hs_, [h_hs], ())
                E.barrier()
                AR.pop()
                AR.push()
                Wc = AR.alloc([8, 1024], BF16); h_Wc = E.H("Wc")
                E.begin_group()
                for k in range(8):
                    dma('pool', Wc[:, k, :], winv[:, k, 0:1024], (), [h_Wc])
                E.end_group()
                uf = AR.alloc([4, 512], F32); h_uf = E.H("uf")
                sg = AR.alloc([512], F32); h_sg = E.H("sg")
                uT = AR.alloc([4, 542], BF16); h_uT = E.H("uT")
                uTs = AR.alloc([4, 16, 38], BF16); h_uTs = E.H("uTs")
                Dg = AR.alloc([31, 128], BF16); h_Dg = E.H("Dg")
                ysb = AR.alloc([4, 512], F32); h_ysb = E.H("ysb")
                ybf = AR.alloc([4, 512], BF16); h_ybf = E.H("ybf")
                ysq = AR.alloc([4, 512], BF16); h_ysq = E.H("ysq")
                mean = AR.alloc([512], F32); h_mean = E.H("mean")
                m2 = AR.alloc([512], F32); h_m2 = E.H("m2")
                rs2 = AR.alloc([512], F32); h_rs2 = E.H("rs2")
                tl = AR.alloc([512], F32); h_tl = E.H("tl")
                for g, (c, n, kind) in enumerate(groups):
                    if kind == 'p':
                        if g == 0:
                            cp('pool', uT[:, :, 0:30], st_conv[:, l, :, :], [h_stconv[l]], [h_uT])
                        else:
                            cp('pool', uT[:, :, 0:30], uT[:, :, 512:542], [h_uT], [h_uT])
                    else:
                        E.begin_group()
                        for c4 in range(4):
                            dma('pool', uTs[:, c4, :, 0:30],
                                d_conv[l][:, c4 * 128:(c4 + 1) * 128, :].rearrange("b p t -> p b t"), (), [h_uTs])
                        E.end_group()
                    for c4 in range(4):
                        pbv, hbv = bank()
                        for k in range(8):
                            mm(pbv[:, 0:n], Wc[:, k, c4 * 128:(c4 + 1) * 128], hT[:, k, 1 + c:1 + c + n], k == 0, k == 7,
                               [h_Wc, h_h[g]], [hbv])
                        pbg, hbg = bank()
                        for k in range(8):
                            mm(pbg[:, 0:n], Wc[:, k, 512 + c4 * 128:512 + (c4 + 1) * 128], hT[:, k, 1 + c:1 + c + n],
                               k == 0, k == 7, [h_Wc, h_h[g]], [hbg])
                        act(sg[:, 0:n], pbg[:, 0:n], AF.Sigmoid, [hbg], [h_sg])
                        tt('dve', uf[:, c4, 0:n], pbv[:, 0:n], sg[:, 0:n], ALU.mult, [hbv, h_sg], [h_uf])
                        if kind == 'p':
                            cp('act', uT[:, c4, 30:30 + n], uf[:, c4, 0:n], [h_uf], [h_uT])
                        else:
                            cp('act', uTs[:, c4, :, 30:38], uf[:, c4, 0:128].rearrange("p (b t) -> p b t", t=8),
                               [h_uf], [h_uTs])
                    if kind == 'p' and g == 1:
                        if half == 0:
                            cp('pool', st_conv[:, l, :, :], uT[:, :, 512:542], [h_uT], [h_stconv[l]])
                        else:
                            dma('sp', o_convp[l].rearrange("(c p) t -> p c t", p=128), uf[:, :, 482:512], [h_uf], ())
                    if kind == 's':
                        dma('sp', o_convs[l][:, :, 0:22], d_conv[l][:, :, 8:30], (), ())
                        for c4 in range(4):
                            dma('sp', o_convs[l][:, c4 * 128:(c4 + 1) * 128, 22:30].rearrange("b p t -> p b t"),
                                uf[:, c4, 0:128].rearrange("p (b t) -> p b t", t=8), [h_uf], ())
                    for c4 in range(4):
                        o_cw = PP_CW + (l * 4 + c4) * 31
                        tt('pool', Dg, ident_b.unsqueeze(1).to_broadcast([128, 31, 128]),
                           pp[:, o_cw:o_cw + 31].unsqueeze(2).to_broadcast([128, 31, 128]), ALU.mult,
                           [h_con, h_pp], [h_Dg])
                        pby, hby = bank()
                        for j in range(31):
                            rhs = uT[:, c4, j:j + n] if kind == 'p' else uTs[:, c4, :, j:j + 8]
                            mm(pby[:, 0:n], Dg[:, j, :], rhs, j == 0, j == 30, [h_Dg, h_uT, h_uTs], [hby])
                        cbc = pp[:, PP_CB + l * 4 + c4:PP_CB + l * 4 + c4 + 1]
                        act(ysb[:, c4, 0:n], pby[:, 0:n], AF.Identity, [hby, h_pp], [h_ysb], bias=cbc, scale=1.0)
                        act(ybf[:, c4, 0:n], pby[:, 0:n], AF.Identity, [hby, h_pp], [h_ybf], bias=cbc, scale=1.0)
                        act(ysq[:, c4, 0:n], pby[:, 0:n], AF.Square, [hby, h_pp], [h_ysq], bias=cbc, scale=1.0)
                    pb1, hb1 = bank()
                    for c4 in range(4):
                        mm(pb1[:, 0:n], onesb, ybf[:, c4, 0:n], c4 == 0, c4 == 3, [h_ybf, h_con], [hb1])
                    pb2, hb2 = bank()
                    for c4 in range(4):
                        mm(pb2[:, 0:n], onesb, ysq[:, c4, 0:n], c4 == 0, c4 == 3, [h_ysq, h_con], [hb2])
                    act(mean[:, 0:n], pb1[:, 0:n], AF.Identity, [hb1], [h_mean], scale=1.0 / 512.0, bias=epsc(3))
                    tt('dve', m2[:, 0:n], mean[:, 0:n], mean[:, 0:n], ALU.mult, [h_mean], [h_m2])
                    stt(m2[:, 0:n], pb2[:, 0:n], 1.0 / 512.0, m2[:, 0:n], ALU.mult, ALU.subtract, [hb2, h_m2], [h_m2])
                    act(rs2[:, 0:n], m2[:, 0:n], AF.Sqrt, [h_m2, h_con], [h_rs2], scale=1.0, bias=epsc(1))
                    recip(rs2[:, 0:n], rs2[:, 0:n], [h_rs2], [h_rs2])
                    for c4 in range(4):
                        tt('dve', tl[:, 0:n], ysb[:, c4, 0:n], mean[:, 0:n], ALU.subtract, [h_ysb, h_mean], [h_tl])
                        tt('dve', tl[:, 0:n], tl[:, 0:n], rs2[:, 0:n], ALU.mult, [h_tl, h_rs2], [h_tl])
                        ol = l * 4 + c4
                        ts('dve', tl[:, 0:n], tl[:, 0:n], pp[:, PP_LG + ol:PP_LG + ol + 1], ALU.mult, [h_tl, h_pp], [h_tl],
                           s2=pp[:, PP_LB + ol:PP_LB + ol + 1], op1=ALU.add)
                        act(mixT[:, c4, c:c + n], tl[:, 0:n], AF.Silu, [h_tl], [h_mix[g]])
                E.barrier()
                AR.pop()
                if DBG.get('rwkv'):
                    rwkv_sub(mixT, h_mix)
                else:
                    for g, (c, n, kind) in enumerate(groups):
                        mset('pool', mixT[:, 4:8, c:c + n], 0.0, [h_mix[g]])
                wout = AR.alloc([8, 1024], BF16); h_wout = E.H("wout")
                load_w('pool', wout, d_wout[l], h_wout)
                sq = AR.alloc([8, 512], BF16); h_t1 = E.H("t1m3")
                rstd = AR.alloc([512], F32)
                tmp = AR.alloc([512], F32); h_t2 = E.H("t2m")
                msb = AR.alloc([8, 512], F32); h_msb = E.H("msbm")
                for g, (c, n, kind) in enumerate(groups):
                    for oc in range(8):
                        pb, hb = bank()
                        for k in range(8):
                            mm(pb[:, 0:n], wout[:, k, oc * 128:(oc + 1) * 128], mixT[:, k, c:c + n], k == 0, k == 7,
                               [h_wout, h_mix[g]], [hb])
                        cp('act', msb[:, oc, 0:n], pb[:, 0:n], [hb], [h_msb])
                    post_res(1, g, [msb[:, k, 0:n] for k in range(8)], [h_msb], sq, rstd, tmp, h_t1, h_t2)
                E.barrier()
                AR.pop()

            if DBG.get('mix'):
                mixer_sub()
            if DBG['att']:
                attention_sub()
            if DBG['ffn']:
                ffn_sub()

        for half in range(DBG['halves']):
            groups = [(0, 512, 'p'), (512, 512, 'p')] + ([(1024, 128, 's')] if (half == 1 and DBG['samp']) else [])
            NTOK = 1024 + (128 if half == 1 else 0)
            h_x = [E.H("x%d" % g) for g in range(len(groups))]
            h_h = [E.H("h%d" % g) for g in range(len(groups))]
            xv = d_xT.rearrange("(k p) n -> p k n", p=128)
            yv = o_yT.rearrange("(k p) n -> p k n", p=128)
            for g, (c, n, kind) in enumerate(groups):
                src = xv[:, :, half * 1024 + c: half * 1024 + c + n] if kind == 'p' else xv[:, :, 2048:2176]
                dma('sp', xT[:, :, c:c + n], src, (), [h_x[g]])

            if half == 0:
                AR.push()
                mraw = AR.alloc([8, 256], F32); h_mraw = E.H("mraw")
                msq = AR.alloc([8, 256], BF16)
                mr = AR.alloc([256], F32); h_mt = E.H("mtmp")
                dma('sp', mraw, d_memT.rearrange("(k p) n -> p k n", p=128), (), [h_mraw])
                rms_rstd([mraw[:, k, :] for k in range(8)], 256, [h_mraw], msq, mr, h_mt)
                for k in range(8):
                    stt(memn[:, k, :], mraw[:, k, :], pp[:, PP_MG + k:PP_MG + k + 1], mr, ALU.mult, ALU.mult,
                        [h_mraw, h_mt, h_pp], [h_memn])
                E.barrier()
                AR.pop()

            for l in range(DBG['layers']):
                layer(half, l, groups, h_x, h_h)

            for g, (c, n, kind) in enumerate(groups):
                dst = yv[:, :, half * 1024 + c: half * 1024 + c + n] if kind == 'p' else yv[:, :, 2048:2176]
                dma('sp', dst, xT[:, :, c:c + n], [h_x[g]], ())
            E.barrier()

        E.finalize()
        import contextlib
        with contextlib.ExitStack() as es:
            csem = {e: es.enter_context(nc.semaphore("c_" + e)) for e in ENGS}
            dsem = [es.enter_context(nc.semaphore("d%d" % i)) for i in range(NDSEM)]
            block = es.enter_context(nc.Block())

            @block.tensor
            def _(e):
                E.emit('pe', e, csem, dsem)

            @block.scalar
            def _(e):
                E.emit('act', e, csem, dsem)

            @block.vector
            def _(e):
                E.emit('dve', e, csem, dsem)

            @block.gpsimd
            def _(e):
                E.emit('pool', e, csem, dsem)

            @block.sync
            def _(e):
                E.emit('sp', e, csem, dsem)
    return nc


_PROG = {}


def _merge_masks():
    m = np.zeros((128, 896), np.float32)
    s, t = np.arange(128)[:, None], np.arange(128)[None, :]
    for lv in range(7):
        b = 1 << lv
        m[:, lv * 128:(lv + 1) * 128] = ((s // (2 * b)) == (t // (2 * b))) & ((s % (2 * b)) < b) & ((t % (2 * b)) >= b)
    return m


def _consts():
    c = np.zeros((128, NCON), np.float32)
    idx = np.arange(128)
    c[:, C_ID:C_ID + 128] = np.eye(128, dtype=np.float32)
    s, t = idx[:, None], idx[None, :]
    c[:, C_MU:C_MU + 128] = (s < t)
    c[:, C_MI:C_MI + 128] = (s <= t)
    c[:, C_ML:C_ML + 128] = (t < s)
    same = (s // 8) == (t // 8)
    c[:, C_MUS:C_MUS + 128] = (s < t) & same
    c[:, C_MIS:C_MIS + 128] = (s <= t) & same
    c[:, C_MLS:C_MLS + 128] = (t < s) & same
    c[:, C_SEQ:C_SEQ + 16] = (idx[:, None] // 8) == np.arange(16)[None, :]
    c[:, C_EPS + 0] = 1e-6
    c[:, C_EPS + 1] = 1e-5
    c[:, C_EPS + 2] = 64e-5
    c[:, C_EPS + 3] = 0.0
    c[:, C_EPS + 4] = 1.0
    return c


def _prepare(inp):
    f = lambda a: np.ascontiguousarray(np.asarray(a), dtype=np.float32)
    I = {k: f(v) for k, v in inp.items()}
    pp = np.zeros((128, NPP), np.float32)
    ng = I['norm_g'].reshape(4, 6, 8, 128)
    pp[:, PP_NG:PP_NG + 192] = ng.transpose(3, 0, 1, 2).reshape(128, 192)
    pp[:, PP_MG:PP_MG + 8] = I['mem_norm_g'].reshape(8, 128).T
    cw = I['conv_w'].reshape(4, 31, 4, 128)
    pp[:, PP_CW:PP_CW + 496] = cw.transpose(3, 0, 2, 1).reshape(128, 496)
    pp[:, PP_CB:PP_CB + 16] = I['conv_b'].reshape(4, 4, 128).transpose(2, 0, 1).reshape(128, 16)
    pp[:, PP_LG:PP_LG + 16] = I['conv_ln_g'].reshape(4, 4, 128).transpose(2, 0, 1).reshape(128, 16)
    pp[:, PP_LB:PP_LB + 16] = I['conv_ln_b'].reshape(4, 4, 128).transpose(2, 0, 1).reshape(128, 16)
    pp[:, PP_MUD:PP_MUD + 8] = I['mu_shift'][:, 1536:1792].reshape(4, 2, 128).transpose(2, 0, 1).reshape(128, 8)
    bc = np.zeros((4, 128, NBC), np.float32)
    for j, nm in enumerate(['rwkv_k_k', 'rwkv_k_a', 'rwkv_r_k', 'rwkv_gn_g', 'rwkv_gn_b']):
        bc[:, :, j * 512:(j + 1) * 512] = I[nm].reshape(4, 1, 512)
    bc[:, :, 2560:] = I['mu_shift'].reshape(4, 1, 1792)
    rows = np.concatenate([I['rwkv_w0'], I['rwkv_a0']], axis=1)
    wup = np.concatenate([I['rwkv_w_up'], I['rwkv_a_up']], axis=1)
    gup = I['rwkv_g_up']
    consts = _consts()
    mrg = _merge_masks()
    in_maps = []
    for c in range(NCORE):
        sb = slice(16 * c, 16 * c + 16)
        xs = I['x_sample'][sb].reshape(128, 1024)
        xT = np.ascontiguousarray(np.concatenate([I['x_prompt'][c], xs], axis=0).T)
        m = {
            "xT": xT,
            "memT": np.ascontiguousarray(I['mem_prompt'][c].T),
            "kTc": np.ascontiguousarray(I['cache_mem_k'][:, sb].transpose(0, 1, 3, 4, 2)),
            "vc": np.ascontiguousarray(I['cache_mem_v'][:, sb].reshape(4, 16, 256, 1024)),
            "wkvT": np.ascontiguousarray(I['state_wkv'][:, sb].transpose(0, 1, 2, 4, 3)),
            "convT": np.ascontiguousarray(I['state_conv'][:, sb].transpose(0, 1, 3, 2)),
            "shiftT": np.ascontiguousarray(I['state_shift'][:, sb].transpose(0, 2, 1)),
            "w_in": I['w_in'], "w_out": I['w_out'], "w_q": I['w_q'], "w_k": I['w_k'], "w_v": I['w_v'],
            "w_o": I['w_o'], "w_ffn1": I['w_ffn1'], "w_ffn2": I['w_ffn2'],
            "pp": pp, "bc": bc, "rows": rows, "wup": wup, "gup": gup, "consts": consts, "mrg": mrg,
        }
        in_maps.append(m)
    return in_maps


def kernel(**inp):
    in_maps = _prepare(inp)
    if 'nc' not in _PROG:
        _PROG['nc'] = build_program()
    res = run_bass_kernel_spmd(_PROG['nc'], in_maps, core_ids=list(range(NCORE)))
    R = res.results
    y_prompt = np.zeros((8, 2048, 1024), np.float32)
    y_sample = np.zeros((128, 8, 1024), np.float32)
    nk = np.zeros((4, 8, 256, 4, 256), np.float32)
    nv = np.zeros((4, 8, 256, 4, 256), np.float32)
    wkvp = np.zeros((4, 8, 8, 64, 64), np.float32)
    convp = np.zeros((4, 8, 30, 512), np.float32)
    shiftp = np.zeros((4, 8, 1024), np.float32)
    wkvs = np.zeros((4, 128, 8, 64, 64), np.float32)
    convs = np.zeros((4, 128, 30, 512), np.float32)
    shifts = np.zeros((4, 128, 1024), np.float32)
    for c in range(NCORE):
        r = R[c]
        sb = slice(16 * c, 16 * c + 16)
        yT = np.asarray(r["yT"])
        y_prompt[c] = yT[:, :2048].T
        y_sample[sb] = yT[:, 2048:].T.reshape(16, 8, 1024)
        nk[:, c] = np.asarray(r["nk"]).reshape(4, 256, 4, 256)
        nv[:, c] = np.asarray(r["nv"]).reshape(4, 256, 4, 256)
        wkvp[:, c] = np.asarray(r["wkvp"]).transpose(0, 1, 3, 2)
        convp[:, c] = np.asarray(r["convp"]).transpose(0, 2, 1)
        shiftp[:, c] = np.asarray(r["shiftp"]).transpose(0, 2, 1).reshape(4, 1024)
        wkvs[:, sb] = np.asarray(r["wkvs"]).transpose(0, 1, 2, 4, 3)
        convs[:, sb] = np.asarray(r["convs"]).transpose(0, 1, 3, 2)
        shifts[:, sb] = np.asarray(r["shifts"]).transpose(0, 3, 2, 1).reshape(4, 16, 1024)
    return (y_prompt, y_sample, nk, nv, wkvp, convp, shiftp, wkvs, convs, shifts)
```

```python
import numpy as np
import concourse.bass as bass
import concourse.mybir as mybir
from concourse.bass_utils import run_bass_kernel_spmd

F32 = mybir.dt.float32
BF16 = mybir.dt.bfloat16
AF = mybir.ActivationFunctionType
ALU = mybir.AluOpType
AX = mybir.AxisListType

DEPTH = 4
NCORE = 8
CDEC = 0.6065306597126334

PP_NG, PP_MG, PP_CW, PP_CB, PP_LG, PP_LB, PP_MUD, NPP = 0, 192, 200, 696, 712, 728, 744, 752
C_ID, C_MU, C_MI, C_ML, C_MUS, C_MIS, C_MLS, C_SEQ, C_EPS, NCON = 0, 128, 256, 384, 512, 640, 768, 896, 912, 920
NBC = 2560 + 1792

ENGS = ['pe', 'act', 'dve', 'pool', 'sp']
HOLE_LO, HOLE_HI = 207 * 256, 207 * 256
NDSEM = 90
NSW = 45
DBGV = {}
DBG = dict(layers=4, halves=2, att=True, ffn=True, samp=True, stop=99, mix=True, rwkv=True)


class Hd:
    __slots__ = ('name', 'w', 'r', 'excl')

    def __init__(self, name, excl=False):
        self.name = name
        self.w = None
        self.r = []
        self.excl = excl


class Emitter:
    def __init__(self):
        self.ops = []
        self.handles = []
        self.last_compute = {e: None for e in ENGS}
        self.out_dma = []
        self.group = None
        self.ngroups_phase = 0
        self.groups = []

    def H(self, name, excl=False):
        h = Hd(name, excl)
        self.handles.append(h)
        return h

    def op(self, eng, fn, r=(), w=(), dma=False):
        i = len(self.ops)
        deps = set()
        w = list(w) + [h for h in r if h.excl]
        r = [h for h in r if not h.excl]
        for h in r:
            if h.w is not None:
                deps.add(h.w)
        for h in w:
            if h.w is not None:
                deps.add(h.w)
            deps.update(h.r)
        for h in r:
            h.r.append(i)
        for h in w:
            h.w = i
            h.r = []
        deps.discard(i)
        gid = None
        if dma:
            if self.group is not None:
                gid = self.group
            else:
                gid = len(self.groups)
                self.groups.append([])
                self.ngroups_phase += 1
            if self.group is not None:
                deps = set(d for d in deps if not (self.ops[d]['dma'] and self.ops[d]['gid'] == gid))
            self.groups[gid].append(i)
            self.out_dma.append(i)
        else:
            self.last_compute[eng] = i
        self.ops.append(dict(eng=eng, fn=fn, deps=deps, dma=dma, gid=gid, sig=False))
        return i

    def begin_group(self):
        self.group = len(self.groups)
        self.groups.append([])
        self.ngroups_phase += 1

    def end_group(self):
        self.group = None

    def barrier(self):
        deps = set(v for v in self.last_compute.values() if v is not None) | set(self.out_dma)
        for e in ENGS:
            self.ops.append(dict(eng=e, fn=None, deps=set(deps), dma=False, gid=None, sig=False, bar=True))
        for h in self.handles:
            h.w = None
            h.r = []
        self.out_dma = []
        self.ngroups_phase = 0
        self.ops[-1]['phase_end'] = True

    def finalize(self):
        ops = self.ops
        for j, o in enumerate(ops):
            for d in o['deps']:
                p = ops[d]
                if p['dma']:
                    continue
                if p['eng'] == 'pe' and o['eng'] == 'pe' and not o.get('bar'):
                    continue
                p['sig'] = True
        cnt = {e: 0 for e in ENGS}
        for o in ops:
            if o['sig']:
                cnt[o['eng']] += 1
                o['sigval'] = cnt[o['eng']]
        gsem = {}
        semcnt = [0] * NDSEM
        nxt = {'pool': 0, 'sp': NSW}
        lim = {'pool': NSW, 'sp': NDSEM}
        gfinal = {}
        for j, o in enumerate(ops):
            if o['dma']:
                g = o['gid']
                q = o['eng']
                if g not in gsem:
                    gsem[g] = nxt[q]
                    nxt[q] += 1
                    assert nxt[q] <= lim[q], (q, nxt[q])
                k = gsem[g]
                semcnt[k] += 16
                gfinal[g] = semcnt[k]
            if o.get('phase_end'):
                nxt = {'pool': 0, 'sp': NSW}
        self.gsem, self.gfinal = gsem, gfinal

    def emit(self, eng_name, engobj, csem, dsem):
        ops = self.ops
        waited = {}
        for o in ops:
            if o['eng'] != eng_name:
                continue
            for d in sorted(o['deps']):
                p = ops[d]
                if p['dma']:
                    key = ('d', self.gsem[p['gid']])
                    val = self.gfinal[p['gid']]
                    sem = dsem[self.gsem[p['gid']]]
                else:
                    if not p['sig']:
                        continue
                    key = ('c', p['eng'])
                    val = p['sigval']
                    sem = csem[p['eng']]
                if waited.get(key, 0) < val:
                    engobj.wait_ge(sem, val)
                    waited[key] = val
            if o['fn'] is None:
                continue
            ins = o['fn'](engobj)
            if o['dma']:
                ins.then_inc(dsem[self.gsem[o['gid']]], 16)
            elif o['sig']:
                ins.then_inc(csem[eng_name], 1)


class Arena:
    def __init__(self, ap, words):
        self.ap = ap
        self.words = words
        self.tops = [0, HOLE_HI]
        self.lims = [HOLE_LO, words]
        self.stack = []

    def alloc(self, free_shape, dtype):
        n = 1
        for s in free_shape:
            n *= s
        nbytes = n * (2 if dtype == BF16 else 4)
        w = (nbytes + 31) // 32 * 8
        for r in (0, 1):
            if self.tops[r] + w <= self.lims[r]:
                off = self.tops[r]
                self.tops[r] += w
                self.last = (off, n, dtype == BF16)
                break
        else:
            raise AssertionError(("arena overflow", self.tops, w, self.lims))
        v = self.ap[:, off:off + w]
        if dtype == BF16:
            v = v.bitcast(BF16)[:, 0:n]
        else:
            v = v[:, 0:n]
        if len(free_shape) == 2:
            v = v.rearrange("p (a b) -> p a b", a=free_shape[0])
        elif len(free_shape) == 3:
            v = v.rearrange("p (a b c) -> p a b c", a=free_shape[0], b=free_shape[1])
        elif len(free_shape) == 4:
            v = v.rearrange("p (a b c d) -> p a b c d", a=free_shape[0], b=free_shape[1], c=free_shape[2])
        return v

    def push(self):
        self.stack.append(list(self.tops))

    def pop(self):
        self.tops = self.stack.pop()


def build_program():
    nc = bass.Bass("TRN2", target_bir_lowering=False)
    E = Emitter()

    def din(name, shape):
        return nc.dram_tensor(name, list(shape), F32, kind="ExternalInput").ap()

    def dout(name, shape):
        return nc.dram_tensor(name, list(shape), F32, kind="ExternalOutput").ap()

    d_xT = din("xT", [1024, 2176])
    d_memT = din("memT", [1024, 256])
    d_kT = din("kTc", [4, 16, 4, 256, 256])
    d_v = din("vc", [4, 16, 256, 1024])
    d_wkv = din("wkvT", [4, 16, 8, 64, 64])
    d_conv = din("convT", [4, 16, 512, 30])
    d_shift = din("shiftT", [4, 1024, 16])
    d_win = din("w_in", [4, 1024, 2816])
    d_wout = din("w_out", [4, 1024, 1024])
    d_wq = din("w_q", [4, 1024, 1024])
    d_wk = din("w_k", [4, 1024, 1024])
    d_wv = din("w_v", [4, 1024, 1024])
    d_wo = din("w_o", [4, 1024, 1024])
    d_w1 = din("w_ffn1", [4, 1024, 4096])
    d_w2 = din("w_ffn2", [4, 4096, 1024])
    d_pp = din("pp", [128, NPP])
    d_bc = din("bc", [4, 128, NBC])
    d_rows = din("rows", [4, 1024])
    d_wup = din("wup", [4, 128, 512])
    d_gup = din("gup", [4, 128, 512])
    d_con = din("consts", [128, NCON])
    d_mrg = din("mrg", [128, 1792])

    o_yT = dout("yT", [1024, 2176])
    o_nk = dout("nk", [4, 256, 1024])
    o_nv = dout("nv", [4, 256, 1024])
    o_wkvp = dout("wkvp", [4, 8, 64, 64])
    o_convp = dout("convp", [4, 512, 30])
    o_shiftp = dout("shiftp", [4, 128, 8])
    o_wkvs = dout("wkvs", [4, 16, 8, 64, 64])
    o_convs = dout("convs", [4, 16, 512, 30])
    o_shifts = dout("shifts", [4, 128, 8, 16])

    ARENA_WORDS = 207 * 256
    with (
        nc.sbuf_tensor("arena", [128, ARENA_WORDS], F32) as arena_t,
        nc.psum_tensor("ps0", [128, 512], F32) as ps0, nc.psum_tensor("ps1", [128, 512], F32) as ps1,
        nc.psum_tensor("ps2", [128, 512], F32) as ps2, nc.psum_tensor("ps3", [128, 512], F32) as ps3,
        nc.psum_tensor("ps4", [128, 512], F32) as ps4, nc.psum_tensor("ps5", [128, 512], F32) as ps5,
        nc.psum_tensor("ps6", [128, 512], F32) as ps6, nc.psum_tensor("ps7", [128, 512], F32) as ps7,
    ):
        AR = Arena(arena_t[:, :], ARENA_WORDS)
        banks = [(p[:, :], E.H("bank%d" % i, excl=True)) for i, p in enumerate([ps0, ps1, ps2, ps3, ps4, ps5, ps6, ps7])]
        bank_ctr = [0]

        def bank():
            b = banks[bank_ctr[0] % 8]
            bank_ctr[0] += 1
            return b

        def mm(out, lhsT, rhs, start, stop, r, w):
            E.op('pe', lambda e: e.matmul(out, lhsT, rhs, start=start, stop=stop), r, w)

        def tr(out, in_, ident, r, w):
            E.op('pe', lambda e: e.transpose(out, in_, ident), r, w)

        def act(out, in_, func, r, w, scale=None, bias=None):
            kw = {}
            if scale is not None:
                kw['scale'] = scale
            if bias is not None:
                kw['bias'] = bias
            E.op('act', lambda e: e.activation(out, in_, func, **kw), r, w)

        def tt(eng, out, in0, in1, op, r, w):
            E.op(eng, lambda e: e.tensor_tensor(out, in0, in1, op), r, w)

        def ts(eng, out, in0, s1, op0, r, w, s2=None, op1=None):
            if op1 is None:
                E.op(eng, lambda e: e.tensor_scalar(out, in0, s1, None, op0), r, w)
            else:
                E.op(eng, lambda e: e.tensor_scalar(out, in0, s1, s2, op0, op1), r, w)

        def stt(out, in0, scalar, in1, op0, op1, r, w):
            E.op('dve', lambda e: e.scalar_tensor_tensor(out, in0, scalar, in1, op0, op1), r, w)

        def cp(eng, out, in_, r, w):
            if eng == 'act':
                E.op('act', lambda e: e.copy(out, in_), r, w)
            else:
                E.op(eng, lambda e: e.tensor_copy(out, in_), r, w)

        def red(out, in_, r, w):
            E.op('dve', lambda e: e.tensor_reduce(out, in_, AX.X, ALU.add), r, w)

        def recip(out, in_, r, w):
            E.op('dve', lambda e: e.reciprocal(out, in_), r, w)

        def mset(eng, ap, val, w):
            E.op(eng, lambda e: e.memset(ap, val), (), w)

        def dma(q, out, in_, r, w):
            E.op(q, lambda e: e.dma_start(out=out, in_=in_), r, w, dma=True)

        con_f = AR.alloc([NCON], F32); h_con = E.H("con")
        con_b = AR.alloc([NCON], BF16)
        pp = AR.alloc([NPP], F32); h_pp = E.H("pp")
        xT = AR.alloc([8, 1152], F32)
        hT = AR.alloc([8, 1153], BF16)
        memn = AR.alloc([8, 256], BF16); h_memn = E.H("memn")
        st_S = AR.alloc([DEPTH, 4, 64], F32); h_stS = [E.H("stS%d" % l) for l in range(DEPTH)]
        st_conv = AR.alloc([DEPTH, 4, 30], BF16); h_stconv = [E.H("stc%d" % l) for l in range(DEPTH)]
        st_shift = AR.alloc([DEPTH, 8], BF16); h_stshift = [E.H("sts%d" % l) for l in range(DEPTH)]

        ident_b = con_b[:, C_ID:C_ID + 128]
        ident_f = con_f[:, C_ID:C_ID + 128]
        ones_b = None

        def epsc(i):
            return con_f[:, C_EPS + i:C_EPS + i + 1]

        dma('sp', con_f, d_con, (), [h_con])
        dma('pool', con_b, d_con, (), [h_con])
        dma('sp', pp, d_pp, (), [h_pp])
        onesb = AR.alloc([128], BF16)
        mset('dve', onesb, 1.0, [h_con])
        mset('dve', st_S, 0.0, h_stS)
        mset('dve', st_conv, 0.0, h_stconv)
        mset('dve', st_shift, 0.0, h_stshift)
        h_hinit = E.H('hinit')
        mset('dve', hT[:, :, 0:1], 0.0, [h_hinit])

        def rms_rstd(src_list, n, rsrc, sqbuf, rstd_out, h_tmp, eps_i=0, dim=1024.0):
            pb, hb = bank()
            nk = len(src_list)
            for k, s in enumerate(src_list):
                act(sqbuf[:, k, 0:n], s, AF.Square, rsrc, [h_tmp])
            for k in range(nk):
                mm(pb[:, 0:n], onesb, sqbuf[:, k, 0:n], k == 0, k == nk - 1, [h_tmp, h_con], [hb])
            act(rstd_out, pb[:, 0:n], AF.Sqrt, [hb, h_con], [h_tmp], scale=1.0 / dim, bias=epsc(eps_i))
            recip(rstd_out, rstd_out, [h_tmp], [h_tmp])


        def layer(half, l, groups, h_x, h_h):
            def ngc(i, k):
                o = PP_NG + l * 48 + i * 8 + k
                return pp[:, o:o + 1]

            def norm_to_h(i, sq, rstd, h_tmp):
                for g, (c, n, kind) in enumerate(groups):
                    rms_rstd([xT[:, k, c:c + n] for k in range(8)], n, [h_x[g]], sq, rstd[:, 0:n], h_tmp)
                    for k in range(8):
                        stt(hT[:, k, 1 + c:1 + c + n], xT[:, k, c:c + n], ngc(i, k), rstd[:, 0:n],
                            ALU.mult, ALU.mult, [h_x[g], h_tmp, h_pp], [h_h[g]])

            def post_res(i, g, src_list, rsrc, sq, rstd, tmp, h_tmp, h_tmp2):
                c, n, kind = groups[g]
                rms_rstd(src_list, n, rsrc, sq, rstd[:, 0:n], h_tmp)
                for k in range(8):
                    tt('pool', src_list[k], src_list[k], rstd[:, 0:n], ALU.mult, rsrc + [h_tmp], rsrc)
                for k in range(8):
                    stt(xT[:, k, c:c + n], src_list[k], ngc(i, k), xT[:, k, c:c + n], ALU.mult, ALU.add,
                        rsrc + [h_pp, h_x[g]], [h_x[g]])

            def load_w(q, dst, dsrc, h, ncol_split=None):
                v = dsrc.rearrange("(k p) n -> p k n", p=128)
                E.begin_group()
                for k in range(8):
                    dma(q, dst[:, k, :], v[:, k, :], (), [h])
                E.end_group()

            def attention_sub():
                AR.push()
                wq = AR.alloc([8, 1024], BF16); h_wq = E.H("wq")
                wk = AR.alloc([8, 1024], BF16); h_wk = E.H("wk")
                wv = AR.alloc([8, 1024], BF16); h_wv = E.H("wv")
                wo = AR.alloc([8, 1024], BF16); h_wo = E.H("wo")
                load_w('pool', wq, d_wq[l], h_wq)
                if DBG.get('nw', 4) > 1:
                    load_w('pool', wk, d_wk[l], h_wk)
                    load_w('pool', wv, d_wv[l], h_wv)
                    load_w('pool', wo, d_wo[l], h_wo)
                if DBG['stop'] <= 0:
                    E.barrier(); AR.pop(); return
                if DBG.get('wbar'):
                    E.barrier()
                stg = AR.alloc([2, 512], F32); h_stg = [E.H("stg0"), E.H("stg1")]
                sq = AR.alloc([8, 512], BF16); h_t1 = E.H("t1")
                rstd = AR.alloc([512], F32)
                tmp = AR.alloc([512], F32); h_t2 = E.H("t2")
                msb = AR.alloc([8, 512], F32); h_msb = E.H("msb")
                KT = AR.alloc([8, 256], BF16); h_KT = E.H("KT")
                Vb = AR.alloc([2, 1024], BF16); h_Vb = E.H("Vb")
                qT = sq; h_qT = h_t1
                oT = AR.alloc([8, 512], BF16); h_oT = E.H("oT")
                if DBG.get('stghi'):
                    stg = AR.alloc([2, 512], F32)
                PT = AR.alloc([2, 512], BF16); h_PT = E.H("PT")
                rs = AR.alloc([512], F32); h_rs = E.H("rs")
                if not DBG.get('skip1'):
                    norm_to_h(2, sq, rstd, h_t1)
                if DBG['stop'] <= 1:
                    E.barrier(); AR.pop(); return
                for oc in range(0 if DBG.get('skip2') else 8):
                    pb, hb = bank()
                    for k in range(8):
                        mm(pb[:, 0:256], wk[:, k, oc * 128:(oc + 1) * 128], memn[:, k, :], k == 0, k == 7,
                           [h_wk, h_memn], [hb])
                    cp('act', KT[:, oc, :], pb[:, 0:256], [hb], [h_KT])
                if DBG['stop'] <= 2:
                    E.barrier(); AR.pop(); return
                si = 0
                for which, wsrc, hw, dout_ in ((0, wv, h_wv, o_nv), (1, wk, h_wk, o_nk)):
                    if which == 1 and half == 1:
                        continue
                    if which >= DBG.get('vw', 2):
                        continue
                    for mc in range(DBG.get('vmc', 2)):
                        for hf in range(DBG.get('vhf', 2)):
                            pb, hb = bank()
                            for k in range(8):
                                mm(pb, memn[:, k, mc * 128:(mc + 1) * 128], wsrc[:, k, hf * 512:(hf + 1) * 512],
                                   k == 0, k == 7, [hw, h_memn], [hb])
                            if which == 0 and not DBG.get('novb'):
                                cp('act', Vb[:, mc, hf * 512:(hf + 1) * 512], pb, [hb], [h_Vb])
                            if half == 0:
                                s = (si % 2) if not DBG.get('stg0') else 0
                                si += 1
                                if not DBG.get('nostg'):
                                    cp(DBG.get('stgeng', 'dve'), (msb if DBG.get('stgmsb') else stg)[:, s, :], pb, ([h_Vb] if DBG.get('chain') else [hb]), [h_stg[s]])
                                if not DBG.get('nodma'):
                                    dma('sp', dout_[l, mc * 128:(mc + 1) * 128, hf * 512:(hf + 1) * 512], stg[:, s, :],
                                        [h_stg[s]], ())
                if DBG['stop'] <= 3:
                    E.barrier(); AR.pop(); return
                for g, (c, n, kind) in enumerate(groups):
                    for oc in range(8):
                        pb, hb = bank()
                        for k in range(8):
                            mm(pb[:, 0:n], wq[:, k, oc * 128:(oc + 1) * 128], hT[:, k, 1 + c:1 + c + n], k == 0, k == 7,
                               [h_wq, h_h[g]], [hb])
                        cp('act', qT[:, oc, 0:n], pb[:, 0:n], [hb], [h_qT])
                    if kind == 'p':
                        for hd in range(4):
                            for mc in range(2):
                                pb, hb = bank()
                                for dc in range(2):
                                    mm(pb[:, 0:n], KT[:, 2 * hd + dc, mc * 128:(mc + 1) * 128], qT[:, 2 * hd + dc, 0:n],
                                       dc == 0, dc == 1, [h_KT, h_qT], [hb])
                                act(PT[:, mc, 0:n], pb[:, 0:n], AF.Exp, [hb], [h_PT], scale=1.0 / 16.0)
                            pb, hb = bank()
                            for mc in range(2):
                                mm(pb[:, 0:n], onesb, PT[:, mc, 0:n], mc == 0, mc == 1, [h_PT, h_con], [hb])
                            recip(rs[:, 0:n], pb[:, 0:n], [hb], [h_rs])
                            for dc in range(2):
                                pb, hb = bank()
                                for mc in range(2):
                                    mm(pb[:, 0:n], Vb[:, mc, (2 * hd + dc) * 128:(2 * hd + dc + 1) * 128], PT[:, mc, 0:n],
                                       mc == 0, mc == 1, [h_Vb, h_PT], [hb])
                                tt('dve', oT[:, 2 * hd + dc, 0:n], pb[:, 0:n], rs[:, 0:n], ALU.mult, [hb, h_rs], [h_oT])
                    else:
                        AR.push()
                        KTs = AR.alloc([1, 4, 2, 256], BF16); h_KTs = [E.H("KTs0")]
                        Vs = AR.alloc([1, 2, 1024], BF16); h_Vs = [E.H("Vs0")]
                        PTs = PT.rearrange("p a (b h m t) -> p (a b) h m t", b=8, h=4, m=2); h_PTs = h_PT
                        rss = rs.rearrange("p (b h t) -> p b h t", b=16, h=4); h_rss = h_rs
                        for bh in range(2):
                            pbs, hbs = bank()
                            pbo0, hbo0 = bank()
                            sview = pbs.rearrange("p (b h m t) -> p b h m t", b=8, h=4, m=2)
                            oview = pbo0.rearrange("p (b h d t) -> p b h d t", b=8, h=4, d=2)
                            for bb in range(8):
                                b = bh * 8 + bb
                                s = 0
                                dma('pool', KTs[:, s], d_kT[l, b].rearrange("h (dc p) m -> p h dc m", p=128), (), [h_KTs[s]])
                                dma('pool', Vs[:, s], d_v[l, b].rearrange("(mc p) n -> p mc n", p=128), (), [h_Vs[s]])
                                for hd in range(4):
                                    for mc in range(2):
                                        for dc in range(2):
                                            mm(sview[:, bb, hd, mc, :], KTs[:, s, hd, dc, mc * 128:(mc + 1) * 128],
                                               qT[:, 2 * hd + dc, b * 8:(b + 1) * 8], dc == 0, dc == 1,
                                               [h_KTs[s], h_qT], [hbs])
                                act(PTs[:, b], sview[:, bb], AF.Exp, [hbs], [h_PTs], scale=1.0 / 16.0)
                                for hd in range(4):
                                    for dc in range(2):
                                        for mc in range(2):
                                            mm(oview[:, bb, hd, dc, :], Vs[:, s, mc, (2 * hd + dc) * 128:(2 * hd + dc + 1) * 128],
                                               PTs[:, b, hd, mc, :], mc == 0, mc == 1, [h_Vs[s], h_PTs], [hbo0])
                            pbr, hbr = bank()
                            rview = pbr[:, 0:256].rearrange("p (b h t) -> p b h t", b=8, h=4)
                            for mc in range(2):
                                mm(rview, onesb, PTs[:, bh * 8:(bh + 1) * 8, :, mc, :], mc == 0, mc == 1, [h_PTs, h_con], [hbr])
                            recip(rss[:, bh * 8:(bh + 1) * 8], rview, [hbr], [h_rss])
                            for dc in range(2):
                                ov = oT[:, :, 0:128].rearrange("p (h d) (b t) -> p d b h t", d=2, t=8)[:, dc, bh * 8:(bh + 1) * 8]
                                tt('dve', ov, oview[:, :, :, dc, :], rss[:, bh * 8:(bh + 1) * 8], ALU.mult,
                                   [hbo0, h_rss], [h_oT])
                        AR.pop()
                    for oc in range(8):
                        pb, hb = bank()
                        for k in range(8):
                            mm(pb[:, 0:n], wo[:, k, oc * 128:(oc + 1) * 128], oT[:, k, 0:n], k == 0, k == 7,
                               [h_wo, h_oT], [hb])
                        cp('act', msb[:, oc, 0:n], pb[:, 0:n], [hb], [h_msb])
                    post_res(3, g, [msb[:, k, 0:n] for k in range(8)], [h_msb], sq, rstd, tmp, h_t1, h_t2)
                E.barrier()
                AR.pop()


            def ffn_sub():
                AR.push()
                NT_ = 1024 + (128 if half == 1 else 0)
                facc = AR.alloc([8, NT_], F32); h_f = [E.H("f%d" % g) for g in range(len(groups))]
                w1 = AR.alloc([2, 8, 512], BF16); h_w1 = [E.H("w1a"), E.H("w1b")]
                w2 = AR.alloc([2, 4, 1024], BF16); h_w2 = [E.H("w2a"), E.H("w2b")]
                sq = AR.alloc([8, 512], BF16); h_t1 = E.H("t1f")
                rstd = AR.alloc([512], F32)
                tmp = AR.alloc([512], F32); h_t2 = E.H("t2f")
                rl = AR.alloc([2, 512], BF16); h_rl = [E.H("rl0"), E.H("rl1")]
                hid = AR.alloc([2, 4, 512], BF16); h_hid = [E.H("hid0"), E.H("hid1")]
                norm_to_h(4, sq, rstd, h_t1)
                w1v = d_w1[l].rearrange("(k p) n -> p k n", p=128)
                w2v = d_w2[l].rearrange("(c p) n -> p c n", p=128)
                hi = 0
                for hg in range(8):
                    s = hg % 2
                    E.begin_group()
                    for k in range(8):
                        dma('pool', w1[:, s, k, :], w1v[:, k, hg * 512:(hg + 1) * 512], (), [h_w1[s]])
                    E.end_group()
                    E.begin_group()
                    for cc in range(4):
                        dma('pool', w2[:, s, cc, :], w2v[:, hg * 4 + cc, :], (), [h_w2[s]])
                    E.end_group()
                    for g, (c, n, kind) in enumerate(groups):
                        hs = hi % 2
                        hi += 1
                        for hc in range(4):
                            pb, hb = bank()
                            for k in range(8):
                                mm(pb[:, 0:n], w1[:, s, k, hc * 128:(hc + 1) * 128], hT[:, k, 1 + c:1 + c + n], k == 0, k == 7,
                                   [h_w1[s], h_h[g]], [hb])
                            rs_ = hc % 2
                            act(rl[:, rs_, 0:n], pb[:, 0:n], AF.Relu, [hb], [h_rl[rs_]])
                            tt('dve', hid[:, hs, hc, 0:n], rl[:, rs_, 0:n], rl[:, rs_, 0:n], ALU.mult, [h_rl[rs_]], [h_hid[hs]])
                        for oc in range(8):
                            pb, hb = bank()
                            for hc in range(4):
                                mm(pb[:, 0:n], w2[:, s, hc, oc * 128:(oc + 1) * 128], hid[:, hs, hc, 0:n], hc == 0, hc == 3,
                                   [h_w2[s], h_hid[hs]], [hb])
                            if hg == 0:
                                cp('act', facc[:, oc, c:c + n], pb[:, 0:n], [hb], [h_f[g]])
                            else:
                                tt('dve', facc[:, oc, c:c + n], pb[:, 0:n], facc[:, oc, c:c + n], ALU.add, [hb, h_f[g]], [h_f[g]])
                for g, (c, n, kind) in enumerate(groups):
                    post_res(5, g, [facc[:, k, c:c + n] for k in range(8)], [h_f[g]], sq, rstd, tmp, h_t1, h_t2)
                E.barrier()
                AR.pop()


            def rwkv_sub(mixT, h_mix):
                AR.push()
                winv = d_win[l].rearrange("(k p) n -> p k n", p=128)
                Wr = AR.alloc([8, 1792], BF16); h_Wr = E.H("Wr")
                E.begin_group()
                for k in range(8):
                    dma('pool', Wr[:, k, :], winv[:, k, 1024:2816], (), [h_Wr])
                E.end_group()
                bct = AR.alloc([4096], F32); h_bc = E.H("bct")
                dma('sp', bct, d_bc[l][:, 0:4096], (), [h_bc])
                kk_t, ka_t, rk_t, gg_t, gb_t = [bct[:, i * 512:(i + 1) * 512] for i in range(5)]
                mu_t = [bct[:, 2560 + i * 512:2560 + (i + 1) * 512] for i in range(3)]
                rows_b = AR.alloc([1024], BF16); h_rows = E.H("rowsb")
                E.begin_group()
                dma('pool', rows_b[0:1, 0:512], d_rows[l:l + 1, 0:512], (), [h_rows])
                dma('pool', rows_b[64:65, 0:512], d_rows[l:l + 1, 512:1024], (), [h_rows])
                E.end_group()
                wup_b = AR.alloc([512], BF16); h_wup = E.H("wupb")
                dma('pool', wup_b, d_wup[l], (), [h_wup])
                gup_b = AR.alloc([512], BF16); h_gup = E.H("gupb")
                dma('pool', gup_b, d_gup[l], (), [h_gup])
                S_b = AR.alloc([4, 64], BF16); h_Sb = E.H("Sb")
                cp('act', S_b, st_S[:, l, :, :], [h_stS[l]], [h_Sb])
                FT = {}
                for nm in ("Asb", "dlt", "Zr", "Zk", "sw", "av", "gate", "G", "Ginv", "Gp", "kk", "sq_"):
                    FT[nm] = (AR.alloc([512], F32), E.H("f_" + nm))
                    DBGV[nm] = AR.last
                FT["kkn"] = FT["kk"]
                FT["t1"] = FT["Asb"]
                FT["kmod"] = FT["sw"]
                DBGV["kkn"] = DBGV["kk"]; DBGV["kmod"] = DBGV["sw"]
                BT = {}
                for nm in ("at", "bt", "kt", "rt", "vb", "Xb", "ro"):
                    BT[nm] = (AR.alloc([512], BF16), E.H("b_" + nm))
                    DBGV[nm] = AR.last
                arT = AR.alloc([4, 2, 128], BF16); h_arT = E.H("arT")
                kbT = AR.alloc([4, 2, 128], BF16); h_kbT = E.H("kbT")
                AkaAkr = AR.alloc([8, 256], BF16); h_Aka = E.H("Aka")
                NAbr = AR.alloc([8, 256], BF16); h_NAbr = E.H("NAbr")
                mrg_b = AR.alloc([1792], BF16); h_mrg = E.H("mrg")
                dma('pool', mrg_b, d_mrg, (), [h_mrg])
                Ad = AR.alloc([2, 128], F32); h_Ad = E.H("Ad")
                Dd = AR.alloc([2, 128], F32); h_Dd = E.H("Dd")
                tda = AR.alloc([128], BF16); h_tda = E.H("tda")
                sdg = AR.alloc([128], BF16); h_sdg = E.H("sdg")
                sm8 = AR.alloc([6, 8], F32); h_sm8 = E.H("sm8")
                gcT = AR.alloc([4, 16], F32); h_gc = E.H("gcT")
                tmpS = AR.alloc([4, 64], F32); h_tmpS = E.H("tmpS")
                has_s = (half == 1 and DBG['samp'])
                if has_s:
                    sst = AR.alloc([8, 16], F32); h_sst = E.H("sst")
                    dma('sp', sst, d_shift[l].rearrange("(k p) b -> p k b", p=128), (), [h_sst])
                    hprev_s = AR.alloc([8, 16, 8], BF16); h_hps = E.H("hps")
                    hsv = hT[:, :, 1025:1153].rearrange("p k (b t) -> p k b t", t=8)
                    cp('dve', hprev_s[:, :, :, 0], sst, [h_sst], [h_hps])
                    cp('dve', hprev_s[:, :, :, 1:8], hsv[:, :, :, 0:7], [h_h[2]], [h_hps])
                    S_sb = AR.alloc([16, 4, 64], BF16); h_Ssb = E.H("Ssb")
                    P1sb = AR.alloc([4, 128], BF16); h_P1sb = E.H("P1sb")
                    S_sf = FT["Gp"][0].rearrange("p (a b c) -> p a b c", a=2, b=4); h_Ssf = [FT["Gp"][1], FT["Gp"][1]]
                    vm = FT["G"][0].bitcast(BF16).rearrange("p (a b) -> p a b", a=2); h_vm = [FT["G"][1], FT["G"][1]]
                    um = FT["Ginv"][0].bitcast(BF16).rearrange("p (a b) -> p a b", a=2); h_um = [FT["Ginv"][1], FT["Ginv"][1]]
                    wkv_in = d_wkv[l].rearrange("b (p4 h2) j i -> h2 j b p4 i", h2=2)
                    wkv_out = o_wkvs[l].rearrange("b (p4 h2) j i -> h2 j b p4 i", h2=2)
                    E.begin_group()
                    for h2 in range(2):
                        for p4 in range(4):
                            dma('pool', S_sb[h2 * 64:(h2 + 1) * 64, :, p4, :], wkv_in[h2][:, :, p4, :], (), [h_Ssb])
                    E.end_group()

                def F(nm):
                    return FT[nm]

                def rw_tile(tc, kind, g):
                    smp = (kind == 's')
                    nseq = 16 if smp else 1
                    nap = 3 if smp else 7
                    MuMi = con_b[:, (C_MUS if smp else C_MU):(C_MUS if smp else C_MU) + 256]
                    MLb = con_b[:, (C_MLS if smp else C_ML):(C_MLS if smp else C_ML) + 128]
                    Tri = con_f[:, (C_MIS if smp else C_MI):(C_MIS if smp else C_MI) + 128]
                    seqf = con_f[:, C_SEQ:C_SEQ + 16] if smp else con_f[:, C_EPS + 4:C_EPS + 5]
                    hg = [h_h[g]] + ([h_h[g - 1]] if g > 0 else []) + ([h_hps] if smp else [])

                    def hc(k):
                        return hT[:, k, 1 + tc:1 + tc + 128]

                    def hp(k):
                        return hprev_s[:, k, :, :] if smp else hT[:, k, tc:tc + 128]
                    (Asb, hA), (dlt, hD) = F("Asb"), F("dlt")
                    Z = {}
                    for qi, qn in enumerate(("Zr", "Zk", "vb")):
                        bA, hbA = bank()
                        for k in range(8):
                            mm(bA, hc(k), Wr[:, k, qi * 512:(qi + 1) * 512], k == 0, k == 7, hg + [h_Wr], [hbA])
                        bB, hbB = bank()
                        for k in range(8):
                            mm(bB, hp(k), Wr[:, k, qi * 512:(qi + 1) * 512], k == 0, k == 7, hg + [h_Wr], [hbB])
                        cp('act', Asb, bA, [hbA], [hA])
                        tt('dve', dlt, bB, Asb, ALU.subtract, [hbB, hA], [hD])
                        tt('dve', dlt, dlt, mu_t[qi], ALU.mult, [hD, h_bc], [hD])
                        dst, hdst = (F(qn) if qi < 2 else BT["vb"])
                        tt('dve', dst, dlt, Asb, ALU.add, [hD, hA], [hdst])
                    (Zr, hZr), (Zk, hZk) = F("Zr"), F("Zk")
                    vb, hvb = BT["vb"]
                    bD, hbD = bank()
                    for ci, c0 in enumerate((1536, 1664)):
                        for k in range(8):
                            mm(bD[:, ci * 256:ci * 256 + 128], Wr[:, k, c0:c0 + 128], hc(k), k == 0, k == 7, hg + [h_Wr], [hbD])
                        for k in range(8):
                            mm(bD[:, ci * 256 + 128:ci * 256 + 256], Wr[:, k, c0:c0 + 128], hp(k), k == 0, k == 7, hg + [h_Wr], [hbD])
                    for ci in range(2):
                        mucol = pp[:, PP_MUD + l * 2 + ci:PP_MUD + l * 2 + ci + 1]
                        cp('act', Ad[:, ci, :], bD[:, ci * 256:ci * 256 + 128], [hbD], [h_Ad])
                        tt('dve', Dd[:, ci, :], bD[:, ci * 256 + 128:ci * 256 + 256], Ad[:, ci, :], ALU.subtract, [hbD, h_Ad], [h_Dd])
                        stt(Dd[:, ci, :], Dd[:, ci, :], mucol, Ad[:, ci, :], ALU.mult, ALU.add, [h_Dd, h_Ad, h_pp], [h_Dd])
                    act(tda[0:64, :], Dd[0:64, 0, :], AF.Tanh, [h_Dd], [h_tda])
                    cp('act', tda[64:128, :], Dd[64:128, 0, :], [h_Dd], [h_tda])
                    act(sdg, Dd[:, 1, :], AF.Sigmoid, [h_Dd], [h_sdg])
                    (sw, hsw), (av, hav), (gate, hgate) = F("sw"), F("av"), F("gate")
                    bW, hbW = bank()
                    mm(bW, onesb[0:1, 0:128], rows_b[0:1, 0:512], True, False, [h_con, h_rows], [hbW])
                    mm(bW, tda[0:64, :], wup_b[0:64, :], False, True, [h_tda, h_wup], [hbW])
                    act(sw, bW, AF.Sigmoid, [hbW], [hsw])
                    bAa, hbAa = bank()
                    mm(bAa, onesb[64:65, 0:128], rows_b[64:65, 0:512], True, False, [h_con, h_rows], [hbAa])
                    mm(bAa, tda[64:128, :], wup_b[64:128, :], False, True, [h_tda, h_wup], [hbAa])
                    act(av, bAa, AF.Sigmoid, [hbAa], [hav])
                    bG, hbG = bank()
                    mm(bG, sdg, gup_b, True, True, [h_sdg, h_gup], [hbG])
                    cp('act', gate, bG, [hbG], [hgate])
                    (G, hG), (Ginv, hGi), (Gp, hGp) = F("G"), F("Ginv"), F("Gp")
                    bC, hbC = bank()
                    mm(bC, Tri, sw, True, True, [h_con, hsw], [hbC])
                    act(G, bC, AF.Exp, [hbC], [hG], scale=-CDEC)
                    act(Ginv, bC, AF.Exp, [hbC], [hGi], scale=CDEC)
                    tt('dve', Gp, bC, sw, ALU.subtract, [hbC, hsw], [hGp])
                    act(Gp, Gp, AF.Exp, [hGp], [hGp], scale=-CDEC)
                    bGC, hbGC = bank()
                    for p4 in range(4):
                        mm(bGC[:, p4 * 16:p4 * 16 + nseq], sw[:, p4 * 128:(p4 + 1) * 128], seqf, True, True, [hsw, h_con], [hbGC])
                    act(gcT[:, :, 0:nseq], bGC[:, 0:64].rearrange("p (a b) -> p a b", a=4)[:, :, 0:nseq], AF.Exp, [hbGC], [h_gc], scale=-CDEC)
                    (kk, hkk), (sq_, hsq), (kkn, hkkn), (t1, ht1), (kmod, hkm) = F("kk"), F("sq_"), F("kkn"), F("t1"), F("kmod")
                    tt('dve', kk, Zk, kk_t, ALU.mult, [hZk, h_bc], [hkk])
                    tt('dve', sq_, kk, kk, ALU.mult, [hkk], [hsq])
                    red(sm8[:, 0, :], sq_.rearrange("p (h d) -> p h d", h=8), [hsq], [h_sm8])
                    act(sm8[:, 0, :], sm8[:, 0, :], AF.Sqrt, [h_sm8], [h_sm8])
                    ts('dve', sm8[:, 0, :], sm8[:, 0, :], 1e-12, ALU.max, [h_sm8], [h_sm8])
                    recip(sm8[:, 0, :], sm8[:, 0, :], [h_sm8], [h_sm8])
                    tt('dve', kkn.rearrange("p (h d) -> p h d", h=8), kk.rearrange("p (h d) -> p h d", h=8),
                       sm8[:, 0, :].unsqueeze(2).to_broadcast([128, 8, 64]), ALU.mult, [hkk, h_sm8], [hkkn])
                    stt(t1, av, -1.0, ka_t, ALU.add, ALU.mult, [hav, h_bc], [ht1])
                    stt(kmod, t1, 1.0, Zk, ALU.add, ALU.mult, [ht1, hZk], [hkm])
                    (at, hat), (bt, hbt), (kt, hkt), (rt, hrt) = BT["at"], BT["bt"], BT["kt"], BT["rt"]
                    stt(at, kkn, -1.0, Gp, ALU.mult, ALU.mult, [hkkn, hGp], [hat])
                    tt('pool', t1, kkn, av, ALU.mult, [hkkn, hav], [ht1])
                    tt('pool', bt, t1, Ginv, ALU.mult, [ht1, hGi], [hbt])
                    tt('dve', kt, kmod, Ginv, ALU.mult, [hkm, hGi], [hkt])
                    tt('dve', rt, Zr, G, ALU.mult, [hZr, hG], [hrt])
                    tt('dve', sq_, Zr, kmod, ALU.mult, [hZr, hkm], [hsq])
                    tt('dve', sq_, sq_, rk_t, ALU.mult, [hsq, h_bc], [hsq])
                    red(sm8[:, 1, :], sq_.rearrange("p (h d) -> p h d", h=8), [hsq], [h_sm8])
                    for (srcs, dstT, hdT) in (((at, hat, rt, hrt), arT, h_arT), ((kt, hkt, bt, hbt), kbT, h_kbT)):
                        bT_, hbT_ = bank()
                        tv = bT_.bitcast(BF16)[:, 0:1024].rearrange("p (a b c) -> p a b c", a=4, b=2)
                        for p4 in range(4):
                            tr(tv[:, p4, 0, :], srcs[0][:, p4 * 128:(p4 + 1) * 128], ident_b, [srcs[1], h_con], [hbT_])
                            tr(tv[:, p4, 1, :], srcs[2][:, p4 * 128:(p4 + 1) * 128], ident_b, [srcs[3], h_con], [hbT_])
                        cp('act', dstT, tv, [hbT_], [hdT])
                    mk = MuMi.unsqueeze(1).to_broadcast([128, 2, 256])
                    for hb4 in range(2):
                        for par in range(2):
                            bA_, hbA_ = bank()
                            bB_, hbB_ = bank()
                            base = 64 * par
                            for hh in range(2):
                                h = hb4 * 4 + 2 * hh + par
                                p4 = h // 2
                                ar2 = arT[base:base + 64, p4, :, :]
                                mm(bA_[:, hh * 256:(hh + 1) * 256], kbT[base:base + 64, p4, 0, :], ar2, True, True, [h_kbT, h_arT], [hbA_])
                                mm(bB_[:, hh * 256:(hh + 1) * 256], kbT[base:base + 64, p4, 1, :], ar2, True, True, [h_kbT, h_arT], [hbB_])
                            h0 = hb4 * 4 + par
                            tt('dve', AkaAkr[:, h0:h0 + 3:2, :], bA_.rearrange("p (a b) -> p a b", a=2), mk, ALU.mult, [hbA_, h_con], [h_Aka])
                            tt('dve', NAbr[:, h0:h0 + 3:2, :], bB_.rearrange("p (a b) -> p a b", a=2), mk, ALU.mult, [hbB_, h_con], [h_NAbr])
                    (Xb, hXb) = BT["Xb"]
                    X, hX = F("Asb")

                    def s0_terms(bX_, hbX_, which, post):
                        if not smp:
                            for h in range(8):
                                p4, base = h // 2, 64 * (h % 2)
                                mm(bX_[:, h * 64:(h + 1) * 64], arT[base:base + 64, p4, which, :], S_b[base:base + 64, p4, :],
                                   True, False, [h_arT, h_Sb], [hbX_])
                                post(h, True)
                        else:
                            for h2 in range(2):
                                base = 64 * h2
                                bP, hbP = bank()
                                for p4 in range(4):
                                    for s in range(16):
                                        mm(bP[base:base + 64, p4 * 128 + s * 8:p4 * 128 + s * 8 + 8], S_sb[base:base + 64, s, p4, :],
                                           arT[base:base + 64, p4, which, s * 8:(s + 1) * 8], True, True, [h_Ssb, h_arT], [hbP])
                                cp('act', P1sb[base:base + 64, :, :], bP[base:base + 64, :].rearrange("p (a b) -> p a b", a=4), [hbP], [h_P1sb])
                            for p4 in range(4):
                                mm(bX_[:, p4 * 128:(p4 + 1) * 128], P1sb[:, p4, :], ident_b, True, False, [h_P1sb, h_con], [hbX_])
                                post(2 * p4, False)
                                post(2 * p4 + 1, True)
                    bX, hbX = bank()

                    def post_x(h, last):
                        mm(bX[:, h * 64:(h + 1) * 64], AkaAkr[:, h, 0:128], vb[:, h * 64:(h + 1) * 64], False, last, [h_Aka, hvb], [hbX])
                    s0_terms(bX, hbX, 0, post_x)
                    cp('act', X, bX, [hbX], [hX])
                    cp('dve', Xb, X, [hX], [hXb])
                    def b8(nm):
                        return FT[nm][0].bitcast(BF16).rearrange("p (a b) -> p a b", a=8), FT[nm][1]
                    (Tm, hTm), (TmT, hTmT), (Noff, hNo), (YT, hYT) = b8("kk"), b8("sq_"), b8("Zr"), b8("av")
                    idb = ident_b.unsqueeze(1).to_broadcast([128, 8, 128])
                    cp('pool', Tm, idb, [h_con], [hTm])
                    cp('pool', TmT, idb, [h_con], [hTmT])
                    for lv in (range(3) if smp else range(7)):
                        moT = mrg_b[:, 896 + lv * 128:896 + (lv + 1) * 128].unsqueeze(1).to_broadcast([128, 4, 128])
                        for q in range(2):
                            bY_, hbY_ = bank()
                            for hh in range(4):
                                h = q * 4 + hh
                                mm(bY_[:, hh * 128:(hh + 1) * 128], NAbr[:, h, 0:128], TmT[:, h, :], True, True, [h_NAbr, hTmT], [hbY_])
                            tt('dve', YT[:, q * 4:q * 4 + 4, :], bY_.rearrange("p (a b) -> p a b", a=4), moT, ALU.mult, [hbY_, h_mrg], [hYT])
                        zb = []
                        for q in range(2):
                            bZ, hbZ = bank()
                            bZT, hbZT = bank()
                            for hh in range(4):
                                h = q * 4 + hh
                                mm(bZ[:, hh * 128:(hh + 1) * 128], YT[:, h, :], Tm[:, h, :], True, True, [hYT, hTm], [hbZ])
                                mm(bZT[:, hh * 128:(hh + 1) * 128], Tm[:, h, :], YT[:, h, :], True, True, [hYT, hTm], [hbZT])
                            zb.append((bZ, hbZ, bZT, hbZT))
                        for q in range(2):
                            bZ, hbZ, bZT, hbZT = zb[q]
                            tt('dve', Tm[:, q * 4:q * 4 + 4, :], bZ.rearrange("p (a b) -> p a b", a=4), Tm[:, q * 4:q * 4 + 4, :], ALU.add, [hbZ, hTm], [hTm])
                            tt('dve', TmT[:, q * 4:q * 4 + 4, :], bZT.rearrange("p (a b) -> p a b", a=4), TmT[:, q * 4:q * 4 + 4, :], ALU.add, [hbZT, hTmT], [hTmT])
                    bY, hbY = bank()
                    for h in range(8):
                        mm(bY[:, h * 64:(h + 1) * 64], Tm[:, h, :], Xb[:, h * 64:(h + 1) * 64], True, True, [hTm, hXb], [hbY])
                    cp('act', X, bY, [hbY], [hX])
                    cp('dve', Xb, X, [hX], [hXb])
                    bO, hbO = bank()

                    def post_o(h, last):
                        mm(bO[:, h * 64:(h + 1) * 64], AkaAkr[:, h, 128:256], vb[:, h * 64:(h + 1) * 64], False, False, [h_Aka, hvb], [hbO])
                        mm(bO[:, h * 64:(h + 1) * 64], NAbr[:, h, 128:256], Xb[:, h * 64:(h + 1) * 64], False, last, [h_NAbr, hXb], [hbO])
                    s0_terms(bO, hbO, 1, post_o)
                    (osb, hos) = F("dlt")
                    cp('act', osb, bO, [hbO], [hos])
                    if not smp:
                        bS, hbS = bank()
                        for h in range(8):
                            p4, base = h // 2, 64 * (h % 2)
                            mm(bS[base:base + 64, p4 * 64:(p4 + 1) * 64], kt[:, h * 64:(h + 1) * 64], vb[:, h * 64:(h + 1) * 64], True, False, [hkt, hvb], [hbS])
                            mm(bS[base:base + 64, p4 * 64:(p4 + 1) * 64], bt[:, h * 64:(h + 1) * 64], Xb[:, h * 64:(h + 1) * 64], False, True, [hbt, hXb], [hbS])
                        tt('dve', tmpS, bS[:, 0:256].rearrange("p (a b) -> p a b", a=4), st_S[:, l, :, :], ALU.add, [hbS, h_stS[l]], [h_tmpS])
                        tt('dve', st_S[:, l, :, :], tmpS, gcT[:, :, 0:1].to_broadcast([128, 4, 64]), ALU.mult, [h_tmpS, h_gc], [h_stS[l]])
                        cp('act', S_b, st_S[:, l, :, :], [h_stS[l]], [h_Sb])
                    else:
                        for s in range(16):
                            ss_ = s % 2
                            E.begin_group()
                            for h2 in range(2):
                                dma('sp', S_sf[h2 * 64:(h2 + 1) * 64, ss_, :, :], wkv_in[h2][:, s, :, :], (), [h_Ssf[ss_]])
                            E.end_group()
                            ts('pool', vm[:, ss_, :], vb, seqf[:, s:s + 1], ALU.mult, [hvb, h_con], [h_vm[ss_]])
                            ts('pool', um[:, ss_, :], Xb, seqf[:, s:s + 1], ALU.mult, [hXb, h_con], [h_um[ss_]])
                            bS, hbS = bank()
                            for h in range(8):
                                p4, base = h // 2, 64 * (h % 2)
                                mm(bS[base:base + 64, p4 * 64:(p4 + 1) * 64], kt[:, h * 64:(h + 1) * 64], vm[:, ss_, h * 64:(h + 1) * 64], True, False, [hkt, h_vm[ss_]], [hbS])
                                mm(bS[base:base + 64, p4 * 64:(p4 + 1) * 64], bt[:, h * 64:(h + 1) * 64], um[:, ss_, h * 64:(h + 1) * 64], False, True, [hbt, h_um[ss_]], [hbS])
                            tt('dve', S_sf[:, ss_, :, :], bS[:, 0:256].rearrange("p (a b) -> p a b", a=4), S_sf[:, ss_, :, :], ALU.add, [hbS, h_Ssf[ss_]], [h_Ssf[ss_]])
                            tt('dve', S_sf[:, ss_, :, :], S_sf[:, ss_, :, :], gcT[:, :, s:s + 1].to_broadcast([128, 4, 64]), ALU.mult, [h_Ssf[ss_], h_gc], [h_Ssf[ss_]])
                            E.begin_group()
                            for h2 in range(2):
                                dma('sp', wkv_out[h2][:, s, :, :], S_sf[h2 * 64:(h2 + 1) * 64, ss_, :, :], [h_Ssf[ss_]], ())
                            E.end_group()
                    (osq, hoq), (on, hon), (bon, hbon) = F("G"), F("Ginv"), F("Gp")
                    o3 = osb.rearrange("p (h d) -> p h d", h=8)
                    red(sm8[:, 2, :], o3, [hos], [h_sm8])
                    tt('pool', osq, osb, osb, ALU.mult, [hos], [hoq])
                    red(sm8[:, 3, :], osq.rearrange("p (h d) -> p h d", h=8), [hoq], [h_sm8])
                    ts('dve', sm8[:, 2, :], sm8[:, 2, :], 1.0 / 64.0, ALU.mult, [h_sm8], [h_sm8])
                    tt('dve', sm8[:, 4, :], sm8[:, 2, :], sm8[:, 2, :], ALU.mult, [h_sm8], [h_sm8])
                    stt(sm8[:, 3, :], sm8[:, 3, :], 1.0 / 64.0, sm8[:, 4, :], ALU.mult, ALU.subtract, [h_sm8], [h_sm8])
                    act(sm8[:, 3, :], sm8[:, 3, :], AF.Sqrt, [h_sm8, h_con], [h_sm8], scale=1.0, bias=epsc(2))
                    recip(sm8[:, 3, :], sm8[:, 3, :], [h_sm8], [h_sm8])
                    on3 = on.rearrange("p (h d) -> p h d", h=8)
                    tt('dve', on3, o3, sm8[:, 2, :].unsqueeze(2).to_broadcast([128, 8, 64]), ALU.subtract, [hos, h_sm8], [hon])
                    tt('dve', on3, on3, sm8[:, 3, :].unsqueeze(2).to_broadcast([128, 8, 64]), ALU.mult, [hon, h_sm8], [hon])
                    tt('dve', on, on, gg_t, ALU.mult, [hon, h_bc], [hon])
                    tt('dve', on, on, gb_t, ALU.add, [hon, h_bc], [hon])
                    tt('dve', bon.rearrange("p (h d) -> p h d", h=8), vb.rearrange("p (h d) -> p h d", h=8),
                       sm8[:, 1, :].unsqueeze(2).to_broadcast([128, 8, 64]), ALU.mult, [hvb, h_sm8], [hbon])
                    tt('dve', on, on, bon, ALU.add, [hon, hbon], [hon])
                    ro, hro = BT["ro"]
                    tt('dve', ro, on, gate, ALU.mult, [hon, hgate], [hro])
                    bT_, hbT_ = bank()
                    tv = bT_.bitcast(BF16)[:, 0:512].rearrange("p (a b) -> p a b", a=4)
                    for p4 in range(4):
                        tr(tv[:, p4, :], ro[:, p4 * 128:(p4 + 1) * 128], ident_b, [hro, h_con], [hbT_])
                    cp('act', mixT[:, 4:8, tc:tc + 128], tv, [hbT_], [h_mix[g]])

                ntl = 0
                if DBG.get('rwtiles', 99) < 99 or DBG.get('rwonly') is not None:
                    for g, (c, n, kind) in enumerate(groups):
                        mset('pool', mixT[:, 4:8, c:c + n], 0.0, [h_mix[g]])
                for g, (c, n, kind) in enumerate(groups):
                    for ti in range(n // 128):
                        if ntl < DBG.get('rwtiles', 99) and (DBG.get('rwonly') is None or half == 0 or ntl == DBG['rwonly']):
                            rw_tile(c + ti * 128, kind, g)
                        ntl += 1
                if half == 1:
                    for h2 in range(2):
                        dma('sp', o_wkvp[l].rearrange("(p4 h2) j i -> h2 j p4 i", h2=2)[h2], st_S[h2 * 64:(h2 + 1) * 64, l, :, :],
                            [h_stS[l]], ())
                E.barrier()
                AR.pop()

            def mixer_sub():
                AR.push()
                NT_ = 1024 + (128 if (half == 1 and DBG['samp']) else 0)
                winv = d_win[l].rearrange("(k p) n -> p k n", p=128)
                mixT = AR.alloc([8, NT_], BF16); h_mix = [E.H("mix%d" % g) for g in range(len(groups))]
                AR.push()
                sq = AR.alloc([8, 512], BF16); h_t1 = E.H("t1m")
                rstd = AR.alloc([512], F32)
                hl = AR.alloc([8], F32); h_hl = E.H("hl")
                hs_ = AR.alloc([8, 16], F32); h_hs = E.H("hs")
                if half == 1:
                    cp('dve', hT[:, :, 0], st_shift[:, l, :], [h_stshift[l]], [h_h[0]])
                for g, (c, n, kind) in enumerate(groups):
                    rms_rstd([xT[:, k, c:c + n] for k in range(8)], n, [h_x[g]], sq, rstd[:, 0:n], h_t1)
                    for k in range(8):
                        stt(hT[:, k, 1 + c:1 + c + n], xT[:, k, c:c + n], ngc(0, k), rstd[:, 0:n],
                            ALU.mult, ALU.mult, [h_x[g], h_t1, h_pp], [h_h[g]])
                    if kind == 'p' and g == 1:
                        if half == 0:
                            cp('pool', st_shift[:, l, :], hT[:, :, 1024], [h_h[g]], [h_stshift[l]])
                        else:
                            for k in range(8):
                                stt(hl[:, k:k + 1], xT[:, k, 1023:1024], ngc(0, k), rstd[:, 511:512], ALU.mult, ALU.mult,
                                    [h_x[g], h_t1, h_pp], [h_hl])
                            dma('sp', o_shiftp[l], hl, [h_hl], ())
                    if kind == 's':
                        xs7 = xT[:, :, 1024:1152].rearrange("p k (b t) -> p k b t", t=8)
                        r7 = rstd[:, 0:128].rearrange("p (b t) -> p b t", t=8)[:, :, 7]
                        for k in range(8):
                            stt(hs_[:, k, :], xs7[:, k, :, 7], ngc(0, k), r7, ALU.mult, ALU.mult,
                                [h_x[g], h_t1, h_pp], [h_hs])
                        dma('sp', o_shifts[l], hs_, [h_hs], ())
                E.barrier()
                AR.pop()
                AR.push()
                Wc = AR.alloc([8, 1024], BF16); h_Wc = E.H("Wc")
                E.begin_group()
                for k in range(8):
                    dma('pool', Wc[:, k, :], winv[:, k, 0:1024], (), [h_Wc])
                E.end_group()
                uf = AR.alloc([4, 512], F32); h_uf = E.H("uf")
                sg = AR.alloc([512], F32); h_sg = E.H("sg")
                uT = AR.alloc([4, 542], BF16); h_uT = E.H("uT")
                uTs = AR.alloc([4, 16, 38], BF16); h_uTs = E.H("uTs")
                Dg = AR.alloc([31, 128], BF16); h_Dg = E.H("Dg")
                ysb = AR.alloc([4, 512], F32); h_ysb = E.H("ysb")
                ybf = AR.alloc([4, 512], BF16); h_ybf = E.H("ybf")
                ysq = AR.alloc([4, 512], BF16); h_ysq = E.H("ysq")
                mean = AR.alloc([512], F32); h_mean = E.H("mean")
                m2 = AR.alloc([512], F32); h_m2 = E.H("m2")
                rs2 = AR.alloc([512], F32); h_rs2 = E.H("rs2")
                tl = AR.alloc([512], F32); h_tl = E.H("tl")
                for g, (c, n, kind) in enumerate(groups):
                    if kind == 'p':
                        if g == 0:
                            cp('pool', uT[:, :, 0:30], st_conv[:, l, :, :], [h_stconv[l]], [h_uT])
                        else:
                            cp('pool', uT[:, :, 0:30], uT[:, :, 512:542], [h_uT], [h_uT])
                    else:
                        E.begin_group()
                        for c4 in range(4):
                            dma('pool', uTs[:, c4, :, 0:30],
                                d_conv[l][:, c4 * 128:(c4 + 1) * 128, :].rearrange("b p t -> p b t"), (), [h_uTs])
                        E.end_group()
                    for c4 in range(4):
                        pbv, hbv = bank()
                        for k in range(8):
                            mm(pbv[:, 0:n], Wc[:, k, c4 * 128:(c4 + 1) * 128], hT[:, k, 1 + c:1 + c + n], k == 0, k == 7,
                               [h_Wc, h_h[g]], [hbv])
                        pbg, hbg = bank()
                        for k in range(8):
                            mm(pbg[:, 0:n], Wc[:, k, 512 + c4 * 128:512 + (c4 + 1) * 128], hT[:, k, 1 + c:1 + c + n],
                               k == 0, k == 7, [h_Wc, h_h[g]], [hbg])
                        act(sg[:, 0:n], pbg[:, 0:n], AF.Sigmoid, [hbg], [h_sg])
                        tt('dve', uf[:, c4, 0:n], pbv[:, 0:n], sg[:, 0:n], ALU.mult, [hbv, h_sg], [h_uf])
                        if kind == 'p':
                            cp('act', uT[:, c4, 30:30 + n], uf[:, c4, 0:n], [h_uf], [h_uT])
                        else:
                            cp('act', uTs[:, c4, :, 30:38], uf[:, c4, 0:128].rearrange("p (b t) -> p b t", t=8),
                               [h_uf], [h_uTs])
                    if kind == 'p' and g == 1:
                        if half == 0:
                            cp('pool', st_conv[:, l, :, :], uT[:, :, 512:542], [h_uT], [h_stconv[l]])
                        else:
                            dma('sp', o_convp[l].rearrange("(c p) t -> p c t", p=128), uf[:, :, 482:512], [h_uf], ())
                    if kind == 's':
                        dma('sp', o_convs[l][:, :, 0:22], d_conv[l][:, :, 8:30], (), ())
                        for c4 in range(4):
                            dma('sp', o_convs[l][:, c4 * 128:(c4 + 1) * 128, 22:30].rearrange("b p t -> p b t"),
                                uf[:, c4, 0:128].rearrange("p (b t) -> p b t", t=8), [h_uf], ())
                    for c4 in range(4):
                        o_cw = PP_CW + (l * 4 + c4) * 31
                        tt('pool', Dg, ident_b.unsqueeze(1).to_broadcast([128, 31, 128]),
                           pp[:, o_cw:o_cw + 31].unsqueeze(2).to_broadcast([128, 31, 128]), ALU.mult,
                           [h_con, h_pp], [h_Dg])
                        pby, hby = bank()
                        for j in range(31):
                            rhs = uT[:, c4, j:j + n] if kind == 'p' else uTs[:, c4, :, j:j + 8]
                            mm(pby[:, 0:n], Dg[:, j, :], rhs, j == 0, j == 30, [h_Dg, h_uT, h_uTs], [hby])
                        cbc = pp[:, PP_CB + l * 4 + c4:PP_CB + l * 4 + c4 + 1]
                        act(ysb[:, c4, 0:n], pby[:, 0:n], AF.Identity, [hby, h_pp], [h_ysb], bias=cbc, scale=1.0)
                        act(ybf[:, c4, 0:n], pby[:, 0:n], AF.Identity, [hby, h_pp], [h_ybf], bias=cbc, scale=1.0)
                        act(ysq[:, c4, 0:n], pby[:, 0:n], AF.Square, [hby, h_pp], [h_ysq], bias=cbc, scale=1.0)
                    pb1, hb1 = bank()
                    for c4 in range(4):
                        mm(pb1[:, 0:n], onesb, ybf[:, c4, 0:n], c4 == 0, c4 == 3, [h_ybf, h_con], [hb1])
                    pb2, hb2 = bank()
                    for c4 in range(4):
                        mm(pb2[:, 0:n], onesb, ysq[:, c4, 0:n], c4 == 0, c4 == 3, [h_ysq, h_con], [hb2])
                    act(mean[:, 0:n], pb1[:, 0:n], AF.Identity, [hb1], [h_mean], scale=1.0 / 512.0, bias=epsc(3))
                    tt('dve', m2[:, 0:n], mean[:, 0:n], mean[:, 0:n], ALU.mult, [h_mean], [h_m2])
                    stt(m2[:, 0:n], pb2[:, 0:n], 1.0 / 512.0, m2[:, 0:n], ALU.mult, ALU.subtract, [hb2, h_m2], [h_m2])
                    act(rs2[:, 0:n], m2[:, 0:n], AF.Sqrt, [h_m2, h_con], [h_rs2], scale=1.0, bias=epsc(1))
                    recip(rs2[:, 0:n], rs2[:, 0:n], [h_rs2], [h_rs2])
                    for c4 in range(4):
                        tt('dve', tl[:, 0:n], ysb[:, c4, 0:n], mean[:, 0:n], ALU.subtract, [h_ysb, h_mean], [h_tl])
                        tt('dve', tl[:, 0:n], tl[:, 0:n], rs2[:, 0:n], ALU.mult, [h_tl, h_rs2], [h_tl])
                        ol = l * 4 + c4
                        ts('dve', tl[:, 0:n], tl[:, 0:n], pp[:, PP_LG + ol:PP_LG + ol + 1], ALU.mult, [h_tl, h_pp], [h_tl],
                           s2=pp[:, PP_LB + ol:PP_LB + ol + 1], op1=ALU.add)
                        act(mixT[:, c4, c:c + n], tl[:, 0:n], AF.Silu, [h_tl], [h_mix[g]])
                E.barrier()
                AR.pop()
                if DBG.get('rwkv'):
                    rwkv_sub(mixT, h_mix)
                else:
                    for g, (c, n, kind) in enumerate(groups):
                        mset('pool', mixT[:, 4:8, c:c + n], 0.0, [h_mix[g]])
                wout = AR.alloc([8, 1024], BF16); h_wout = E.H("wout")
                load_w('pool', wout, d_wout[l], h_wout)
                sq = AR.alloc([8, 512], BF16); h_t1 = E.H("t1m3")
                rstd = AR.alloc([512], F32)
                tmp = AR.alloc([512], F32); h_t2 = E.H("t2m")
                msb = AR.alloc([8, 512], F32); h_msb = E.H("msbm")
                for g, (c, n, kind) in enumerate(groups):
                    for oc in range(8):
                        pb, hb = bank()
                        for k in range(8):
                            mm(pb[:, 0:n], wout[:, k, oc * 128:(oc + 1) * 128], mixT[:, k, c:c + n], k == 0, k == 7,
                               [h_wout, h_mix[g]], [hb])
                        cp('act', msb[:, oc, 0:n], pb[:, 0:n], [hb], [h_msb])
                    post_res(1, g, [msb[:, k, 0:n] for k in range(8)], [h_msb], sq, rstd, tmp, h_t1, h_t2)
                E.barrier()
                AR.pop()

            if DBG.get('mix'):
                mixer_sub()
            if DBG['att']:
                attention_sub()
            if DBG['ffn']:
                ffn_sub()

        for half in range(DBG['halves']):
            groups = [(0, 512, 'p'), (512, 512, 'p')] + ([(1024, 128, 's')] if (half == 1 and DBG['samp']) else [])
            NTOK = 1024 + (128 if half == 1 else 0)
            h_x = [E.H("x%d" % g) for g in range(len(groups))]
            h_h = [E.H("h%d" % g) for g in range(len(groups))]
            xv = d_xT.rearrange("(k p) n -> p k n", p=128)
            yv = o_yT.rearrange("(k p) n -> p k n", p=128)
            for g, (c, n, kind) in enumerate(groups):
                src = xv[:, :, half * 1024 + c: half * 1024 + c + n] if kind == 'p' else xv[:, :, 2048:2176]
                dma('sp', xT[:, :, c:c + n], src, (), [h_x[g]])

            if half == 0:
                AR.push()
                mraw = AR.alloc([8, 256], F32); h_mraw = E.H("mraw")
                msq = AR.alloc([8, 256], BF16)
                mr = AR.alloc([256], F32); h_mt = E.H("mtmp")
                dma('sp', mraw, d_memT.rearrange("(k p) n -> p k n", p=128), (), [h_mraw])
                rms_rstd([mraw[:, k, :] for k in range(8)], 256, [h_mraw], msq, mr, h_mt)
                for k in range(8):
                    stt(memn[:, k, :], mraw[:, k, :], pp[:, PP_MG + k:PP_MG + k + 1], mr, ALU.mult, ALU.mult,
                        [h_mraw, h_mt, h_pp], [h_memn])
                E.barrier()
                AR.pop()

            for l in range(DBG['layers']):
                layer(half, l, groups, h_x, h_h)

            for g, (c, n, kind) in enumerate(groups):
                dst = yv[:, :, half * 1024 + c: half * 1024 + c + n] if kind == 'p' else yv[:, :, 2048:2176]
                dma('sp', dst, xT[:, :, c:c + n], [h_x[g]], ())
            E.barrier()

        E.finalize()
        import contextlib
        with contextlib.ExitStack() as es:
            csem = {e: es.enter_context(nc.semaphore("c_" + e)) for e in ENGS}
            dsem = [es.enter_context(nc.semaphore("d%d" % i)) for i in range(NDSEM)]
            block = es.enter_context(nc.Block())

            @block.tensor
            def _(e):
                E.emit('pe', e, csem, dsem)

            @block.scalar
            def _(e):
                E.emit('act', e, csem, dsem)

            @block.vector
            def _(e):
                E.emit('dve', e, csem, dsem)

            @block.gpsimd
            def _(e):
                E.emit('pool', e, csem, dsem)

            @block.sync
            def _(e):
                E.emit('sp', e, csem, dsem)
    return nc


_PROG = {}


def _merge_masks():
    m = np.zeros((128, 1792), np.float32)
    s, t = np.arange(128)[:, None], np.arange(128)[None, :]
    for lv in range(7):
        b = 1 << lv
        q = ((s // (2 * b)) == (t // (2 * b))) & ((s % (2 * b)) < b) & ((t % (2 * b)) >= b)
        m[:, lv * 128:(lv + 1) * 128] = q
        m[:, 896 + lv * 128:896 + (lv + 1) * 128] = q.T
    return m


def _consts():
    c = np.zeros((128, NCON), np.float32)
    idx = np.arange(128)
    c[:, C_ID:C_ID + 128] = np.eye(128, dtype=np.float32)
    s, t = idx[:, None], idx[None, :]
    c[:, C_MU:C_MU + 128] = (s < t)
    c[:, C_MI:C_MI + 128] = (s <= t)
    c[:, C_ML:C_ML + 128] = (t < s)
    same = (s // 8) == (t // 8)
    c[:, C_MUS:C_MUS + 128] = (s < t) & same
    c[:, C_MIS:C_MIS + 128] = (s <= t) & same
    c[:, C_MLS:C_MLS + 128] = (t < s) & same
    c[:, C_SEQ:C_SEQ + 16] = (idx[:, None] // 8) == np.arange(16)[None, :]
    c[:, C_EPS + 0] = 1e-6
    c[:, C_EPS + 1] = 1e-5
    c[:, C_EPS + 2] = 64e-5
    c[:, C_EPS + 3] = 0.0
    c[:, C_EPS + 4] = 1.0
    return c


def _prepare(inp):
    f = lambda a: np.ascontiguousarray(np.asarray(a), dtype=np.float32)
    I = {k: f(v) for k, v in inp.items()}
    pp = np.zeros((128, NPP), np.float32)
    ng = I['norm_g'].reshape(4, 6, 8, 128)
    pp[:, PP_NG:PP_NG + 192] = ng.transpose(3, 0, 1, 2).reshape(128, 192)
    pp[:, PP_MG:PP_MG + 8] = I['mem_norm_g'].reshape(8, 128).T
    cw = I['conv_w'].reshape(4, 31, 4, 128)
    pp[:, PP_CW:PP_CW + 496] = cw.transpose(3, 0, 2, 1).reshape(128, 496)
    pp[:, PP_CB:PP_CB + 16] = I['conv_b'].reshape(4, 4, 128).transpose(2, 0, 1).reshape(128, 16)
    pp[:, PP_LG:PP_LG + 16] = I['conv_ln_g'].reshape(4, 4, 128).transpose(2, 0, 1).reshape(128, 16)
    pp[:, PP_LB:PP_LB + 16] = I['conv_ln_b'].reshape(4, 4, 128).transpose(2, 0, 1).reshape(128, 16)
    pp[:, PP_MUD:PP_MUD + 8] = I['mu_shift'][:, 1536:1792].reshape(4, 2, 128).transpose(2, 0, 1).reshape(128, 8)
    bc = np.zeros((4, 128, NBC), np.float32)
    for j, nm in enumerate(['rwkv_k_k', 'rwkv_k_a', 'rwkv_r_k', 'rwkv_gn_g', 'rwkv_gn_b']):
        bc[:, :, j * 512:(j + 1) * 512] = I[nm].reshape(4, 1, 512)
    bc[:, :, 2560:] = I['mu_shift'].reshape(4, 1, 1792)
    rows = np.concatenate([I['rwkv_w0'], I['rwkv_a0']], axis=1)
    wup = np.concatenate([I['rwkv_w_up'], I['rwkv_a_up']], axis=1)
    gup = I['rwkv_g_up']
    consts = _consts()
    mrg = _merge_masks()
    in_maps = []
    for c in range(NCORE):
        sb = slice(16 * c, 16 * c + 16)
        xs = I['x_sample'][sb].reshape(128, 1024)
        xT = np.ascontiguousarray(np.concatenate([I['x_prompt'][c], xs], axis=0).T)
        m = {
            "xT": xT,
            "memT": np.ascontiguousarray(I['mem_prompt'][c].T),
            "kTc": np.ascontiguousarray(I['cache_mem_k'][:, sb].transpose(0, 1, 3, 4, 2)),
            "vc": np.ascontiguousarray(I['cache_mem_v'][:, sb].reshape(4, 16, 256, 1024)),
            "wkvT": np.ascontiguousarray(I['state_wkv'][:, sb].transpose(0, 1, 2, 4, 3)),
            "convT": np.ascontiguousarray(I['state_conv'][:, sb].transpose(0, 1, 3, 2)),
            "shiftT": np.ascontiguousarray(I['state_shift'][:, sb].transpose(0, 2, 1)),
            "w_in": I['w_in'], "w_out": I['w_out'], "w_q": I['w_q'], "w_k": I['w_k'], "w_v": I['w_v'],
            "w_o": I['w_o'], "w_ffn1": I['w_ffn1'], "w_ffn2": I['w_ffn2'],
            "pp": pp, "bc": bc, "rows": rows, "wup": wup, "gup": gup, "consts": consts, "mrg": mrg,
        }
        in_maps.append(m)
    return in_maps


def kernel(**inp):
    in_maps = _prepare(inp)
    if 'nc' not in _PROG:
        _PROG['nc'] = build_program()
    res = run_bass_kernel_spmd(_PROG['nc'], in_maps, core_ids=list(range(NCORE)))
    R = res.results
    y_prompt = np.zeros((8, 2048, 1024), np.float32)
    y_sample = np.zeros((128, 8, 1024), np.float32)
    nk = np.zeros((4, 8, 256, 4, 256), np.float32)
    nv = np.zeros((4, 8, 256, 4, 256), np.float32)
    wkvp = np.zeros((4, 8, 8, 64, 64), np.float32)
    convp = np.zeros((4, 8, 30, 512), np.float32)
    shiftp = np.zeros((4, 8, 1024), np.float32)
    wkvs = np.zeros((4, 128, 8, 64, 64), np.float32)
    convs = np.zeros((4, 128, 30, 512), np.float32)
    shifts = np.zeros((4, 128, 1024), np.float32)
    for c in range(NCORE):
        r = R[c]
        sb = slice(16 * c, 16 * c + 16)
        yT = np.asarray(r["yT"])
        y_prompt[c] = yT[:, :2048].T
        y_sample[sb] = yT[:, 2048:].T.reshape(16, 8, 1024)
        nk[:, c] = np.asarray(r["nk"]).reshape(4, 256, 4, 256)
        nv[:, c] = np.asarray(r["nv"]).reshape(4, 256, 4, 256)
        wkvp[:, c] = np.asarray(r["wkvp"]).transpose(0, 1, 3, 2)
        convp[:, c] = np.asarray(r["convp"]).transpose(0, 2, 1)
        shiftp[:, c] = np.asarray(r["shiftp"]).transpose(0, 2, 1).reshape(4, 1024)
        wkvs[:, sb] = np.asarray(r["wkvs"]).transpose(0, 1, 2, 4, 3)
        convs[:, sb] = np.asarray(r["convs"]).transpose(0, 1, 3, 2)
        shifts[:, sb] = np.asarray(r["shifts"]).transpose(0, 3, 2, 1).reshape(4, 16, 1024)
    return (y_prompt, y_sample, nk, nv, wkvp, convp, shiftp, wkvs, convs, shifts)
```

```python
import numpy as np
import concourse.bass as bass
import concourse.mybir as mybir
from concourse.bass_utils import run_bass_kernel_spmd

F32 = mybir.dt.float32
BF16 = mybir.dt.bfloat16
AF = mybir.ActivationFunctionType
ALU = mybir.AluOpType
AX = mybir.AxisListType

DEPTH = 4
NCORE = 8
CDEC = 0.6065306597126334

PP_NG, PP_MG, PP_CW, PP_CB, PP_LG, PP_LB, PP_MUD, NPP = 0, 192, 200, 696, 712, 728, 744, 752
C_ID, C_MU, C_MI, C_ML, C_MUS, C_MIS, C_MLS, C_SEQ, C_EPS, NCON = 0, 128, 256, 384, 512, 640, 768, 896, 912, 920
NBC = 2560 + 1792

ENGS = ['pe', 'act', 'dve', 'pool', 'sp']
HOLE_LO, HOLE_HI = 207 * 256, 207 * 256
NDSEM = 90
NSW = 45
DBGV = {}
DBG = dict(layers=4, halves=2, att=True, ffn=True, samp=True, stop=99, mix=True, rwkv=True)


class Hd:
    __slots__ = ('name', 'w', 'r', 'excl')

    def __init__(self, name, excl=False):
        self.name = name
        self.w = None
        self.r = []
        self.excl = excl


class Emitter:
    def __init__(self):
        self.ops = []
        self.handles = []
        self.last_compute = {e: None for e in ENGS}
        self.out_dma = []
        self.group = None
        self.ngroups_phase = 0
        self.groups = []

    def H(self, name, excl=False):
        h = Hd(name, excl)
        self.handles.append(h)
        return h

    def op(self, eng, fn, r=(), w=(), dma=False):
        i = len(self.ops)
        deps = set()
        w = list(w) + [h for h in r if h.excl]
        r = [h for h in r if not h.excl]
        for h in r:
            if h.w is not None:
                deps.add(h.w)
        for h in w:
            if h.w is not None:
                deps.add(h.w)
            deps.update(h.r)
        for h in r:
            h.r.append(i)
        for h in w:
            h.w = i
            h.r = []
        deps.discard(i)
        gid = None
        if dma:
            if self.group is not None:
                gid = self.group
            else:
                gid = len(self.groups)
                self.groups.append([])
                self.ngroups_phase += 1
            if self.group is not None:
                deps = set(d for d in deps if not (self.ops[d]['dma'] and self.ops[d]['gid'] == gid))
            self.groups[gid].append(i)
            self.out_dma.append(i)
        else:
            self.last_compute[eng] = i
        self.ops.append(dict(eng=eng, fn=fn, deps=deps, dma=dma, gid=gid, sig=False))
        return i

    def begin_group(self):
        self.group = len(self.groups)
        self.groups.append([])
        self.ngroups_phase += 1

    def end_group(self):
        self.group = None

    def barrier(self):
        deps = set(v for v in self.last_compute.values() if v is not None) | set(self.out_dma)
        for e in ENGS:
            self.ops.append(dict(eng=e, fn=None, deps=set(deps), dma=False, gid=None, sig=False, bar=True))
        for h in self.handles:
            h.w = None
            h.r = []
        self.out_dma = []
        self.ngroups_phase = 0
        self.ops[-1]['phase_end'] = True

    def finalize(self):
        ops = self.ops
        for j, o in enumerate(ops):
            for d in o['deps']:
                p = ops[d]
                if p['dma']:
                    continue
                if p['eng'] == 'pe' and o['eng'] == 'pe' and not o.get('bar'):
                    continue
                p['sig'] = True
        cnt = {e: 0 for e in ENGS}
        for o in ops:
            if o['sig']:
                cnt[o['eng']] += 1
                o['sigval'] = cnt[o['eng']]
        gsem = {}
        semcnt = [0] * NDSEM
        nxt = {'pool': 0, 'sp': NSW}
        lim = {'pool': NSW, 'sp': NDSEM}
        gfinal = {}
        for j, o in enumerate(ops):
            if o['dma']:
                g = o['gid']
                q = o['eng']
                if g not in gsem:
                    gsem[g] = nxt[q]
                    nxt[q] += 1
                    assert nxt[q] <= lim[q], (q, nxt[q])
                k = gsem[g]
                semcnt[k] += 16
                gfinal[g] = semcnt[k]
            if o.get('phase_end'):
                nxt = {'pool': 0, 'sp': NSW}
        self.gsem, self.gfinal = gsem, gfinal

    def emit(self, eng_name, engobj, csem, dsem):
        ops = self.ops
        waited = {}
        for o in ops:
            if o['eng'] != eng_name:
                continue
            for d in sorted(o['deps']):
                p = ops[d]
                if p['dma']:
                    key = ('d', self.gsem[p['gid']])
                    val = self.gfinal[p['gid']]
                    sem = dsem[self.gsem[p['gid']]]
                else:
                    if not p['sig']:
                        continue
                    key = ('c', p['eng'])
                    val = p['sigval']
                    sem = csem[p['eng']]
                if waited.get(key, 0) < val:
                    engobj.wait_ge(sem, val)
                    waited[key] = val
            if o['fn'] is None:
                continue
            ins = o['fn'](engobj)
            if o['dma']:
                ins.then_inc(dsem[self.gsem[o['gid']]], 16)
            elif o['sig']:
                ins.then_inc(csem[eng_name], 1)


class Arena:
    def __init__(self, ap, words):
        self.ap = ap
        self.words = words
        self.tops = [0, HOLE_HI]
        self.lims = [HOLE_LO, words]
        self.stack = []

    def alloc(self, free_shape, dtype):
        n = 1
        for s in free_shape:
            n *= s
        nbytes = n * (2 if dtype == BF16 else 4)
        w = (nbytes + 31) // 32 * 8
        for r in (0, 1):
            if self.tops[r] + w <= self.lims[r]:
                off = self.tops[r]
                self.tops[r] += w
                self.last = (off, n, dtype == BF16)
                break
        else:
            raise AssertionError(("arena overflow", self.tops, w, self.lims))
        v = self.ap[:, off:off + w]
        if dtype == BF16:
            v = v.bitcast(BF16)[:, 0:n]
        else:
            v = v[:, 0:n]
        if len(free_shape) == 2:
            v = v.rearrange("p (a b) -> p a b", a=free_shape[0])
        elif len(free_shape) == 3:
            v = v.rearrange("p (a b c) -> p a b c", a=free_shape[0], b=free_shape[1])
        elif len(free_shape) == 4:
            v = v.rearrange("p (a b c d) -> p a b c d", a=free_shape[0], b=free_shape[1], c=free_shape[2])
        return v

    def push(self):
        self.stack.append(list(self.tops))

    def pop(self):
        self.tops = self.stack.pop()


def build_program():
    nc = bass.Bass("TRN2", target_bir_lowering=False)
    E = Emitter()

    def din(name, shape):
        return nc.dram_tensor(name, list(shape), F32, kind="ExternalInput").ap()

    def dout(name, shape):
        return nc.dram_tensor(name, list(shape), F32, kind="ExternalOutput").ap()

    d_xT = din("xT", [1024, 2176])
    d_memT = din("memT", [1024, 256])
    d_kT = din("kTc", [4, 16, 4, 256, 256])
    d_v = din("vc", [4, 16, 256, 1024])
    d_wkv = din("wkvT", [4, 16, 8, 64, 64])
    d_conv = din("convT", [4, 16, 512, 30])
    d_shift = din("shiftT", [4, 1024, 16])
    d_win = din("w_in", [4, 1024, 2816])
    d_wout = din("w_out", [4, 1024, 1024])
    d_wq = din("w_q", [4, 1024, 1024])
    d_wk = din("w_k", [4, 1024, 1024])
    d_wv = din("w_v", [4, 1024, 1024])
    d_wo = din("w_o", [4, 1024, 1024])
    d_w1 = din("w_ffn1", [4, 1024, 4096])
    d_w2 = din("w_ffn2", [4, 4096, 1024])
    d_pp = din("pp", [128, NPP])
    d_bc = din("bc", [4, 128, NBC])
    d_rows = din("rows", [4, 1024])
    d_wup = din("wup", [4, 128, 512])
    d_gup = din("gup", [4, 128, 512])
    d_con = din("consts", [128, NCON])
    d_mrg = din("mrg", [128, 1792])

    o_yT = dout("yT", [1024, 2176])
    o_nk = dout("nk", [4, 256, 1024])
    o_nv = dout("nv", [4, 256, 1024])
    o_wkvp = dout("wkvp", [4, 8, 64, 64])
    o_convp = dout("convp", [4, 512, 30])
    o_shiftp = dout("shiftp", [4, 128, 8])
    o_wkvs = dout("wkvs", [4, 16, 8, 64, 64])
    o_convs = dout("convs", [4, 16, 512, 30])
    o_shifts = dout("shifts", [4, 128, 8, 16])

    ARENA_WORDS = 207 * 256
    with (
        nc.sbuf_tensor("arena", [128, ARENA_WORDS], F32) as arena_t,
        nc.psum_tensor("ps0", [128, 512], F32) as ps0, nc.psum_tensor("ps1", [128, 512], F32) as ps1,
        nc.psum_tensor("ps2", [128, 512], F32) as ps2, nc.psum_tensor("ps3", [128, 512], F32) as ps3,
        nc.psum_tensor("ps4", [128, 512], F32) as ps4, nc.psum_tensor("ps5", [128, 512], F32) as ps5,
        nc.psum_tensor("ps6", [128, 512], F32) as ps6, nc.psum_tensor("ps7", [128, 512], F32) as ps7,
    ):
        AR = Arena(arena_t[:, :], ARENA_WORDS)
        banks = [(p[:, :], E.H("bank%d" % i, excl=True)) for i, p in enumerate([ps0, ps1, ps2, ps3, ps4, ps5, ps6, ps7])]
        bank_ctr = [0]

        def bank():
            b = banks[bank_ctr[0] % 8]
            bank_ctr[0] += 1
            return b

        def mm(out, lhsT, rhs, start, stop, r, w):
            E.op('pe', lambda e: e.matmul(out, lhsT, rhs, start=start, stop=stop), r, w)

        def tr(out, in_, ident, r, w):
            E.op('pe', lambda e: e.transpose(out, in_, ident), r, w)

        def act(out, in_, func, r, w, scale=None, bias=None):
            kw = {}
            if scale is not None:
                kw['scale'] = scale
            if bias is not None:
                kw['bias'] = bias
            E.op('act', lambda e: e.activation(out, in_, func, **kw), r, w)

        def tt(eng, out, in0, in1, op, r, w):
            E.op(eng, lambda e: e.tensor_tensor(out, in0, in1, op), r, w)

        def ts(eng, out, in0, s1, op0, r, w, s2=None, op1=None):
            if op1 is None:
                E.op(eng, lambda e: e.tensor_scalar(out, in0, s1, None, op0), r, w)
            else:
                E.op(eng, lambda e: e.tensor_scalar(out, in0, s1, s2, op0, op1), r, w)

        def stt(out, in0, scalar, in1, op0, op1, r, w):
            E.op('dve', lambda e: e.scalar_tensor_tensor(out, in0, scalar, in1, op0, op1), r, w)

        def cp(eng, out, in_, r, w):
            if eng == 'act':
                E.op('act', lambda e: e.copy(out, in_), r, w)
            else:
                E.op(eng, lambda e: e.tensor_copy(out, in_), r, w)

        def red(out, in_, r, w):
            E.op('dve', lambda e: e.tensor_reduce(out, in_, AX.X, ALU.add), r, w)

        def recip(out, in_, r, w):
            E.op('dve', lambda e: e.reciprocal(out, in_), r, w)

        def mset(eng, ap, val, w):
            E.op(eng, lambda e: e.memset(ap, val), (), w)

        def dma(q, out, in_, r, w):
            E.op(q, lambda e: e.dma_start(out=out, in_=in_), r, w, dma=True)

        con_f = AR.alloc([NCON], F32); h_con = E.H("con")
        con_b = AR.alloc([NCON], BF16)
        pp = AR.alloc([NPP], F32); h_pp = E.H("pp")
        xT = AR.alloc([8, 1152], F32)
        hT = AR.alloc([8, 1153], BF16)
        memn = AR.alloc([8, 256], BF16); h_memn = E.H("memn")
        st_S = AR.alloc([DEPTH, 4, 64], F32); h_stS = [E.H("stS%d" % l) for l in range(DEPTH)]
        st_conv = AR.alloc([DEPTH, 4, 30], BF16); h_stconv = [E.H("stc%d" % l) for l in range(DEPTH)]
        st_shift = AR.alloc([DEPTH, 8], BF16); h_stshift = [E.H("sts%d" % l) for l in range(DEPTH)]

        ident_b = con_b[:, C_ID:C_ID + 128]
        ident_f = con_f[:, C_ID:C_ID + 128]
        ones_b = None

        def epsc(i):
            return con_f[:, C_EPS + i:C_EPS + i + 1]

        dma('sp', con_f, d_con, (), [h_con])
        dma('pool', con_b, d_con, (), [h_con])
        dma('sp', pp, d_pp, (), [h_pp])
        onesb = AR.alloc([128], BF16)
        mset('dve', onesb, 1.0, [h_con])
        mset('dve', st_S, 0.0, h_stS)
        mset('dve', st_conv, 0.0, h_stconv)
        mset('dve', st_shift, 0.0, h_stshift)
        h_hinit = E.H('hinit')
        mset('dve', hT[:, :, 0:1], 0.0, [h_hinit])

        def rms_rstd(src_list, n, rsrc, sqbuf, rstd_out, h_tmp, eps_i=0, dim=1024.0):
            pb, hb = bank()
            nk = len(src_list)
            for k, s in enumerate(src_list):
                act(sqbuf[:, k, 0:n], s, AF.Square, rsrc, [h_tmp])
            for k in range(nk):
                mm(pb[:, 0:n], onesb, sqbuf[:, k, 0:n], k == 0, k == nk - 1, [h_tmp, h_con], [hb])
            act(rstd_out, pb[:, 0:n], AF.Sqrt, [hb, h_con], [h_tmp], scale=1.0 / dim, bias=epsc(eps_i))
            recip(rstd_out, rstd_out, [h_tmp], [h_tmp])


        def layer(half, l, groups, h_x, h_h):
            def ngc(i, k):
                o = PP_NG + l * 48 + i * 8 + k
                return pp[:, o:o + 1]

            def norm_to_h(i, sq, rstd, h_tmp):
                for g, (c, n, kind) in enumerate(groups):
                    rms_rstd([xT[:, k, c:c + n] for k in range(8)], n, [h_x[g]], sq, rstd[:, 0:n], h_tmp)
                    for k in range(8):
                        stt(hT[:, k, 1 + c:1 + c + n], xT[:, k, c:c + n], ngc(i, k), rstd[:, 0:n],
                            ALU.mult, ALU.mult, [h_x[g], h_tmp, h_pp], [h_h[g]])

            def post_res(i, g, src_list, rsrc, sq, rstd, tmp, h_tmp, h_tmp2):
                c, n, kind = groups[g]
                rms_rstd(src_list, n, rsrc, sq, rstd[:, 0:n], h_tmp)
                for k in range(8):
                    tt('dve', src_list[k], src_list[k], rstd[:, 0:n], ALU.mult, rsrc + [h_tmp], rsrc)
                for k in range(8):
                    stt(xT[:, k, c:c + n], src_list[k], ngc(i, k), xT[:, k, c:c + n], ALU.mult, ALU.add,
                        rsrc + [h_pp, h_x[g]], [h_x[g]])

            def load_w(q, dst, dsrc, h, ncol_split=None):
                v = dsrc.rearrange("(k p) n -> p k n", p=128)
                E.begin_group()
                for k in range(8):
                    dma(q, dst[:, k, :], v[:, k, :], (), [h])
                E.end_group()

            def attention_sub():
                AR.push()
                wq = AR.alloc([8, 1024], BF16); h_wq = E.H("wq")
                wk = AR.alloc([8, 1024], BF16); h_wk = E.H("wk")
                wv = AR.alloc([8, 1024], BF16); h_wv = E.H("wv")
                wo = AR.alloc([8, 1024], BF16); h_wo = E.H("wo")
                load_w('pool', wq, d_wq[l], h_wq)
                if DBG.get('nw', 4) > 1:
                    load_w('pool', wk, d_wk[l], h_wk)
                    load_w('pool', wv, d_wv[l], h_wv)
                    load_w('pool', wo, d_wo[l], h_wo)
                if DBG['stop'] <= 0:
                    E.barrier(); AR.pop(); return
                if DBG.get('wbar'):
                    E.barrier()
                stg = AR.alloc([2, 512], F32); h_stg = [E.H("stg0"), E.H("stg1")]
                sq = AR.alloc([8, 512], BF16); h_t1 = E.H("t1")
                rstd = AR.alloc([512], F32)
                tmp = AR.alloc([512], F32); h_t2 = E.H("t2")
                msb = AR.alloc([8, 512], F32); h_msb = E.H("msb")
                KT = AR.alloc([8, 256], BF16); h_KT = E.H("KT")
                Vb = AR.alloc([2, 1024], BF16); h_Vb = E.H("Vb")
                qT = sq; h_qT = h_t1
                oT = AR.alloc([8, 512], BF16); h_oT = E.H("oT")
                if DBG.get('stghi'):
                    stg = AR.alloc([2, 512], F32)
                PT = AR.alloc([2, 512], BF16); h_PT = E.H("PT")
                rs = AR.alloc([512], F32); h_rs = E.H("rs")
                if not DBG.get('skip1'):
                    norm_to_h(2, sq, rstd, h_t1)
                if DBG['stop'] <= 1:
                    E.barrier(); AR.pop(); return
                for oc in range(0 if DBG.get('skip2') else 8):
                    pb, hb = bank()
                    for k in range(8):
                        mm(pb[:, 0:256], wk[:, k, oc * 128:(oc + 1) * 128], memn[:, k, :], k == 0, k == 7,
                           [h_wk, h_memn], [hb])
                    cp('act', KT[:, oc, :], pb[:, 0:256], [hb], [h_KT])
                if DBG['stop'] <= 2:
                    E.barrier(); AR.pop(); return
                si = 0
                for which, wsrc, hw, dout_ in ((0, wv, h_wv, o_nv), (1, wk, h_wk, o_nk)):
                    if which == 1 and half == 1:
                        continue
                    if which >= DBG.get('vw', 2):
                        continue
                    for mc in range(DBG.get('vmc', 2)):
                        for hf in range(DBG.get('vhf', 2)):
                            pb, hb = bank()
                            for k in range(8):
                                mm(pb, memn[:, k, mc * 128:(mc + 1) * 128], wsrc[:, k, hf * 512:(hf + 1) * 512],
                                   k == 0, k == 7, [hw, h_memn], [hb])
                            if which == 0 and not DBG.get('novb'):
                                cp('act', Vb[:, mc, hf * 512:(hf + 1) * 512], pb, [hb], [h_Vb])
                            if half == 0:
                                s = (si % 2) if not DBG.get('stg0') else 0
                                si += 1
                                if not DBG.get('nostg'):
                                    cp(DBG.get('stgeng', 'dve'), (msb if DBG.get('stgmsb') else stg)[:, s, :], pb, ([h_Vb] if DBG.get('chain') else [hb]), [h_stg[s]])
                                if not DBG.get('nodma'):
                                    dma('sp', dout_[l, mc * 128:(mc + 1) * 128, hf * 512:(hf + 1) * 512], stg[:, s, :],
                                        [h_stg[s]], ())
                if DBG['stop'] <= 3:
                    E.barrier(); AR.pop(); return
                for g, (c, n, kind) in enumerate(groups):
                    for oc in range(8):
                        pb, hb = bank()
                        for k in range(8):
                            mm(pb[:, 0:n], wq[:, k, oc * 128:(oc + 1) * 128], hT[:, k, 1 + c:1 + c + n], k == 0, k == 7,
                               [h_wq, h_h[g]], [hb])
                        cp('act', qT[:, oc, 0:n], pb[:, 0:n], [hb], [h_qT])
                    if kind == 'p':
                        for hd in range(4):
                            for mc in range(2):
                                pb, hb = bank()
                                for dc in range(2):
                                    mm(pb[:, 0:n], KT[:, 2 * hd + dc, mc * 128:(mc + 1) * 128], qT[:, 2 * hd + dc, 0:n],
                                       dc == 0, dc == 1, [h_KT, h_qT], [hb])
                                act(PT[:, mc, 0:n], pb[:, 0:n], AF.Exp, [hb], [h_PT], scale=1.0 / 16.0)
                            pb, hb = bank()
                            for mc in range(2):
                                mm(pb[:, 0:n], onesb, PT[:, mc, 0:n], mc == 0, mc == 1, [h_PT, h_con], [hb])
                            recip(rs[:, 0:n], pb[:, 0:n], [hb], [h_rs])
                            for dc in range(2):
                                pb, hb = bank()
                                for mc in range(2):
                                    mm(pb[:, 0:n], Vb[:, mc, (2 * hd + dc) * 128:(2 * hd + dc + 1) * 128], PT[:, mc, 0:n],
                                       mc == 0, mc == 1, [h_Vb, h_PT], [hb])
                                tt('dve', oT[:, 2 * hd + dc, 0:n], pb[:, 0:n], rs[:, 0:n], ALU.mult, [hb, h_rs], [h_oT])
                    else:
                        AR.push()
                        KTs = AR.alloc([1, 4, 2, 256], BF16); h_KTs = [E.H("KTs0")]
                        Vs = AR.alloc([1, 2, 1024], BF16); h_Vs = [E.H("Vs0")]
                        PTs = PT.rearrange("p a (b h m t) -> p (a b) h m t", b=8, h=4, m=2); h_PTs = h_PT
                        rss = rs.rearrange("p (b h t) -> p b h t", b=16, h=4); h_rss = h_rs
                        for bh in range(2):
                            pbs, hbs = bank()
                            pbo0, hbo0 = bank()
                            sview = pbs.rearrange("p (b h m t) -> p b h m t", b=8, h=4, m=2)
                            oview = pbo0.rearrange("p (b h d t) -> p b h d t", b=8, h=4, d=2)
                            for bb in range(8):
                                b = bh * 8 + bb
                                s = 0
                                dma('pool', KTs[:, s], d_kT[l, b].rearrange("h (dc p) m -> p h dc m", p=128), (), [h_KTs[s]])
                                dma('pool', Vs[:, s], d_v[l, b].rearrange("(mc p) n -> p mc n", p=128), (), [h_Vs[s]])
                                for hd in range(4):
                                    for mc in range(2):
                                        for dc in range(2):
                                            mm(sview[:, bb, hd, mc, :], KTs[:, s, hd, dc, mc * 128:(mc + 1) * 128],
                                               qT[:, 2 * hd + dc, b * 8:(b + 1) * 8], dc == 0, dc == 1,
                                               [h_KTs[s], h_qT], [hbs])
                                act(PTs[:, b], sview[:, bb], AF.Exp, [hbs], [h_PTs], scale=1.0 / 16.0)
                                for hd in range(4):
                                    for dc in range(2):
                                        for mc in range(2):
                                            mm(oview[:, bb, hd, dc, :], Vs[:, s, mc, (2 * hd + dc) * 128:(2 * hd + dc + 1) * 128],
                                               PTs[:, b, hd, mc, :], mc == 0, mc == 1, [h_Vs[s], h_PTs], [hbo0])
                            pbr, hbr = bank()
                            rview = pbr[:, 0:256].rearrange("p (b h t) -> p b h t", b=8, h=4)
                            for mc in range(2):
                                mm(rview, onesb, PTs[:, bh * 8:(bh + 1) * 8, :, mc, :], mc == 0, mc == 1, [h_PTs, h_con], [hbr])
                            recip(rss[:, bh * 8:(bh + 1) * 8], rview, [hbr], [h_rss])
                            for dc in range(2):
                                ov = oT[:, :, 0:128].rearrange("p (h d) (b t) -> p d b h t", d=2, t=8)[:, dc, bh * 8:(bh + 1) * 8]
                                tt('dve', ov, oview[:, :, :, dc, :], rss[:, bh * 8:(bh + 1) * 8], ALU.mult,
                                   [hbo0, h_rss], [h_oT])
                        AR.pop()
                    for oc in range(8):
                        pb, hb = bank()
                        for k in range(8):
                            mm(pb[:, 0:n], wo[:, k, oc * 128:(oc + 1) * 128], oT[:, k, 0:n], k == 0, k == 7,
                               [h_wo, h_oT], [hb])
                        cp('act', msb[:, oc, 0:n], pb[:, 0:n], [hb], [h_msb])
                    post_res(3, g, [msb[:, k, 0:n] for k in range(8)], [h_msb], sq, rstd, tmp, h_t1, h_t2)
                E.barrier()
                AR.pop()


            def ffn_sub():
                AR.push()
                NT_ = 1024 + (128 if half == 1 else 0)
                facc = AR.alloc([8, NT_], F32); h_f = [E.H("f%d" % g) for g in range(len(groups))]
                w1 = AR.alloc([2, 8, 512], BF16); h_w1 = [E.H("w1a"), E.H("w1b")]
                w2 = AR.alloc([2, 4, 1024], BF16); h_w2 = [E.H("w2a"), E.H("w2b")]
                sq = AR.alloc([8, 512], BF16); h_t1 = E.H("t1f")
                rstd = AR.alloc([512], F32)
                tmp = AR.alloc([512], F32); h_t2 = E.H("t2f")
                rl = AR.alloc([2, 512], BF16); h_rl = [E.H("rl0"), E.H("rl1")]
                hid = AR.alloc([2, 4, 512], BF16); h_hid = [E.H("hid0"), E.H("hid1")]
                norm_to_h(4, sq, rstd, h_t1)
                w1v = d_w1[l].rearrange("(k p) n -> p k n", p=128)
                w2v = d_w2[l].rearrange("(c p) n -> p c n", p=128)
                hi = 0
                for hg in range(8):
                    s = hg % 2
                    E.begin_group()
                    for k in range(8):
                        dma('pool', w1[:, s, k, :], w1v[:, k, hg * 512:(hg + 1) * 512], (), [h_w1[s]])
                    E.end_group()
                    E.begin_group()
                    for cc in range(4):
                        dma('pool', w2[:, s, cc, :], w2v[:, hg * 4 + cc, :], (), [h_w2[s]])
                    E.end_group()
                    for g, (c, n, kind) in enumerate(groups):
                        hs = hi % 2
                        hi += 1
                        for hc in range(4):
                            pb, hb = bank()
                            for k in range(8):
                                mm(pb[:, 0:n], w1[:, s, k, hc * 128:(hc + 1) * 128], hT[:, k, 1 + c:1 + c + n], k == 0, k == 7,
                                   [h_w1[s], h_h[g]], [hb])
                            rs_ = hc % 2
                            act(rl[:, rs_, 0:n], pb[:, 0:n], AF.Relu, [hb], [h_rl[rs_]])
                            tt('dve', hid[:, hs, hc, 0:n], rl[:, rs_, 0:n], rl[:, rs_, 0:n], ALU.mult, [h_rl[rs_]], [h_hid[hs]])
                        for oc in range(8):
                            pb, hb = bank()
                            for hc in range(4):
                                mm(pb[:, 0:n], w2[:, s, hc, oc * 128:(oc + 1) * 128], hid[:, hs, hc, 0:n], hc == 0, hc == 3,
                                   [h_w2[s], h_hid[hs]], [hb])
                            if hg == 0:
                                cp('act', facc[:, oc, c:c + n], pb[:, 0:n], [hb], [h_f[g]])
                            else:
                                tt('dve', facc[:, oc, c:c + n], pb[:, 0:n], facc[:, oc, c:c + n], ALU.add, [hb, h_f[g]], [h_f[g]])
                for g, (c, n, kind) in enumerate(groups):
                    post_res(5, g, [facc[:, k, c:c + n] for k in range(8)], [h_f[g]], sq, rstd, tmp, h_t1, h_t2)
                E.barrier()
                AR.pop()


            def rwkv_sub(mixT, h_mix):
                AR.push()
                winv = d_win[l].rearrange("(k p) n -> p k n", p=128)
                Wr = AR.alloc([8, 1792], BF16); h_Wr = E.H("Wr")
                E.begin_group()
                for k in range(8):
                    dma('pool', Wr[:, k, :], winv[:, k, 1024:2816], (), [h_Wr])
                E.end_group()
                bct = AR.alloc([4096], F32); h_bc = E.H("bct")
                dma('sp', bct, d_bc[l][:, 0:4096], (), [h_bc])
                kk_t, ka_t, rk_t, gg_t, gb_t = [bct[:, i * 512:(i + 1) * 512] for i in range(5)]
                mu_t = [bct[:, 2560 + i * 512:2560 + (i + 1) * 512] for i in range(3)]
                rows_b = AR.alloc([1024], BF16); h_rows = E.H("rowsb")
                E.begin_group()
                dma('pool', rows_b[0:1, 0:512], d_rows[l:l + 1, 0:512], (), [h_rows])
                dma('pool', rows_b[64:65, 0:512], d_rows[l:l + 1, 512:1024], (), [h_rows])
                E.end_group()
                wup_b = AR.alloc([512], BF16); h_wup = E.H("wupb")
                dma('pool', wup_b, d_wup[l], (), [h_wup])
                gup_b = AR.alloc([512], BF16); h_gup = E.H("gupb")
                dma('pool', gup_b, d_gup[l], (), [h_gup])
                S_b = AR.alloc([4, 64], BF16); h_Sb = E.H("Sb")
                cp('act', S_b, st_S[:, l, :, :], [h_stS[l]], [h_Sb])
                FT = {}
                for nm in ("Asb", "dlt", "Zr", "Zk", "sw", "av", "gate", "G", "Ginv", "Gp", "kk", "sq_"):
                    FT[nm] = (AR.alloc([512], F32), E.H("f_" + nm))
                    DBGV[nm] = AR.last
                FT["kkn"] = FT["kk"]
                FT["t1"] = FT["Asb"]
                FT["kmod"] = FT["sw"]
                DBGV["kkn"] = DBGV["kk"]; DBGV["kmod"] = DBGV["sw"]
                BT = {}
                for nm in ("at", "bt", "kt", "rt", "vb", "Xb", "ro"):
                    BT[nm] = (AR.alloc([512], BF16), E.H("b_" + nm))
                    DBGV[nm] = AR.last
                arT = AR.alloc([4, 2, 128], BF16); h_arT = E.H("arT")
                kbT = AR.alloc([4, 2, 128], BF16); h_kbT = E.H("kbT")
                AkaAkr = AR.alloc([8, 256], BF16); h_Aka = E.H("Aka")
                NAbr = AR.alloc([8, 256], BF16); h_NAbr = E.H("NAbr")
                mrg_b = AR.alloc([1792], BF16); h_mrg = E.H("mrg")
                dma('pool', mrg_b, d_mrg, (), [h_mrg])
                Ad = AR.alloc([2, 128], F32); h_Ad = E.H("Ad")
                Dd = AR.alloc([2, 128], F32); h_Dd = E.H("Dd")
                tda = AR.alloc([128], BF16); h_tda = E.H("tda")
                sdg = AR.alloc([128], BF16); h_sdg = E.H("sdg")
                sm8 = AR.alloc([6, 8], F32); h_sm8 = E.H("sm8")
                gcT = AR.alloc([4, 16], F32); h_gc = E.H("gcT")
                tmpS = AR.alloc([4, 64], F32); h_tmpS = E.H("tmpS")
                has_s = (half == 1 and DBG['samp'])
                if has_s:
                    sst = AR.alloc([8, 16], F32); h_sst = E.H("sst")
                    dma('sp', sst, d_shift[l].rearrange("(k p) b -> p k b", p=128), (), [h_sst])
                    hprev_s = AR.alloc([8, 16, 8], BF16); h_hps = E.H("hps")
                    hsv = hT[:, :, 1025:1153].rearrange("p k (b t) -> p k b t", t=8)
                    cp('dve', hprev_s[:, :, :, 0], sst, [h_sst], [h_hps])
                    cp('dve', hprev_s[:, :, :, 1:8], hsv[:, :, :, 0:7], [h_h[2]], [h_hps])
                    S_sb = AR.alloc([16, 4, 64], BF16); h_Ssb = E.H("Ssb")
                    P1sb = AR.alloc([4, 128], BF16); h_P1sb = E.H("P1sb")
                    S_sf = FT["Gp"][0].rearrange("p (a b c) -> p a b c", a=2, b=4); h_Ssf = [FT["Gp"][1], FT["Gp"][1]]
                    vm = FT["G"][0].bitcast(BF16).rearrange("p (a b) -> p a b", a=2); h_vm = [FT["G"][1], FT["G"][1]]
                    um = FT["Ginv"][0].bitcast(BF16).rearrange("p (a b) -> p a b", a=2); h_um = [FT["Ginv"][1], FT["Ginv"][1]]
                    wkv_in = d_wkv[l].rearrange("b (p4 h2) j i -> h2 j b p4 i", h2=2)
                    wkv_out = o_wkvs[l].rearrange("b (p4 h2) j i -> h2 j b p4 i", h2=2)
                    E.begin_group()
                    for h2 in range(2):
                        for p4 in range(4):
                            dma('pool', S_sb[h2 * 64:(h2 + 1) * 64, :, p4, :], wkv_in[h2][:, :, p4, :], (), [h_Ssb])
                    E.end_group()

                def F(nm):
                    return FT[nm]

                def rw_tile(tc, kind, g):
                    smp = (kind == 's')
                    nseq = 16 if smp else 1
                    nap = 3 if smp else 7
                    MuMi = con_b[:, (C_MUS if smp else C_MU):(C_MUS if smp else C_MU) + 256]
                    MLb = con_b[:, (C_MLS if smp else C_ML):(C_MLS if smp else C_ML) + 128]
                    Tri = con_f[:, (C_MIS if smp else C_MI):(C_MIS if smp else C_MI) + 128]
                    seqf = con_f[:, C_SEQ:C_SEQ + 16] if smp else con_f[:, C_EPS + 4:C_EPS + 5]
                    hg = [h_h[g]] + ([h_h[g - 1]] if g > 0 else []) + ([h_hps] if smp else [])

                    def hc(k):
                        return hT[:, k, 1 + tc:1 + tc + 128]

                    def hp(k):
                        return hprev_s[:, k, :, :] if smp else hT[:, k, tc:tc + 128]
                    (Asb, hA), (dlt, hD) = F("Asb"), F("dlt")
                    Z = {}
                    for qi, qn in enumerate(("Zr", "Zk", "vb")):
                        bA, hbA = bank()
                        for k in range(8):
                            mm(bA, hc(k), Wr[:, k, qi * 512:(qi + 1) * 512], k == 0, k == 7, hg + [h_Wr], [hbA])
                        bB, hbB = bank()
                        for k in range(8):
                            mm(bB, hp(k), Wr[:, k, qi * 512:(qi + 1) * 512], k == 0, k == 7, hg + [h_Wr], [hbB])
                        cp('act', Asb, bA, [hbA], [hA])
                        tt('dve', dlt, bB, Asb, ALU.subtract, [hbB, hA], [hD])
                        tt('dve', dlt, dlt, mu_t[qi], ALU.mult, [hD, h_bc], [hD])
                        dst, hdst = (F(qn) if qi < 2 else BT["vb"])
                        tt('dve', dst, dlt, Asb, ALU.add, [hD, hA], [hdst])
                    (Zr, hZr), (Zk, hZk) = F("Zr"), F("Zk")
                    vb, hvb = BT["vb"]
                    bD, hbD = bank()
                    for ci, c0 in enumerate((1536, 1664)):
                        for k in range(8):
                            mm(bD[:, ci * 256:ci * 256 + 128], Wr[:, k, c0:c0 + 128], hc(k), k == 0, k == 7, hg + [h_Wr], [hbD])
                        for k in range(8):
                            mm(bD[:, ci * 256 + 128:ci * 256 + 256], Wr[:, k, c0:c0 + 128], hp(k), k == 0, k == 7, hg + [h_Wr], [hbD])
                    for ci in range(2):
                        mucol = pp[:, PP_MUD + l * 2 + ci:PP_MUD + l * 2 + ci + 1]
                        cp('act', Ad[:, ci, :], bD[:, ci * 256:ci * 256 + 128], [hbD], [h_Ad])
                        tt('dve', Dd[:, ci, :], bD[:, ci * 256 + 128:ci * 256 + 256], Ad[:, ci, :], ALU.subtract, [hbD, h_Ad], [h_Dd])
                        stt(Dd[:, ci, :], Dd[:, ci, :], mucol, Ad[:, ci, :], ALU.mult, ALU.add, [h_Dd, h_Ad, h_pp], [h_Dd])
                    act(tda[0:64, :], Dd[0:64, 0, :], AF.Tanh, [h_Dd], [h_tda])
                    cp('act', tda[64:128, :], Dd[64:128, 0, :], [h_Dd], [h_tda])
                    act(sdg, Dd[:, 1, :], AF.Sigmoid, [h_Dd], [h_sdg])
                    (sw, hsw), (av, hav), (gate, hgate) = F("sw"), F("av"), F("gate")
                    bW, hbW = bank()
                    mm(bW, onesb[0:1, 0:128], rows_b[0:1, 0:512], True, False, [h_con, h_rows], [hbW])
                    mm(bW, tda[0:64, :], wup_b[0:64, :], False, True, [h_tda, h_wup], [hbW])
                    act(sw, bW, AF.Sigmoid, [hbW], [hsw])
                    bAa, hbAa = bank()
                    mm(bAa, onesb[64:65, 0:128], rows_b[64:65, 0:512], True, False, [h_con, h_rows], [hbAa])
                    mm(bAa, tda[64:128, :], wup_b[64:128, :], False, True, [h_tda, h_wup], [hbAa])
                    act(av, bAa, AF.Sigmoid, [hbAa], [hav])
                    bG, hbG = bank()
                    mm(bG, sdg, gup_b, True, True, [h_sdg, h_gup], [hbG])
                    cp('act', gate, bG, [hbG], [hgate])
                    (G, hG), (Ginv, hGi), (Gp, hGp) = F("G"), F("Ginv"), F("Gp")
                    bC, hbC = bank()
                    mm(bC, Tri, sw, True, True, [h_con, hsw], [hbC])
                    act(G, bC, AF.Exp, [hbC], [hG], scale=-CDEC)
                    act(Ginv, bC, AF.Exp, [hbC], [hGi], scale=CDEC)
                    tt('dve', Gp, bC, sw, ALU.subtract, [hbC, hsw], [hGp])
                    act(Gp, Gp, AF.Exp, [hGp], [hGp], scale=-CDEC)
                    bGC, hbGC = bank()
                    for p4 in range(4):
                        mm(bGC[:, p4 * 16:p4 * 16 + nseq], sw[:, p4 * 128:(p4 + 1) * 128], seqf, True, True, [hsw, h_con], [hbGC])
                    act(gcT[:, :, 0:nseq], bGC[:, 0:64].rearrange("p (a b) -> p a b", a=4)[:, :, 0:nseq], AF.Exp, [hbGC], [h_gc], scale=-CDEC)
                    (kk, hkk), (sq_, hsq), (kkn, hkkn), (t1, ht1), (kmod, hkm) = F("kk"), F("sq_"), F("kkn"), F("t1"), F("kmod")
                    tt('dve', kk, Zk, kk_t, ALU.mult, [hZk, h_bc], [hkk])
                    tt('dve', sq_, kk, kk, ALU.mult, [hkk], [hsq])
                    red(sm8[:, 0, :], sq_.rearrange("p (h d) -> p h d", h=8), [hsq], [h_sm8])
                    act(sm8[:, 0, :], sm8[:, 0, :], AF.Sqrt, [h_sm8], [h_sm8])
                    ts('dve', sm8[:, 0, :], sm8[:, 0, :], 1e-12, ALU.max, [h_sm8], [h_sm8])
                    recip(sm8[:, 0, :], sm8[:, 0, :], [h_sm8], [h_sm8])
                    tt('dve', kkn.rearrange("p (h d) -> p h d", h=8), kk.rearrange("p (h d) -> p h d", h=8),
                       sm8[:, 0, :].unsqueeze(2).to_broadcast([128, 8, 64]), ALU.mult, [hkk, h_sm8], [hkkn])
                    stt(t1, av, -1.0, ka_t, ALU.add, ALU.mult, [hav, h_bc], [ht1])
                    stt(kmod, t1, 1.0, Zk, ALU.add, ALU.mult, [ht1, hZk], [hkm])
                    (at, hat), (bt, hbt), (kt, hkt), (rt, hrt) = BT["at"], BT["bt"], BT["kt"], BT["rt"]
                    stt(at, kkn, -1.0, Gp, ALU.mult, ALU.mult, [hkkn, hGp], [hat])
                    tt('dve', t1, kkn, av, ALU.mult, [hkkn, hav], [ht1])
                    tt('dve', bt, t1, Ginv, ALU.mult, [ht1, hGi], [hbt])
                    tt('dve', kt, kmod, Ginv, ALU.mult, [hkm, hGi], [hkt])
                    tt('dve', rt, Zr, G, ALU.mult, [hZr, hG], [hrt])
                    tt('dve', sq_, Zr, kmod, ALU.mult, [hZr, hkm], [hsq])
                    tt('dve', sq_, sq_, rk_t, ALU.mult, [hsq, h_bc], [hsq])
                    red(sm8[:, 1, :], sq_.rearrange("p (h d) -> p h d", h=8), [hsq], [h_sm8])
                    for (srcs, dstT, hdT) in (((at, hat, rt, hrt), arT, h_arT), ((kt, hkt, bt, hbt), kbT, h_kbT)):
                        bT_, hbT_ = bank()
                        tv = bT_.bitcast(BF16)[:, 0:1024].rearrange("p (a b c) -> p a b c", a=4, b=2)
                        for p4 in range(4):
                            tr(tv[:, p4, 0, :], srcs[0][:, p4 * 128:(p4 + 1) * 128], ident_b, [srcs[1], h_con], [hbT_])
                            tr(tv[:, p4, 1, :], srcs[2][:, p4 * 128:(p4 + 1) * 128], ident_b, [srcs[3], h_con], [hbT_])
                        cp('act', dstT, tv, [hbT_], [hdT])
                    mk = MuMi.unsqueeze(1).to_broadcast([128, 2, 256])
                    for hb4 in range(2):
                        for par in range(2):
                            bA_, hbA_ = bank()
                            bB_, hbB_ = bank()
                            base = 64 * par
                            for hh in range(2):
                                h = hb4 * 4 + 2 * hh + par
                                p4 = h // 2
                                ar2 = arT[base:base + 64, p4, :, :]
                                mm(bA_[:, hh * 256:(hh + 1) * 256], kbT[base:base + 64, p4, 0, :], ar2, True, True, [h_kbT, h_arT], [hbA_])
                                mm(bB_[:, hh * 256:(hh + 1) * 256], kbT[base:base + 64, p4, 1, :], ar2, True, True, [h_kbT, h_arT], [hbB_])
                            h0 = hb4 * 4 + par
                            tt('dve', AkaAkr[:, h0:h0 + 3:2, :], bA_.rearrange("p (a b) -> p a b", a=2), mk, ALU.mult, [hbA_, h_con], [h_Aka])
                            tt('dve', NAbr[:, h0:h0 + 3:2, :], bB_.rearrange("p (a b) -> p a b", a=2), mk, ALU.mult, [hbB_, h_con], [h_NAbr])
                    (Xb, hXb) = BT["Xb"]
                    X, hX = F("Asb")

                    def s0_terms(bX_, hbX_, which, post):
                        if not smp:
                            for h in range(8):
                                p4, base = h // 2, 64 * (h % 2)
                                mm(bX_[:, h * 64:(h + 1) * 64], arT[base:base + 64, p4, which, :], S_b[base:base + 64, p4, :],
                                   True, False, [h_arT, h_Sb], [hbX_])
                                post(h, True)
                        else:
                            for h2 in range(2):
                                base = 64 * h2
                                bP, hbP = bank()
                                for p4 in range(4):
                                    for s in range(16):
                                        mm(bP[base:base + 64, p4 * 128 + s * 8:p4 * 128 + s * 8 + 8], S_sb[base:base + 64, s, p4, :],
                                           arT[base:base + 64, p4, which, s * 8:(s + 1) * 8], True, True, [h_Ssb, h_arT], [hbP])
                                cp('act', P1sb[base:base + 64, :, :], bP[base:base + 64, :].rearrange("p (a b) -> p a b", a=4), [hbP], [h_P1sb])
                            for p4 in range(4):
                                mm(bX_[:, p4 * 128:(p4 + 1) * 128], P1sb[:, p4, :], ident_b, True, False, [h_P1sb, h_con], [hbX_])
                                post(2 * p4, False)
                                post(2 * p4 + 1, True)
                    bX, hbX = bank()

                    def post_x(h, last):
                        mm(bX[:, h * 64:(h + 1) * 64], AkaAkr[:, h, 0:128], vb[:, h * 64:(h + 1) * 64], False, last, [h_Aka, hvb], [hbX])
                    s0_terms(bX, hbX, 0, post_x)
                    cp('act', X, bX, [hbX], [hX])
                    cp('dve', Xb, X, [hX], [hXb])
                    def b8(nm):
                        return FT[nm][0].bitcast(BF16).rearrange("p (a b) -> p a b", a=8), FT[nm][1]
                    (Tm, hTm), (TmT, hTmT), (Noff, hNo), (YT, hYT) = b8("kk"), b8("sq_"), b8("Zr"), b8("av")
                    idb = ident_b.unsqueeze(1).to_broadcast([128, 8, 128])
                    cp('act', Tm, idb, [h_con], [hTm])
                    cp('dve', TmT, idb, [h_con], [hTmT])
                    for lv in (range(3) if smp else range(7)):
                        moT = mrg_b[:, 896 + lv * 128:896 + (lv + 1) * 128].unsqueeze(1).to_broadcast([128, 4, 128])
                        for q in range(2):
                            bY_, hbY_ = bank()
                            for hh in range(4):
                                h = q * 4 + hh
                                mm(bY_[:, hh * 128:(hh + 1) * 128], NAbr[:, h, 0:128], TmT[:, h, :], True, True, [h_NAbr, hTmT], [hbY_])
                            tt('dve', YT[:, q * 4:q * 4 + 4, :], bY_.rearrange("p (a b) -> p a b", a=4), moT, ALU.mult, [hbY_, h_mrg], [hYT])
                        zb = []
                        for q in range(2):
                            bZ, hbZ = bank()
                            bZT, hbZT = bank()
                            for hh in range(4):
                                h = q * 4 + hh
                                mm(bZ[:, hh * 128:(hh + 1) * 128], YT[:, h, :], Tm[:, h, :], True, True, [hYT, hTm], [hbZ])
                                mm(bZT[:, hh * 128:(hh + 1) * 128], Tm[:, h, :], YT[:, h, :], True, True, [hYT, hTm], [hbZT])
                            zb.append((bZ, hbZ, bZT, hbZT))
                        for q in range(2):
                            bZ, hbZ, bZT, hbZT = zb[q]
                            tt('dve', Tm[:, q * 4:q * 4 + 4, :], bZ.rearrange("p (a b) -> p a b", a=4), Tm[:, q * 4:q * 4 + 4, :], ALU.add, [hbZ, hTm], [hTm])
                            tt('dve', TmT[:, q * 4:q * 4 + 4, :], bZT.rearrange("p (a b) -> p a b", a=4), TmT[:, q * 4:q * 4 + 4, :], ALU.add, [hbZT, hTmT], [hTmT])
                    bY, hbY = bank()
                    for h in range(8):
                        mm(bY[:, h * 64:(h + 1) * 64], Tm[:, h, :], Xb[:, h * 64:(h + 1) * 64], True, True, [hTm, hXb], [hbY])
                    cp('act', X, bY, [hbY], [hX])
                    cp('dve', Xb, X, [hX], [hXb])
                    bO, hbO = bank()

                    def post_o(h, last):
                        mm(bO[:, h * 64:(h + 1) * 64], AkaAkr[:, h, 128:256], vb[:, h * 64:(h + 1) * 64], False, False, [h_Aka, hvb], [hbO])
                        mm(bO[:, h * 64:(h + 1) * 64], NAbr[:, h, 128:256], Xb[:, h * 64:(h + 1) * 64], False, last, [h_NAbr, hXb], [hbO])
                    s0_terms(bO, hbO, 1, post_o)
                    (osb, hos) = F("dlt")
                    cp('act', osb, bO, [hbO], [hos])
                    if not smp:
                        bS, hbS = bank()
                        for h in range(8):
                            p4, base = h // 2, 64 * (h % 2)
                            mm(bS[base:base + 64, p4 * 64:(p4 + 1) * 64], kt[:, h * 64:(h + 1) * 64], vb[:, h * 64:(h + 1) * 64], True, False, [hkt, hvb], [hbS])
                            mm(bS[base:base + 64, p4 * 64:(p4 + 1) * 64], bt[:, h * 64:(h + 1) * 64], Xb[:, h * 64:(h + 1) * 64], False, True, [hbt, hXb], [hbS])
                        tt('dve', tmpS, bS[:, 0:256].rearrange("p (a b) -> p a b", a=4), st_S[:, l, :, :], ALU.add, [hbS, h_stS[l]], [h_tmpS])
                        tt('dve', st_S[:, l, :, :], tmpS, gcT[:, :, 0:1].to_broadcast([128, 4, 64]), ALU.mult, [h_tmpS, h_gc], [h_stS[l]])
                        cp('act', S_b, st_S[:, l, :, :], [h_stS[l]], [h_Sb])
                    else:
                        for s in range(16):
                            ss_ = s % 2
                            E.begin_group()
                            for h2 in range(2):
                                dma('sp', S_sf[h2 * 64:(h2 + 1) * 64, ss_, :, :], wkv_in[h2][:, s, :, :], (), [h_Ssf[ss_]])
                            E.end_group()
                            ts('dve', vm[:, ss_, :], vb, seqf[:, s:s + 1], ALU.mult, [hvb, h_con], [h_vm[ss_]])
                            ts('dve', um[:, ss_, :], Xb, seqf[:, s:s + 1], ALU.mult, [hXb, h_con], [h_um[ss_]])
                            bS, hbS = bank()
                            for h in range(8):
                                p4, base = h // 2, 64 * (h % 2)
                                mm(bS[base:base + 64, p4 * 64:(p4 + 1) * 64], kt[:, h * 64:(h + 1) * 64], vm[:, ss_, h * 64:(h + 1) * 64], True, False, [hkt, h_vm[ss_]], [hbS])
                                mm(bS[base:base + 64, p4 * 64:(p4 + 1) * 64], bt[:, h * 64:(h + 1) * 64], um[:, ss_, h * 64:(h + 1) * 64], False, True, [hbt, h_um[ss_]], [hbS])
                            tt('dve', S_sf[:, ss_, :, :], bS[:, 0:256].rearrange("p (a b) -> p a b", a=4), S_sf[:, ss_, :, :], ALU.add, [hbS, h_Ssf[ss_]], [h_Ssf[ss_]])
                            tt('dve', S_sf[:, ss_, :, :], S_sf[:, ss_, :, :], gcT[:, :, s:s + 1].to_broadcast([128, 4, 64]), ALU.mult, [h_Ssf[ss_], h_gc], [h_Ssf[ss_]])
                            E.begin_group()
                            for h2 in range(2):
                                dma('sp', wkv_out[h2][:, s, :, :], S_sf[h2 * 64:(h2 + 1) * 64, ss_, :, :], [h_Ssf[ss_]], ())
                            E.end_group()
                    (osq, hoq), (on, hon), (bon, hbon) = F("G"), F("Ginv"), F("Gp")
                    o3 = osb.rearrange("p (h d) -> p h d", h=8)
                    red(sm8[:, 2, :], o3, [hos], [h_sm8])
                    tt('pool', osq, osb, osb, ALU.mult, [hos], [hoq])
                    red(sm8[:, 3, :], osq.rearrange("p (h d) -> p h d", h=8), [hoq], [h_sm8])
                    ts('dve', sm8[:, 2, :], sm8[:, 2, :], 1.0 / 64.0, ALU.mult, [h_sm8], [h_sm8])
                    tt('dve', sm8[:, 4, :], sm8[:, 2, :], sm8[:, 2, :], ALU.mult, [h_sm8], [h_sm8])
                    stt(sm8[:, 3, :], sm8[:, 3, :], 1.0 / 64.0, sm8[:, 4, :], ALU.mult, ALU.subtract, [h_sm8], [h_sm8])
                    act(sm8[:, 3, :], sm8[:, 3, :], AF.Sqrt, [h_sm8, h_con], [h_sm8], scale=1.0, bias=epsc(2))
                    recip(sm8[:, 3, :], sm8[:, 3, :], [h_sm8], [h_sm8])
                    on3 = on.rearrange("p (h d) -> p h d", h=8)
                    tt('dve', on3, o3, sm8[:, 2, :].unsqueeze(2).to_broadcast([128, 8, 64]), ALU.subtract, [hos, h_sm8], [hon])
                    tt('dve', on3, on3, sm8[:, 3, :].unsqueeze(2).to_broadcast([128, 8, 64]), ALU.mult, [hon, h_sm8], [hon])
                    tt('dve', on, on, gg_t, ALU.mult, [hon, h_bc], [hon])
                    tt('dve', on, on, gb_t, ALU.add, [hon, h_bc], [hon])
                    tt('dve', bon.rearrange("p (h d) -> p h d", h=8), vb.rearrange("p (h d) -> p h d", h=8),
                       sm8[:, 1, :].unsqueeze(2).to_broadcast([128, 8, 64]), ALU.mult, [hvb, h_sm8], [hbon])
                    tt('dve', on, on, bon, ALU.add, [hon, hbon], [hon])
                    ro, hro = BT["ro"]
                    tt('dve', ro, on, gate, ALU.mult, [hon, hgate], [hro])
                    bT_, hbT_ = bank()
                    tv = bT_.bitcast(BF16)[:, 0:512].rearrange("p (a b) -> p a b", a=4)
                    for p4 in range(4):
                        tr(tv[:, p4, :], ro[:, p4 * 128:(p4 + 1) * 128], ident_b, [hro, h_con], [hbT_])
                    cp('act', mixT[:, 4:8, tc:tc + 128], tv, [hbT_], [h_mix[g]])

                ntl = 0
                if DBG.get('rwtiles', 99) < 99 or DBG.get('rwonly') is not None:
                    for g, (c, n, kind) in enumerate(groups):
                        mset('pool', mixT[:, 4:8, c:c + n], 0.0, [h_mix[g]])
                for g, (c, n, kind) in enumerate(groups):
                    for ti in range(n // 128):
                        if ntl < DBG.get('rwtiles', 99) and (DBG.get('rwonly') is None or half == 0 or ntl == DBG['rwonly']):
                            rw_tile(c + ti * 128, kind, g)
                        ntl += 1
                if half == 1:
                    for h2 in range(2):
                        dma('sp', o_wkvp[l].rearrange("(p4 h2) j i -> h2 j p4 i", h2=2)[h2], st_S[h2 * 64:(h2 + 1) * 64, l, :, :],
                            [h_stS[l]], ())
                E.barrier()
                AR.pop()

            def mixer_sub():
                AR.push()
                NT_ = 1024 + (128 if (half == 1 and DBG['samp']) else 0)
                winv = d_win[l].rearrange("(k p) n -> p k n", p=128)
                mixT = AR.alloc([8, NT_], BF16); h_mix = [E.H("mix%d" % g) for g in range(len(groups))]
                AR.push()
                sq = AR.alloc([8, 512], BF16); h_t1 = E.H("t1m")
                rstd = AR.alloc([512], F32)
                hl = AR.alloc([8], F32); h_hl = E.H("hl")
                hs_ = AR.alloc([8, 16], F32); h_hs = E.H("hs")
                if half == 1:
                    cp('dve', hT[:, :, 0], st_shift[:, l, :], [h_stshift[l]], [h_h[0]])
                for g, (c, n, kind) in enumerate(groups):
                    rms_rstd([xT[:, k, c:c + n] for k in range(8)], n, [h_x[g]], sq, rstd[:, 0:n], h_t1)
                    for k in range(8):
                        stt(hT[:, k, 1 + c:1 + c + n], xT[:, k, c:c + n], ngc(0, k), rstd[:, 0:n],
                            ALU.mult, ALU.mult, [h_x[g], h_t1, h_pp], [h_h[g]])
                    if kind == 'p' and g == 1:
                        if half == 0:
                            cp('pool', st_shift[:, l, :], hT[:, :, 1024], [h_h[g]], [h_stshift[l]])
                        else:
                            for k in range(8):
                                stt(hl[:, k:k + 1], xT[:, k, 1023:1024], ngc(0, k), rstd[:, 511:512], ALU.mult, ALU.mult,
                                    [h_x[g], h_t1, h_pp], [h_hl])
                            dma('sp', o_shiftp[l], hl, [h_hl], ())
                    if kind == 's':
                        xs7 = xT[:, :, 1024:1152].rearrange("p k (b t) -> p k b t", t=8)
                        r7 = rstd[:, 0:128].rearrange("p (b t) -> p b t", t=8)[:, :, 7]
                        for k in range(8):
                            stt(hs_[:, k, :], xs7[:, k, :, 7], ngc(0, k), r7, ALU.mult, ALU.mult,
                                [h_x[g], h_t1, h_pp], [h_hs])
                        dma('sp', o_shifts[l], hs_, [h_hs], ())
                E.barrier()
                AR.pop()
                AR.push()
                Wc = AR.alloc([8, 1024], BF16); h_Wc = E.H("Wc")
                E.begin_group()
                for k in range(8):
                    dma('pool', Wc[:, k, :], winv[:, k, 0:1024], (), [h_Wc])
                E.end_group()
                uf = AR.alloc([4, 512], F32); h_uf = E.H("uf")
                sg = AR.alloc([512], F32); h_sg = E.H("sg")
                uT = AR.alloc([4, 542], BF16); h_uT = E.H("uT")
                uTs = AR.alloc([4, 16, 38], BF16); h_uTs = E.H("uTs")
                Dg = AR.alloc([31, 128], BF16); h_Dg = E.H("Dg")
                ysb = AR.alloc([4, 512], F32); h_ysb = E.H("ysb")
                ybf = AR.alloc([4, 512], BF16); h_ybf = E.H("ybf")
                ysq = AR.alloc([4, 512], BF16); h_ysq = E.H("ysq")
                mean = AR.alloc([512], F32); h_mean = E.H("mean")
                m2 = AR.alloc([512], F32); h_m2 = E.H("m2")
                rs2 = AR.alloc([512], F32); h_rs2 = E.H("rs2")
                tl = AR.alloc([512], F32); h_tl = E.H("tl")
                for g, (c, n, kind) in enumerate(groups):
                    if kind == 'p':
                        if g == 0:
                            cp('pool', uT[:, :, 0:30], st_conv[:, l, :, :], [h_stconv[l]], [h_uT])
                        else:
                            cp('pool', uT[:, :, 0:30], uT[:, :, 512:542], [h_uT], [h_uT])
                    else:
                        E.begin_group()
                        for c4 in range(4):
                            dma('pool', uTs[:, c4, :, 0:30],
                                d_conv[l][:, c4 * 128:(c4 + 1) * 128, :].rearrange("b p t -> p b t"), (), [h_uTs])
                        E.end_group()
                    for c4 in range(4):
                        pbv, hbv = bank()
                        for k in range(8):
                            mm(pbv[:, 0:n], Wc[:, k, c4 * 128:(c4 + 1) * 128], hT[:, k, 1 + c:1 + c + n], k == 0, k == 7,
                               [h_Wc, h_h[g]], [hbv])
                        pbg, hbg = bank()
                        for k in range(8):
                            mm(pbg[:, 0:n], Wc[:, k, 512 + c4 * 128:512 + (c4 + 1) * 128], hT[:, k, 1 + c:1 + c + n],
                               k == 0, k == 7, [h_Wc, h_h[g]], [hbg])
                        act(sg[:, 0:n], pbg[:, 0:n], AF.Sigmoid, [hbg], [h_sg])
                        tt('dve', uf[:, c4, 0:n], pbv[:, 0:n], sg[:, 0:n], ALU.mult, [hbv, h_sg], [h_uf])
                        if kind == 'p':
                            cp('act', uT[:, c4, 30:30 + n], uf[:, c4, 0:n], [h_uf], [h_uT])
                        else:
                            cp('act', uTs[:, c4, :, 30:38], uf[:, c4, 0:128].rearrange("p (b t) -> p b t", t=8),
                               [h_uf], [h_uTs])
                    if kind == 'p' and g == 1:
                        if half == 0:
                            cp('pool', st_conv[:, l, :, :], uT[:, :, 512:542], [h_uT], [h_stconv[l]])
                        else:
                            dma('sp', o_convp[l].rearrange("(c p) t -> p c t", p=128), uf[:, :, 482:512], [h_uf], ())
                    if kind == 's':
                        dma('sp', o_convs[l][:, :, 0:22], d_conv[l][:, :, 8:30], (), ())
                        for c4 in range(4):
                            dma('sp', o_convs[l][:, c4 * 128:(c4 + 1) * 128, 22:30].rearrange("b p t -> p b t"),
                                uf[:, c4, 0:128].rearrange("p (b t) -> p b t", t=8), [h_uf], ())
                    for c4 in range(4):
                        o_cw = PP_CW + (l * 4 + c4) * 31
                        tt('pool', Dg, ident_b.unsqueeze(1).to_broadcast([128, 31, 128]),
                           pp[:, o_cw:o_cw + 31].unsqueeze(2).to_broadcast([128, 31, 128]), ALU.mult,
                           [h_con, h_pp], [h_Dg])
                        pby, hby = bank()
                        for j in range(31):
                            rhs = uT[:, c4, j:j + n] if kind == 'p' else uTs[:, c4, :, j:j + 8]
                            mm(pby[:, 0:n], Dg[:, j, :], rhs, j == 0, j == 30, [h_Dg, h_uT, h_uTs], [hby])
                        cbc = pp[:, PP_CB + l * 4 + c4:PP_CB + l * 4 + c4 + 1]
                        act(ysb[:, c4, 0:n], pby[:, 0:n], AF.Identity, [hby, h_pp], [h_ysb], bias=cbc, scale=1.0)
                        act(ybf[:, c4, 0:n], pby[:, 0:n], AF.Identity, [hby, h_pp], [h_ybf], bias=cbc, scale=1.0)
                        act(ysq[:, c4, 0:n], pby[:, 0:n], AF.Square, [hby, h_pp], [h_ysq], bias=cbc, scale=1.0)
                    pb1, hb1 = bank()
                    for c4 in range(4):
                        mm(pb1[:, 0:n], onesb, ybf[:, c4, 0:n], c4 == 0, c4 == 3, [h_ybf, h_con], [hb1])
                    pb2, hb2 = bank()
                    for c4 in range(4):
                        mm(pb2[:, 0:n], onesb, ysq[:, c4, 0:n], c4 == 0, c4 == 3, [h_ysq, h_con], [hb2])
                    act(mean[:, 0:n], pb1[:, 0:n], AF.Identity, [hb1], [h_mean], scale=1.0 / 512.0, bias=epsc(3))
                    tt('dve', m2[:, 0:n], mean[:, 0:n], mean[:, 0:n], ALU.mult, [h_mean], [h_m2])
                    stt(m2[:, 0:n], pb2[:, 0:n], 1.0 / 512.0, m2[:, 0:n], ALU.mult, ALU.subtract, [hb2, h_m2], [h_m2])
                    act(rs2[:, 0:n], m2[:, 0:n], AF.Sqrt, [h_m2, h_con], [h_rs2], scale=1.0, bias=epsc(1))
                    recip(rs2[:, 0:n], rs2[:, 0:n], [h_rs2], [h_rs2])
                    for c4 in range(4):
                        tt('dve', tl[:, 0:n], ysb[:, c4, 0:n], mean[:, 0:n], ALU.subtract, [h_ysb, h_mean], [h_tl])
                        tt('dve', tl[:, 0:n], tl[:, 0:n], rs2[:, 0:n], ALU.mult, [h_tl, h_rs2], [h_tl])
                        ol = l * 4 + c4
                        ts('dve', tl[:, 0:n], tl[:, 0:n], pp[:, PP_LG + ol:PP_LG + ol + 1], ALU.mult, [h_tl, h_pp], [h_tl],
                           s2=pp[:, PP_LB + ol:PP_LB + ol + 1], op1=ALU.add)
                        act(mixT[:, c4, c:c + n], tl[:, 0:n], AF.Silu, [h_tl], [h_mix[g]])
                E.barrier()
                AR.pop()
                if DBG.get('rwkv'):
                    rwkv_sub(mixT, h_mix)
                else:
                    for g, (c, n, kind) in enumerate(groups):
                        mset('pool', mixT[:, 4:8, c:c + n], 0.0, [h_mix[g]])
                wout = AR.alloc([8, 1024], BF16); h_wout = E.H("wout")
                load_w('pool', wout, d_wout[l], h_wout)
                sq = AR.alloc([8, 512], BF16); h_t1 = E.H("t1m3")
                rstd = AR.alloc([512], F32)
                tmp = AR.alloc([512], F32); h_t2 = E.H("t2m")
                msb = AR.alloc([8, 512], F32); h_msb = E.H("msbm")
                for g, (c, n, kind) in enumerate(groups):
                    for oc in range(8):
                        pb, hb = bank()
                        for k in range(8):
                            mm(pb[:, 0:n], wout[:, k, oc * 128:(oc + 1) * 128], mixT[:, k, c:c + n], k == 0, k == 7,
                               [h_wout, h_mix[g]], [hb])
                        cp('act', msb[:, oc, 0:n], pb[:, 0:n], [hb], [h_msb])
                    post_res(1, g, [msb[:, k, 0:n] for k in range(8)], [h_msb], sq, rstd, tmp, h_t1, h_t2)
                E.barrier()
                AR.pop()

            if DBG.get('mix'):
                mixer_sub()
            if DBG['att']:
                attention_sub()
            if DBG['ffn']:
                ffn_sub()

        for half in range(DBG['halves']):
            groups = [(0, 512, 'p'), (512, 512, 'p')] + ([(1024, 128, 's')] if (half == 1 and DBG['samp']) else [])
            NTOK = 1024 + (128 if half == 1 else 0)
            h_x = [E.H("x%d" % g) for g in range(len(groups))]
            h_h = [E.H("h%d" % g) for g in range(len(groups))]
            xv = d_xT.rearrange("(k p) n -> p k n", p=128)
            yv = o_yT.rearrange("(k p) n -> p k n", p=128)
            for g, (c, n, kind) in enumerate(groups):
                src = xv[:, :, half * 1024 + c: half * 1024 + c + n] if kind == 'p' else xv[:, :, 2048:2176]
                dma('sp', xT[:, :, c:c + n], src, (), [h_x[g]])

            if half == 0:
                AR.push()
                mraw = AR.alloc([8, 256], F32); h_mraw = E.H("mraw")
                msq = AR.alloc([8, 256], BF16)
                mr = AR.alloc([256], F32); h_mt = E.H("mtmp")
                dma('sp', mraw, d_memT.rearrange("(k p) n -> p k n", p=128), (), [h_mraw])
                rms_rstd([mraw[:, k, :] for k in range(8)], 256, [h_mraw], msq, mr, h_mt)
                for k in range(8):
                    stt(memn[:, k, :], mraw[:, k, :], pp[:, PP_MG + k:PP_MG + k + 1], mr, ALU.mult, ALU.mult,
                        [h_mraw, h_mt, h_pp], [h_memn])
                E.barrier()
                AR.pop()

            for l in range(DBG['layers']):
                layer(half, l, groups, h_x, h_h)

            for g, (c, n, kind) in enumerate(groups):
                dst = yv[:, :, half * 1024 + c: half * 1024 + c + n] if kind == 'p' else yv[:, :, 2048:2176]
                dma('sp', dst, xT[:, :, c:c + n], [h_x[g]], ())
            E.barrier()

        E.finalize()
        import contextlib
        with contextlib.ExitStack() as es:
            csem = {e: es.enter_context(nc.semaphore("c_" + e)) for e in ENGS}
            dsem = [es.enter_context(nc.semaphore("d%d" % i)) for i in range(NDSEM)]
            block = es.enter_context(nc.Block())

            @block.tensor
            def _(e):
                E.emit('pe', e, csem, dsem)

            @block.scalar
            def _(e):
                E.emit('act', e, csem, dsem)

            @block.vector
            def _(e):
                E.emit('dve', e, csem, dsem)

            @block.gpsimd
            def _(e):
                E.emit('pool', e, csem, dsem)

            @block.sync
            def _(e):
                E.emit('sp', e, csem, dsem)
    return nc


_PROG = {}


def _merge_masks():
    m = np.zeros((128, 1792), np.float32)
    s, t = np.arange(128)[:, None], np.arange(128)[None, :]
    for lv in range(7):
        b = 1 << lv
        q = ((s // (2 * b)) == (t // (2 * b))) & ((s % (2 * b)) < b) & ((t % (2 * b)) >= b)
        m[:, lv * 128:(lv + 1) * 128] = q
        m[:, 896 + lv * 128:896 + (lv + 1) * 128] = q.T
    return m


def _consts():
    c = np.zeros((128, NCON), np.float32)
    idx = np.arange(128)
    c[:, C_ID:C_ID + 128] = np.eye(128, dtype=np.float32)
    s, t = idx[:, None], idx[None, :]
    c[:, C_MU:C_MU + 128] = (s < t)
    c[:, C_MI:C_MI + 128] = (s <= t)
    c[:, C_ML:C_ML + 128] = (t < s)
    same = (s // 8) == (t // 8)
    c[:, C_MUS:C_MUS + 128] = (s < t) & same
    c[:, C_MIS:C_MIS + 128] = (s <= t) & same
    c[:, C_MLS:C_MLS + 128] = (t < s) & same
    c[:, C_SEQ:C_SEQ + 16] = (idx[:, None] // 8) == np.arange(16)[None, :]
    c[:, C_EPS + 0] = 1e-6
    c[:, C_EPS + 1] = 1e-5
    c[:, C_EPS + 2] = 64e-5
    c[:, C_EPS + 3] = 0.0
    c[:, C_EPS + 4] = 1.0
    return c


def _prepare(inp):
    f = lambda a: np.ascontiguousarray(np.asarray(a), dtype=np.float32)
    I = {k: f(v) for k, v in inp.items()}
    pp = np.zeros((128, NPP), np.float32)
    ng = I['norm_g'].reshape(4, 6, 8, 128)
    pp[:, PP_NG:PP_NG + 192] = ng.transpose(3, 0, 1, 2).reshape(128, 192)
    pp[:, PP_MG:PP_MG + 8] = I['mem_norm_g'].reshape(8, 128).T
    cw = I['conv_w'].reshape(4, 31, 4, 128)
    pp[:, PP_CW:PP_CW + 496] = cw.transpose(3, 0, 2, 1).reshape(128, 496)
    pp[:, PP_CB:PP_CB + 16] = I['conv_b'].reshape(4, 4, 128).transpose(2, 0, 1).reshape(128, 16)
    pp[:, PP_LG:PP_LG + 16] = I['conv_ln_g'].reshape(4, 4, 128).transpose(2, 0, 1).reshape(128, 16)
    pp[:, PP_LB:PP_LB + 16] = I['conv_ln_b'].reshape(4, 4, 128).transpose(2, 0, 1).reshape(128, 16)
    pp[:, PP_MUD:PP_MUD + 8] = I['mu_shift'][:, 1536:1792].reshape(4, 2, 128).transpose(2, 0, 1).reshape(128, 8)
    bc = np.zeros((4, 128, NBC), np.float32)
    for j, nm in enumerate(['rwkv_k_k', 'rwkv_k_a', 'rwkv_r_k', 'rwkv_gn_g', 'rwkv_gn_b']):
        bc[:, :, j * 512:(j + 1) * 512] = I[nm].reshape(4, 1, 512)
    bc[:, :, 2560:] = I['mu_shift'].reshape(4, 1, 1792)
    rows = np.concatenate([I['rwkv_w0'], I['rwkv_a0']], axis=1)
    wup = np.concatenate([I['rwkv_w_up'], I['rwkv_a_up']], axis=1)
    gup = I['rwkv_g_up']
    consts = _consts()
    mrg = _merge_masks()
    in_maps = []
    for c in range(NCORE):
        sb = slice(16 * c, 16 * c + 16)
        xs = I['x_sample'][sb].reshape(128, 1024)
        xT = np.ascontiguousarray(np.concatenate([I['x_prompt'][c], xs], axis=0).T)
        m = {
            "xT": xT,
            "memT": np.ascontiguousarray(I['mem_prompt'][c].T),
            "kTc": np.ascontiguousarray(I['cache_mem_k'][:, sb].transpose(0, 1, 3, 4, 2)),
            "vc": np.ascontiguousarray(I['cache_mem_v'][:, sb].reshape(4, 16, 256, 1024)),
            "wkvT": np.ascontiguousarray(I['state_wkv'][:, sb].transpose(0, 1, 2, 4, 3)),
            "convT": np.ascontiguousarray(I['state_conv'][:, sb].transpose(0, 1, 3, 2)),
            "shiftT": np.ascontiguousarray(I['state_shift'][:, sb].transpose(0, 2, 1)),
            "w_in": I['w_in'], "w_out": I['w_out'], "w_q": I['w_q'], "w_k": I['w_k'], "w_v": I['w_v'],
            "w_o": I['w_o'], "w_ffn1": I['w_ffn1'], "w_ffn2": I['w_ffn2'],
            "pp": pp, "bc": bc, "rows": rows, "wup": wup, "gup": gup, "consts": consts, "mrg": mrg,
        }
        in_maps.append(m)
    return in_maps


def kernel(**inp):
    in_maps = _prepare(inp)
    if 'nc' not in _PROG:
        _PROG['nc'] = build_program()
    res = run_bass_kernel_spmd(_PROG['nc'], in_maps, core_ids=list(range(NCORE)))
    R = res.results
    y_prompt = np.zeros((8, 2048, 1024), np.float32)
    y_sample = np.zeros((128, 8, 1024), np.float32)
    nk = np.zeros((4, 8, 256, 4, 256), np.float32)
    nv = np.zeros((4, 8, 256, 4, 256), np.float32)
    wkvp = np.zeros((4, 8, 8, 64, 64), np.float32)
    convp = np.zeros((4, 8, 30, 512), np.float32)
    shiftp = np.zeros((4, 8, 1024), np.float32)
    wkvs = np.zeros((4, 128, 8, 64, 64), np.float32)
    convs = np.zeros((4, 128, 30, 512), np.float32)
    shifts = np.zeros((4, 128, 1024), np.float32)
    for c in range(NCORE):
        r = R[c]
        sb = slice(16 * c, 16 * c + 16)
        yT = np.asarray(r["yT"])
        y_prompt[c] = yT[:, :2048].T
        y_sample[sb] = yT[:, 2048:].T.reshape(16, 8, 1024)
        nk[:, c] = np.asarray(r["nk"]).reshape(4, 256, 4, 256)
        nv[:, c] = np.asarray(r["nv"]).reshape(4, 256, 4, 256)
        wkvp[:, c] = np.asarray(r["wkvp"]).transpose(0, 1, 3, 2)
        convp[:, c] = np.asarray(r["convp"]).transpose(0, 2, 1)
        shiftp[:, c] = np.asarray(r["shiftp"]).transpose(0, 2, 1).reshape(4, 1024)
        wkvs[:, sb] = np.asarray(r["wkvs"]).transpose(0, 1, 2, 4, 3)
        convs[:, sb] = np.asarray(r["convs"]).transpose(0, 1, 3, 2)
        shifts[:, sb] = np.asarray(r["shifts"]).transpose(0, 3, 2, 1).reshape(4, 16, 1024)
    return (y_prompt, y_sample, nk, nv, wkvp, convp, shiftp, wkvs, convs, shifts)
```

```python
import numpy as np
import concourse.bass as bass
import concourse.mybir as mybir
from concourse.bass_utils import run_bass_kernel_spmd

F32 = mybir.dt.float32
BF16 = mybir.dt.bfloat16
AF = mybir.ActivationFunctionType
ALU = mybir.AluOpType
AX = mybir.AxisListType

DEPTH = 4
NCORE = 8
CDEC = 0.6065306597126334

PP_NG, PP_MG, PP_CW, PP_CB, PP_LG, PP_LB, PP_MUD, NPP = 0, 192, 200, 696, 712, 728, 744, 752
C_ID, C_MU, C_MI, C_ML, C_MUS, C_MIS, C_MLS, C_SEQ, C_EPS, NCON = 0, 128, 256, 384, 512, 640, 768, 896, 912, 920
NBC = 2560 + 1792

ENGS = ['pe', 'act', 'dve', 'pool', 'sp']
HOLE_LO, HOLE_HI = 207 * 256, 207 * 256
NDSEM = 90
NSW = 45
DBGV = {}
DBG = dict(layers=4, halves=2, att=True, ffn=True, samp=True, stop=99, mix=True, rwkv=True)


class Hd:
    __slots__ = ('name', 'w', 'r', 'excl')

    def __init__(self, name, excl=False):
        self.name = name
        self.w = None
        self.r = []
        self.excl = excl


class Emitter:
    def __init__(self):
        self.ops = []
        self.handles = []
        self.last_compute = {e: None for e in ENGS}
        self.out_dma = []
        self.group = None
        self.ngroups_phase = 0
        self.groups = []

    def H(self, name, excl=False):
        h = Hd(name, excl)
        self.handles.append(h)
        return h

    def op(self, eng, fn, r=(), w=(), dma=False):
        i = len(self.ops)
        deps = set()
        w = list(w) + [h for h in r if h.excl]
        r = [h for h in r if not h.excl]
        for h in r:
            if h.w is not None:
                deps.add(h.w)
        for h in w:
            if h.w is not None:
                deps.add(h.w)
            deps.update(h.r)
        for h in r:
            h.r.append(i)
        for h in w:
            h.w = i
            h.r = []
        deps.discard(i)
        gid = None
        if dma:
            if self.group is not None:
                gid = self.group
            else:
                gid = len(self.groups)
                self.groups.append([])
                self.ngroups_phase += 1
            if self.group is not None:
                deps = set(d for d in deps if not (self.ops[d]['dma'] and self.ops[d]['gid'] == gid))
            self.groups[gid].append(i)
            self.out_dma.append(i)
        else:
            self.last_compute[eng] = i
        self.ops.append(dict(eng=eng, fn=fn, deps=deps, dma=dma, gid=gid, sig=False))
        return i

    def begin_group(self):
        self.group = len(self.groups)
        self.groups.append([])
        self.ngroups_phase += 1

    def end_group(self):
        self.group = None

    def barrier(self):
        deps = set(v for v in self.last_compute.values() if v is not None) | set(self.out_dma)
        for e in ENGS:
            self.ops.append(dict(eng=e, fn=None, deps=set(deps), dma=False, gid=None, sig=False, bar=True))
        for h in self.handles:
            h.w = None
            h.r = []
        self.out_dma = []
        self.ngroups_phase = 0
        self.ops[-1]['phase_end'] = True

    def finalize(self):
        ops = self.ops
        for j, o in enumerate(ops):
            for d in o['deps']:
                p = ops[d]
                if p['dma']:
                    continue
                if p['eng'] == 'pe' and o['eng'] == 'pe' and not o.get('bar'):
                    continue
                p['sig'] = True
        cnt = {e: 0 for e in ENGS}
        for o in ops:
            if o['sig']:
                cnt[o['eng']] += 1
                o['sigval'] = cnt[o['eng']]
        gsem = {}
        semcnt = [0] * NDSEM
        nxt = {'pool': 0, 'sp': NSW}
        lim = {'pool': NSW, 'sp': NDSEM}
        gfinal = {}
        for j, o in enumerate(ops):
            if o['dma']:
                g = o['gid']
                q = o['eng']
                if g not in gsem:
                    gsem[g] = nxt[q]
                    nxt[q] += 1
                    assert nxt[q] <= lim[q], (q, nxt[q])
                k = gsem[g]
                semcnt[k] += 16
                gfinal[g] = semcnt[k]
            if o.get('phase_end'):
                nxt = {'pool': 0, 'sp': NSW}
        self.gsem, self.gfinal = gsem, gfinal

    def emit(self, eng_name, engobj, csem, dsem):
        ops = self.ops
        waited = {}
        for o in ops:
            if o['eng'] != eng_name:
                continue
            for d in sorted(o['deps']):
                p = ops[d]
                if p['dma']:
                    key = ('d', self.gsem[p['gid']])
                    val = self.gfinal[p['gid']]
                    sem = dsem[self.gsem[p['gid']]]
                else:
                    if not p['sig']:
                        continue
                    key = ('c', p['eng'])
                    val = p['sigval']
                    sem = csem[p['eng']]
                if waited.get(key, 0) < val:
                    engobj.wait_ge(sem, val)
                    waited[key] = val
            if o['fn'] is None:
                continue
            ins = o['fn'](engobj)
            if o['dma']:
                ins.then_inc(dsem[self.gsem[o['gid']]], 16)
            elif o['sig']:
                ins.then_inc(csem[eng_name], 1)


class Arena:
    def __init__(self, ap, words):
        self.ap = ap
        self.words = words
        self.tops = [0, HOLE_HI]
        self.lims = [HOLE_LO, words]
        self.stack = []

    def alloc(self, free_shape, dtype):
        n = 1
        for s in free_shape:
            n *= s
        nbytes = n * (2 if dtype == BF16 else 4)
        w = (nbytes + 31) // 32 * 8
        for r in (0, 1):
            if self.tops[r] + w <= self.lims[r]:
                off = self.tops[r]
                self.tops[r] += w
                self.last = (off, n, dtype == BF16)
                break
        else:
            raise AssertionError(("arena overflow", self.tops, w, self.lims))
        v = self.ap[:, off:off + w]
        if dtype == BF16:
            v = v.bitcast(BF16)[:, 0:n]
        else:
            v = v[:, 0:n]
        if len(free_shape) == 2:
            v = v.rearrange("p (a b) -> p a b", a=free_shape[0])
        elif len(free_shape) == 3:
            v = v.rearrange("p (a b c) -> p a b c", a=free_shape[0], b=free_shape[1])
        elif len(free_shape) == 4:
            v = v.rearrange("p (a b c d) -> p a b c d", a=free_shape[0], b=free_shape[1], c=free_shape[2])
        return v

    def push(self):
        self.stack.append(list(self.tops))

    def pop(self):
        self.tops = self.stack.pop()


def build_program():
    nc = bass.Bass("TRN2", target_bir_lowering=False)
    E = Emitter()

    def din(name, shape):
        return nc.dram_tensor(name, list(shape), F32, kind="ExternalInput").ap()

    def dout(name, shape):
        return nc.dram_tensor(name, list(shape), F32, kind="ExternalOutput").ap()

    d_xT = din("xT", [1024, 2176])
    d_memT = din("memT", [1024, 256])
    d_kT = din("kTc", [4, 16, 4, 256, 256])
    d_v = din("vc", [4, 16, 256, 1024])
    d_wkv = din("wkvT", [4, 16, 8, 64, 64])
    d_conv = din("convT", [4, 16, 512, 30])
    d_shift = din("shiftT", [4, 1024, 16])
    d_win = din("w_in", [4, 1024, 2816])
    d_wout = din("w_out", [4, 1024, 1024])
    d_wq = din("w_q", [4, 1024, 1024])
    d_wk = din("w_k", [4, 1024, 1024])
    d_wv = din("w_v", [4, 1024, 1024])
    d_wo = din("w_o", [4, 1024, 1024])
    d_w1 = din("w_ffn1", [4, 1024, 4096])
    d_w2 = din("w_ffn2", [4, 4096, 1024])
    d_pp = din("pp", [128, NPP])
    d_bc = din("bc", [4, 128, NBC])
    d_rows = din("rows", [4, 1024])
    d_wup = din("wup", [4, 128, 512])
    d_gup = din("gup", [4, 128, 512])
    d_con = din("consts", [128, NCON])
    d_mrg = din("mrg", [128, 1792])

    o_yT = dout("yT", [1024, 2176])
    o_nk = dout("nk", [4, 256, 1024])
    o_nv = dout("nv", [4, 256, 1024])
    o_wkvp = dout("wkvp", [4, 8, 64, 64])
    o_convp = dout("convp", [4, 512, 30])
    o_shiftp = dout("shiftp", [4, 128, 8])
    o_wkvs = dout("wkvs", [4, 16, 8, 64, 64])
    o_convs = dout("convs", [4, 16, 512, 30])
    o_shifts = dout("shifts", [4, 128, 8, 16])

    ARENA_WORDS = 207 * 256
    with (
        nc.sbuf_tensor("arena", [128, ARENA_WORDS], F32) as arena_t,
        nc.psum_tensor("ps0", [128, 512], F32) as ps0, nc.psum_tensor("ps1", [128, 512], F32) as ps1,
        nc.psum_tensor("ps2", [128, 512], F32) as ps2, nc.psum_tensor("ps3", [128, 512], F32) as ps3,
        nc.psum_tensor("ps4", [128, 512], F32) as ps4, nc.psum_tensor("ps5", [128, 512], F32) as ps5,
        nc.psum_tensor("ps6", [128, 512], F32) as ps6, nc.psum_tensor("ps7", [128, 512], F32) as ps7,
    ):
        AR = Arena(arena_t[:, :], ARENA_WORDS)
        banks = [(p[:, :], E.H("bank%d" % i, excl=True)) for i, p in enumerate([ps0, ps1, ps2, ps3, ps4, ps5, ps6, ps7])]
        bank_ctr = [0]

        def bank():
            b = banks[bank_ctr[0] % 8]
            bank_ctr[0] += 1
            return b

        def mm(out, lhsT, rhs, start, stop, r, w):
            E.op('pe', lambda e: e.matmul(out, lhsT, rhs, start=start, stop=stop), r, w)

        def tr(out, in_, ident, r, w):
            E.op('pe', lambda e: e.transpose(out, in_, ident), r, w)

        def act(out, in_, func, r, w, scale=None, bias=None):
            kw = {}
            if scale is not None:
                kw['scale'] = scale
            if bias is not None:
                kw['bias'] = bias
            E.op('act', lambda e: e.activation(out, in_, func, **kw), r, w)

        def tt(eng, out, in0, in1, op, r, w):
            E.op(eng, lambda e: e.tensor_tensor(out, in0, in1, op), r, w)

        def ts(eng, out, in0, s1, op0, r, w, s2=None, op1=None):
            if op1 is None:
                E.op(eng, lambda e: e.tensor_scalar(out, in0, s1, None, op0), r, w)
            else:
                E.op(eng, lambda e: e.tensor_scalar(out, in0, s1, s2, op0, op1), r, w)

        def stt(out, in0, scalar, in1, op0, op1, r, w):
            E.op('dve', lambda e: e.scalar_tensor_tensor(out, in0, scalar, in1, op0, op1), r, w)

        def cp(eng, out, in_, r, w):
            if eng == 'act':
                E.op('act', lambda e: e.copy(out, in_), r, w)
            else:
                E.op(eng, lambda e: e.tensor_copy(out, in_), r, w)

        def red(out, in_, r, w):
            E.op('dve', lambda e: e.tensor_reduce(out, in_, AX.X, ALU.add), r, w)

        def recip(out, in_, r, w):
            E.op('dve', lambda e: e.reciprocal(out, in_), r, w)

        def mset(eng, ap, val, w):
            E.op(eng, lambda e: e.memset(ap, val), (), w)

        def dma(q, out, in_, r, w):
            E.op(q, lambda e: e.dma_start(out=out, in_=in_), r, w, dma=True)

        con_f = AR.alloc([NCON], F32); h_con = E.H("con")
        con_b = AR.alloc([NCON], BF16)
        pp = AR.alloc([NPP], F32); h_pp = E.H("pp")
        xT = AR.alloc([8, 1152], F32)
        hT = AR.alloc([8, 1153], BF16)
        memn = AR.alloc([8, 256], BF16); h_memn = E.H("memn")
        st_S = AR.alloc([DEPTH, 4, 64], F32); h_stS = [E.H("stS%d" % l) for l in range(DEPTH)]
        st_conv = AR.alloc([DEPTH, 4, 30], BF16); h_stconv = [E.H("stc%d" % l) for l in range(DEPTH)]
        st_shift = AR.alloc([DEPTH, 8], BF16); h_stshift = [E.H("sts%d" % l) for l in range(DEPTH)]

        ident_b = con_b[:, C_ID:C_ID + 128]
        ident_f = con_f[:, C_ID:C_ID + 128]
        ones_b = None

        def epsc(i):
            return con_f[:, C_EPS + i:C_EPS + i + 1]

        dma('sp', con_f, d_con, (), [h_con])
        dma('pool', con_b, d_con, (), [h_con])
        dma('sp', pp, d_pp, (), [h_pp])
        onesb = AR.alloc([128], BF16)
        mset('dve', onesb, 1.0, [h_con])
        mset('dve', st_S, 0.0, h_stS)
        mset('dve', st_conv, 0.0, h_stconv)
        mset('dve', st_shift, 0.0, h_stshift)
        h_hinit = E.H('hinit')
        mset('dve', hT[:, :, 0:1], 0.0, [h_hinit])

        def rms_rstd(src_list, n, rsrc, sqbuf, rstd_out, h_tmp, eps_i=0, dim=1024.0):
            pb, hb = bank()
            nk = len(src_list)
            for k, s in enumerate(src_list):
                act(sqbuf[:, k, 0:n], s, AF.Square, rsrc, [h_tmp])
            for k in range(nk):
                mm(pb[:, 0:n], onesb, sqbuf[:, k, 0:n], k == 0, k == nk - 1, [h_tmp, h_con], [hb])
            act(rstd_out, pb[:, 0:n], AF.Sqrt, [hb, h_con], [h_tmp], scale=1.0 / dim, bias=epsc(eps_i))
            recip(rstd_out, rstd_out, [h_tmp], [h_tmp])


        def layer(half, l, groups, h_x, h_h):
            def ngc(i, k):
                o = PP_NG + l * 48 + i * 8 + k
                return pp[:, o:o + 1]

            def norm_to_h(i, sq, rstd, h_tmp):
                for g, (c, n, kind) in enumerate(groups):
                    rms_rstd([xT[:, k, c:c + n] for k in range(8)], n, [h_x[g]], sq, rstd[:, 0:n], h_tmp)
                    for k in range(8):
                        stt(hT[:, k, 1 + c:1 + c + n], xT[:, k, c:c + n], ngc(i, k), rstd[:, 0:n],
                            ALU.mult, ALU.mult, [h_x[g], h_tmp, h_pp], [h_h[g]])

            def post_res(i, g, src_list, rsrc, sq, rstd, tmp, h_tmp, h_tmp2):
                c, n, kind = groups[g]
                rms_rstd(src_list, n, rsrc, sq, rstd[:, 0:n], h_tmp)
                for k in range(8):
                    tt('dve', src_list[k], src_list[k], rstd[:, 0:n], ALU.mult, rsrc + [h_tmp], rsrc)
                for k in range(8):
                    stt(xT[:, k, c:c + n], src_list[k], ngc(i, k), xT[:, k, c:c + n], ALU.mult, ALU.add,
                        rsrc + [h_pp, h_x[g]], [h_x[g]])

            def load_w(q, dst, dsrc, h, ncol_split=None):
                v = dsrc.rearrange("(k p) n -> p k n", p=128)
                E.begin_group()
                for k in range(8):
                    dma(q, dst[:, k, :], v[:, k, :], (), [h])
                E.end_group()

            def attention_sub():
                AR.push()
                wq = AR.alloc([8, 1024], BF16); h_wq = E.H("wq")
                wk = AR.alloc([8, 1024], BF16); h_wk = E.H("wk")
                wv = AR.alloc([8, 1024], BF16); h_wv = E.H("wv")
                wo = AR.alloc([8, 1024], BF16); h_wo = E.H("wo")
                load_w('pool', wq, d_wq[l], h_wq)
                if DBG.get('nw', 4) > 1:
                    load_w('pool', wk, d_wk[l], h_wk)
                    load_w('pool', wv, d_wv[l], h_wv)
                    load_w('pool', wo, d_wo[l], h_wo)
                if DBG['stop'] <= 0:
                    E.barrier(); AR.pop(); return
                if DBG.get('wbar'):
                    E.barrier()
                stg = AR.alloc([2, 512], F32); h_stg = [E.H("stg0"), E.H("stg1")]
                sq = AR.alloc([8, 512], BF16); h_t1 = E.H("t1")
                rstd = AR.alloc([512], F32)
                tmp = AR.alloc([512], F32); h_t2 = E.H("t2")
                msb = AR.alloc([8, 512], F32); h_msb = E.H("msb")
                KT = AR.alloc([8, 256], BF16); h_KT = E.H("KT")
                Vb = AR.alloc([2, 1024], BF16); h_Vb = E.H("Vb")
                qT = sq; h_qT = h_t1
                oT = AR.alloc([8, 512], BF16); h_oT = E.H("oT")
                if DBG.get('stghi'):
                    stg = AR.alloc([2, 512], F32)
                PT = AR.alloc([2, 512], BF16); h_PT = E.H("PT")
                rs = AR.alloc([512], F32); h_rs = E.H("rs")
                if not DBG.get('skip1'):
                    norm_to_h(2, sq, rstd, h_t1)
                if DBG['stop'] <= 1:
                    E.barrier(); AR.pop(); return
                for oc in range(0 if DBG.get('skip2') else 8):
                    pb, hb = bank()
                    for k in range(8):
                        mm(pb[:, 0:256], wk[:, k, oc * 128:(oc + 1) * 128], memn[:, k, :], k == 0, k == 7,
                           [h_wk, h_memn], [hb])
                    cp('act', KT[:, oc, :], pb[:, 0:256], [hb], [h_KT])
                if DBG['stop'] <= 2:
                    E.barrier(); AR.pop(); return
                si = 0
                for which, wsrc, hw, dout_ in ((0, wv, h_wv, o_nv), (1, wk, h_wk, o_nk)):
                    if which == 1 and half == 1:
                        continue
                    if which >= DBG.get('vw', 2):
                        continue
                    for mc in range(DBG.get('vmc', 2)):
                        for hf in range(DBG.get('vhf', 2)):
                            pb, hb = bank()
                            for k in range(8):
                                mm(pb, memn[:, k, mc * 128:(mc + 1) * 128], wsrc[:, k, hf * 512:(hf + 1) * 512],
                                   k == 0, k == 7, [hw, h_memn], [hb])
                            if which == 0 and not DBG.get('novb'):
                                cp('act', Vb[:, mc, hf * 512:(hf + 1) * 512], pb, [hb], [h_Vb])
                            if half == 0:
                                s = (si % 2) if not DBG.get('stg0') else 0
                                si += 1
                                if not DBG.get('nostg'):
                                    cp(DBG.get('stgeng', 'dve'), (msb if DBG.get('stgmsb') else stg)[:, s, :], pb, ([h_Vb] if DBG.get('chain') else [hb]), [h_stg[s]])
                                if not DBG.get('nodma'):
                                    dma('sp', dout_[l, mc * 128:(mc + 1) * 128, hf * 512:(hf + 1) * 512], stg[:, s, :],
                                        [h_stg[s]], ())
                if DBG['stop'] <= 3:
                    E.barrier(); AR.pop(); return
                for g, (c, n, kind) in enumerate(groups):
                    for oc in range(8):
                        pb, hb = bank()
                        for k in range(8):
                            mm(pb[:, 0:n], wq[:, k, oc * 128:(oc + 1) * 128], hT[:, k, 1 + c:1 + c + n], k == 0, k == 7,
                               [h_wq, h_h[g]], [hb])
                        cp('act', qT[:, oc, 0:n], pb[:, 0:n], [hb], [h_qT])
                    if kind == 'p':
                        for hd in range(4):
                            for mc in range(2):
                                pb, hb = bank()
                                for dc in range(2):
                                    mm(pb[:, 0:n], KT[:, 2 * hd + dc, mc * 128:(mc + 1) * 128], qT[:, 2 * hd + dc, 0:n],
                                       dc == 0, dc == 1, [h_KT, h_qT], [hb])
                                act(PT[:, mc, 0:n], pb[:, 0:n], AF.Exp, [hb], [h_PT], scale=1.0 / 16.0)
                            pb, hb = bank()
                            for mc in range(2):
                                mm(pb[:, 0:n], onesb, PT[:, mc, 0:n], mc == 0, mc == 1, [h_PT, h_con], [hb])
                            recip(rs[:, 0:n], pb[:, 0:n], [hb], [h_rs])
                            for dc in range(2):
                                pb, hb = bank()
                                for mc in range(2):
                                    mm(pb[:, 0:n], Vb[:, mc, (2 * hd + dc) * 128:(2 * hd + dc + 1) * 128], PT[:, mc, 0:n],
                                       mc == 0, mc == 1, [h_Vb, h_PT], [hb])
                                tt('dve', oT[:, 2 * hd + dc, 0:n], pb[:, 0:n], rs[:, 0:n], ALU.mult, [hb, h_rs], [h_oT])
                    else:
                        AR.push()
                        KTs = AR.alloc([1, 4, 2, 256], BF16); h_KTs = [E.H("KTs0")]
                        Vs = AR.alloc([1, 2, 1024], BF16); h_Vs = [E.H("Vs0")]
                        PTs = PT.rearrange("p a (b h m t) -> p (a b) h m t", b=8, h=4, m=2); h_PTs = h_PT
                        rss = rs.rearrange("p (b h t) -> p b h t", b=16, h=4); h_rss = h_rs
                        for bh in range(2):
                            pbs, hbs = bank()
                            pbo0, hbo0 = bank()
                            sview = pbs.rearrange("p (b h m t) -> p b h m t", b=8, h=4, m=2)
                            oview = pbo0.rearrange("p (b h d t) -> p b h d t", b=8, h=4, d=2)
                            for bb in range(8):
                                b = bh * 8 + bb
                                s = 0
                                dma('pool', KTs[:, s], d_kT[l, b].rearrange("h (dc p) m -> p h dc m", p=128), (), [h_KTs[s]])
                                dma('pool', Vs[:, s], d_v[l, b].rearrange("(mc p) n -> p mc n", p=128), (), [h_Vs[s]])
                                for hd in range(4):
                                    for mc in range(2):
                                        for dc in range(2):
                                            mm(sview[:, bb, hd, mc, :], KTs[:, s, hd, dc, mc * 128:(mc + 1) * 128],
                                               qT[:, 2 * hd + dc, b * 8:(b + 1) * 8], dc == 0, dc == 1,
                                               [h_KTs[s], h_qT], [hbs])
                                act(PTs[:, b], sview[:, bb], AF.Exp, [hbs], [h_PTs], scale=1.0 / 16.0)
                                for hd in range(4):
                                    for dc in range(2):
                                        for mc in range(2):
                                            mm(oview[:, bb, hd, dc, :], Vs[:, s, mc, (2 * hd + dc) * 128:(2 * hd + dc + 1) * 128],
                                               PTs[:, b, hd, mc, :], mc == 0, mc == 1, [h_Vs[s], h_PTs], [hbo0])
                            pbr, hbr = bank()
                            rview = pbr[:, 0:256].rearrange("p (b h t) -> p b h t", b=8, h=4)
                            for mc in range(2):
                                mm(rview, onesb, PTs[:, bh * 8:(bh + 1) * 8, :, mc, :], mc == 0, mc == 1, [h_PTs, h_con], [hbr])
                            recip(rss[:, bh * 8:(bh + 1) * 8], rview, [hbr], [h_rss])
                            for dc in range(2):
                                ov = oT[:, :, 0:128].rearrange("p (h d) (b t) -> p d b h t", d=2, t=8)[:, dc, bh * 8:(bh + 1) * 8]
                                tt('dve', ov, oview[:, :, :, dc, :], rss[:, bh * 8:(bh + 1) * 8], ALU.mult,
                                   [hbo0, h_rss], [h_oT])
                        AR.pop()
                    for oc in range(8):
                        pb, hb = bank()
                        for k in range(8):
                            mm(pb[:, 0:n], wo[:, k, oc * 128:(oc + 1) * 128], oT[:, k, 0:n], k == 0, k == 7,
                               [h_wo, h_oT], [hb])
                        cp('act', msb[:, oc, 0:n], pb[:, 0:n], [hb], [h_msb])
                    post_res(3, g, [msb[:, k, 0:n] for k in range(8)], [h_msb], sq, rstd, tmp, h_t1, h_t2)
                E.barrier()
                AR.pop()


            def ffn_sub():
                AR.push()
                NT_ = 1024 + (128 if half == 1 else 0)
                facc = AR.alloc([8, NT_], F32); h_f = [E.H("f%d" % g) for g in range(len(groups))]
                w1 = AR.alloc([2, 8, 512], BF16); h_w1 = [E.H("w1a"), E.H("w1b")]
                w2 = AR.alloc([2, 4, 1024], BF16); h_w2 = [E.H("w2a"), E.H("w2b")]
                sq = AR.alloc([8, 512], BF16); h_t1 = E.H("t1f")
                rstd = AR.alloc([512], F32)
                tmp = AR.alloc([512], F32); h_t2 = E.H("t2f")
                rl = AR.alloc([2, 512], BF16); h_rl = [E.H("rl0"), E.H("rl1")]
                hid = AR.alloc([2, 4, 512], BF16); h_hid = [E.H("hid0"), E.H("hid1")]
                norm_to_h(4, sq, rstd, h_t1)
                w1v = d_w1[l].rearrange("(k p) n -> p k n", p=128)
                w2v = d_w2[l].rearrange("(c p) n -> p c n", p=128)
                items = [(hg, g) for hg in range(8) for g in range(len(groups))]

                def ffn1(idx):
                    hg, g = items[idx]
                    c, n, kind = groups[g]
                    s, hs = hg % 2, idx % 2
                    if g == 0:
                        E.begin_group()
                        for k in range(8):
                            dma('pool', w1[:, s, k, :], w1v[:, k, hg * 512:(hg + 1) * 512], (), [h_w1[s]])
                        E.end_group()
                        E.begin_group()
                        for cc in range(4):
                            dma('pool', w2[:, s, cc, :], w2v[:, hg * 4 + cc, :], (), [h_w2[s]])
                        E.end_group()
                    for hc in range(4):
                        pb, hb = bank()
                        for k in range(8):
                            mm(pb[:, 0:n], w1[:, s, k, hc * 128:(hc + 1) * 128], hT[:, k, 1 + c:1 + c + n], k == 0, k == 7,
                               [h_w1[s], h_h[g]], [hb])
                        rs_ = hc % 2
                        act(rl[:, rs_, 0:n], pb[:, 0:n], AF.Relu, [hb], [h_rl[rs_]])
                        tt('dve', hid[:, hs, hc, 0:n], rl[:, rs_, 0:n], rl[:, rs_, 0:n], ALU.mult, [h_rl[rs_]], [h_hid[hs]])

                def ffn2(idx):
                    hg, g = items[idx]
                    c, n, kind = groups[g]
                    s, hs = hg % 2, idx % 2
                    for oc in range(8):
                        pb, hb = bank()
                        for hc in range(4):
                            mm(pb[:, 0:n], w2[:, s, hc, oc * 128:(oc + 1) * 128], hid[:, hs, hc, 0:n], hc == 0, hc == 3,
                               [h_w2[s], h_hid[hs]], [hb])
                        if hg == 0:
                            cp('act', facc[:, oc, c:c + n], pb[:, 0:n], [hb], [h_f[g]])
                        else:
                            tt('dve', facc[:, oc, c:c + n], pb[:, 0:n], facc[:, oc, c:c + n], ALU.add, [hb, h_f[g]], [h_f[g]])
                ffn1(0)
                for i in range(len(items)):
                    if i + 1 < len(items):
                        ffn1(i + 1)
                    ffn2(i)
                for g, (c, n, kind) in enumerate(groups):
                    post_res(5, g, [facc[:, k, c:c + n] for k in range(8)], [h_f[g]], sq, rstd, tmp, h_t1, h_t2)
                E.barrier()
                AR.pop()


            def rwkv_sub(mixT, h_mix):
                AR.push()
                winv = d_win[l].rearrange("(k p) n -> p k n", p=128)
                Wr = AR.alloc([8, 1792], BF16); h_Wr = E.H("Wr")
                E.begin_group()
                for k in range(8):
                    dma('pool', Wr[:, k, :], winv[:, k, 1024:2816], (), [h_Wr])
                E.end_group()
                bct = AR.alloc([4096], F32); h_bc = E.H("bct")
                dma('sp', bct, d_bc[l][:, 0:4096], (), [h_bc])
                kk_t, ka_t, rk_t, gg_t, gb_t = [bct[:, i * 512:(i + 1) * 512] for i in range(5)]
                mu_t = [bct[:, 2560 + i * 512:2560 + (i + 1) * 512] for i in range(3)]
                rows_b = AR.alloc([1024], BF16); h_rows = E.H("rowsb")
                E.begin_group()
                dma('pool', rows_b[0:1, 0:512], d_rows[l:l + 1, 0:512], (), [h_rows])
                dma('pool', rows_b[64:65, 0:512], d_rows[l:l + 1, 512:1024], (), [h_rows])
                E.end_group()
                wup_b = AR.alloc([512], BF16); h_wup = E.H("wupb")
                dma('pool', wup_b, d_wup[l], (), [h_wup])
                gup_b = AR.alloc([512], BF16); h_gup = E.H("gupb")
                dma('pool', gup_b, d_gup[l], (), [h_gup])
                S_b = AR.alloc([4, 64], BF16); h_Sb = E.H("Sb")
                cp('act', S_b, st_S[:, l, :, :], [h_stS[l]], [h_Sb])
                FT = {}
                for nm in ("Asb", "dlt", "Zr", "Zk", "sw", "av", "gate", "G", "Ginv", "Gp", "kk", "sq_"):
                    FT[nm] = (AR.alloc([512], F32), E.H("f_" + nm))
                    DBGV[nm] = AR.last
                FT["kkn"] = FT["kk"]
                FT["t1"] = FT["Asb"]
                FT["kmod"] = FT["sw"]
                DBGV["kkn"] = DBGV["kk"]; DBGV["kmod"] = DBGV["sw"]
                BT = {}
                for nm in ("at", "bt", "kt", "rt", "vb", "Xb", "ro"):
                    BT[nm] = (AR.alloc([512], BF16), E.H("b_" + nm))
                    DBGV[nm] = AR.last
                arT = AR.alloc([4, 2, 128], BF16); h_arT = E.H("arT")
                kbT = AR.alloc([4, 2, 128], BF16); h_kbT = E.H("kbT")
                AkaAkr = AR.alloc([8, 256], BF16); h_Aka = E.H("Aka")
                NAbr = AR.alloc([8, 256], BF16); h_NAbr = E.H("NAbr")
                mrg_b = AR.alloc([1792], BF16); h_mrg = E.H("mrg")
                dma('pool', mrg_b, d_mrg, (), [h_mrg])
                Ad = AR.alloc([2, 128], F32); h_Ad = E.H("Ad")
                Dd = AR.alloc([2, 128], F32); h_Dd = E.H("Dd")
                tda = AR.alloc([128], BF16); h_tda = E.H("tda")
                sdg = AR.alloc([128], BF16); h_sdg = E.H("sdg")
                sm8 = AR.alloc([6, 8], F32); h_sm8 = E.H("sm8")
                gcT = AR.alloc([4, 16], F32); h_gc = E.H("gcT")
                tmpS = AR.alloc([4, 64], F32); h_tmpS = E.H("tmpS")
                has_s = (half == 1 and DBG['samp'])
                if has_s:
                    sst = AR.alloc([8, 16], F32); h_sst = E.H("sst")
                    dma('sp', sst, d_shift[l].rearrange("(k p) b -> p k b", p=128), (), [h_sst])
                    hprev_s = AR.alloc([8, 16, 8], BF16); h_hps = E.H("hps")
                    hsv = hT[:, :, 1025:1153].rearrange("p k (b t) -> p k b t", t=8)
                    cp('dve', hprev_s[:, :, :, 0], sst, [h_sst], [h_hps])
                    cp('dve', hprev_s[:, :, :, 1:8], hsv[:, :, :, 0:7], [h_h[2]], [h_hps])
                    S_sb = AR.alloc([16, 4, 64], BF16); h_Ssb = E.H("Ssb")
                    P1sb = AR.alloc([4, 128], BF16); h_P1sb = E.H("P1sb")
                    S_sf = FT["Gp"][0].rearrange("p (a b c) -> p a b c", a=2, b=4); h_Ssf = [FT["Gp"][1], FT["Gp"][1]]
                    vm = FT["G"][0].bitcast(BF16).rearrange("p (a b) -> p a b", a=2); h_vm = [FT["G"][1], FT["G"][1]]
                    um = FT["Ginv"][0].bitcast(BF16).rearrange("p (a b) -> p a b", a=2); h_um = [FT["Ginv"][1], FT["Ginv"][1]]
                    wkv_in = d_wkv[l].rearrange("b (p4 h2) j i -> h2 j b p4 i", h2=2)
                    wkv_out = o_wkvs[l].rearrange("b (p4 h2) j i -> h2 j b p4 i", h2=2)
                    E.begin_group()
                    for h2 in range(2):
                        for p4 in range(4):
                            dma('pool', S_sb[h2 * 64:(h2 + 1) * 64, :, p4, :], wkv_in[h2][:, :, p4, :], (), [h_Ssb])
                    E.end_group()

                def F(nm):
                    return FT[nm]

                def rw_tile(tc, kind, g):
                    smp = (kind == 's')
                    nseq = 16 if smp else 1
                    nap = 3 if smp else 7
                    MuMi = con_b[:, (C_MUS if smp else C_MU):(C_MUS if smp else C_MU) + 256]
                    MLb = con_b[:, (C_MLS if smp else C_ML):(C_MLS if smp else C_ML) + 128]
                    Tri = con_f[:, (C_MIS if smp else C_MI):(C_MIS if smp else C_MI) + 128]
                    seqf = con_f[:, C_SEQ:C_SEQ + 16] if smp else con_f[:, C_EPS + 4:C_EPS + 5]
                    hg = [h_h[g]] + ([h_h[g - 1]] if g > 0 else []) + ([h_hps] if smp else [])

                    def hc(k):
                        return hT[:, k, 1 + tc:1 + tc + 128]

                    def hp(k):
                        return hprev_s[:, k, :, :] if smp else hT[:, k, tc:tc + 128]
                    (Asb, hA), (dlt, hD) = F("Asb"), F("dlt")
                    Z = {}
                    for qi, qn in enumerate(("Zr", "Zk", "vb")):
                        bA, hbA = bank()
                        for k in range(8):
                            mm(bA, hc(k), Wr[:, k, qi * 512:(qi + 1) * 512], k == 0, k == 7, hg + [h_Wr], [hbA])
                        bB, hbB = bank()
                        for k in range(8):
                            mm(bB, hp(k), Wr[:, k, qi * 512:(qi + 1) * 512], k == 0, k == 7, hg + [h_Wr], [hbB])
                        cp('act', Asb, bA, [hbA], [hA])
                        tt('dve', dlt, bB, Asb, ALU.subtract, [hbB, hA], [hD])
                        tt('dve', dlt, dlt, mu_t[qi], ALU.mult, [hD, h_bc], [hD])
                        dst, hdst = (F(qn) if qi < 2 else BT["vb"])
                        tt('dve', dst, dlt, Asb, ALU.add, [hD, hA], [hdst])
                    (Zr, hZr), (Zk, hZk) = F("Zr"), F("Zk")
                    vb, hvb = BT["vb"]
                    bD, hbD = bank()
                    for ci, c0 in enumerate((1536, 1664)):
                        for k in range(8):
                            mm(bD[:, ci * 256:ci * 256 + 128], Wr[:, k, c0:c0 + 128], hc(k), k == 0, k == 7, hg + [h_Wr], [hbD])
                        for k in range(8):
                            mm(bD[:, ci * 256 + 128:ci * 256 + 256], Wr[:, k, c0:c0 + 128], hp(k), k == 0, k == 7, hg + [h_Wr], [hbD])
                    for ci in range(2):
                        mucol = pp[:, PP_MUD + l * 2 + ci:PP_MUD + l * 2 + ci + 1]
                        cp('act', Ad[:, ci, :], bD[:, ci * 256:ci * 256 + 128], [hbD], [h_Ad])
                        tt('dve', Dd[:, ci, :], bD[:, ci * 256 + 128:ci * 256 + 256], Ad[:, ci, :], ALU.subtract, [hbD, h_Ad], [h_Dd])
                        stt(Dd[:, ci, :], Dd[:, ci, :], mucol, Ad[:, ci, :], ALU.mult, ALU.add, [h_Dd, h_Ad, h_pp], [h_Dd])
                    act(tda[0:64, :], Dd[0:64, 0, :], AF.Tanh, [h_Dd], [h_tda])
                    cp('act', tda[64:128, :], Dd[64:128, 0, :], [h_Dd], [h_tda])
                    act(sdg, Dd[:, 1, :], AF.Sigmoid, [h_Dd], [h_sdg])
                    (sw, hsw), (av, hav), (gate, hgate) = F("sw"), F("av"), F("gate")
                    bW, hbW = bank()
                    mm(bW, onesb[0:1, 0:128], rows_b[0:1, 0:512], True, False, [h_con, h_rows], [hbW])
                    mm(bW, tda[0:64, :], wup_b[0:64, :], False, True, [h_tda, h_wup], [hbW])
                    act(sw, bW, AF.Sigmoid, [hbW], [hsw])
                    bAa, hbAa = bank()
                    mm(bAa, onesb[64:65, 0:128], rows_b[64:65, 0:512], True, False, [h_con, h_rows], [hbAa])
                    mm(bAa, tda[64:128, :], wup_b[64:128, :], False, True, [h_tda, h_wup], [hbAa])
                    act(av, bAa, AF.Sigmoid, [hbAa], [hav])
                    bG, hbG = bank()
                    mm(bG, sdg, gup_b, True, True, [h_sdg, h_gup], [hbG])
                    cp('act', gate, bG, [hbG], [hgate])
                    (G, hG), (Ginv, hGi), (Gp, hGp) = F("G"), F("Ginv"), F("Gp")
                    bC, hbC = bank()
                    mm(bC, Tri, sw, True, True, [h_con, hsw], [hbC])
                    act(G, bC, AF.Exp, [hbC], [hG], scale=-CDEC)
                    act(Ginv, bC, AF.Exp, [hbC], [hGi], scale=CDEC)
                    tt('dve', Gp, bC, sw, ALU.subtract, [hbC, hsw], [hGp])
                    act(Gp, Gp, AF.Exp, [hGp], [hGp], scale=-CDEC)
                    bGC, hbGC = bank()
                    for p4 in range(4):
                        mm(bGC[:, p4 * 16:p4 * 16 + nseq], sw[:, p4 * 128:(p4 + 1) * 128], seqf, True, True, [hsw, h_con], [hbGC])
                    act(gcT[:, :, 0:nseq], bGC[:, 0:64].rearrange("p (a b) -> p a b", a=4)[:, :, 0:nseq], AF.Exp, [hbGC], [h_gc], scale=-CDEC)
                    (kk, hkk), (sq_, hsq), (kkn, hkkn), (t1, ht1), (kmod, hkm) = F("kk"), F("sq_"), F("kkn"), F("t1"), F("kmod")
                    tt('dve', kk, Zk, kk_t, ALU.mult, [hZk, h_bc], [hkk])
                    tt('dve', sq_, kk, kk, ALU.mult, [hkk], [hsq])
                    red(sm8[:, 0, :], sq_.rearrange("p (h d) -> p h d", h=8), [hsq], [h_sm8])
                    act(sm8[:, 0, :], sm8[:, 0, :], AF.Sqrt, [h_sm8], [h_sm8])
                    ts('dve', sm8[:, 0, :], sm8[:, 0, :], 1e-12, ALU.max, [h_sm8], [h_sm8])
                    recip(sm8[:, 0, :], sm8[:, 0, :], [h_sm8], [h_sm8])
                    tt('dve', kkn.rearrange("p (h d) -> p h d", h=8), kk.rearrange("p (h d) -> p h d", h=8),
                       sm8[:, 0, :].unsqueeze(2).to_broadcast([128, 8, 64]), ALU.mult, [hkk, h_sm8], [hkkn])
                    stt(t1, av, -1.0, ka_t, ALU.add, ALU.mult, [hav, h_bc], [ht1])
                    stt(kmod, t1, 1.0, Zk, ALU.add, ALU.mult, [ht1, hZk], [hkm])
                    (at, hat), (bt, hbt), (kt, hkt), (rt, hrt) = BT["at"], BT["bt"], BT["kt"], BT["rt"]
                    stt(at, kkn, -1.0, Gp, ALU.mult, ALU.mult, [hkkn, hGp], [hat])
                    tt('dve', t1, kkn, av, ALU.mult, [hkkn, hav], [ht1])
                    tt('dve', bt, t1, Ginv, ALU.mult, [ht1, hGi], [hbt])
                    tt('dve', kt, kmod, Ginv, ALU.mult, [hkm, hGi], [hkt])
                    tt('dve', rt, Zr, G, ALU.mult, [hZr, hG], [hrt])
                    tt('dve', sq_, Zr, kmod, ALU.mult, [hZr, hkm], [hsq])
                    tt('dve', sq_, sq_, rk_t, ALU.mult, [hsq, h_bc], [hsq])
                    red(sm8[:, 1, :], sq_.rearrange("p (h d) -> p h d", h=8), [hsq], [h_sm8])
                    for (srcs, dstT, hdT) in (((at, hat, rt, hrt), arT, h_arT), ((kt, hkt, bt, hbt), kbT, h_kbT)):
                        bT_, hbT_ = bank()
                        tv = bT_.bitcast(BF16)[:, 0:1024].rearrange("p (a b c) -> p a b c", a=4, b=2)
                        for p4 in range(4):
                            tr(tv[:, p4, 0, :], srcs[0][:, p4 * 128:(p4 + 1) * 128], ident_b, [srcs[1], h_con], [hbT_])
                            tr(tv[:, p4, 1, :], srcs[2][:, p4 * 128:(p4 + 1) * 128], ident_b, [srcs[3], h_con], [hbT_])
                        cp('act', dstT, tv, [hbT_], [hdT])
                    mk = MuMi.unsqueeze(1).to_broadcast([128, 2, 256])
                    for hb4 in range(2):
                        for par in range(2):
                            bA_, hbA_ = bank()
                            bB_, hbB_ = bank()
                            base = 64 * par
                            for hh in range(2):
                                h = hb4 * 4 + 2 * hh + par
                                p4 = h // 2
                                ar2 = arT[base:base + 64, p4, :, :]
                                mm(bA_[:, hh * 256:(hh + 1) * 256], kbT[base:base + 64, p4, 0, :], ar2, True, True, [h_kbT, h_arT], [hbA_])
                                mm(bB_[:, hh * 256:(hh + 1) * 256], kbT[base:base + 64, p4, 1, :], ar2, True, True, [h_kbT, h_arT], [hbB_])
                            h0 = hb4 * 4 + par
                            tt('dve', AkaAkr[:, h0:h0 + 3:2, :], bA_.rearrange("p (a b) -> p a b", a=2), mk, ALU.mult, [hbA_, h_con], [h_Aka])
                            tt('dve', NAbr[:, h0:h0 + 3:2, :], bB_.rearrange("p (a b) -> p a b", a=2), mk, ALU.mult, [hbB_, h_con], [h_NAbr])
                    (Xb, hXb) = BT["Xb"]
                    X, hX = F("Asb")

                    def s0_terms(bX_, hbX_, which, post):
                        if not smp:
                            for h in range(8):
                                p4, base = h // 2, 64 * (h % 2)
                                mm(bX_[:, h * 64:(h + 1) * 64], arT[base:base + 64, p4, which, :], S_b[base:base + 64, p4, :],
                                   True, False, [h_arT, h_Sb], [hbX_])
                                post(h, True)
                        else:
                            for h2 in range(2):
                                base = 64 * h2
                                bP, hbP = bank()
                                for p4 in range(4):
                                    for s in range(16):
                                        mm(bP[base:base + 64, p4 * 128 + s * 8:p4 * 128 + s * 8 + 8], S_sb[base:base + 64, s, p4, :],
                                           arT[base:base + 64, p4, which, s * 8:(s + 1) * 8], True, True, [h_Ssb, h_arT], [hbP])
                                cp('act', P1sb[base:base + 64, :, :], bP[base:base + 64, :].rearrange("p (a b) -> p a b", a=4), [hbP], [h_P1sb])
                            for p4 in range(4):
                                mm(bX_[:, p4 * 128:(p4 + 1) * 128], P1sb[:, p4, :], ident_b, True, False, [h_P1sb, h_con], [hbX_])
                                post(2 * p4, False)
                                post(2 * p4 + 1, True)
                    bX, hbX = bank()

                    def post_x(h, last):
                        mm(bX[:, h * 64:(h + 1) * 64], AkaAkr[:, h, 0:128], vb[:, h * 64:(h + 1) * 64], False, last, [h_Aka, hvb], [hbX])
                    s0_terms(bX, hbX, 0, post_x)
                    cp('act', X, bX, [hbX], [hX])
                    cp('dve', Xb, X, [hX], [hXb])
                    def b8(nm):
                        return FT[nm][0].bitcast(BF16).rearrange("p (a b) -> p a b", a=8), FT[nm][1]
                    (Tm, hTm), (TmT, hTmT), (Noff, hNo), (YT, hYT) = b8("kk"), b8("sq_"), b8("Zr"), b8("av")
                    idb = ident_b.unsqueeze(1).to_broadcast([128, 8, 128])
                    cp('act', Tm, idb, [h_con], [hTm])
                    cp('dve', TmT, idb, [h_con], [hTmT])
                    for lv in (range(3) if smp else range(7)):
                        moT = mrg_b[:, 896 + lv * 128:896 + (lv + 1) * 128].unsqueeze(1).to_broadcast([128, 4, 128])
                        for q in range(2):
                            bY_, hbY_ = bank()
                            for hh in range(4):
                                h = q * 4 + hh
                                mm(bY_[:, hh * 128:(hh + 1) * 128], NAbr[:, h, 0:128], TmT[:, h, :], True, True, [h_NAbr, hTmT], [hbY_])
                            tt('dve', YT[:, q * 4:q * 4 + 4, :], bY_.rearrange("p (a b) -> p a b", a=4), moT, ALU.mult, [hbY_, h_mrg], [hYT])
                        zb = []
                        for q in range(2):
                            bZ, hbZ = bank()
                            bZT, hbZT = bank()
                            for hh in range(4):
                                h = q * 4 + hh
                                mm(bZ[:, hh * 128:(hh + 1) * 128], YT[:, h, :], Tm[:, h, :], True, True, [hYT, hTm], [hbZ])
                                mm(bZT[:, hh * 128:(hh + 1) * 128], Tm[:, h, :], YT[:, h, :], True, True, [hYT, hTm], [hbZT])
                            zb.append((bZ, hbZ, bZT, hbZT))
                        for q in range(2):
                            bZ, hbZ, bZT, hbZT = zb[q]
                            tt('dve', Tm[:, q * 4:q * 4 + 4, :], bZ.rearrange("p (a b) -> p a b", a=4), Tm[:, q * 4:q * 4 + 4, :], ALU.add, [hbZ, hTm], [hTm])
                            tt('dve', TmT[:, q * 4:q * 4 + 4, :], bZT.rearrange("p (a b) -> p a b", a=4), TmT[:, q * 4:q * 4 + 4, :], ALU.add, [hbZT, hTmT], [hTmT])
                    bY, hbY = bank()
                    for h in range(8):
                        mm(bY[:, h * 64:(h + 1) * 64], Tm[:, h, :], Xb[:, h * 64:(h + 1) * 64], True, True, [hTm, hXb], [hbY])
                    cp('act', X, bY, [hbY], [hX])
                    cp('dve', Xb, X, [hX], [hXb])
                    bO, hbO = bank()

                    def post_o(h, last):
                        mm(bO[:, h * 64:(h + 1) * 64], AkaAkr[:, h, 128:256], vb[:, h * 64:(h + 1) * 64], False, False, [h_Aka, hvb], [hbO])
                        mm(bO[:, h * 64:(h + 1) * 64], NAbr[:, h, 128:256], Xb[:, h * 64:(h + 1) * 64], False, last, [h_NAbr, hXb], [hbO])
                    s0_terms(bO, hbO, 1, post_o)
                    (osb, hos) = F("dlt")
                    cp('act', osb, bO, [hbO], [hos])
                    if not smp:
                        bS, hbS = bank()
                        for h in range(8):
                            p4, base = h // 2, 64 * (h % 2)
                            mm(bS[base:base + 64, p4 * 64:(p4 + 1) * 64], kt[:, h * 64:(h + 1) * 64], vb[:, h * 64:(h + 1) * 64], True, False, [hkt, hvb], [hbS])
                            mm(bS[base:base + 64, p4 * 64:(p4 + 1) * 64], bt[:, h * 64:(h + 1) * 64], Xb[:, h * 64:(h + 1) * 64], False, True, [hbt, hXb], [hbS])
                        tt('dve', tmpS, bS[:, 0:256].rearrange("p (a b) -> p a b", a=4), st_S[:, l, :, :], ALU.add, [hbS, h_stS[l]], [h_tmpS])
                        tt('dve', st_S[:, l, :, :], tmpS, gcT[:, :, 0:1].to_broadcast([128, 4, 64]), ALU.mult, [h_tmpS, h_gc], [h_stS[l]])
                        cp('act', S_b, st_S[:, l, :, :], [h_stS[l]], [h_Sb])
                    else:
                        for s in range(16):
                            ss_ = s % 2
                            E.begin_group()
                            for h2 in range(2):
                                dma('sp', S_sf[h2 * 64:(h2 + 1) * 64, ss_, :, :], wkv_in[h2][:, s, :, :], (), [h_Ssf[ss_]])
                            E.end_group()
                            ts('dve', vm[:, ss_, :], vb, seqf[:, s:s + 1], ALU.mult, [hvb, h_con], [h_vm[ss_]])
                            ts('dve', um[:, ss_, :], Xb, seqf[:, s:s + 1], ALU.mult, [hXb, h_con], [h_um[ss_]])
                            bS, hbS = bank()
                            for h in range(8):
                                p4, base = h // 2, 64 * (h % 2)
                                mm(bS[base:base + 64, p4 * 64:(p4 + 1) * 64], kt[:, h * 64:(h + 1) * 64], vm[:, ss_, h * 64:(h + 1) * 64], True, False, [hkt, h_vm[ss_]], [hbS])
                                mm(bS[base:base + 64, p4 * 64:(p4 + 1) * 64], bt[:, h * 64:(h + 1) * 64], um[:, ss_, h * 64:(h + 1) * 64], False, True, [hbt, h_um[ss_]], [hbS])
                            tt('dve', S_sf[:, ss_, :, :], bS[:, 0:256].rearrange("p (a b) -> p a b", a=4), S_sf[:, ss_, :, :], ALU.add, [hbS, h_Ssf[ss_]], [h_Ssf[ss_]])
                            tt('dve', S_sf[:, ss_, :, :], S_sf[:, ss_, :, :], gcT[:, :, s:s + 1].to_broadcast([128, 4, 64]), ALU.mult, [h_Ssf[ss_], h_gc], [h_Ssf[ss_]])
                            E.begin_group()
                            for h2 in range(2):
                                dma('sp', wkv_out[h2][:, s, :, :], S_sf[h2 * 64:(h2 + 1) * 64, ss_, :, :], [h_Ssf[ss_]], ())
                            E.end_group()
                    (osq, hoq), (on, hon), (bon, hbon) = F("G"), F("Ginv"), F("Gp")
                    o3 = osb.rearrange("p (h d) -> p h d", h=8)
                    red(sm8[:, 2, :], o3, [hos], [h_sm8])
                    tt('pool', osq, osb, osb, ALU.mult, [hos], [hoq])
                    red(sm8[:, 3, :], osq.rearrange("p (h d) -> p h d", h=8), [hoq], [h_sm8])
                    ts('dve', sm8[:, 2, :], sm8[:, 2, :], 1.0 / 64.0, ALU.mult, [h_sm8], [h_sm8])
                    tt('dve', sm8[:, 4, :], sm8[:, 2, :], sm8[:, 2, :], ALU.mult, [h_sm8], [h_sm8])
                    stt(sm8[:, 3, :], sm8[:, 3, :], 1.0 / 64.0, sm8[:, 4, :], ALU.mult, ALU.subtract, [h_sm8], [h_sm8])
                    act(sm8[:, 3, :], sm8[:, 3, :], AF.Sqrt, [h_sm8, h_con], [h_sm8], scale=1.0, bias=epsc(2))
                    recip(sm8[:, 3, :], sm8[:, 3, :], [h_sm8], [h_sm8])
                    on3 = on.rearrange("p (h d) -> p h d", h=8)
                    tt('dve', on3, o3, sm8[:, 2, :].unsqueeze(2).to_broadcast([128, 8, 64]), ALU.subtract, [hos, h_sm8], [hon])
                    tt('dve', on3, on3, sm8[:, 3, :].unsqueeze(2).to_broadcast([128, 8, 64]), ALU.mult, [hon, h_sm8], [hon])
                    tt('dve', on, on, gg_t, ALU.mult, [hon, h_bc], [hon])
                    tt('dve', on, on, gb_t, ALU.add, [hon, h_bc], [hon])
                    tt('dve', bon.rearrange("p (h d) -> p h d", h=8), vb.rearrange("p (h d) -> p h d", h=8),
                       sm8[:, 1, :].unsqueeze(2).to_broadcast([128, 8, 64]), ALU.mult, [hvb, h_sm8], [hbon])
                    tt('dve', on, on, bon, ALU.add, [hon, hbon], [hon])
                    ro, hro = BT["ro"]
                    tt('dve', ro, on, gate, ALU.mult, [hon, hgate], [hro])
                    bT_, hbT_ = bank()
                    tv = bT_.bitcast(BF16)[:, 0:512].rearrange("p (a b) -> p a b", a=4)
                    for p4 in range(4):
                        tr(tv[:, p4, :], ro[:, p4 * 128:(p4 + 1) * 128], ident_b, [hro, h_con], [hbT_])
                    cp('act', mixT[:, 4:8, tc:tc + 128], tv, [hbT_], [h_mix[g]])

                ntl = 0
                if DBG.get('rwtiles', 99) < 99 or DBG.get('rwonly') is not None:
                    for g, (c, n, kind) in enumerate(groups):
                        mset('pool', mixT[:, 4:8, c:c + n], 0.0, [h_mix[g]])
                for g, (c, n, kind) in enumerate(groups):
                    for ti in range(n // 128):
                        if ntl < DBG.get('rwtiles', 99) and (DBG.get('rwonly') is None or half == 0 or ntl == DBG['rwonly']):
                            rw_tile(c + ti * 128, kind, g)
                        ntl += 1
                if half == 1:
                    for h2 in range(2):
                        dma('sp', o_wkvp[l].rearrange("(p4 h2) j i -> h2 j p4 i", h2=2)[h2], st_S[h2 * 64:(h2 + 1) * 64, l, :, :],
                            [h_stS[l]], ())
                E.barrier()
                AR.pop()

            def mixer_sub():
                AR.push()
                NT_ = 1024 + (128 if (half == 1 and DBG['samp']) else 0)
                winv = d_win[l].rearrange("(k p) n -> p k n", p=128)
                mixT = AR.alloc([8, NT_], BF16); h_mix = [E.H("mix%d" % g) for g in range(len(groups))]
                AR.push()
                sq = AR.alloc([8, 512], BF16); h_t1 = E.H("t1m")
                rstd = AR.alloc([512], F32)
                hl = AR.alloc([8], F32); h_hl = E.H("hl")
                hs_ = AR.alloc([8, 16], F32); h_hs = E.H("hs")
                if half == 1:
                    cp('dve', hT[:, :, 0], st_shift[:, l, :], [h_stshift[l]], [h_h[0]])
                for g, (c, n, kind) in enumerate(groups):
                    rms_rstd([xT[:, k, c:c + n] for k in range(8)], n, [h_x[g]], sq, rstd[:, 0:n], h_t1)
                    for k in range(8):
                        stt(hT[:, k, 1 + c:1 + c + n], xT[:, k, c:c + n], ngc(0, k), rstd[:, 0:n],
                            ALU.mult, ALU.mult, [h_x[g], h_t1, h_pp], [h_h[g]])
                    if kind == 'p' and g == 1:
                        if half == 0:
                            cp('pool', st_shift[:, l, :], hT[:, :, 1024], [h_h[g]], [h_stshift[l]])
                        else:
                            for k in range(8):
                                stt(hl[:, k:k + 1], xT[:, k, 1023:1024], ngc(0, k), rstd[:, 511:512], ALU.mult, ALU.mult,
                                    [h_x[g], h_t1, h_pp], [h_hl])
                            dma('sp', o_shiftp[l], hl, [h_hl], ())
                    if kind == 's':
                        xs7 = xT[:, :, 1024:1152].rearrange("p k (b t) -> p k b t", t=8)
                        r7 = rstd[:, 0:128].rearrange("p (b t) -> p b t", t=8)[:, :, 7]
                        for k in range(8):
                            stt(hs_[:, k, :], xs7[:, k, :, 7], ngc(0, k), r7, ALU.mult, ALU.mult,
                                [h_x[g], h_t1, h_pp], [h_hs])
                        dma('sp', o_shifts[l], hs_, [h_hs], ())
                E.barrier()
                AR.pop()
                AR.push()
                Wc = AR.alloc([8, 1024], BF16); h_Wc = E.H("Wc")
                E.begin_group()
                for k in range(8):
                    dma('pool', Wc[:, k, :], winv[:, k, 0:1024], (), [h_Wc])
                E.end_group()
                uf = AR.alloc([4, 512], F32); h_uf = E.H("uf")
                sg = AR.alloc([512], F32); h_sg = E.H("sg")
                uT = AR.alloc([4, 542], BF16); h_uT = E.H("uT")
                uTs = AR.alloc([4, 16, 38], BF16); h_uTs = E.H("uTs")
                Dg4 = AR.alloc([4, 31, 128], BF16); h_Dg = E.H("Dg")
                for c4 in range(4):
                    o_cw = PP_CW + (l * 4 + c4) * 31
                    tt('dve' if c4 % 2 == 0 else 'pool', Dg4[:, c4, :, :], ident_b.unsqueeze(1).to_broadcast([128, 31, 128]),
                       pp[:, o_cw:o_cw + 31].unsqueeze(2).to_broadcast([128, 31, 128]), ALU.mult, [h_con, h_pp], [h_Dg])
                ysb = AR.alloc([4, 512], F32); h_ysb = E.H("ysb")
                ybf = AR.alloc([4, 512], BF16); h_ybf = E.H("ybf")
                ysq = AR.alloc([4, 512], BF16); h_ysq = E.H("ysq")
                mean = AR.alloc([512], F32); h_mean = E.H("mean")
                m2 = AR.alloc([512], F32); h_m2 = E.H("m2")
                rs2 = AR.alloc([512], F32); h_rs2 = E.H("rs2")
                tl = AR.alloc([512], F32); h_tl = E.H("tl")
                for g, (c, n, kind) in enumerate(groups):
                    if kind == 'p':
                        if g == 0:
                            cp('pool', uT[:, :, 0:30], st_conv[:, l, :, :], [h_stconv[l]], [h_uT])
                        else:
                            cp('pool', uT[:, :, 0:30], uT[:, :, 512:542], [h_uT], [h_uT])
                    else:
                        E.begin_group()
                        for c4 in range(4):
                            dma('pool', uTs[:, c4, :, 0:30],
                                d_conv[l][:, c4 * 128:(c4 + 1) * 128, :].rearrange("b p t -> p b t"), (), [h_uTs])
                        E.end_group()
                    for c4 in range(4):
                        pbv, hbv = bank()
                        for k in range(8):
                            mm(pbv[:, 0:n], Wc[:, k, c4 * 128:(c4 + 1) * 128], hT[:, k, 1 + c:1 + c + n], k == 0, k == 7,
                               [h_Wc, h_h[g]], [hbv])
                        pbg, hbg = bank()
                        for k in range(8):
                            mm(pbg[:, 0:n], Wc[:, k, 512 + c4 * 128:512 + (c4 + 1) * 128], hT[:, k, 1 + c:1 + c + n],
                               k == 0, k == 7, [h_Wc, h_h[g]], [hbg])
                        act(sg[:, 0:n], pbg[:, 0:n], AF.Sigmoid, [hbg], [h_sg])
                        tt('dve', uf[:, c4, 0:n], pbv[:, 0:n], sg[:, 0:n], ALU.mult, [hbv, h_sg], [h_uf])
                        if kind == 'p':
                            cp('act', uT[:, c4, 30:30 + n], uf[:, c4, 0:n], [h_uf], [h_uT])
                        else:
                            cp('act', uTs[:, c4, :, 30:38], uf[:, c4, 0:128].rearrange("p (b t) -> p b t", t=8),
                               [h_uf], [h_uTs])
                    if kind == 'p' and g == 1:
                        if half == 0:
                            cp('pool', st_conv[:, l, :, :], uT[:, :, 512:542], [h_uT], [h_stconv[l]])
                        else:
                            dma('sp', o_convp[l].rearrange("(c p) t -> p c t", p=128), uf[:, :, 482:512], [h_uf], ())
                    if kind == 's':
                        dma('sp', o_convs[l][:, :, 0:22], d_conv[l][:, :, 8:30], (), ())
                        for c4 in range(4):
                            dma('sp', o_convs[l][:, c4 * 128:(c4 + 1) * 128, 22:30].rearrange("b p t -> p b t"),
                                uf[:, c4, 0:128].rearrange("p (b t) -> p b t", t=8), [h_uf], ())
                    for c4 in range(4):
                        pby, hby = bank()
                        for j in range(31):
                            rhs = uT[:, c4, j:j + n] if kind == 'p' else uTs[:, c4, :, j:j + 8]
                            mm(pby[:, 0:n], Dg4[:, c4, j, :], rhs, j == 0, j == 30, [h_Dg, h_uT, h_uTs], [hby])
                        cbc = pp[:, PP_CB + l * 4 + c4:PP_CB + l * 4 + c4 + 1]
                        act(ysb[:, c4, 0:n], pby[:, 0:n], AF.Identity, [hby, h_pp], [h_ysb], bias=cbc, scale=1.0)
                        act(ybf[:, c4, 0:n], pby[:, 0:n], AF.Identity, [hby, h_pp], [h_ybf], bias=cbc, scale=1.0)
                        act(ysq[:, c4, 0:n], pby[:, 0:n], AF.Square, [hby, h_pp], [h_ysq], bias=cbc, scale=1.0)
                    pb1, hb1 = bank()
                    for c4 in range(4):
                        mm(pb1[:, 0:n], onesb, ybf[:, c4, 0:n], c4 == 0, c4 == 3, [h_ybf, h_con], [hb1])
                    pb2, hb2 = bank()
                    for c4 in range(4):
                        mm(pb2[:, 0:n], onesb, ysq[:, c4, 0:n], c4 == 0, c4 == 3, [h_ysq, h_con], [hb2])
                    act(mean[:, 0:n], pb1[:, 0:n], AF.Identity, [hb1], [h_mean], scale=1.0 / 512.0, bias=epsc(3))
                    tt('dve', m2[:, 0:n], mean[:, 0:n], mean[:, 0:n], ALU.mult, [h_mean], [h_m2])
                    stt(m2[:, 0:n], pb2[:, 0:n], 1.0 / 512.0, m2[:, 0:n], ALU.mult, ALU.subtract, [hb2, h_m2], [h_m2])
                    act(rs2[:, 0:n], m2[:, 0:n], AF.Sqrt, [h_m2, h_con], [h_rs2], scale=1.0, bias=epsc(1))
                    recip(rs2[:, 0:n], rs2[:, 0:n], [h_rs2], [h_rs2])
                    for c4 in range(4):
                        tt('dve', tl[:, 0:n], ysb[:, c4, 0:n], mean[:, 0:n], ALU.subtract, [h_ysb, h_mean], [h_tl])
                        tt('dve', tl[:, 0:n], tl[:, 0:n], rs2[:, 0:n], ALU.mult, [h_tl, h_rs2], [h_tl])
                        ol = l * 4 + c4
                        ts('dve', tl[:, 0:n], tl[:, 0:n], pp[:, PP_LG + ol:PP_LG + ol + 1], ALU.mult, [h_tl, h_pp], [h_tl],
                           s2=pp[:, PP_LB + ol:PP_LB + ol + 1], op1=ALU.add)
                        act(mixT[:, c4, c:c + n], tl[:, 0:n], AF.Silu, [h_tl], [h_mix[g]])
                E.barrier()
                AR.pop()
                if DBG.get('rwkv'):
                    rwkv_sub(mixT, h_mix)
                else:
                    for g, (c, n, kind) in enumerate(groups):
                        mset('pool', mixT[:, 4:8, c:c + n], 0.0, [h_mix[g]])
                wout = AR.alloc([8, 1024], BF16); h_wout = E.H("wout")
                load_w('pool', wout, d_wout[l], h_wout)
                sq = AR.alloc([8, 512], BF16); h_t1 = E.H("t1m3")
                rstd = AR.alloc([512], F32)
                tmp = AR.alloc([512], F32); h_t2 = E.H("t2m")
                msb = AR.alloc([8, 512], F32); h_msb = E.H("msbm")
                for g, (c, n, kind) in enumerate(groups):
                    for oc in range(8):
                        pb, hb = bank()
                        for k in range(8):
                            mm(pb[:, 0:n], wout[:, k, oc * 128:(oc + 1) * 128], mixT[:, k, c:c + n], k == 0, k == 7,
                               [h_wout, h_mix[g]], [hb])
                        cp('act', msb[:, oc, 0:n], pb[:, 0:n], [hb], [h_msb])
                    post_res(1, g, [msb[:, k, 0:n] for k in range(8)], [h_msb], sq, rstd, tmp, h_t1, h_t2)
                E.barrier()
                AR.pop()

            if DBG.get('mix'):
                mixer_sub()
            if DBG['att']:
                attention_sub()
            if DBG['ffn']:
                ffn_sub()

        for half in range(DBG['halves']):
            groups = [(0, 512, 'p'), (512, 512, 'p')] + ([(1024, 128, 's')] if (half == 1 and DBG['samp']) else [])
            NTOK = 1024 + (128 if half == 1 else 0)
            h_x = [E.H("x%d" % g) for g in range(len(groups))]
            h_h = [E.H("h%d" % g) for g in range(len(groups))]
            xv = d_xT.rearrange("(k p) n -> p k n", p=128)
            yv = o_yT.rearrange("(k p) n -> p k n", p=128)
            for g, (c, n, kind) in enumerate(groups):
                src = xv[:, :, half * 1024 + c: half * 1024 + c + n] if kind == 'p' else xv[:, :, 2048:2176]
                dma('sp', xT[:, :, c:c + n], src, (), [h_x[g]])

            if half == 0:
                AR.push()
                mraw = AR.alloc([8, 256], F32); h_mraw = E.H("mraw")
                msq = AR.alloc([8, 256], BF16)
                mr = AR.alloc([256], F32); h_mt = E.H("mtmp")
                dma('sp', mraw, d_memT.rearrange("(k p) n -> p k n", p=128), (), [h_mraw])
                rms_rstd([mraw[:, k, :] for k in range(8)], 256, [h_mraw], msq, mr, h_mt)
                for k in range(8):
                    stt(memn[:, k, :], mraw[:, k, :], pp[:, PP_MG + k:PP_MG + k + 1], mr, ALU.mult, ALU.mult,
                        [h_mraw, h_mt, h_pp], [h_memn])
                E.barrier()
                AR.pop()

            for l in range(DBG['layers']):
                layer(half, l, groups, h_x, h_h)

            for g, (c, n, kind) in enumerate(groups):
                dst = yv[:, :, half * 1024 + c: half * 1024 + c + n] if kind == 'p' else yv[:, :, 2048:2176]
                dma('sp', dst, xT[:, :, c:c + n], [h_x[g]], ())
            E.barrier()

        E.finalize()
        import contextlib
        with contextlib.ExitStack() as es:
            csem = {e: es.enter_context(nc.semaphore("c_" + e)) for e in ENGS}
            dsem = [es.enter_context(nc.semaphore("d%d" % i)) for i in range(NDSEM)]
            block = es.enter_context(nc.Block())

            @block.tensor
            def _(e):
                E.emit('pe', e, csem, dsem)

            @block.scalar
            def _(e):
                E.emit('act', e, csem, dsem)

            @block.vector
            def _(e):
                E.emit('dve', e, csem, dsem)

            @block.gpsimd
            def _(e):
                E.emit('pool', e, csem, dsem)

            @block.sync
            def _(e):
                E.emit('sp', e, csem, dsem)
    return nc


_PROG = {}


def _merge_masks():
    m = np.zeros((128, 1792), np.float32)
    s, t = np.arange(128)[:, None], np.arange(128)[None, :]
    for lv in range(7):
        b = 1 << lv
        q = ((s // (2 * b)) == (t // (2 * b))) & ((s % (2 * b)) < b) & ((t % (2 * b)) >= b)
        m[:, lv * 128:(lv + 1) * 128] = q
        m[:, 896 + lv * 128:896 + (lv + 1) * 128] = q.T
    return m


def _consts():
    c = np.zeros((128, NCON), np.float32)
    idx = np.arange(128)
    c[:, C_ID:C_ID + 128] = np.eye(128, dtype=np.float32)
    s, t = idx[:, None], idx[None, :]
    c[:, C_MU:C_MU + 128] = (s < t)
    c[:, C_MI:C_MI + 128] = (s <= t)
    c[:, C_ML:C_ML + 128] = (t < s)
    same = (s // 8) == (t // 8)
    c[:, C_MUS:C_MUS + 128] = (s < t) & same
    c[:, C_MIS:C_MIS + 128] = (s <= t) & same
    c[:, C_MLS:C_MLS + 128] = (t < s) & same
    c[:, C_SEQ:C_SEQ + 16] = (idx[:, None] // 8) == np.arange(16)[None, :]
    c[:, C_EPS + 0] = 1e-6
    c[:, C_EPS + 1] = 1e-5
    c[:, C_EPS + 2] = 64e-5
    c[:, C_EPS + 3] = 0.0
    c[:, C_EPS + 4] = 1.0
    return c


def _prepare(inp):
    f = lambda a: np.ascontiguousarray(np.asarray(a), dtype=np.float32)
    I = {k: f(v) for k, v in inp.items()}
    pp = np.zeros((128, NPP), np.float32)
    ng = I['norm_g'].reshape(4, 6, 8, 128)
    pp[:, PP_NG:PP_NG + 192] = ng.transpose(3, 0, 1, 2).reshape(128, 192)
    pp[:, PP_MG:PP_MG + 8] = I['mem_norm_g'].reshape(8, 128).T
    cw = I['conv_w'].reshape(4, 31, 4, 128)
    pp[:, PP_CW:PP_CW + 496] = cw.transpose(3, 0, 2, 1).reshape(128, 496)
    pp[:, PP_CB:PP_CB + 16] = I['conv_b'].reshape(4, 4, 128).transpose(2, 0, 1).reshape(128, 16)
    pp[:, PP_LG:PP_LG + 16] = I['conv_ln_g'].reshape(4, 4, 128).transpose(2, 0, 1).reshape(128, 16)
    pp[:, PP_LB:PP_LB + 16] = I['conv_ln_b'].reshape(4, 4, 128).transpose(2, 0, 1).reshape(128, 16)
    pp[:, PP_MUD:PP_MUD + 8] = I['mu_shift'][:, 1536:1792].reshape(4, 2, 128).transpose(2, 0, 1).reshape(128, 8)
    bc = np.zeros((4, 128, NBC), np.float32)
    for j, nm in enumerate(['rwkv_k_k', 'rwkv_k_a', 'rwkv_r_k', 'rwkv_gn_g', 'rwkv_gn_b']):
        bc[:, :, j * 512:(j + 1) * 512] = I[nm].reshape(4, 1, 512)
    bc[:, :, 2560:] = I['mu_shift'].reshape(4, 1, 1792)
    rows = np.concatenate([I['rwkv_w0'], I['rwkv_a0']], axis=1)
    wup = np.concatenate([I['rwkv_w_up'], I['rwkv_a_up']], axis=1)
    gup = I['rwkv_g_up']
    consts = _consts()
    mrg = _merge_masks()
    in_maps = []
    for c in range(NCORE):
        sb = slice(16 * c, 16 * c + 16)
        xs = I['x_sample'][sb].reshape(128, 1024)
        xT = np.ascontiguousarray(np.concatenate([I['x_prompt'][c], xs], axis=0).T)
        m = {
            "xT": xT,
            "memT": np.ascontiguousarray(I['mem_prompt'][c].T),
            "kTc": np.ascontiguousarray(I['cache_mem_k'][:, sb].transpose(0, 1, 3, 4, 2)),
            "vc": np.ascontiguousarray(I['cache_mem_v'][:, sb].reshape(4, 16, 256, 1024)),
            "wkvT": np.ascontiguousarray(I['state_wkv'][:, sb].transpose(0, 1, 2, 4, 3)),
            "convT": np.ascontiguousarray(I['state_conv'][:, sb].transpose(0, 1, 3, 2)),
            "shiftT": np.ascontiguousarray(I['state_shift'][:, sb].transpose(0, 2, 1)),
            "w_in": I['w_in'], "w_out": I['w_out'], "w_q": I['w_q'], "w_k": I['w_k'], "w_v": I['w_v'],
            "w_o": I['w_o'], "w_ffn1": I['w_ffn1'], "w_ffn2": I['w_ffn2'],
            "pp": pp, "bc": bc, "rows": rows, "wup": wup, "gup": gup, "consts": consts, "mrg": mrg,
        }
        in_maps.append(m)
    return in_maps


def kernel(**inp):
    in_maps = _prepare(inp)
    if 'nc' not in _PROG:
        _PROG['nc'] = build_program()
    res = run_bass_kernel_spmd(_PROG['nc'], in_maps, core_ids=list(range(NCORE)))
    R = res.results
    y_prompt = np.zeros((8, 2048, 1024), np.float32)
    y_sample = np.zeros((128, 8, 1024), np.float32)
    nk = np.zeros((4, 8, 256, 4, 256), np.float32)
    nv = np.zeros((4, 8, 256, 4, 256), np.float32)
    wkvp = np.zeros((4, 8, 8, 64, 64), np.float32)
    convp = np.zeros((4, 8, 30, 512), np.float32)
    shiftp = np.zeros((4, 8, 1024), np.float32)
    wkvs = np.zeros((4, 128, 8, 64, 64), np.float32)
    convs = np.zeros((4, 128, 30, 512), np.float32)
    shifts = np.zeros((4, 128, 1024), np.float32)
    for c in range(NCORE):
        r = R[c]
        sb = slice(16 * c, 16 * c + 16)
        yT = np.asarray(r["yT"])
        y_prompt[c] = yT[:, :2048].T
        y_sample[sb] = yT[:, 2048:].T.reshape(16, 8, 1024)
        nk[:, c] = np.asarray(r["nk"]).reshape(4, 256, 4, 256)
        nv[:, c] = np.asarray(r["nv"]).reshape(4, 256, 4, 256)
        wkvp[:, c] = np.asarray(r["wkvp"]).transpose(0, 1, 3, 2)
        convp[:, c] = np.asarray(r["convp"]).transpose(0, 2, 1)
        shiftp[:, c] = np.asarray(r["shiftp"]).transpose(0, 2, 1).reshape(4, 1024)
        wkvs[:, sb] = np.asarray(r["wkvs"]).transpose(0, 1, 2, 4, 3)
        convs[:, sb] = np.asarray(r["convs"]).transpose(0, 1, 3, 2)
        shifts[:, sb] = np.asarray(r["shifts"]).transpose(0, 3, 2, 1).reshape(4, 16, 1024)
    return (y_prompt, y_sample, nk, nv, wkvp, convp, shiftp, wkvs, convs, shifts)
```

```python
import numpy as np
import concourse.bass as bass
import concourse.mybir as mybir
from concourse.bass_utils import run_bass_kernel_spmd

F32 = mybir.dt.float32
BF16 = mybir.dt.bfloat16
AF = mybir.ActivationFunctionType
ALU = mybir.AluOpType
AX = mybir.AxisListType

DEPTH = 4
NCORE = 8
CDEC = 0.6065306597126334

PP_NG, PP_MG, PP_CW, PP_CB, PP_LG, PP_LB, PP_MUD, NPP = 0, 192, 200, 696, 712, 728, 744, 752
C_ID, C_MU, C_MI, C_ML, C_MUS, C_MIS, C_MLS, C_SEQ, C_EPS, NCON = 0, 128, 256, 384, 512, 640, 768, 896, 912, 920
NBC = 2560 + 1792

ENGS = ['pe', 'act', 'dve', 'pool', 'sp']
HOLE_LO, HOLE_HI = 207 * 256, 207 * 256
NDSEM = 90
NSW = 45
DBGV = {}
DBG = dict(layers=4, halves=2, att=True, ffn=True, samp=True, stop=99, mix=True, rwkv=True)


class Hd:
    __slots__ = ('name', 'w', 'r', 'excl')

    def __init__(self, name, excl=False):
        self.name = name
        self.w = None
        self.r = []
        self.excl = excl


class Emitter:
    def __init__(self):
        self.ops = []
        self.handles = []
        self.last_compute = {e: None for e in ENGS}
        self.out_dma = []
        self.group = None
        self.ngroups_phase = 0
        self.groups = []

    def H(self, name, excl=False):
        h = Hd(name, excl)
        self.handles.append(h)
        return h

    def op(self, eng, fn, r=(), w=(), dma=False):
        i = len(self.ops)
        deps = set()
        w = list(w) + [h for h in r if h.excl]
        r = [h for h in r if not h.excl]
        for h in r:
            if h.w is not None:
                deps.add(h.w)
        for h in w:
            if h.w is not None:
                deps.add(h.w)
            deps.update(h.r)
        for h in r:
            h.r.append(i)
        for h in w:
            h.w = i
            h.r = []
        deps.discard(i)
        gid = None
        if dma:
            if self.group is not None:
                gid = self.group
            else:
                gid = len(self.groups)
                self.groups.append([])
                self.ngroups_phase += 1
            if self.group is not None:
                deps = set(d for d in deps if not (self.ops[d]['dma'] and self.ops[d]['gid'] == gid))
            self.groups[gid].append(i)
            self.out_dma.append(i)
        else:
            self.last_compute[eng] = i
        self.ops.append(dict(eng=eng, fn=fn, deps=deps, dma=dma, gid=gid, sig=False))
        return i

    def begin_group(self):
        self.group = len(self.groups)
        self.groups.append([])
        self.ngroups_phase += 1

    def end_group(self):
        self.group = None

    def barrier(self):
        deps = set(v for v in self.last_compute.values() if v is not None) | set(self.out_dma)
        for e in ENGS:
            self.ops.append(dict(eng=e, fn=None, deps=set(deps), dma=False, gid=None, sig=False, bar=True))
        for h in self.handles:
            h.w = None
            h.r = []
        self.out_dma = []
        self.ngroups_phase = 0
        self.ops[-1]['phase_end'] = True

    def finalize(self):
        ops = self.ops
        for j, o in enumerate(ops):
            for d in o['deps']:
                p = ops[d]
                if p['dma']:
                    continue
                if p['eng'] == 'pe' and o['eng'] == 'pe' and not o.get('bar'):
                    continue
                p['sig'] = True
        cnt = {e: 0 for e in ENGS}
        for o in ops:
            if o['sig']:
                cnt[o['eng']] += 1
                o['sigval'] = cnt[o['eng']]
        gsem = {}
        semcnt = [0] * NDSEM
        nxt = {'pool': 0, 'sp': NSW}
        lim = {'pool': NSW, 'sp': NDSEM}
        gfinal = {}
        for j, o in enumerate(ops):
            if o['dma']:
                g = o['gid']
                q = o['eng']
                if g not in gsem:
                    gsem[g] = nxt[q]
                    nxt[q] += 1
                    assert nxt[q] <= lim[q], (q, nxt[q])
                k = gsem[g]
                semcnt[k] += 16
                gfinal[g] = semcnt[k]
            if o.get('phase_end'):
                nxt = {'pool': 0, 'sp': NSW}
        self.gsem, self.gfinal = gsem, gfinal

    def emit(self, eng_name, engobj, csem, dsem):
        ops = self.ops
        waited = {}
        for o in ops:
            if o['eng'] != eng_name:
                continue
            for d in sorted(o['deps']):
                p = ops[d]
                if p['dma']:
                    key = ('d', self.gsem[p['gid']])
                    val = self.gfinal[p['gid']]
                    sem = dsem[self.gsem[p['gid']]]
                else:
                    if not p['sig']:
                        continue
                    key = ('c', p['eng'])
                    val = p['sigval']
                    sem = csem[p['eng']]
                if waited.get(key, 0) < val:
                    engobj.wait_ge(sem, val)
                    waited[key] = val
            if o['fn'] is None:
                continue
            ins = o['fn'](engobj)
            if o['dma']:
                ins.then_inc(dsem[self.gsem[o['gid']]], 16)
            elif o['sig']:
                ins.then_inc(csem[eng_name], 1)


class Arena:
    def __init__(self, ap, words):
        self.ap = ap
        self.words = words
        self.tops = [0, HOLE_HI]
        self.lims = [HOLE_LO, words]
        self.stack = []

    def alloc(self, free_shape, dtype):
        n = 1
        for s in free_shape:
            n *= s
        nbytes = n * (2 if dtype == BF16 else 4)
        w = (nbytes + 31) // 32 * 8
        for r in (0, 1):
            if self.tops[r] + w <= self.lims[r]:
                off = self.tops[r]
                self.tops[r] += w
                self.last = (off, n, dtype == BF16)
                break
        else:
            raise AssertionError(("arena overflow", self.tops, w, self.lims))
        v = self.ap[:, off:off + w]
        if dtype == BF16:
            v = v.bitcast(BF16)[:, 0:n]
        else:
            v = v[:, 0:n]
        if len(free_shape) == 2:
            v = v.rearrange("p (a b) -> p a b", a=free_shape[0])
        elif len(free_shape) == 3:
            v = v.rearrange("p (a b c) -> p a b c", a=free_shape[0], b=free_shape[1])
        elif len(free_shape) == 4:
            v = v.rearrange("p (a b c d) -> p a b c d", a=free_shape[0], b=free_shape[1], c=free_shape[2])
        return v

    def push(self):
        self.stack.append(list(self.tops))

    def pop(self):
        self.tops = self.stack.pop()


def build_program():
    nc = bass.Bass("TRN2", target_bir_lowering=False)
    E = Emitter()

    def din(name, shape):
        return nc.dram_tensor(name, list(shape), F32, kind="ExternalInput").ap()

    def dout(name, shape):
        return nc.dram_tensor(name, list(shape), F32, kind="ExternalOutput").ap()

    d_xT = din("xT", [1024, 2176])
    d_memT = din("memT", [1024, 256])
    d_kT = din("kTc", [4, 16, 4, 256, 256])
    d_v = din("vc", [4, 16, 256, 1024])
    d_wkv = din("wkvT", [4, 16, 8, 64, 64])
    d_conv = din("convT", [4, 16, 512, 30])
    d_shift = din("shiftT", [4, 1024, 16])
    d_win = din("w_in", [4, 1024, 2816])
    d_wout = din("w_out", [4, 1024, 1024])
    d_wq = din("w_q", [4, 1024, 1024])
    d_wk = din("w_k", [4, 1024, 1024])
    d_wv = din("w_v", [4, 1024, 1024])
    d_wo = din("w_o", [4, 1024, 1024])
    d_w1 = din("w_ffn1", [4, 1024, 4096])
    d_w2 = din("w_ffn2", [4, 4096, 1024])
    d_pp = din("pp", [128, NPP])
    d_bc = din("bc", [4, 128, NBC])
    d_rows = din("rows", [4, 1024])
    d_wup = din("wup", [4, 128, 512])
    d_gup = din("gup", [4, 128, 512])
    d_con = din("consts", [128, NCON])
    d_mrg = din("mrg", [128, 1792])

    o_yT = dout("yT", [1024, 2176])
    o_nk = dout("nk", [4, 256, 1024])
    o_nv = dout("nv", [4, 256, 1024])
    o_wkvp = dout("wkvp", [4, 8, 64, 64])
    o_convp = dout("convp", [4, 512, 30])
    o_shiftp = dout("shiftp", [4, 128, 8])
    o_wkvs = dout("wkvs", [4, 16, 8, 64, 64])
    o_convs = dout("convs", [4, 16, 512, 30])
    o_shifts = dout("shifts", [4, 128, 8, 16])

    ARENA_WORDS = 207 * 256
    with (
        nc.sbuf_tensor("arena", [128, ARENA_WORDS], F32) as arena_t,
        nc.psum_tensor("ps0", [128, 512], F32) as ps0, nc.psum_tensor("ps1", [128, 512], F32) as ps1,
        nc.psum_tensor("ps2", [128, 512], F32) as ps2, nc.psum_tensor("ps3", [128, 512], F32) as ps3,
        nc.psum_tensor("ps4", [128, 512], F32) as ps4, nc.psum_tensor("ps5", [128, 512], F32) as ps5,
        nc.psum_tensor("ps6", [128, 512], F32) as ps6, nc.psum_tensor("ps7", [128, 512], F32) as ps7,
    ):
        AR = Arena(arena_t[:, :], ARENA_WORDS)
        banks = [(p[:, :], E.H("bank%d" % i, excl=True)) for i, p in enumerate([ps0, ps1, ps2, ps3, ps4, ps5, ps6, ps7])]
        bank_ctr = [0]

        def bank():
            b = banks[bank_ctr[0] % 8]
            bank_ctr[0] += 1
            return b

        def mm(out, lhsT, rhs, start, stop, r, w):
            E.op('pe', lambda e: e.matmul(out, lhsT, rhs, start=start, stop=stop), r, w)

        def tr(out, in_, ident, r, w):
            E.op('pe', lambda e: e.transpose(out, in_, ident), r, w)

        def act(out, in_, func, r, w, scale=None, bias=None):
            kw = {}
            if scale is not None:
                kw['scale'] = scale
            if bias is not None:
                kw['bias'] = bias
            E.op('act', lambda e: e.activation(out, in_, func, **kw), r, w)

        def tt(eng, out, in0, in1, op, r, w):
            E.op(eng, lambda e: e.tensor_tensor(out, in0, in1, op), r, w)

        def ts(eng, out, in0, s1, op0, r, w, s2=None, op1=None):
            if op1 is None:
                E.op(eng, lambda e: e.tensor_scalar(out, in0, s1, None, op0), r, w)
            else:
                E.op(eng, lambda e: e.tensor_scalar(out, in0, s1, s2, op0, op1), r, w)

        def stt(out, in0, scalar, in1, op0, op1, r, w):
            E.op('dve', lambda e: e.scalar_tensor_tensor(out, in0, scalar, in1, op0, op1), r, w)

        def cp(eng, out, in_, r, w):
            if eng == 'act':
                E.op('act', lambda e: e.copy(out, in_), r, w)
            else:
                E.op(eng, lambda e: e.tensor_copy(out, in_), r, w)

        def red(out, in_, r, w):
            E.op('dve', lambda e: e.tensor_reduce(out, in_, AX.X, ALU.add), r, w)

        def recip(out, in_, r, w):
            E.op('dve', lambda e: e.reciprocal(out, in_), r, w)

        def mset(eng, ap, val, w):
            E.op(eng, lambda e: e.memset(ap, val), (), w)

        def dma(q, out, in_, r, w):
            E.op(q, lambda e: e.dma_start(out=out, in_=in_), r, w, dma=True)

        con_f = AR.alloc([NCON], F32); h_con = E.H("con")
        con_b = AR.alloc([NCON], BF16)
        pp = AR.alloc([NPP], F32); h_pp = E.H("pp")
        xT = AR.alloc([8, 1152], F32)
        hT = AR.alloc([8, 1153], BF16)
        memn = AR.alloc([8, 256], BF16); h_memn = E.H("memn")
        st_S = AR.alloc([DEPTH, 4, 64], F32); h_stS = [E.H("stS%d" % l) for l in range(DEPTH)]
        st_conv = AR.alloc([DEPTH, 4, 30], BF16); h_stconv = [E.H("stc%d" % l) for l in range(DEPTH)]
        st_shift = AR.alloc([DEPTH, 8], BF16); h_stshift = [E.H("sts%d" % l) for l in range(DEPTH)]

        ident_b = con_b[:, C_ID:C_ID + 128]
        ident_f = con_f[:, C_ID:C_ID + 128]
        ones_b = None

        def epsc(i):
            return con_f[:, C_EPS + i:C_EPS + i + 1]

        dma('sp', con_f, d_con, (), [h_con])
        dma('pool', con_b, d_con, (), [h_con])
        dma('sp', pp, d_pp, (), [h_pp])
        onesb = AR.alloc([128], BF16)
        mset('dve', onesb, 1.0, [h_con])
        mset('dve', st_S, 0.0, h_stS)
        mset('dve', st_conv, 0.0, h_stconv)
        mset('dve', st_shift, 0.0, h_stshift)
        h_hinit = E.H('hinit')
        mset('dve', hT[:, :, 0:1], 0.0, [h_hinit])

        def rms_rstd(src_list, n, rsrc, sqbuf, rstd_out, h_tmp, eps_i=0, dim=1024.0):
            pb, hb = bank()
            nk = len(src_list)
            for k, s in enumerate(src_list):
                act(sqbuf[:, k, 0:n], s, AF.Square, rsrc, [h_tmp])
            for k in range(nk):
                mm(pb[:, 0:n], onesb, sqbuf[:, k, 0:n], k == 0, k == nk - 1, [h_tmp, h_con], [hb])
            act(rstd_out, pb[:, 0:n], AF.Sqrt, [hb, h_con], [h_tmp], scale=1.0 / dim, bias=epsc(eps_i))
            recip(rstd_out, rstd_out, [h_tmp], [h_tmp])


        def layer(half, l, groups, h_x, h_h):
            def ngc(i, k):
                o = PP_NG + l * 48 + i * 8 + k
                return pp[:, o:o + 1]

            def norm_to_h(i, sq, rstd, h_tmp):
                for g, (c, n, kind) in enumerate(groups):
                    rms_rstd([xT[:, k, c:c + n] for k in range(8)], n, [h_x[g]], sq, rstd[:, 0:n], h_tmp)
                    for k in range(8):
                        stt(hT[:, k, 1 + c:1 + c + n], xT[:, k, c:c + n], ngc(i, k), rstd[:, 0:n],
                            ALU.mult, ALU.mult, [h_x[g], h_tmp, h_pp], [h_h[g]])

            def post_res(i, g, src_list, rsrc, sq, rstd, tmp, h_tmp, h_tmp2):
                c, n, kind = groups[g]
                rms_rstd(src_list, n, rsrc, sq, rstd[:, 0:n], h_tmp)
                for k in range(8):
                    tt('dve', src_list[k], src_list[k], rstd[:, 0:n], ALU.mult, rsrc + [h_tmp], rsrc)
                for k in range(8):
                    stt(xT[:, k, c:c + n], src_list[k], ngc(i, k), xT[:, k, c:c + n], ALU.mult, ALU.add,
                        rsrc + [h_pp, h_x[g]], [h_x[g]])

            def load_w(q, dst, dsrc, h, ncol_split=None):
                v = dsrc.rearrange("(k p) n -> p k n", p=128)
                E.begin_group()
                for k in range(8):
                    dma(q, dst[:, k, :], v[:, k, :], (), [h])
                E.end_group()

            def attention_sub():
                AR.push()
                wq = AR.alloc([8, 1024], BF16); h_wq = E.H("wq")
                wk = AR.alloc([8, 1024], BF16); h_wk = E.H("wk")
                wv = AR.alloc([8, 1024], BF16); h_wv = E.H("wv")
                wo = AR.alloc([8, 1024], BF16); h_wo = E.H("wo")
                load_w('pool', wq, d_wq[l], h_wq)
                if DBG.get('nw', 4) > 1:
                    load_w('pool', wk, d_wk[l], h_wk)
                    load_w('pool', wv, d_wv[l], h_wv)
                    load_w('pool', wo, d_wo[l], h_wo)
                if DBG['stop'] <= 0:
                    E.barrier(); AR.pop(); return
                if DBG.get('wbar'):
                    E.barrier()
                stg = AR.alloc([2, 512], F32); h_stg = [E.H("stg0"), E.H("stg1")]
                sq = AR.alloc([8, 512], BF16); h_t1 = E.H("t1")
                rstd = AR.alloc([512], F32)
                tmp = AR.alloc([512], F32); h_t2 = E.H("t2")
                msb = AR.alloc([8, 512], F32); h_msb = E.H("msb")
                KT = AR.alloc([8, 256], BF16); h_KT = E.H("KT")
                Vb = AR.alloc([2, 1024], BF16); h_Vb = E.H("Vb")
                qT = sq; h_qT = h_t1
                oT = AR.alloc([8, 512], BF16); h_oT = E.H("oT")
                if DBG.get('stghi'):
                    stg = AR.alloc([2, 512], F32)
                PT = AR.alloc([2, 512], BF16); h_PT = E.H("PT")
                rs = AR.alloc([512], F32); h_rs = E.H("rs")
                if not DBG.get('skip1'):
                    norm_to_h(2, sq, rstd, h_t1)
                if DBG['stop'] <= 1:
                    E.barrier(); AR.pop(); return
                for oc in range(0 if DBG.get('skip2') else 8):
                    pb, hb = bank()
                    for k in range(8):
                        mm(pb[:, 0:256], wk[:, k, oc * 128:(oc + 1) * 128], memn[:, k, :], k == 0, k == 7,
                           [h_wk, h_memn], [hb])
                    cp('act', KT[:, oc, :], pb[:, 0:256], [hb], [h_KT])
                if DBG['stop'] <= 2:
                    E.barrier(); AR.pop(); return
                si = 0
                for which, wsrc, hw, dout_ in ((0, wv, h_wv, o_nv), (1, wk, h_wk, o_nk)):
                    if which == 1 and half == 1:
                        continue
                    if which >= DBG.get('vw', 2):
                        continue
                    for mc in range(DBG.get('vmc', 2)):
                        for hf in range(DBG.get('vhf', 2)):
                            pb, hb = bank()
                            for k in range(8):
                                mm(pb, memn[:, k, mc * 128:(mc + 1) * 128], wsrc[:, k, hf * 512:(hf + 1) * 512],
                                   k == 0, k == 7, [hw, h_memn], [hb])
                            if which == 0 and not DBG.get('novb'):
                                cp('act', Vb[:, mc, hf * 512:(hf + 1) * 512], pb, [hb], [h_Vb])
                            if half == 0:
                                s = (si % 2) if not DBG.get('stg0') else 0
                                si += 1
                                if not DBG.get('nostg'):
                                    cp(DBG.get('stgeng', 'dve'), (msb if DBG.get('stgmsb') else stg)[:, s, :], pb, ([h_Vb] if DBG.get('chain') else [hb]), [h_stg[s]])
                                if not DBG.get('nodma'):
                                    dma('sp', dout_[l, mc * 128:(mc + 1) * 128, hf * 512:(hf + 1) * 512], stg[:, s, :],
                                        [h_stg[s]], ())
                if DBG['stop'] <= 3:
                    E.barrier(); AR.pop(); return
                for g, (c, n, kind) in enumerate(groups):
                    for oc in range(8):
                        pb, hb = bank()
                        for k in range(8):
                            mm(pb[:, 0:n], wq[:, k, oc * 128:(oc + 1) * 128], hT[:, k, 1 + c:1 + c + n], k == 0, k == 7,
                               [h_wq, h_h[g]], [hb])
                        cp('act', qT[:, oc, 0:n], pb[:, 0:n], [hb], [h_qT])
                    if kind == 'p':
                        for hd in range(4):
                            for mc in range(2):
                                pb, hb = bank()
                                for dc in range(2):
                                    mm(pb[:, 0:n], KT[:, 2 * hd + dc, mc * 128:(mc + 1) * 128], qT[:, 2 * hd + dc, 0:n],
                                       dc == 0, dc == 1, [h_KT, h_qT], [hb])
                                act(PT[:, mc, 0:n], pb[:, 0:n], AF.Exp, [hb], [h_PT], scale=1.0 / 16.0)
                            pb, hb = bank()
                            for mc in range(2):
                                mm(pb[:, 0:n], onesb, PT[:, mc, 0:n], mc == 0, mc == 1, [h_PT, h_con], [hb])
                            recip(rs[:, 0:n], pb[:, 0:n], [hb], [h_rs])
                            for dc in range(2):
                                pb, hb = bank()
                                for mc in range(2):
                                    mm(pb[:, 0:n], Vb[:, mc, (2 * hd + dc) * 128:(2 * hd + dc + 1) * 128], PT[:, mc, 0:n],
                                       mc == 0, mc == 1, [h_Vb, h_PT], [hb])
                                tt('dve', oT[:, 2 * hd + dc, 0:n], pb[:, 0:n], rs[:, 0:n], ALU.mult, [hb, h_rs], [h_oT])
                    else:
                        AR.push()
                        KTs = AR.alloc([1, 4, 2, 256], BF16); h_KTs = [E.H("KTs0")]
                        Vs = AR.alloc([1, 2, 1024], BF16); h_Vs = [E.H("Vs0")]
                        PTs = PT.rearrange("p a (b h m t) -> p (a b) h m t", b=8, h=4, m=2); h_PTs = h_PT
                        rss = rs.rearrange("p (b h t) -> p b h t", b=16, h=4); h_rss = h_rs
                        for bh in range(2):
                            pbs, hbs = bank()
                            pbo0, hbo0 = bank()
                            sview = pbs.rearrange("p (b h m t) -> p b h m t", b=8, h=4, m=2)
                            oview = pbo0.rearrange("p (b h d t) -> p b h d t", b=8, h=4, d=2)
                            for bb in range(8):
                                b = bh * 8 + bb
                                s = 0
                                dma('pool', KTs[:, s], d_kT[l, b].rearrange("h (dc p) m -> p h dc m", p=128), (), [h_KTs[s]])
                                dma('pool', Vs[:, s], d_v[l, b].rearrange("(mc p) n -> p mc n", p=128), (), [h_Vs[s]])
                                for hd in range(4):
                                    for mc in range(2):
                                        for dc in range(2):
                                            mm(sview[:, bb, hd, mc, :], KTs[:, s, hd, dc, mc * 128:(mc + 1) * 128],
                                               qT[:, 2 * hd + dc, b * 8:(b + 1) * 8], dc == 0, dc == 1,
                                               [h_KTs[s], h_qT], [hbs])
                                act(PTs[:, b], sview[:, bb], AF.Exp, [hbs], [h_PTs], scale=1.0 / 16.0)
                                for hd in range(4):
                                    for dc in range(2):
                                        for mc in range(2):
                                            mm(oview[:, bb, hd, dc, :], Vs[:, s, mc, (2 * hd + dc) * 128:(2 * hd + dc + 1) * 128],
                                               PTs[:, b, hd, mc, :], mc == 0, mc == 1, [h_Vs[s], h_PTs], [hbo0])
                            pbr, hbr = bank()
                            rview = pbr[:, 0:256].rearrange("p (b h t) -> p b h t", b=8, h=4)
                            for mc in range(2):
                                mm(rview, onesb, PTs[:, bh * 8:(bh + 1) * 8, :, mc, :], mc == 0, mc == 1, [h_PTs, h_con], [hbr])
                            recip(rss[:, bh * 8:(bh + 1) * 8], rview, [hbr], [h_rss])
                            for dc in range(2):
                                ov = oT[:, :, 0:128].rearrange("p (h d) (b t) -> p d b h t", d=2, t=8)[:, dc, bh * 8:(bh + 1) * 8]
                                tt('dve', ov, oview[:, :, :, dc, :], rss[:, bh * 8:(bh + 1) * 8], ALU.mult,
                                   [hbo0, h_rss], [h_oT])
                        AR.pop()
                    for oc in range(8):
                        pb, hb = bank()
                        for k in range(8):
                            mm(pb[:, 0:n], wo[:, k, oc * 128:(oc + 1) * 128], oT[:, k, 0:n], k == 0, k == 7,
                               [h_wo, h_oT], [hb])
                        cp('act', msb[:, oc, 0:n], pb[:, 0:n], [hb], [h_msb])
                    post_res(3, g, [msb[:, k, 0:n] for k in range(8)], [h_msb], sq, rstd, tmp, h_t1, h_t2)
                E.barrier()
                AR.pop()


            def ffn_sub():
                AR.push()
                NT_ = 1024 + (128 if half == 1 else 0)
                facc = AR.alloc([8, NT_], F32); h_f = [E.H("f%d" % g) for g in range(len(groups))]
                w1 = AR.alloc([2, 8, 512], BF16); h_w1 = [E.H("w1a"), E.H("w1b")]
                w2 = AR.alloc([2, 4, 1024], BF16); h_w2 = [E.H("w2a"), E.H("w2b")]
                sq = AR.alloc([8, 512], BF16); h_t1 = E.H("t1f")
                rstd = AR.alloc([512], F32)
                tmp = AR.alloc([512], F32); h_t2 = E.H("t2f")
                rl = AR.alloc([2, 512], BF16); h_rl = [E.H("rl0"), E.H("rl1")]
                hid = AR.alloc([2, 4, 512], BF16); h_hid = [E.H("hid0"), E.H("hid1")]
                norm_to_h(4, sq, rstd, h_t1)
                w1v = d_w1[l].rearrange("(k p) n -> p k n", p=128)
                w2v = d_w2[l].rearrange("(c p) n -> p c n", p=128)
                items = [(hg, g) for hg in range(8) for g in range(len(groups))]

                def ffn1(idx):
                    hg, g = items[idx]
                    c, n, kind = groups[g]
                    s, hs = hg % 2, idx % 2
                    if g == 0:
                        E.begin_group()
                        for k in range(8):
                            dma('pool', w1[:, s, k, :], w1v[:, k, hg * 512:(hg + 1) * 512], (), [h_w1[s]])
                        E.end_group()
                        E.begin_group()
                        for cc in range(4):
                            dma('pool', w2[:, s, cc, :], w2v[:, hg * 4 + cc, :], (), [h_w2[s]])
                        E.end_group()
                    for hc in range(4):
                        pb, hb = bank()
                        for k in range(8):
                            mm(pb[:, 0:n], w1[:, s, k, hc * 128:(hc + 1) * 128], hT[:, k, 1 + c:1 + c + n], k == 0, k == 7,
                               [h_w1[s], h_h[g]], [hb])
                        rs_ = hc % 2
                        act(rl[:, rs_, 0:n], pb[:, 0:n], AF.Relu, [hb], [h_rl[rs_]])
                        tt('dve', hid[:, hs, hc, 0:n], rl[:, rs_, 0:n], rl[:, rs_, 0:n], ALU.mult, [h_rl[rs_]], [h_hid[hs]])

                def ffn2(idx):
                    hg, g = items[idx]
                    c, n, kind = groups[g]
                    s, hs = hg % 2, idx % 2
                    for oc in range(8):
                        pb, hb = bank()
                        for hc in range(4):
                            mm(pb[:, 0:n], w2[:, s, hc, oc * 128:(oc + 1) * 128], hid[:, hs, hc, 0:n], hc == 0, hc == 3,
                               [h_w2[s], h_hid[hs]], [hb])
                        if hg == 0:
                            cp('act', facc[:, oc, c:c + n], pb[:, 0:n], [hb], [h_f[g]])
                        else:
                            tt('dve', facc[:, oc, c:c + n], pb[:, 0:n], facc[:, oc, c:c + n], ALU.add, [hb, h_f[g]], [h_f[g]])
                ffn1(0)
                for i in range(len(items)):
                    if i + 1 < len(items):
                        ffn1(i + 1)
                    ffn2(i)
                for g, (c, n, kind) in enumerate(groups):
                    post_res(5, g, [facc[:, k, c:c + n] for k in range(8)], [h_f[g]], sq, rstd, tmp, h_t1, h_t2)
                E.barrier()
                AR.pop()


            def rwkv_sub(mixT, h_mix):
                AR.push()
                winv = d_win[l].rearrange("(k p) n -> p k n", p=128)
                Wr = AR.alloc([8, 1792], BF16); h_Wr = E.H("Wr")
                E.begin_group()
                for k in range(8):
                    dma('pool', Wr[:, k, :], winv[:, k, 1024:2816], (), [h_Wr])
                E.end_group()
                bct = AR.alloc([4096], F32); h_bc = E.H("bct")
                dma('sp', bct, d_bc[l][:, 0:4096], (), [h_bc])
                kk_t, ka_t, rk_t, gg_t, gb_t = [bct[:, i * 512:(i + 1) * 512] for i in range(5)]
                mu_t = [bct[:, 2560 + i * 512:2560 + (i + 1) * 512] for i in range(3)]
                rows_b = AR.alloc([1024], BF16); h_rows = E.H("rowsb")
                E.begin_group()
                dma('pool', rows_b[0:1, 0:512], d_rows[l:l + 1, 0:512], (), [h_rows])
                dma('pool', rows_b[64:65, 0:512], d_rows[l:l + 1, 512:1024], (), [h_rows])
                E.end_group()
                wup_b = AR.alloc([512], BF16); h_wup = E.H("wupb")
                dma('pool', wup_b, d_wup[l], (), [h_wup])
                gup_b = AR.alloc([512], BF16); h_gup = E.H("gupb")
                dma('pool', gup_b, d_gup[l], (), [h_gup])
                S_b = AR.alloc([4, 64], BF16); h_Sb = E.H("Sb")
                cp('act', S_b, st_S[:, l, :, :], [h_stS[l]], [h_Sb])
                FT = {}
                for nm in ("Asb", "dlt", "Zr", "Zk", "sw", "av", "gate", "G", "Ginv", "Gp", "kk", "sq_"):
                    FT[nm] = (AR.alloc([512], F32), E.H("f_" + nm))
                    DBGV[nm] = AR.last
                FT["kkn"] = FT["kk"]
                FT["t1"] = FT["Asb"]
                FT["kmod"] = FT["sw"]
                DBGV["kkn"] = DBGV["kk"]; DBGV["kmod"] = DBGV["sw"]
                BT = {}
                for nm in ("at", "bt", "kt", "rt", "vb", "Xb", "ro"):
                    BT[nm] = (AR.alloc([512], BF16), E.H("b_" + nm))
                    DBGV[nm] = AR.last
                arT = AR.alloc([4, 2, 128], BF16); h_arT = E.H("arT")
                kbT = AR.alloc([4, 2, 128], BF16); h_kbT = E.H("kbT")
                AkaAkr = AR.alloc([8, 256], BF16); h_Aka = E.H("Aka")
                NAbr = AR.alloc([8, 256], BF16); h_NAbr = E.H("NAbr")
                mrg_b = AR.alloc([1792], BF16); h_mrg = E.H("mrg")
                dma('pool', mrg_b, d_mrg, (), [h_mrg])
                Ad = AR.alloc([2, 128], F32); h_Ad = E.H("Ad")
                Dd = AR.alloc([2, 128], F32); h_Dd = E.H("Dd")
                tda = AR.alloc([128], BF16); h_tda = E.H("tda")
                sdg = AR.alloc([128], BF16); h_sdg = E.H("sdg")
                sm8 = AR.alloc([6, 8], F32); h_sm8 = E.H("sm8")
                gcT = AR.alloc([4, 16], F32); h_gc = E.H("gcT")
                tmpS = AR.alloc([4, 64], F32); h_tmpS = E.H("tmpS")
                has_s = (half == 1 and DBG['samp'])
                if has_s:
                    sst = AR.alloc([8, 16], F32); h_sst = E.H("sst")
                    dma('sp', sst, d_shift[l].rearrange("(k p) b -> p k b", p=128), (), [h_sst])
                    hprev_s = AR.alloc([8, 16, 8], BF16); h_hps = E.H("hps")
                    hsv = hT[:, :, 1025:1153].rearrange("p k (b t) -> p k b t", t=8)
                    cp('dve', hprev_s[:, :, :, 0], sst, [h_sst], [h_hps])
                    cp('dve', hprev_s[:, :, :, 1:8], hsv[:, :, :, 0:7], [h_h[2]], [h_hps])
                    S_sb = AR.alloc([16, 4, 64], BF16); h_Ssb = E.H("Ssb")
                    P1sb = AR.alloc([4, 128], BF16); h_P1sb = E.H("P1sb")
                    S_sf = FT["Gp"][0].rearrange("p (a b c) -> p a b c", a=2, b=4); h_Ssf = [FT["Gp"][1], FT["Gp"][1]]
                    vm = FT["G"][0].bitcast(BF16).rearrange("p (a b) -> p a b", a=2); h_vm = [FT["G"][1], FT["G"][1]]
                    um = FT["Ginv"][0].bitcast(BF16).rearrange("p (a b) -> p a b", a=2); h_um = [FT["Ginv"][1], FT["Ginv"][1]]
                    wkv_in = d_wkv[l].rearrange("b (p4 h2) j i -> h2 j b p4 i", h2=2)
                    wkv_out = o_wkvs[l].rearrange("b (p4 h2) j i -> h2 j b p4 i", h2=2)
                    E.begin_group()
                    for h2 in range(2):
                        for p4 in range(4):
                            dma('pool', S_sb[h2 * 64:(h2 + 1) * 64, :, p4, :], wkv_in[h2][:, :, p4, :], (), [h_Ssb])
                    E.end_group()

                def F(nm):
                    return FT[nm]

                def rw_tile(tc, kind, g):
                    smp = (kind == 's')
                    nseq = 16 if smp else 1
                    nap = 3 if smp else 7
                    MuMi = con_b[:, (C_MUS if smp else C_MU):(C_MUS if smp else C_MU) + 256]
                    MLb = con_b[:, (C_MLS if smp else C_ML):(C_MLS if smp else C_ML) + 128]
                    Tri = con_f[:, (C_MIS if smp else C_MI):(C_MIS if smp else C_MI) + 128]
                    seqf = con_f[:, C_SEQ:C_SEQ + 16] if smp else con_f[:, C_EPS + 4:C_EPS + 5]
                    hg = [h_h[g]] + ([h_h[g - 1]] if g > 0 else []) + ([h_hps] if smp else [])

                    def hc(k):
                        return hT[:, k, 1 + tc:1 + tc + 128]

                    def hp(k):
                        return hprev_s[:, k, :, :] if smp else hT[:, k, tc:tc + 128]
                    (Asb, hA), (dlt, hD) = F("Asb"), F("dlt")
                    Z = {}
                    for qi, qn in enumerate(("Zr", "Zk", "vb")):
                        bA, hbA = bank()
                        for k in range(8):
                            mm(bA, hc(k), Wr[:, k, qi * 512:(qi + 1) * 512], k == 0, k == 7, hg + [h_Wr], [hbA])
                        bB, hbB = bank()
                        for k in range(8):
                            mm(bB, hp(k), Wr[:, k, qi * 512:(qi + 1) * 512], k == 0, k == 7, hg + [h_Wr], [hbB])
                        cp('act', Asb, bA, [hbA], [hA])
                        tt('dve', dlt, bB, Asb, ALU.subtract, [hbB, hA], [hD])
                        tt('dve', dlt, dlt, mu_t[qi], ALU.mult, [hD, h_bc], [hD])
                        dst, hdst = (F(qn) if qi < 2 else BT["vb"])
                        tt('dve', dst, dlt, Asb, ALU.add, [hD, hA], [hdst])
                    (Zr, hZr), (Zk, hZk) = F("Zr"), F("Zk")
                    vb, hvb = BT["vb"]
                    bD, hbD = bank()
                    for ci, c0 in enumerate((1536, 1664)):
                        for k in range(8):
                            mm(bD[:, ci * 256:ci * 256 + 128], Wr[:, k, c0:c0 + 128], hc(k), k == 0, k == 7, hg + [h_Wr], [hbD])
                        for k in range(8):
                            mm(bD[:, ci * 256 + 128:ci * 256 + 256], Wr[:, k, c0:c0 + 128], hp(k), k == 0, k == 7, hg + [h_Wr], [hbD])
                    for ci in range(2):
                        mucol = pp[:, PP_MUD + l * 2 + ci:PP_MUD + l * 2 + ci + 1]
                        cp('act', Ad[:, ci, :], bD[:, ci * 256:ci * 256 + 128], [hbD], [h_Ad])
                        tt('dve', Dd[:, ci, :], bD[:, ci * 256 + 128:ci * 256 + 256], Ad[:, ci, :], ALU.subtract, [hbD, h_Ad], [h_Dd])
                        stt(Dd[:, ci, :], Dd[:, ci, :], mucol, Ad[:, ci, :], ALU.mult, ALU.add, [h_Dd, h_Ad, h_pp], [h_Dd])
                    act(tda[0:64, :], Dd[0:64, 0, :], AF.Tanh, [h_Dd], [h_tda])
                    cp('act', tda[64:128, :], Dd[64:128, 0, :], [h_Dd], [h_tda])
                    act(sdg, Dd[:, 1, :], AF.Sigmoid, [h_Dd], [h_sdg])
                    (sw, hsw), (av, hav), (gate, hgate) = F("sw"), F("av"), F("gate")
                    bW, hbW = bank()
                    mm(bW, onesb[0:1, 0:128], rows_b[0:1, 0:512], True, False, [h_con, h_rows], [hbW])
                    mm(bW, tda[0:64, :], wup_b[0:64, :], False, True, [h_tda, h_wup], [hbW])
                    act(sw, bW, AF.Sigmoid, [hbW], [hsw])
                    bAa, hbAa = bank()
                    mm(bAa, onesb[64:65, 0:128], rows_b[64:65, 0:512], True, False, [h_con, h_rows], [hbAa])
                    mm(bAa, tda[64:128, :], wup_b[64:128, :], False, True, [h_tda, h_wup], [hbAa])
                    act(av, bAa, AF.Sigmoid, [hbAa], [hav])
                    bG, hbG = bank()
                    mm(bG, sdg, gup_b, True, True, [h_sdg, h_gup], [hbG])
                    cp('act', gate, bG, [hbG], [hgate])
                    (G, hG), (Ginv, hGi), (Gp, hGp) = F("G"), F("Ginv"), F("Gp")
                    bC, hbC = bank()
                    mm(bC, Tri, sw, True, True, [h_con, hsw], [hbC])
                    act(G, bC, AF.Exp, [hbC], [hG], scale=-CDEC)
                    act(Ginv, bC, AF.Exp, [hbC], [hGi], scale=CDEC)
                    tt('dve', Gp, bC, sw, ALU.subtract, [hbC, hsw], [hGp])
                    act(Gp, Gp, AF.Exp, [hGp], [hGp], scale=-CDEC)
                    bGC, hbGC = bank()
                    for p4 in range(4):
                        mm(bGC[:, p4 * 16:p4 * 16 + nseq], sw[:, p4 * 128:(p4 + 1) * 128], seqf, True, True, [hsw, h_con], [hbGC])
                    act(gcT[:, :, 0:nseq], bGC[:, 0:64].rearrange("p (a b) -> p a b", a=4)[:, :, 0:nseq], AF.Exp, [hbGC], [h_gc], scale=-CDEC)
                    (kk, hkk), (sq_, hsq), (kkn, hkkn), (t1, ht1), (kmod, hkm) = F("kk"), F("sq_"), F("kkn"), F("t1"), F("kmod")
                    tt('dve', kk, Zk, kk_t, ALU.mult, [hZk, h_bc], [hkk])
                    tt('dve', sq_, kk, kk, ALU.mult, [hkk], [hsq])
                    red(sm8[:, 0, :], sq_.rearrange("p (h d) -> p h d", h=8), [hsq], [h_sm8])
                    act(sm8[:, 0, :], sm8[:, 0, :], AF.Sqrt, [h_sm8], [h_sm8])
                    ts('dve', sm8[:, 0, :], sm8[:, 0, :], 1e-12, ALU.max, [h_sm8], [h_sm8])
                    recip(sm8[:, 0, :], sm8[:, 0, :], [h_sm8], [h_sm8])
                    tt('dve', kkn.rearrange("p (h d) -> p h d", h=8), kk.rearrange("p (h d) -> p h d", h=8),
                       sm8[:, 0, :].unsqueeze(2).to_broadcast([128, 8, 64]), ALU.mult, [hkk, h_sm8], [hkkn])
                    stt(t1, av, -1.0, ka_t, ALU.add, ALU.mult, [hav, h_bc], [ht1])
                    stt(kmod, t1, 1.0, Zk, ALU.add, ALU.mult, [ht1, hZk], [hkm])
                    (at, hat), (bt, hbt), (kt, hkt), (rt, hrt) = BT["at"], BT["bt"], BT["kt"], BT["rt"]
                    stt(at, kkn, -1.0, Gp, ALU.mult, ALU.mult, [hkkn, hGp], [hat])
                    tt('dve', t1, kkn, av, ALU.mult, [hkkn, hav], [ht1])
                    tt('dve', bt, t1, Ginv, ALU.mult, [ht1, hGi], [hbt])
                    tt('dve', kt, kmod, Ginv, ALU.mult, [hkm, hGi], [hkt])
                    tt('dve', rt, Zr, G, ALU.mult, [hZr, hG], [hrt])
                    tt('dve', sq_, Zr, kmod, ALU.mult, [hZr, hkm], [hsq])
                    tt('dve', sq_, sq_, rk_t, ALU.mult, [hsq, h_bc], [hsq])
                    red(sm8[:, 1, :], sq_.rearrange("p (h d) -> p h d", h=8), [hsq], [h_sm8])
                    for (srcs, dstT, hdT) in (((at, hat, rt, hrt), arT, h_arT), ((kt, hkt, bt, hbt), kbT, h_kbT)):
                        bT_, hbT_ = bank()
                        tv = bT_.bitcast(BF16)[:, 0:1024].rearrange("p (a b c) -> p a b c", a=4, b=2)
                        for p4 in range(4):
                            tr(tv[:, p4, 0, :], srcs[0][:, p4 * 128:(p4 + 1) * 128], ident_b, [srcs[1], h_con], [hbT_])
                            tr(tv[:, p4, 1, :], srcs[2][:, p4 * 128:(p4 + 1) * 128], ident_b, [srcs[3], h_con], [hbT_])
                        cp('act', dstT, tv, [hbT_], [hdT])
                    mk = MuMi.unsqueeze(1).to_broadcast([128, 2, 256])
                    for hb4 in range(2):
                        for par in range(2):
                            bA_, hbA_ = bank()
                            bB_, hbB_ = bank()
                            base = 64 * par
                            for hh in range(2):
                                h = hb4 * 4 + 2 * hh + par
                                p4 = h // 2
                                ar2 = arT[base:base + 64, p4, :, :]
                                mm(bA_[:, hh * 256:(hh + 1) * 256], kbT[base:base + 64, p4, 0, :], ar2, True, True, [h_kbT, h_arT], [hbA_])
                                mm(bB_[:, hh * 256:(hh + 1) * 256], kbT[base:base + 64, p4, 1, :], ar2, True, True, [h_kbT, h_arT], [hbB_])
                            h0 = hb4 * 4 + par
                            tt('dve', AkaAkr[:, h0:h0 + 3:2, :], bA_.rearrange("p (a b) -> p a b", a=2), mk, ALU.mult, [hbA_, h_con], [h_Aka])
                            tt('dve', NAbr[:, h0:h0 + 3:2, :], bB_.rearrange("p (a b) -> p a b", a=2), mk, ALU.mult, [hbB_, h_con], [h_NAbr])
                    (Xb, hXb) = BT["Xb"]
                    X, hX = F("Asb")

                    def s0_terms(bX_, hbX_, which, post):
                        if not smp:
                            for h in range(8):
                                p4, base = h // 2, 64 * (h % 2)
                                mm(bX_[:, h * 64:(h + 1) * 64], arT[base:base + 64, p4, which, :], S_b[base:base + 64, p4, :],
                                   True, False, [h_arT, h_Sb], [hbX_])
                                post(h, True)
                        else:
                            for h2 in range(2):
                                base = 64 * h2
                                bP, hbP = bank()
                                for p4 in range(4):
                                    for s in range(16):
                                        mm(bP[base:base + 64, p4 * 128 + s * 8:p4 * 128 + s * 8 + 8], S_sb[base:base + 64, s, p4, :],
                                           arT[base:base + 64, p4, which, s * 8:(s + 1) * 8], True, True, [h_Ssb, h_arT], [hbP])
                                cp('act', P1sb[base:base + 64, :, :], bP[base:base + 64, :].rearrange("p (a b) -> p a b", a=4), [hbP], [h_P1sb])
                            for p4 in range(4):
                                mm(bX_[:, p4 * 128:(p4 + 1) * 128], P1sb[:, p4, :], ident_b, True, False, [h_P1sb, h_con], [hbX_])
                                post(2 * p4, False)
                                post(2 * p4 + 1, True)
                    bX, hbX = bank()

                    def post_x(h, last):
                        mm(bX[:, h * 64:(h + 1) * 64], AkaAkr[:, h, 0:128], vb[:, h * 64:(h + 1) * 64], False, last, [h_Aka, hvb], [hbX])
                    s0_terms(bX, hbX, 0, post_x)
                    cp('act', X, bX, [hbX], [hX])
                    cp('dve', Xb, X, [hX], [hXb])
                    def b8(nm):
                        return FT[nm][0].bitcast(BF16).rearrange("p (a b) -> p a b", a=8), FT[nm][1]
                    (Tm, hTm), (TmT, hTmT), (Noff, hNo), (YT, hYT) = b8("kk"), b8("sq_"), b8("Zr"), b8("av")
                    idb = ident_b.unsqueeze(1).to_broadcast([128, 8, 128])
                    cp('dve', TmT, idb, [h_con], [hTmT])
                    lvs = list(range(3) if smp else range(7))
                    for lv in lvs:
                        last_lv = (lv == lvs[-1])
                        moT = mrg_b[:, 896 + lv * 128:896 + (lv + 1) * 128].unsqueeze(1).to_broadcast([128, 4, 128])
                        for q in range(2):
                            bY_, hbY_ = bank()
                            for hh in range(4):
                                h = q * 4 + hh
                                mm(bY_[:, hh * 128:(hh + 1) * 128], NAbr[:, h, 0:128], TmT[:, h, :], True, True, [h_NAbr, hTmT], [hbY_])
                            tt('dve', YT[:, q * 4:q * 4 + 4, :], bY_.rearrange("p (a b) -> p a b", a=4), moT, ALU.mult, [hbY_, h_mrg], [hYT])
                        if lv == 0:
                            mo0 = mrg_b[:, 0:128].unsqueeze(1).to_broadcast([128, 8, 128])
                            tt('dve', Tm, NAbr[:, :, 0:128], mo0, ALU.mult, [h_NAbr, h_mrg], [hTm])
                            tt('dve', Tm, Tm, idb, ALU.add, [hTm, h_con], [hTm])
                            tt('dve', TmT, YT, idb, ALU.add, [hYT, h_con], [hTmT])
                            continue
                        zb = []
                        for q in range(2):
                            bZ, hbZ = bank()
                            bZT, hbZT = bank() if not last_lv else (None, None)
                            for hh in range(4):
                                h = q * 4 + hh
                                mm(bZ[:, hh * 128:(hh + 1) * 128], YT[:, h, :], Tm[:, h, :], True, True, [hYT, hTm], [hbZ])
                                if not last_lv:
                                    mm(bZT[:, hh * 128:(hh + 1) * 128], Tm[:, h, :], YT[:, h, :], True, True, [hYT, hTm], [hbZT])
                            zb.append((bZ, hbZ, bZT, hbZT))
                        for q in range(2):
                            bZ, hbZ, bZT, hbZT = zb[q]
                            tt('dve', Tm[:, q * 4:q * 4 + 4, :], bZ.rearrange("p (a b) -> p a b", a=4), Tm[:, q * 4:q * 4 + 4, :], ALU.add, [hbZ, hTm], [hTm])
                            if not last_lv:
                                tt('dve', TmT[:, q * 4:q * 4 + 4, :], bZT.rearrange("p (a b) -> p a b", a=4), TmT[:, q * 4:q * 4 + 4, :], ALU.add, [hbZT, hTmT], [hTmT])
                    bY, hbY = bank()
                    for h in range(8):
                        mm(bY[:, h * 64:(h + 1) * 64], Tm[:, h, :], Xb[:, h * 64:(h + 1) * 64], True, True, [hTm, hXb], [hbY])
                    cp('act', X, bY, [hbY], [hX])
                    cp('dve', Xb, X, [hX], [hXb])
                    bO, hbO = bank()

                    def post_o(h, last):
                        mm(bO[:, h * 64:(h + 1) * 64], AkaAkr[:, h, 128:256], vb[:, h * 64:(h + 1) * 64], False, False, [h_Aka, hvb], [hbO])
                        mm(bO[:, h * 64:(h + 1) * 64], NAbr[:, h, 128:256], Xb[:, h * 64:(h + 1) * 64], False, last, [h_NAbr, hXb], [hbO])
                    s0_terms(bO, hbO, 1, post_o)
                    (osb, hos) = F("dlt")
                    cp('act', osb, bO, [hbO], [hos])
                    if not smp:
                        bS, hbS = bank()
                        for h in range(8):
                            p4, base = h // 2, 64 * (h % 2)
                            mm(bS[base:base + 64, p4 * 64:(p4 + 1) * 64], kt[:, h * 64:(h + 1) * 64], vb[:, h * 64:(h + 1) * 64], True, False, [hkt, hvb], [hbS])
                            mm(bS[base:base + 64, p4 * 64:(p4 + 1) * 64], bt[:, h * 64:(h + 1) * 64], Xb[:, h * 64:(h + 1) * 64], False, True, [hbt, hXb], [hbS])
                        tt('dve', tmpS, bS[:, 0:256].rearrange("p (a b) -> p a b", a=4), st_S[:, l, :, :], ALU.add, [hbS, h_stS[l]], [h_tmpS])
                        tt('dve', st_S[:, l, :, :], tmpS, gcT[:, :, 0:1].to_broadcast([128, 4, 64]), ALU.mult, [h_tmpS, h_gc], [h_stS[l]])
                        cp('act', S_b, st_S[:, l, :, :], [h_stS[l]], [h_Sb])
                    else:
                        for s in range(16):
                            ss_ = s % 2
                            E.begin_group()
                            for h2 in range(2):
                                dma('sp', S_sf[h2 * 64:(h2 + 1) * 64, ss_, :, :], wkv_in[h2][:, s, :, :], (), [h_Ssf[ss_]])
                            E.end_group()
                            ts('dve', vm[:, ss_, :], vb, seqf[:, s:s + 1], ALU.mult, [hvb, h_con], [h_vm[ss_]])
                            ts('dve', um[:, ss_, :], Xb, seqf[:, s:s + 1], ALU.mult, [hXb, h_con], [h_um[ss_]])
                            bS, hbS = bank()
                            for h in range(8):
                                p4, base = h // 2, 64 * (h % 2)
                                mm(bS[base:base + 64, p4 * 64:(p4 + 1) * 64], kt[:, h * 64:(h + 1) * 64], vm[:, ss_, h * 64:(h + 1) * 64], True, False, [hkt, h_vm[ss_]], [hbS])
                                mm(bS[base:base + 64, p4 * 64:(p4 + 1) * 64], bt[:, h * 64:(h + 1) * 64], um[:, ss_, h * 64:(h + 1) * 64], False, True, [hbt, h_um[ss_]], [hbS])
                            tt('dve', S_sf[:, ss_, :, :], bS[:, 0:256].rearrange("p (a b) -> p a b", a=4), S_sf[:, ss_, :, :], ALU.add, [hbS, h_Ssf[ss_]], [h_Ssf[ss_]])
                            tt('dve', S_sf[:, ss_, :, :], S_sf[:, ss_, :, :], gcT[:, :, s:s + 1].to_broadcast([128, 4, 64]), ALU.mult, [h_Ssf[ss_], h_gc], [h_Ssf[ss_]])
                            E.begin_group()
                            for h2 in range(2):
                                dma('sp', wkv_out[h2][:, s, :, :], S_sf[h2 * 64:(h2 + 1) * 64, ss_, :, :], [h_Ssf[ss_]], ())
                            E.end_group()
                    (osq, hoq), (on, hon), (bon, hbon) = F("G"), F("Ginv"), F("Gp")
                    o3 = osb.rearrange("p (h d) -> p h d", h=8)
                    red(sm8[:, 2, :], o3, [hos], [h_sm8])
                    tt('pool', osq, osb, osb, ALU.mult, [hos], [hoq])
                    red(sm8[:, 3, :], osq.rearrange("p (h d) -> p h d", h=8), [hoq], [h_sm8])
                    ts('dve', sm8[:, 2, :], sm8[:, 2, :], 1.0 / 64.0, ALU.mult, [h_sm8], [h_sm8])
                    tt('dve', sm8[:, 4, :], sm8[:, 2, :], sm8[:, 2, :], ALU.mult, [h_sm8], [h_sm8])
                    stt(sm8[:, 3, :], sm8[:, 3, :], 1.0 / 64.0, sm8[:, 4, :], ALU.mult, ALU.subtract, [h_sm8], [h_sm8])
                    act(sm8[:, 3, :], sm8[:, 3, :], AF.Sqrt, [h_sm8, h_con], [h_sm8], scale=1.0, bias=epsc(2))
                    recip(sm8[:, 3, :], sm8[:, 3, :], [h_sm8], [h_sm8])
                    on3 = on.rearrange("p (h d) -> p h d", h=8)
                    tt('dve', on3, o3, sm8[:, 2, :].unsqueeze(2).to_broadcast([128, 8, 64]), ALU.subtract, [hos, h_sm8], [hon])
                    tt('dve', on3, on3, sm8[:, 3, :].unsqueeze(2).to_broadcast([128, 8, 64]), ALU.mult, [hon, h_sm8], [hon])
                    tt('dve', on, on, gg_t, ALU.mult, [hon, h_bc], [hon])
                    tt('dve', on, on, gb_t, ALU.add, [hon, h_bc], [hon])
                    tt('dve', bon.rearrange("p (h d) -> p h d", h=8), vb.rearrange("p (h d) -> p h d", h=8),
                       sm8[:, 1, :].unsqueeze(2).to_broadcast([128, 8, 64]), ALU.mult, [hvb, h_sm8], [hbon])
                    tt('dve', on, on, bon, ALU.add, [hon, hbon], [hon])
                    ro, hro = BT["ro"]
                    tt('dve', ro, on, gate, ALU.mult, [hon, hgate], [hro])
                    bT_, hbT_ = bank()
                    tv = bT_.bitcast(BF16)[:, 0:512].rearrange("p (a b) -> p a b", a=4)
                    for p4 in range(4):
                        tr(tv[:, p4, :], ro[:, p4 * 128:(p4 + 1) * 128], ident_b, [hro, h_con], [hbT_])
                    cp('act', mixT[:, 4:8, tc:tc + 128], tv, [hbT_], [h_mix[g]])

                ntl = 0
                if DBG.get('rwtiles', 99) < 99 or DBG.get('rwonly') is not None:
                    for g, (c, n, kind) in enumerate(groups):
                        mset('pool', mixT[:, 4:8, c:c + n], 0.0, [h_mix[g]])
                for g, (c, n, kind) in enumerate(groups):
                    for ti in range(n // 128):
                        if ntl < DBG.get('rwtiles', 99) and (DBG.get('rwonly') is None or half == 0 or ntl == DBG['rwonly']):
                            rw_tile(c + ti * 128, kind, g)
                        ntl += 1
                if half == 1:
                    for h2 in range(2):
                        dma('sp', o_wkvp[l].rearrange("(p4 h2) j i -> h2 j p4 i", h2=2)[h2], st_S[h2 * 64:(h2 + 1) * 64, l, :, :],
                            [h_stS[l]], ())
                E.barrier()
                AR.pop()

            def mixer_sub():
                AR.push()
                NT_ = 1024 + (128 if (half == 1 and DBG['samp']) else 0)
                winv = d_win[l].rearrange("(k p) n -> p k n", p=128)
                mixT = AR.alloc([8, NT_], BF16); h_mix = [E.H("mix%d" % g) for g in range(len(groups))]
                AR.push()
                sq = AR.alloc([8, 512], BF16); h_t1 = E.H("t1m")
                rstd = AR.alloc([512], F32)
                hl = AR.alloc([8], F32); h_hl = E.H("hl")
                hs_ = AR.alloc([8, 16], F32); h_hs = E.H("hs")
                if half == 1:
                    cp('dve', hT[:, :, 0], st_shift[:, l, :], [h_stshift[l]], [h_h[0]])
                for g, (c, n, kind) in enumerate(groups):
                    rms_rstd([xT[:, k, c:c + n] for k in range(8)], n, [h_x[g]], sq, rstd[:, 0:n], h_t1)
                    for k in range(8):
                        stt(hT[:, k, 1 + c:1 + c + n], xT[:, k, c:c + n], ngc(0, k), rstd[:, 0:n],
                            ALU.mult, ALU.mult, [h_x[g], h_t1, h_pp], [h_h[g]])
                    if kind == 'p' and g == 1:
                        if half == 0:
                            cp('pool', st_shift[:, l, :], hT[:, :, 1024], [h_h[g]], [h_stshift[l]])
                        else:
                            for k in range(8):
                                stt(hl[:, k:k + 1], xT[:, k, 1023:1024], ngc(0, k), rstd[:, 511:512], ALU.mult, ALU.mult,
                                    [h_x[g], h_t1, h_pp], [h_hl])
                            dma('sp', o_shiftp[l], hl, [h_hl], ())
                    if kind == 's':
                        xs7 = xT[:, :, 1024:1152].rearrange("p k (b t) -> p k b t", t=8)
                        r7 = rstd[:, 0:128].rearrange("p (b t) -> p b t", t=8)[:, :, 7]
                        for k in range(8):
                            stt(hs_[:, k, :], xs7[:, k, :, 7], ngc(0, k), r7, ALU.mult, ALU.mult,
                                [h_x[g], h_t1, h_pp], [h_hs])
                        dma('sp', o_shifts[l], hs_, [h_hs], ())
                E.barrier()
                AR.pop()
                AR.push()
                Wc = AR.alloc([8, 1024], BF16); h_Wc = E.H("Wc")
                E.begin_group()
                for k in range(8):
                    dma('pool', Wc[:, k, :], winv[:, k, 0:1024], (), [h_Wc])
                E.end_group()
                uf = AR.alloc([4, 512], F32); h_uf = E.H("uf")
                sg = AR.alloc([512], F32); h_sg = E.H("sg")
                uT = AR.alloc([4, 542], BF16); h_uT = E.H("uT")
                uTs = AR.alloc([4, 16, 38], BF16); h_uTs = E.H("uTs")
                Dg4 = AR.alloc([4, 31, 128], BF16); h_Dg = E.H("Dg")
                for c4 in range(4):
                    o_cw = PP_CW + (l * 4 + c4) * 31
                    tt('dve' if c4 % 2 == 0 else 'pool', Dg4[:, c4, :, :], ident_b.unsqueeze(1).to_broadcast([128, 31, 128]),
                       pp[:, o_cw:o_cw + 31].unsqueeze(2).to_broadcast([128, 31, 128]), ALU.mult, [h_con, h_pp], [h_Dg])
                ysb = AR.alloc([4, 512], F32); h_ysb = E.H("ysb")
                ybf = AR.alloc([4, 512], BF16); h_ybf = E.H("ybf")
                ysq = AR.alloc([4, 512], BF16); h_ysq = E.H("ysq")
                mean = AR.alloc([512], F32); h_mean = E.H("mean")
                m2 = AR.alloc([512], F32); h_m2 = E.H("m2")
                rs2 = AR.alloc([512], F32); h_rs2 = E.H("rs2")
                tl = AR.alloc([512], F32); h_tl = E.H("tl")
                for g, (c, n, kind) in enumerate(groups):
                    if kind == 'p':
                        if g == 0:
                            cp('pool', uT[:, :, 0:30], st_conv[:, l, :, :], [h_stconv[l]], [h_uT])
                        else:
                            cp('pool', uT[:, :, 0:30], uT[:, :, 512:542], [h_uT], [h_uT])
                    else:
                        E.begin_group()
                        for c4 in range(4):
                            dma('pool', uTs[:, c4, :, 0:30],
                                d_conv[l][:, c4 * 128:(c4 + 1) * 128, :].rearrange("b p t -> p b t"), (), [h_uTs])
                        E.end_group()
                    for c4 in range(4):
                        pbv, hbv = bank()
                        for k in range(8):
                            mm(pbv[:, 0:n], Wc[:, k, c4 * 128:(c4 + 1) * 128], hT[:, k, 1 + c:1 + c + n], k == 0, k == 7,
                               [h_Wc, h_h[g]], [hbv])
                        pbg, hbg = bank()
                        for k in range(8):
                            mm(pbg[:, 0:n], Wc[:, k, 512 + c4 * 128:512 + (c4 + 1) * 128], hT[:, k, 1 + c:1 + c + n],
                               k == 0, k == 7, [h_Wc, h_h[g]], [hbg])
                        act(sg[:, 0:n], pbg[:, 0:n], AF.Sigmoid, [hbg], [h_sg])
                        tt('dve', uf[:, c4, 0:n], pbv[:, 0:n], sg[:, 0:n], ALU.mult, [hbv, h_sg], [h_uf])
                        if kind == 'p':
                            cp('act', uT[:, c4, 30:30 + n], uf[:, c4, 0:n], [h_uf], [h_uT])
                        else:
                            cp('act', uTs[:, c4, :, 30:38], uf[:, c4, 0:128].rearrange("p (b t) -> p b t", t=8),
                               [h_uf], [h_uTs])
                    if kind == 'p' and g == 1:
                        if half == 0:
                            cp('pool', st_conv[:, l, :, :], uT[:, :, 512:542], [h_uT], [h_stconv[l]])
                        else:
                            dma('sp', o_convp[l].rearrange("(c p) t -> p c t", p=128), uf[:, :, 482:512], [h_uf], ())
                    if kind == 's':
                        dma('sp', o_convs[l][:, :, 0:22], d_conv[l][:, :, 8:30], (), ())
                        for c4 in range(4):
                            dma('sp', o_convs[l][:, c4 * 128:(c4 + 1) * 128, 22:30].rearrange("b p t -> p b t"),
                                uf[:, c4, 0:128].rearrange("p (b t) -> p b t", t=8), [h_uf], ())
                    for c4 in range(4):
                        pby, hby = bank()
                        for j in range(31):
                            rhs = uT[:, c4, j:j + n] if kind == 'p' else uTs[:, c4, :, j:j + 8]
                            mm(pby[:, 0:n], Dg4[:, c4, j, :], rhs, j == 0, j == 30, [h_Dg, h_uT, h_uTs], [hby])
                        cbc = pp[:, PP_CB + l * 4 + c4:PP_CB + l * 4 + c4 + 1]
                        act(ysb[:, c4, 0:n], pby[:, 0:n], AF.Identity, [hby, h_pp], [h_ysb], bias=cbc, scale=1.0)
                        act(ybf[:, c4, 0:n], pby[:, 0:n], AF.Identity, [hby, h_pp], [h_ybf], bias=cbc, scale=1.0)
                        act(ysq[:, c4, 0:n], pby[:, 0:n], AF.Square, [hby, h_pp], [h_ysq], bias=cbc, scale=1.0)
                    pb1, hb1 = bank()
                    for c4 in range(4):
                        mm(pb1[:, 0:n], onesb, ybf[:, c4, 0:n], c4 == 0, c4 == 3, [h_ybf, h_con], [hb1])
                    pb2, hb2 = bank()
                    for c4 in range(4):
                        mm(pb2[:, 0:n], onesb, ysq[:, c4, 0:n], c4 == 0, c4 == 3, [h_ysq, h_con], [hb2])
                    act(mean[:, 0:n], pb1[:, 0:n], AF.Identity, [hb1], [h_mean], scale=1.0 / 512.0, bias=epsc(3))
                    tt('dve', m2[:, 0:n], mean[:, 0:n], mean[:, 0:n], ALU.mult, [h_mean], [h_m2])
                    stt(m2[:, 0:n], pb2[:, 0:n], 1.0 / 512.0, m2[:, 0:n], ALU.mult, ALU.subtract, [hb2, h_m2], [h_m2])
                    act(rs2[:, 0:n], m2[:, 0:n], AF.Sqrt, [h_m2, h_con], [h_rs2], scale=1.0, bias=epsc(1))
                    recip(rs2[:, 0:n], rs2[:, 0:n], [h_rs2], [h_rs2])
                    for c4 in range(4):
                        tt('dve', tl[:, 0:n], ysb[:, c4, 0:n], mean[:, 0:n], ALU.subtract, [h_ysb, h_mean], [h_tl])
                        tt('dve', tl[:, 0:n], tl[:, 0:n], rs2[:, 0:n], ALU.mult, [h_tl, h_rs2], [h_tl])
                        ol = l * 4 + c4
                        ts('dve', tl[:, 0:n], tl[:, 0:n], pp[:, PP_LG + ol:PP_LG + ol + 1], ALU.mult, [h_tl, h_pp], [h_tl],
                           s2=pp[:, PP_LB + ol:PP_LB + ol + 1], op1=ALU.add)
                        act(mixT[:, c4, c:c + n], tl[:, 0:n], AF.Silu, [h_tl], [h_mix[g]])
                E.barrier()
                AR.pop()
                if DBG.get('rwkv'):
                    rwkv_sub(mixT, h_mix)
                else:
                    for g, (c, n, kind) in enumerate(groups):
                        mset('pool', mixT[:, 4:8, c:c + n], 0.0, [h_mix[g]])
                wout = AR.alloc([8, 1024], BF16); h_wout = E.H("wout")
                load_w('pool', wout, d_wout[l], h_wout)
                sq = AR.alloc([8, 512], BF16); h_t1 = E.H("t1m3")
                rstd = AR.alloc([512], F32)
                tmp = AR.alloc([512], F32); h_t2 = E.H("t2m")
                msb = AR.alloc([8, 512], F32); h_msb = E.H("msbm")
                for g, (c, n, kind) in enumerate(groups):
                    for oc in range(8):
                        pb, hb = bank()
                        for k in range(8):
                            mm(pb[:, 0:n], wout[:, k, oc * 128:(oc + 1) * 128], mixT[:, k, c:c + n], k == 0, k == 7,
                               [h_wout, h_mix[g]], [hb])
                        cp('act', msb[:, oc, 0:n], pb[:, 0:n], [hb], [h_msb])
                    post_res(1, g, [msb[:, k, 0:n] for k in range(8)], [h_msb], sq, rstd, tmp, h_t1, h_t2)
                E.barrier()
                AR.pop()

            if DBG.get('mix'):
                mixer_sub()
            if DBG['att']:
                attention_sub()
            if DBG['ffn']:
                ffn_sub()

        for half in range(DBG['halves']):
            groups = [(0, 512, 'p'), (512, 512, 'p')] + ([(1024, 128, 's')] if (half == 1 and DBG['samp']) else [])
            NTOK = 1024 + (128 if half == 1 else 0)
            h_x = [E.H("x%d" % g) for g in range(len(groups))]
            h_h = [E.H("h%d" % g) for g in range(len(groups))]
            xv = d_xT.rearrange("(k p) n -> p k n", p=128)
            yv = o_yT.rearrange("(k p) n -> p k n", p=128)
            for g, (c, n, kind) in enumerate(groups):
                src = xv[:, :, half * 1024 + c: half * 1024 + c + n] if kind == 'p' else xv[:, :, 2048:2176]
                dma('sp', xT[:, :, c:c + n], src, (), [h_x[g]])

            if half == 0:
                AR.push()
                mraw = AR.alloc([8, 256], F32); h_mraw = E.H("mraw")
                msq = AR.alloc([8, 256], BF16)
                mr = AR.alloc([256], F32); h_mt = E.H("mtmp")
                dma('sp', mraw, d_memT.rearrange("(k p) n -> p k n", p=128), (), [h_mraw])
                rms_rstd([mraw[:, k, :] for k in range(8)], 256, [h_mraw], msq, mr, h_mt)
                for k in range(8):
                    stt(memn[:, k, :], mraw[:, k, :], pp[:, PP_MG + k:PP_MG + k + 1], mr, ALU.mult, ALU.mult,
                        [h_mraw, h_mt, h_pp], [h_memn])
                E.barrier()
                AR.pop()

            for l in range(DBG['layers']):
                layer(half, l, groups, h_x, h_h)

            for g, (c, n, kind) in enumerate(groups):
                dst = yv[:, :, half * 1024 + c: half * 1024 + c + n] if kind == 'p' else yv[:, :, 2048:2176]
                dma('sp', dst, xT[:, :, c:c + n], [h_x[g]], ())
            E.barrier()

        E.finalize()
        import contextlib
        with contextlib.ExitStack() as es:
            csem = {e: es.enter_context(nc.semaphore("c_" + e)) for e in ENGS}
            dsem = [es.enter_context(nc.semaphore("d%d" % i)) for i in range(NDSEM)]
            block = es.enter_context(nc.Block())

            @block.tensor
            def _(e):
                E.emit('pe', e, csem, dsem)

            @block.scalar
            def _(e):
                E.emit('act', e, csem, dsem)

            @block.vector
            def _(e):
                E.emit('dve', e, csem, dsem)

            @block.gpsimd
            def _(e):
                E.emit('pool', e, csem, dsem)

            @block.sync
            def _(e):
                E.emit('sp', e, csem, dsem)
    return nc


_PROG = {}


def _merge_masks():
    m = np.zeros((128, 1792), np.float32)
    s, t = np.arange(128)[:, None], np.arange(128)[None, :]
    for lv in range(7):
        b = 1 << lv
        q = ((s // (2 * b)) == (t // (2 * b))) & ((s % (2 * b)) < b) & ((t % (2 * b)) >= b)
        m[:, lv * 128:(lv + 1) * 128] = q
        m[:, 896 + lv * 128:896 + (lv + 1) * 128] = q.T
    return m


def _consts():
    c = np.zeros((128, NCON), np.float32)
    idx = np.arange(128)
    c[:, C_ID:C_ID + 128] = np.eye(128, dtype=np.float32)
    s, t = idx[:, None], idx[None, :]
    c[:, C_MU:C_MU + 128] = (s < t)
    c[:, C_MI:C_MI + 128] = (s <= t)
    c[:, C_ML:C_ML + 128] = (t < s)
    same = (s // 8) == (t // 8)
    c[:, C_MUS:C_MUS + 128] = (s < t) & same
    c[:, C_MIS:C_MIS + 128] = (s <= t) & same
    c[:, C_MLS:C_MLS + 128] = (t < s) & same
    c[:, C_SEQ:C_SEQ + 16] = (idx[:, None] // 8) == np.arange(16)[None, :]
    c[:, C_EPS + 0] = 1e-6
    c[:, C_EPS + 1] = 1e-5
    c[:, C_EPS + 2] = 64e-5
    c[:, C_EPS + 3] = 0.0
    c[:, C_EPS + 4] = 1.0
    return c


def _prepare(inp):
    f = lambda a: np.ascontiguousarray(np.asarray(a), dtype=np.float32)
    I = {k: f(v) for k, v in inp.items()}
    pp = np.zeros((128, NPP), np.float32)
    ng = I['norm_g'].reshape(4, 6, 8, 128)
    pp[:, PP_NG:PP_NG + 192] = ng.transpose(3, 0, 1, 2).reshape(128, 192)
    pp[:, PP_MG:PP_MG + 8] = I['mem_norm_g'].reshape(8, 128).T
    cw = I['conv_w'].reshape(4, 31, 4, 128)
    pp[:, PP_CW:PP_CW + 496] = cw.transpose(3, 0, 2, 1).reshape(128, 496)
    pp[:, PP_CB:PP_CB + 16] = I['conv_b'].reshape(4, 4, 128).transpose(2, 0, 1).reshape(128, 16)
    pp[:, PP_LG:PP_LG + 16] = I['conv_ln_g'].reshape(4, 4, 128).transpose(2, 0, 1).reshape(128, 16)
    pp[:, PP_LB:PP_LB + 16] = I['conv_ln_b'].reshape(4, 4, 128).transpose(2, 0, 1).reshape(128, 16)
    pp[:, PP_MUD:PP_MUD + 8] = I['mu_shift'][:, 1536:1792].reshape(4, 2, 128).transpose(2, 0, 1).reshape(128, 8)
    bc = np.zeros((4, 128, NBC), np.float32)
    for j, nm in enumerate(['rwkv_k_k', 'rwkv_k_a', 'rwkv_r_k', 'rwkv_gn_g', 'rwkv_gn_b']):
        bc[:, :, j * 512:(j + 1) * 512] = I[nm].reshape(4, 1, 512)
    bc[:, :, 2560:] = I['mu_shift'].reshape(4, 1, 1792)
    rows = np.concatenate([I['rwkv_w0'], I['rwkv_a0']], axis=1)
    wup = np.concatenate([I['rwkv_w_up'], I['rwkv_a_up']], axis=1)
    gup = I['rwkv_g_up']
    consts = _consts()
    mrg = _merge_masks()
    in_maps = []
    for c in range(NCORE):
        sb = slice(16 * c, 16 * c + 16)
        xs = I['x_sample'][sb].reshape(128, 1024)
        xT = np.ascontiguousarray(np.concatenate([I['x_prompt'][c], xs], axis=0).T)
        m = {
            "xT": xT,
            "memT": np.ascontiguousarray(I['mem_prompt'][c].T),
            "kTc": np.ascontiguousarray(I['cache_mem_k'][:, sb].transpose(0, 1, 3, 4, 2)),
            "vc": np.ascontiguousarray(I['cache_mem_v'][:, sb].reshape(4, 16, 256, 1024)),
            "wkvT": np.ascontiguousarray(I['state_wkv'][:, sb].transpose(0, 1, 2, 4, 3)),
            "convT": np.ascontiguousarray(I['state_conv'][:, sb].transpose(0, 1, 3, 2)),
            "shiftT": np.ascontiguousarray(I['state_shift'][:, sb].transpose(0, 2, 1)),
            "w_in": I['w_in'], "w_out": I['w_out'], "w_q": I['w_q'], "w_k": I['w_k'], "w_v": I['w_v'],
            "w_o": I['w_o'], "w_ffn1": I['w_ffn1'], "w_ffn2": I['w_ffn2'],
            "pp": pp, "bc": bc, "rows": rows, "wup": wup, "gup": gup, "consts": consts, "mrg": mrg,
        }
        in_maps.append(m)
    return in_maps


def kernel(**inp):
    in_maps = _prepare(inp)
    if 'nc' not in _PROG:
        _PROG['nc'] = build_program()
    res = run_bass_kernel_spmd(_PROG['nc'], in_maps, core_ids=list(range(NCORE)))
    R = res.results
    y_prompt = np.zeros((8, 2048, 1024), np.float32)
    y_sample = np.zeros((128, 8, 1024), np.float32)
    nk = np.zeros((4, 8, 256, 4, 256), np.float32)
    nv = np.zeros((4, 8, 256, 4, 256), np.float32)
    wkvp = np.zeros((4, 8, 8, 64, 64), np.float32)
    convp = np.zeros((4, 8, 30, 512), np.float32)
    shiftp = np.zeros((4, 8, 1024), np.float32)
    wkvs = np.zeros((4, 128, 8, 64, 64), np.float32)
    convs = np.zeros((4, 128, 30, 512), np.float32)
    shifts = np.zeros((4, 128, 1024), np.float32)
    for c in range(NCORE):
        r = R[c]
        sb = slice(16 * c, 16 * c + 16)
        yT = np.asarray(r["yT"])
        y_prompt[c] = yT[:, :2048].T
        y_sample[sb] = yT[:, 2048:].T.reshape(16, 8, 1024)
        nk[:, c] = np.asarray(r["nk"]).reshape(4, 256, 4, 256)
        nv[:, c] = np.asarray(r["nv"]).reshape(4, 256, 4, 256)
        wkvp[:, c] = np.asarray(r["wkvp"]).transpose(0, 1, 3, 2)
        convp[:, c] = np.asarray(r["convp"]).transpose(0, 2, 1)
        shiftp[:, c] = np.asarray(r["shiftp"]).transpose(0, 2, 1).reshape(4, 1024)
        wkvs[:, sb] = np.asarray(r["wkvs"]).transpose(0, 1, 2, 4, 3)
        convs[:, sb] = np.asarray(r["convs"]).transpose(0, 1, 3, 2)
        shifts[:, sb] = np.asarray(r["shifts"]).transpose(0, 3, 2, 1).reshape(4, 16, 1024)
    return (y_prompt, y_sample, nk, nv, wkvp, convp, shiftp, wkvs, convs, shifts)
```
